# Optimizing a Trainium2 kernel written in Bass

```python
import math
import jax, jax.numpy as jnp
from jax import lax
import numpy as np

D_MODEL = 1024
BATCH = 8
SEQ = 2048
DEPTH = 4

CTX_LEN = 256
GRID_W = 64
HEAD_DIM = 64
MIX_W = D_MODEL
N_A = (MIX_W // 2) // (2 * HEAD_DIM)
D_A = HEAD_DIM
N_B = (MIX_W // 4) // HEAD_DIM
D_BK = HEAD_DIM
D_BV = HEAD_DIM
RET_CHUNK = 128
N_C = (MIX_W // 4) // HEAD_DIM
N_C_KV = 2
C_GROUP = N_C // N_C_KV
D_C = HEAD_DIM
WINDOW = 128
Q_BLOCK = 128
D_FF = 4 * D_MODEL
ROPE_BASE = 10000.0
NORM_EPS = 1e-6
ADA_INIT = 0.5

PROJ_SIZES = (N_A * 2 * D_A, N_A * 2 * D_A, N_A * 2 * D_A,
              N_B * D_BK, N_B * D_BK, N_B * D_BV, N_B * D_BV,
              N_C * D_C, N_C_KV * D_C, N_C_KV * D_C)
D_PROJ = sum(PROJ_SIZES)
D_MIX_OUT = N_A * 2 * D_A + N_B * D_BV + N_C * D_C

kernel_name = "hybrid_parallel_mixer_dit"


def rms_norm(x, g):
    xf = x.astype(jnp.float32)
    y = xf * lax.rsqrt(jnp.mean(xf * xf, axis=-1, keepdims=True) + NORM_EPS)
    return (y * g.astype(jnp.float32)).astype(x.dtype)


def modulate(x, g, shift, scale):
    return rms_norm(x, g) * (1 + scale) + shift


def split_proj(p):
    idx = np.cumsum(PROJ_SIZES)[:-1].tolist()
    return jnp.split(p, idx, axis=-1)


def axial_rope_tables(T, d):
    rows = T // GRID_W
    r = jnp.repeat(jnp.arange(rows, dtype=jnp.float32), GRID_W)
    col = jnp.tile(jnp.arange(GRID_W, dtype=jnp.float32), rows)
    nf = d // 4
    inv = ROPE_BASE ** (-jnp.arange(nf, dtype=jnp.float32) / nf)
    ar = r[:, None] * inv
    ac = col[:, None] * inv
    ang = jnp.concatenate([ar, ar, ac, ac], axis=-1)
    return jnp.cos(ang), jnp.sin(ang)


def apply_rope(x, cos, sin):
    x1, x2, x3, x4 = jnp.split(x, 4, axis=-1)
    rot = jnp.concatenate([-x2, x1, -x4, x3], axis=-1)
    shp = (1, x.shape[1]) + (1,) * (x.ndim - 3) + (x.shape[-1],)
    return (x * cos.reshape(shp) + rot * sin.reshape(shp)).astype(x.dtype)


def diff_attn_blocks(q, k, v, lam):
    B, Tq, H, _, d = q.shape
    nb = Tq // Q_BLOCK
    qb = q.reshape(B, nb, Q_BLOCK, H, 2, d).swapaxes(0, 1)
    scale = d ** -0.5

    def one(qblk):
        s = jnp.einsum('bqhcd,bkhcd->bhcqk', qblk, k).astype(jnp.float32) * scale
        p = jax.nn.softmax(s, axis=-1)
        w = p[:, :, 0] - lam * p[:, :, 1]
        return jnp.einsum('bhqk,bkhe->bqhe', w.astype(v.dtype), v)

    o = lax.map(one, qb)
    return o.swapaxes(0, 1).reshape(B, Tq, H, -1)


def diff_attention_mixer(px, pc, lq1, lk1, lq2, lk2, subln_g, lam_init, cos, sin, need_ctx):
    qx, kx, vx = px
    qc, kc, vc = pc
    B, T, _ = qx.shape
    L = qc.shape[1]
    qx = apply_rope(qx.reshape(B, T, N_A, 2, D_A), cos, sin)
    kx = apply_rope(kx.reshape(B, T, N_A, 2, D_A), cos, sin)
    vx = vx.reshape(B, T, N_A, 2 * D_A)
    qc = qc.reshape(B, L, N_A, 2, D_A)
    kc = kc.reshape(B, L, N_A, 2, D_A)
    vc = vc.reshape(B, L, N_A, 2 * D_A)
    f32 = jnp.float32
    lam = (jnp.exp(jnp.sum(lq1.astype(f32) * lk1.astype(f32)))
           - jnp.exp(jnp.sum(lq2.astype(f32) * lk2.astype(f32))) + lam_init)

    def finish(o):
        o = rms_norm(o, subln_g) * (1.0 - lam_init)
        return o.reshape(o.shape[0], o.shape[1], -1)

    k_all = jnp.concatenate([kx, kc], axis=1)
    v_all = jnp.concatenate([vx, vc], axis=1)
    out_x = finish(diff_attn_blocks(qx, k_all, v_all, lam))
    out_c = finish(diff_attn_blocks(qc, kc, vc, lam)) if need_ctx else None
    return out_x, out_c


def retention_chunks(q, k, v, log_gamma, state0, inclusive):
    B, H, T, dk = q.shape
    dv = v.shape[-1]
    C = RET_CHUNK
    n = T // C

    def chunks(a):
        return a.reshape(B, H, n, C, a.shape[-1]).transpose(2, 0, 1, 3, 4)

    i = jnp.arange(C, dtype=jnp.float32)
    diff = i[:, None] - i[None, :]
    mask = (diff >= 0) if inclusive else (diff > 0)
    intra = jnp.where(mask, jnp.exp(jnp.where(mask, diff, 0.0)[None] * log_gamma[:, None, None]), 0.0)
    q_dec = jnp.exp((i + 1.0)[None] * log_gamma[:, None])
    k_dec = jnp.exp((C - 1.0 - i)[None] * log_gamma[:, None])
    c_dec = jnp.exp(C * log_gamma)[:, None, None]

    def step(state, qkv):
        qj, kj, vj = qkv
        s = jnp.einsum('bhid,bhmd->bhim', qj, kj) * intra
        o = (jnp.einsum('bhim,bhme->bhie', s, vj)
             + jnp.einsum('bhid,bhde->bhie', qj * q_dec[..., None], state))
        state = c_dec * state + jnp.einsum('bhmd,bhme->bhde', kj * k_dec[..., None], vj)
        return state, o

    _, o = lax.scan(step, state0, (chunks(q), chunks(k), chunks(v)))
    return o.transpose(1, 2, 0, 3, 4).reshape(B, H, T, dv)


def gated_group_norm(y, g):
    mu = jnp.mean(y, axis=-1, keepdims=True)
    yc = y - mu
    y = yc * lax.rsqrt(jnp.mean(yc * yc, axis=-1, keepdims=True) + NORM_EPS)
    B, H, T, dv = y.shape
    y = y.transpose(0, 2, 1, 3).reshape(B, T, H * dv)
    return (jax.nn.silu(g.astype(jnp.float32)) * y).astype(g.dtype)


def retention_mixer(px, pc, decay_f, decay_b, cos, sin, need_ctx):
    qx, kx, vx, gx = px
    qc, kc, vc, gc = pc
    B, T, _ = qx.shape
    L = qc.shape[1]
    f32 = jnp.float32
    kscale = D_BK ** -0.5

    def bhtd(a, d):
        return a.reshape(a.shape[0], a.shape[1], N_B, d).astype(f32).transpose(0, 2, 1, 3)

    qx_ = apply_rope(qx.reshape(B, T, N_B, D_BK), cos, sin).astype(f32).transpose(0, 2, 1, 3)
    kx_ = (apply_rope(kx.reshape(B, T, N_B, D_BK), cos, sin).astype(f32) * kscale).transpose(0, 2, 1, 3)
    vx_ = bhtd(vx, D_BV)
    qc_ = bhtd(qc, D_BK)
    kc_ = bhtd(kc, D_BK) * kscale
    vc_ = bhtd(vc, D_BV)
    lg_f = jax.nn.log_sigmoid(decay_f.astype(f32))
    lg_b = jax.nn.log_sigmoid(decay_b.astype(f32))

    pos = jnp.arange(L, dtype=f32)
    s_f = jnp.einsum('bhld,bhle,hl->bhde', kc_, vc_, jnp.exp((L - 1.0 - pos)[None] * lg_f[:, None]))
    s_b = jnp.einsum('bhld,bhle,hl->bhde', kc_, vc_, jnp.exp(pos[None] * lg_b[:, None]))

    def flip(a):
        return a[:, :, ::-1]

    of = retention_chunks(qx_, kx_, vx_, lg_f, s_f, True)
    ob = flip(retention_chunks(flip(qx_), flip(kx_), flip(vx_), lg_b, s_b, False))
    out_x = gated_group_norm(of + ob, gx)
    out_c = None
    if need_ctx:
        z = jnp.zeros((B, N_B, D_BK, D_BV), f32)
        ocf = retention_chunks(qc_, kc_, vc_, lg_f, z, True)
        ocb = flip(retention_chunks(flip(qc_), flip(kc_), flip(vc_), lg_b, z, False))
        out_c = gated_group_norm(ocf + ocb, gc)
    return out_x, out_c


def sink_softmax(s, sink):
    m = jnp.maximum(jnp.max(s, axis=-1, keepdims=True), sink)
    p = jnp.exp(s - m)
    return p / (jnp.sum(p, axis=-1, keepdims=True) + jnp.exp(sink - m))


def window_gqa_mixer(px, pc, sink, cos, sin, need_ctx):
    qx, kx, vx = px
    qc, kc, vc = pc
    B, T, _ = qx.shape
    L = qc.shape[1]
    WB = WINDOW
    nb = T // WB
    scale = D_C ** -0.5
    sink_b = sink.astype(jnp.float32).reshape(N_C_KV, C_GROUP, 1, 1)
    qx = apply_rope(qx.reshape(B, T, N_C_KV, C_GROUP, D_C), cos, sin)
    kx = apply_rope(kx.reshape(B, T, N_C_KV, D_C), cos, sin)
    vx = vx.reshape(B, T, N_C_KV, D_C)
    qc = qc.reshape(B, L, N_C_KV, C_GROUP, D_C)
    kc = kc.reshape(B, L, N_C_KV, D_C)
    vc = vc.reshape(B, L, N_C_KV, D_C)

    qb = qx.reshape(B, nb, WB, N_C_KV, C_GROUP, D_C)

    def band(a):
        ab = jnp.pad(a.reshape(B, nb, WB, N_C_KV, D_C), ((0, 0), (1, 1), (0, 0), (0, 0), (0, 0)))
        return jnp.concatenate([ab[:, :-2], ab[:, 1:-1], ab[:, 2:]], axis=2)

    kwin = band(kx)
    vwin = band(vx)
    blk = jnp.arange(nb)[:, None, None]
    qpos = blk * WB + jnp.arange(WB)[None, :, None]
    kpos = (blk - 1) * WB + jnp.arange(3 * WB)[None, None, :]
    valid = (kpos >= 0) & (kpos < T) & (jnp.abs(kpos - qpos) <= WINDOW)
    s_loc = jnp.einsum('bnqhgd,bnkhd->bnhgqk', qb, kwin).astype(jnp.float32) * scale
    s_loc = jnp.where(valid[None, :, None, None], s_loc, -jnp.inf)
    s_ctx = jnp.einsum('bnqhgd,bchd->bnhgqc', qb, kc).astype(jnp.float32) * scale
    p = sink_softmax(jnp.concatenate([s_loc, s_ctx], axis=-1), sink_b)
    p = p.astype(vx.dtype)
    o = (jnp.einsum('bnhgqk,bnkhd->bnqhgd', p[..., :3 * WB], vwin)
         + jnp.einsum('bnhgqc,bchd->bnqhgd', p[..., 3 * WB:], vc))
    out_x = o.reshape(B, T, N_C * D_C)
    out_c = None
    if need_ctx:
        s = jnp.einsum('bqhgd,bkhd->bhgqk', qc, kc).astype(jnp.float32) * scale
        pc_ = sink_softmax(s, sink_b).astype(vc.dtype)
        out_c = jnp.einsum('bhgqk,bkhd->bqhgd', pc_, vc).reshape(B, L, N_C * D_C)
    return out_x, out_c


def sq_relu_mlp(h, w1, w2):
    a = jax.nn.relu(h @ w1)
    return (a * a) @ w2


def setup_inputs(seed: int = 0) -> dict:
    key = jax.random.key(seed)
    ks = jax.random.split(key, 24)
    f32 = jnp.float32

    def nrm(k, shape, scale):
        return jax.random.normal(k, shape, f32) * scale

    ret_base = jnp.log(2.0 ** (5.0 + jnp.arange(N_B, dtype=f32)) - 1.0)
    return {
        "x": nrm(ks[0], (BATCH, SEQ, D_MODEL), 1.0),
        "c": nrm(ks[1], (BATCH, D_MODEL), 1.0),
        "ctx": nrm(ks[2], (BATCH, CTX_LEN, D_MODEL), 1.0),
        "c_ctx": nrm(ks[3], (D_MODEL,), 1.0),
        "w_ada": nrm(ks[4], (DEPTH, D_MODEL, 6 * D_MODEL), ADA_INIT * D_MODEL ** -0.5),
        "b_ada": nrm(ks[5], (DEPTH, 6 * D_MODEL), 0.01),
        "g_mix": 1.0 + nrm(ks[6], (DEPTH, D_MODEL), 0.05),
        "g_mlp": 1.0 + nrm(ks[7], (DEPTH, D_MODEL), 0.05),
        "w_in": nrm(ks[8], (DEPTH, D_MODEL, D_PROJ), D_MODEL ** -0.5),
        "w_out": nrm(ks[9], (DEPTH, D_MIX_OUT, D_MODEL), D_MIX_OUT ** -0.5),
        "lam_q1": nrm(ks[10], (DEPTH, D_A), 0.1),
        "lam_k1": nrm(ks[11], (DEPTH, D_A), 0.1),
        "lam_q2": nrm(ks[12], (DEPTH, D_A), 0.1),
        "lam_k2": nrm(ks[13], (DEPTH, D_A), 0.1),
        "subln_g": 1.0 + nrm(ks[14], (DEPTH, 2 * D_A), 0.05),
        "ret_decay_fwd": ret_base[None] + nrm(ks[15], (DEPTH, N_B), 0.1),
        "ret_decay_bwd": ret_base[None] + nrm(ks[16], (DEPTH, N_B), 0.1),
        "sink_logit": nrm(ks[17], (DEPTH, N_C), 0.5),
        "w_mlp1": nrm(ks[18], (DEPTH, D_MODEL, D_FF), D_MODEL ** -0.5),
        "w_mlp2": nrm(ks[19], (DEPTH, D_FF, D_MODEL), D_FF ** -0.5),
        "g_final": 1.0 + nrm(ks[20], (D_MODEL,), 0.05),
    }


def reference(x, c, ctx, c_ctx, w_ada, b_ada, g_mix, g_mlp, w_in, w_out, lam_q1, lam_k1, lam_q2,
              lam_k2, subln_g, ret_decay_fwd, ret_decay_bwd, sink_logit, w_mlp1, w_mlp2, g_final):
    T = x.shape[1]
    cos, sin = axial_rope_tables(T, HEAD_DIM)
    silu_c = jax.nn.silu(c)
    silu_cc = jax.nn.silu(c_ctx)[None]
    for l in range(DEPTH):
        need_ctx = l < DEPTH - 1
        mx = (silu_c @ w_ada[l] + b_ada[l])[:, None, :]
        mc = (silu_cc @ w_ada[l] + b_ada[l])[:, None, :]
        sh1, sc1, gt1, sh2, sc2, gt2 = jnp.split(mx, 6, axis=-1)
        csh1, csc1, cgt1, csh2, csc2, cgt2 = jnp.split(mc, 6, axis=-1)
        hx = modulate(x, g_mix[l], sh1, sc1)
        hc = modulate(ctx, g_mix[l], csh1, csc1)
        px = split_proj(hx @ w_in[l])
        pc = split_proj(hc @ w_in[l])
        lam_init = 0.8 - 0.6 * math.exp(-0.3 * l)
        ax, ac = diff_attention_mixer(px[0:3], pc[0:3], lam_q1[l], lam_k1[l], lam_q2[l], lam_k2[l],
                                      subln_g[l], lam_init, cos, sin, need_ctx)
        bx, bc = retention_mixer(px[3:7], pc[3:7], ret_decay_fwd[l], ret_decay_bwd[l], cos, sin, need_ctx)
        cx, cc = window_gqa_mixer(px[7:10], pc[7:10], sink_logit[l], cos, sin, need_ctx)
        x = x + gt1 * (jnp.concatenate([ax, bx, cx], axis=-1) @ w_out[l])
        x = x + gt2 * sq_relu_mlp(modulate(x, g_mlp[l], sh2, sc2), w_mlp1[l], w_mlp2[l])
        if need_ctx:
            ctx = ctx + cgt1 * (jnp.concatenate([ac, bc, cc], axis=-1) @ w_out[l])
            ctx = ctx + cgt2 * sq_relu_mlp(modulate(ctx, g_mlp[l], csh2, csc2), w_mlp1[l], w_mlp2[l])
    return rms_norm(x, g_final)
```

```python
import contextlib
import math
import numpy as np
import ml_dtypes
import concourse.bass as bass
import concourse.mybir as mybir
from concourse.bass_utils import run_bass_kernel_spmd

F32 = mybir.dt.float32
BF16 = mybir.dt.bfloat16
ALU = mybir.AluOpType
AF = mybir.ActivationFunctionType
AX = mybir.AxisListType

D = 1024
T = 2048
L = 256
NT = T + L
DEPTH = 4
KC = D // 128
NTB = NT // 128
EPS = 1e-6
TGS = [(0, 512), (512, 512), (1024, 512), (1536, 512), (2048, 256)]


class Op:
    __slots__ = ("eng", "fn", "idx", "deps", "raw", "inc", "count", "dma", "is_dma")

    def __init__(self, eng, fn, idx):
        self.eng = eng
        self.fn = fn
        self.idx = idx
        self.deps = set()
        self.raw = set()
        self.inc = False
        self.count = 0
        self.dma = None
        self.is_dma = False


class Sched:
    ENGS = ["pe", "act", "dve", "pool", "sp"]

    def __init__(self):
        self.ops = {e: [] for e in self.ENGS}
        self.last_w = {}
        self.readers = {}
        self.dma_n = {}

    def add(self, eng, fn, reads=(), writes=(), dma=None):
        op = Op(eng, fn, len(self.ops[eng]))
        xr = [r for r in reads if isinstance(r, tuple) and str(r[0]).startswith(("ps", "oacc"))]
        if xr:
            for r in xr:
                w = self.last_w.get(r)
                if w is not None:
                    op.raw.add(w)
            writes = list(writes) + [r for r in xr if r not in writes]
        for r in reads:
            w = self.last_w.get(r)
            if w is not None:
                op.deps.add(w)
                op.raw.add(w)
        for r in writes:
            w = self.last_w.get(r)
            if w is not None:
                op.deps.add(w)
            rd = self.readers.get(r)
            if rd:
                for o in rd.values():
                    op.deps.add(o)
        for r in reads:
            d = self.readers.setdefault(r, {})
            if dma is not None:
                d[("dma", id(op))] = op
            else:
                d[eng] = op
        for r in writes:
            self.last_w[r] = op
            self.readers[r] = {}
        if dma is not None:
            n = self.dma_n.get(dma, 0) + 1
            self.dma_n[dma] = n
            op.dma = (dma, n)
            op.is_dma = True
        self.ops[eng].append(op)
        return op

    def _needs_wait(self, op, d):
        if d.is_dma:
            return True
        if d.eng != op.eng:
            return True
        if op.is_dma:
            return True
        if op.eng == "pe":
            return False
        return (d in op.raw) and (op.idx - d.idx <= 4)

    def emit(self, nc, stack):
        for e in self.ENGS:
            for op in self.ops[e]:
                for d in op.deps:
                    if not d.is_dma and self._needs_wait(op, d):
                        d.inc = True
        for e in self.ENGS:
            c = 0
            for op in self.ops[e]:
                if op.inc:
                    c += 1
                op.count = c
        sems = {e: stack.enter_context(nc.semaphore("s_" + e)) for e in self.ENGS}
        dsems = {k: stack.enter_context(nc.semaphore("d_%s" % (k,))) for k in self.dma_n}
        block = stack.enter_context(nc.Block())
        stats = {}

        def run(ename, engine):
            known = {}
            nw = 0
            for op in self.ops[ename]:
                need = {}
                for d in op.deps:
                    if not self._needs_wait(op, d):
                        continue
                    if d.is_dma:
                        key = ("d", d.dma[0])
                        val = 16 * d.dma[1]
                    else:
                        key = ("e", d.eng)
                        val = d.count
                    if val > need.get(key, 0):
                        need[key] = val
                for key, val in need.items():
                    if known.get(key, 0) >= val:
                        continue
                    known[key] = val
                    s = dsems[key[1]] if key[0] == "d" else sems[key[1]]
                    engine.wait_ge(s, val)
                    nw += 1
                ins = op.fn(engine)
                if op.is_dma:
                    ins.then_inc(dsems[op.dma[0]], 16)
                elif op.inc:
                    ins.then_inc(sems[ename], 1)
            for op in self.ops[ename]:
                if op.is_dma:
                    key = ("d", op.dma[0])
                    val = 16 * self.dma_n[op.dma[0]]
                    if known.get(key, 0) < val:
                        known[key] = val
                        engine.wait_ge(dsems[op.dma[0]], val)
            stats[ename] = (len(self.ops[ename]), nw)

        block.tensor(lambda e: run("pe", e))
        block.scalar(lambda e: run("act", e))
        block.vector(lambda e: run("dve", e))
        block.gpsimd(lambda e: run("pool", e))
        block.sync(lambda e: run("sp", e))
        self.stats = stats
        return stats


def R(t, *idx):
    return (t.name,) + idx


GBW = 2340
NGB = 8
LAM_INIT = [0.8 - 0.6 * math.exp(-0.3 * l) for l in range(DEPTH)]


def host_consts():
    c = {}
    bf = ml_dtypes.bfloat16
    c["ident_f"] = np.eye(128, dtype=np.float32)
    c["ident_b"] = np.eye(128, dtype=np.float32).astype(bf)
    c["ones_b"] = np.ones((128, 128), dtype=np.float32).astype(bf)
    pm = np.zeros((128, 128), np.float32)
    sign = np.zeros(128, np.float64)
    for f in range(128):
        fl = f % 64
        q = fl // 16
        src = f + 16 if q in (0, 2) else f - 16
        pm[src, f] = 1.0
        sign[f] = -1.0 if q in (0, 2) else 1.0
    c["perm_b"] = pm.astype(bf)
    t = np.arange(T)
    r = (t // 64).astype(np.float64)
    col = (t % 64).astype(np.float64)
    inv = 10000.0 ** (-np.arange(16, dtype=np.float64) / 16)
    ang64 = np.concatenate([r[:, None] * inv, r[:, None] * inv, col[:, None] * inv, col[:, None] * inv], axis=1)
    ang = np.concatenate([ang64, ang64], axis=1).T
    c["cosT"] = np.cos(ang).astype(np.float32).astype(bf)
    c["sinT"] = (np.sin(ang) * sign[:, None]).astype(np.float32).astype(bf)
    il = np.arange(128)
    mp = (il[None, :] <= il[:, None]).astype(np.float32)
    mn = (il[:, None] <= il[None, :]).astype(np.float32)
    c["maskp"] = np.concatenate([mp, mp], axis=1).astype(bf)
    c["maskn"] = np.concatenate([mn, mn], axis=1).astype(bf)
    s_ = il[:, None].astype(np.float32)
    t_ = il[None, :].astype(np.float32)
    c["r1"] = np.maximum(t_ - s_, 0.0).astype(np.float32)
    c["r2"] = np.maximum(s_ - t_, 0.0).astype(np.float32)
    c["i1"] = np.broadcast_to((il + 1.0)[None, :], (128, 128)).astype(np.float32).copy()
    c["i2"] = np.broadcast_to((128.0 - il)[None, :], (128, 128)).astype(np.float32).copy()
    c["kcol"] = np.stack([127.0 - il, il * 1.0], axis=1).astype(np.float32)
    return c


CONST_SPECS = [("ident_f", [128, 128], F32), ("ident_b", [128, 128], BF16), ("ones_b", [128, 128], BF16),
               ("perm_b", [128, 128], BF16), ("cosT", [128, T], BF16), ("sinT", [128, T], BF16),
               ("maskp", [128, 256], BF16), ("maskn", [128, 256], BF16),
               ("r1", [128, 128], F32), ("r2", [128, 128], F32), ("i1", [128, 128], F32), ("i2", [128, 128], F32),
               ("kcol", [128, 2], F32)]

PARAM_SPECS = [("g_final", [128, KC]), ("cc", [128, KC, 2]), ("bada", [128, DEPTH, 48]),
               ("gmix", [128, DEPTH, KC]), ("gmlp", [128, DEPTH, KC]),
               ("lamq", [128, DEPTH, 2, 64]), ("lamk", [128, DEPTH, 2, 64]), ("sublng", [128, DEPTH, 128]),
               ("decbc", [128, DEPTH, 8]), ("deccol", [128, DEPTH, 2, 2]), ("sinkbc", [128, DEPTH, 4])]


def prep_inputs(inputs, b, depth=DEPTH):
    f = np.float32
    m = {}
    m["x"] = np.ascontiguousarray(inputs["x"][b], dtype=f)
    m["ctx"] = np.ascontiguousarray(inputs["ctx"][b], dtype=f)
    m["g_final"] = np.ascontiguousarray(inputs["g_final"].reshape(KC, 128).T, dtype=f)
    cc = np.stack([inputs["c"][b].reshape(KC, 128).T, inputs["c_ctx"].reshape(KC, 128).T], axis=2)
    m["cc"] = np.ascontiguousarray(cc, dtype=f)
    m["bada"] = np.ascontiguousarray(inputs["b_ada"].reshape(DEPTH, 48, 128).transpose(2, 0, 1), dtype=f)
    m["gmix"] = np.ascontiguousarray(inputs["g_mix"].reshape(DEPTH, KC, 128).transpose(2, 0, 1), dtype=f)
    m["gmlp"] = np.ascontiguousarray(inputs["g_mlp"].reshape(DEPTH, KC, 128).transpose(2, 0, 1), dtype=f)
    lamq = np.stack([inputs["lam_q1"], inputs["lam_q2"]], axis=1)
    lamk = np.stack([inputs["lam_k1"], inputs["lam_k2"]], axis=1)
    m["lamq"] = np.ascontiguousarray(np.broadcast_to(lamq[None], (128, DEPTH, 2, 64)), dtype=f)
    m["lamk"] = np.ascontiguousarray(np.broadcast_to(lamk[None], (128, DEPTH, 2, 64)), dtype=f)
    m["sublng"] = np.ascontiguousarray(np.broadcast_to(inputs["subln_g"][None], (128, DEPTH, 128)), dtype=f)
    dec = np.concatenate([inputs["ret_decay_fwd"], inputs["ret_decay_bwd"]], axis=1)
    m["decbc"] = np.ascontiguousarray(np.broadcast_to(dec[None], (128, DEPTH, 8)), dtype=f)
    dcol = np.zeros((128, DEPTH, 2, 2), f)
    for di, nm in enumerate(["ret_decay_fwd", "ret_decay_bwd"]):
        for pp in range(2):
            dcol[0:64, :, di, pp] = inputs[nm][:, 2 * pp][None, :]
            dcol[64:128, :, di, pp] = inputs[nm][:, 2 * pp + 1][None, :]
    m["deccol"] = dcol
    m["sinkbc"] = np.ascontiguousarray(np.broadcast_to(inputs["sink_logit"][None], (128, DEPTH, 4)), dtype=f)
    for k in ["w_ada", "w_in", "w_out", "w_mlp1", "w_mlp2"]:
        m[k] = np.ascontiguousarray(inputs[k][:depth], dtype=f)
    m.update(host_consts())
    return m


class Builder:
    def __init__(self, cfg):
        self.cfg = cfg
        self.depth = cfg.get("depth", DEPTH)
        self.nc = bass.Bass("TRN2", target_bir_lowering=False)
        self.S = Sched()
        self.stack = contextlib.ExitStack()
        self.gp_i = 0
        self.ab_i = 0
        self.pt_i = 0
        self.wst_i = 0
        self.wo_i = 0
        self.oacc_i = 0
        self.rope_i = 0
        self.dbg_names = []
        self.deferred = []

    def dram_in(self, name, shape, dt=F32):
        return self.nc.dram_tensor(name, list(shape), dt, kind="ExternalInput").ap()

    def sb(self, name, shape, dt):
        return self.stack.enter_context(self.nc.sbuf_tensor(name, list(shape), dt))

    def ps(self, name, shape, dt=F32):
        return self.stack.enter_context(self.nc.psum_tensor(name, list(shape), dt))

    def dma(self, out, in_, reads=(), writes=(), key=None, eng="sp", **kw):
        self.S.add(eng, lambda e: e.dma_start(out=out, in_=in_, **kw), reads=reads, writes=writes, dma=key)

    def mm(self, out, lhsT, rhs, start=True, stop=True, reads=(), writes=(), sgc=False):
        if sgc:
            self.S.add("pe", lambda e: e.matmul(out, lhsT=lhsT, rhs=rhs, start=start, stop=stop, skip_group_check=True),
                       reads=reads, writes=writes)
        else:
            self.S.add("pe", lambda e: e.matmul(out, lhsT=lhsT, rhs=rhs, start=start, stop=stop),
                       reads=reads, writes=writes)

    def tr(self, out, in_, ident, reads=(), writes=()):
        self.S.add("pe", lambda e: e.transpose(out, in_, ident), reads=reads, writes=writes)

    def op(self, eng, meth, reads=(), writes=(), **kw):
        self.S.add(eng, lambda e: getattr(e, meth)(**kw), reads=reads, writes=writes)

    def gp(self):
        t = self.PSG[self.gp_i % len(self.PSG)]
        self.gp_i += 1
        return t

    def ab(self):
        i = self.ab_i % 4
        self.ab_i += 1
        return self.OACC[i // 2][:, (i % 2) * 512:(i % 2 + 1) * 512], ("oacc", i // 2, i % 2)

    def ptb(self):
        t = self.PT[self.pt_i % len(self.PT)]
        self.pt_i += 1
        return t

    def gb(self, i):
        return self.GBT[:, i * GBW:(i + 1) * GBW]

    def dbg(self, name, ap, shape, dt, reads):
        d = self.nc.dram_tensor(name, list(shape), dt, kind="ExternalOutput").ap()
        self.dma(d, ap, reads=reads, key="dbg_" + name)
        self.dbg_names.append(name)

    def build(self):
        nc, S = self.nc, self.S
        cfg = self.cfg
        self.x_d = self.dram_in("x", [T, D])
        self.ctx_d = self.dram_in("ctx", [L, D])
        self.out_d = nc.dram_tensor("out", [T, D], F32, kind="ExternalOutput").ap()
        self.cd = {n: self.dram_in(n, shp, dt) for n, shp, dt in CONST_SPECS}
        self.pd = {n: self.dram_in(n, shp, F32) for n, shp in PARAM_SPECS}
        self.wada_d = self.dram_in("w_ada", [self.depth, D, 6 * D])
        self.win_d = self.dram_in("w_in", [self.depth, D, 3 * D])
        self.wout_d = self.dram_in("w_out", [self.depth, D, D])
        self.w1_d = self.dram_in("w_mlp1", [self.depth, D, 4 * D])
        self.w2_d = self.dram_in("w_mlp2", [self.depth, 4 * D, D])

        self.XT = self.sb("XT", [128, KC, NT], F32)
        self.HT = self.sb("HT", [128, KC, NT], BF16)
        self.GBT = self.sb("GBT", [128, NGB * GBW], BF16)
        self.WST = [self.sb("WST%d" % i, [128, KC, 512], BF16) for i in range(2)]
        self.WO = [self.sb("WO%d" % i, [128, D], BF16) for i in range(2)]
        self.C = {}
        for n, shp, dt in CONST_SPECS:
            t = self.sb("c_" + n, shp, dt)
            self.C[n] = t
            self.dma(t[:], self.cd[n], writes=[R(t)], key="c_" + n)
        self.O0N = self.sb("o0n", [128, 4, 128], F32)
        self.OA = self.sb("oa", [128, 4, 128], F32)
        self.OB = self.sb("ob", [128, 4, 128], F32)
        self.GA = self.sb("GA", [128, DEPTH, 128], F32)
        self.P = {}
        alias = {"lamq": self.O0N, "lamk": self.OB, "sublng": self.GA}
        for n, shp in PARAM_SPECS:
            if n in alias:
                t = alias[n]
                self.dma(t[:].rearrange("p a b -> p (a b)"), self.pd[n].rearrange("p a b c -> p (a b c)") if len(shp) == 4 else self.pd[n].rearrange("p a b -> p (a b)"), writes=[R(t)], key="p_" + n)
            else:
                t = self.sb("p_" + n, shp, F32)
                self.dma(t[:], self.pd[n], writes=[R(t)], key="p_" + n)
            self.P[n] = t
        self.iobuf = [self.gb(i)[:, 0:2048].bitcast(F32) for i in range(2)]
        self.iores = [[(("GB", i), g) for g in range(5)] for i in range(2)]
        self.tmpf = [self.sb("tmpf%d" % i, [128, 512], F32) for i in range(4)]
        self.PT = [self.sb("pt%d" % i, [128, 512], BF16) for i in range(4)]
        self.sqb = [self.PT[0], self.PT[1]]
        self.ropeq = [self.PT[2], self.PT[3]]
        self.rstd = self.tmpf[2]
        self.rstd0 = self.tmpf[3]
        self.small = self.sb("small", [128, 64], F32)
        self.epsb = self.sb("epsb", [128, 4], F32)
        S.add("pool", lambda e: e.memset(self.epsb[:, 0:1], float(D * EPS)), writes=[R(self.epsb)])
        S.add("pool", lambda e: e.memset(self.epsb[:, 1:2], float(EPS)), writes=[R(self.epsb)])
        S.add("pool", lambda e: e.memset(self.epsb[:, 2:3], 1.0), writes=[R(self.epsb)])
        self.MOD = self.sb("MOD", [128, DEPTH, 48, 2], F32)
        self.GS = [self.sb("GS%d" % i, [128, DEPTH, KC, 2], F32) for i in range(2)]
        self.G32 = [self.sb("G32_%d" % i, [128, DEPTH, KC], F32) for i in range(2)]
        self.siluT = self.sb("siluT", [128, KC, 2], BF16)
        self.LG = self.sb("LG", [128, DEPTH, 8], F32)
        self.LGC = self.sb("LGC", [128, DEPTH, 2, 2], F32)
        self.ESINK = self.sb("ESINK", [128, DEPTH, 4], F32)
        self.NEGLAM = self.sb("NEGLAM", [128, DEPTH], F32)
        self.lame = self.sb("lame", [128, DEPTH, 2], F32)
        _tqf = self.sb("TQF", [128, 128], F32)
        _tqb = self.sb("TQB", [128, 128], F32)
        _dt = self.sb("DT", [128, 256], F32)
        _tk = self.sb("TK", [128, 4], F32)
        _cfb = self.sb("CFB", [128, 2], F32)
        self.TQF, self.TQB, self.DT, self.TK, self.CFB = [_tqf] * 2, [_tqb] * 2, [_dt] * 2, [_tk] * 2, [_cfb] * 2
        self.stf = self.sb("stf", [128, 128], F32)
        self.stb = self.sb("stb", [128, 128], F32)
        self.kfb = [self.sb("kfb%d" % i, [128, 128], BF16) for i in range(4)]
        self.qfb = [self.sb("qfb%d" % i, [128, 128], BF16) for i in range(4)]
        self.PSG = [self.ps("ps%d" % i, [128, 512], F32) for i in range(4)]
        self.OACC = [self.ps("oacc%d" % i, [128, 1024], F32) for i in range(2)]

        self.load_input()
        self.prologue_params()
        self.adaln_all()
        for l in range(self.depth):
            need_ctx = l < DEPTH - 1
            self.norm(l, 0, skip_ctx=False)
            if cfg.get("dbg_h") == l:
                self.dbg("dbg_h", self.HT[:], [128, KC, NT], BF16, [("HT", kc, g) for kc in range(KC) for g in range(5)])
            order = []
            mx = cfg.get("mixers", "ABC")
            if "A" in mx:
                order += [("A", h) for h in range(4)]
            if "B" in mx:
                order += [("B", 0), ("B", 1)]
            if "C" in mx:
                order += [("C", 0)]
            for kind, i in order:
                if kind == "A":
                    self.chunk_A(l, i, need_ctx)
                elif kind == "B":
                    self.chunk_B(l, i, need_ctx)
                else:
                    self.chunk_C(l, need_ctx)
            if cfg.get("mlp", True):
                self.mlp(l, need_ctx)
        if cfg.get("dbg_x"):
            self.dbg("dbg_x", self.XT[:], [128, KC, NT], F32, [("XT", kc, g) for kc in range(KC) for g in range(5)])
        self.drain()
        self.final_norm()
        S.emit(nc, self.stack)
        return nc

    def xt_res(self, kc, gi):
        return ("XT", kc, gi)

    def load_input(self):
        S = self.S
        XT, PS, identf = self.XT, self.PSG, self.C["ident_f"]
        for tb in range(NTB):
            buf = self.iobuf[tb % 2]
            bres = self.iores[tb % 2]
            gi = min(tb // 4, 4)
            src = self.x_d[tb * 128:(tb + 1) * 128, :] if tb < 16 else self.ctx_d[(tb - 16) * 128:(tb - 15) * 128, :]
            self.dma(buf, src, writes=bres, key="io%d" % (tb % 2))
            for half in range(2):
                pt = self.gp()
                for j in range(4):
                    kc = half * 4 + j
                    self.tr(pt[:, j * 128:(j + 1) * 128], buf[:, kc * 128:(kc + 1) * 128], identf[:],
                            reads=bres + [R(identf)], writes=[R(pt)])
                self.op("dve", "tensor_copy", out=XT[:, half * 4:half * 4 + 4, tb * 128:(tb + 1) * 128],
                        in_=pt[:].rearrange("p (j t) -> p j t", j=4),
                        reads=[R(pt)], writes=[("XT", k, gi) for k in range(half * 4, half * 4 + 4)])

    def prologue_params(self):
        P = self.P
        sm = self.small
        self.op("act", "activation", out=self.siluT[:], in_=P["cc"][:], func=AF.Silu,
                reads=[R(P["cc"])], writes=[R(self.siluT)])
        self.op("dve", "tensor_scalar", out=self.G32[0][:], in0=P["gmix"][:], scalar1=32.0, scalar2=None, op0=ALU.mult,
                reads=[R(P["gmix"])], writes=[R(self.G32[0])])
        self.op("dve", "tensor_scalar", out=self.G32[1][:], in0=P["gmlp"][:], scalar1=32.0, scalar2=None, op0=ALU.mult,
                reads=[R(P["gmlp"])], writes=[R(self.G32[1])])
        for src, dst, n in ((P["decbc"], self.LG, DEPTH * 8), (P["deccol"], self.LGC, DEPTH * 4)):
            sv = src[:].rearrange("p a b -> p (a b)") if len(src.shape) == 3 else src[:].rearrange("p a b c -> p (a b c)")
            dv = dst[:].rearrange("p a b -> p (a b)") if len(dst.shape) == 3 else dst[:].rearrange("p a b c -> p (a b c)")
            self.op("act", "activation", out=sm[:, 0:n], in_=sv, func=AF.Exp, scale=-1.0,
                    reads=[R(src)], writes=[R(sm)])
            self.op("act", "activation", out=sm[:, 32:32 + n], in_=sm[:, 0:n], func=AF.Ln, bias=self.epsb[:, 2:3], scale=1.0,
                    reads=[R(sm), R(self.epsb)], writes=[R(sm)])
            self.op("dve", "tensor_scalar", out=dv, in0=sm[:, 32:32 + n], scalar1=-1.0, scalar2=None, op0=ALU.mult,
                    reads=[R(sm)], writes=[R(dst)])
        self.op("act", "activation", out=self.ESINK[:], in_=P["sinkbc"][:], func=AF.Exp,
                reads=[R(P["sinkbc"])], writes=[R(self.ESINK)])
        fl = lambda t: t[:].rearrange("p a b -> p (a b)")
        self.op("dve", "tensor_tensor", out=fl(self.OA), in0=fl(P["lamq"]), in1=fl(P["lamk"]), op=ALU.mult,
                reads=[R(P["lamq"]), R(P["lamk"])], writes=[R(self.OA)])
        self.op("dve", "tensor_reduce", out=self.lame[:].rearrange("p a b -> p (a b)"),
                in_=fl(self.OA).rearrange("p (a c) -> p a c", c=64), axis=AX.X, op=ALU.add,
                reads=[R(self.OA)], writes=[R(self.lame)])
        self.op("act", "activation", out=self.lame[:], in_=self.lame[:], func=AF.Exp,
                reads=[R(self.lame)], writes=[R(self.lame)])
        for l in range(DEPTH):
            self.op("dve", "tensor_scalar", out=self.NEGLAM[:, l:l + 1], in0=self.lame[:, l, 1:2],
                    scalar1=self.lame[:, l, 0:1], scalar2=-LAM_INIT[l], op0=ALU.subtract, op1=ALU.add,
                    reads=[R(self.lame)], writes=[R(self.NEGLAM)])
            self.op("dve", "tensor_scalar", out=self.GA[:, l, :], in0=self.GA[:, l, :],
                    scalar1=1.0 - LAM_INIT[l], scalar2=None, op0=ALU.mult,
                    reads=[R(self.GA)], writes=[R(self.GA)])

    def wst_load(self, src3, pieces, keyname="wst"):
        i = self.wst_i % 2
        self.wst_i += 1
        buf = self.WST[i]
        res = ("WST", i)
        for (sc, dc, n) in pieces:
            self.dma(buf[:, :, dc:dc + n], src3[:, :, sc:sc + n], writes=[res], key="wst%d" % i, eng="pool")
        return buf, res

    def adaln_all(self):
        P = self.P
        for l in range(self.depth):
            src3 = self.wada_d[l].rearrange("(kc p) n -> p kc n", p=128)
            for piece in range(12):
                buf, res = self.wst_load(src3, [(piece * 512, 0, 512)])
                pst = self.gp()
                for jc in range(4):
                    for kc in range(KC):
                        self.mm(pst[:, jc * 2:jc * 2 + 2], buf[:, kc, jc * 128:(jc + 1) * 128], self.siluT[:, kc, :],
                                start=(kc == 0), stop=(kc == KC - 1), reads=[res, R(self.siluT)], writes=[R(pst)])
                j0 = piece * 4
                self.op("dve", "tensor_tensor", out=self.MOD[:, l, j0:j0 + 4, :],
                        in0=pst[:, 0:8].rearrange("p (j s) -> p j s", s=2),
                        in1=P["bada"][:, l, j0:j0 + 4].unsqueeze(2).to_broadcast([128, 4, 2]), op=ALU.add,
                        reads=[R(pst), R(P["bada"])], writes=[("MOD", l)])
            for w, base in ((0, 8), (1, 32)):
                self.op("dve", "scalar_tensor_tensor", out=self.GS[w][:, l, :, :], in0=self.MOD[:, l, base:base + 8, :],
                        scalar=1.0, in1=self.G32[w][:, l, :].unsqueeze(2).to_broadcast([128, KC, 2]),
                        op0=ALU.add, op1=ALU.mult,
                        reads=[("MOD", l), R(self.G32[w])], writes=[("GS", w, l)])

    def sumsq_rstd(self, gi, acc):
        t0, n = TGS[gi]
        XT, onesb = self.XT, self.C["ones_b"]
        for kc in range(KC):
            sq = self.sqb[kc % 2]
            self.op("act", "activation", out=sq[:, :n], in_=XT[:, kc, t0:t0 + n], func=AF.Square,
                    reads=[("XT", kc, gi)], writes=[R(sq)])
            self.mm(acc[:, :n], onesb[:], sq[:, :n], start=(kc == 0), stop=(kc == KC - 1),
                    reads=[R(sq), R(onesb)], writes=[R(acc)])
        self.op("act", "activation", out=self.rstd0[:, :n], in_=acc[:, :n], func=AF.Ln, bias=self.epsb[:, 0:1], scale=1.0,
                reads=[R(acc), R(self.epsb)], writes=[R(self.rstd0)])
        self.op("act", "activation", out=self.rstd[:, :n], in_=self.rstd0[:, :n], func=AF.Exp, scale=-0.5,
                reads=[R(self.rstd0)], writes=[R(self.rstd)])

    def norm(self, l, w, skip_ctx):
        shbase = 0 if w == 0 else 24
        for gi, (t0, n) in enumerate(TGS):
            if gi == 4 and skip_ctx:
                continue
            s = 0 if gi < 4 else 1
            acc = self.gp()
            self.sumsq_rstd(gi, acc)
            for kc in range(KC):
                tm = self.tmpf[kc % 2]
                self.op("dve", "scalar_tensor_tensor", out=tm[:, :n], in0=self.XT[:, kc, t0:t0 + n],
                        scalar=self.GS[w][:, l, kc, s:s + 1], in1=self.rstd[:, :n], op0=ALU.mult, op1=ALU.mult,
                        reads=[("XT", kc, gi), ("GS", w, l), R(self.rstd)], writes=[R(tm)])
                self.op("act", "activation", out=self.HT[:, kc, t0:t0 + n], in_=tm[:, :n], func=AF.Identity,
                        bias=self.MOD[:, l, shbase + kc, s:s + 1], scale=1.0,
                        reads=[R(tm), ("MOD", l)], writes=[("HT", kc, gi)])

    def proj_fm(self, wbuf, wres, c0, dst, dname, rope, with_ctx, dst_fn=None):
        if dst_fn is None:
            dst_fn = lambda t0, n: [(dst[:, t0:t0 + n], slice(0, 128))]
        pending = None
        for gi, (t0, n) in enumerate(TGS):
            if gi == 4 and not with_ctx:
                continue
            ps = self.gp()
            for kc in range(KC):
                self.mm(ps[:, :n], wbuf[:, kc, c0:c0 + 128], self.HT[:, kc, t0:t0 + n], start=(kc == 0), stop=(kc == KC - 1),
                        reads=[wres, ("HT", kc, gi)], writes=[R(ps)])
            if rope and gi < 4:
                tail = self.rope_head(ps, n)
                if pending is not None:
                    self.rope_tail(*pending)
                pending = (ps, dst_fn(t0, n), t0, n, (dname, gi)) + tail
            else:
                for (oap, psl) in dst_fn(t0, n):
                    self.op("act", "activation", out=oap, in_=ps[psl, :n], func=AF.Copy,
                            reads=[R(ps)], writes=[(dname, gi)])
            self.drain(2)
        if pending is not None:
            self.rope_tail(*pending)

    def rope_head(self, ps, n):
        i = self.rope_i % 2
        self.rope_i += 1
        qb = self.ropeq[i]
        self.op("act", "activation", out=qb[:, :n], in_=ps[:, :n], func=AF.Copy, reads=[R(ps)], writes=[R(qb)])
        return (i, qb)

    def rope_tail(self, ps, dsts, t0, n, dres, i, qb):
        t1, t2 = self.tmpf[2 * i], self.tmpf[2 * i + 1]
        permb, cosT, sinT = self.C["perm_b"], self.C["cosT"], self.C["sinT"]
        ps2 = self.gp()
        self.mm(ps2[:, :n], permb[:], qb[:, :n], reads=[R(qb), R(permb)], writes=[R(ps2)])
        self.op("dve", "tensor_tensor", out=t1[:, :n], in0=ps[:, :n], in1=cosT[:, t0:t0 + n], op=ALU.mult,
                reads=[R(ps), R(cosT)], writes=[R(t1)])
        self.op("dve", "tensor_tensor", out=t2[:, :n], in0=ps2[:, :n], in1=sinT[:, t0:t0 + n], op=ALU.mult,
                reads=[R(ps2), R(sinT)], writes=[R(t2)])
        for (oap, psl) in dsts:
            self.op("pool", "tensor_tensor", out=oap, in0=t1[psl, :n], in1=t2[psl, :n], op=ALU.add,
                    reads=[R(t1), R(t2)], writes=[dres])

    def proj_tm(self, wbuf, wres, c0, dst_fn, dname, func=None):
        for tb0 in range(0, NTB, 4):
            nb = min(4, NTB - tb0)
            gi = min(tb0 // 4, 4)
            ps = self.gp()
            for j in range(nb):
                tb = tb0 + j
                for kc in range(KC):
                    self.mm(ps[:, j * 128:(j + 1) * 128], self.HT[:, kc, tb * 128:(tb + 1) * 128], wbuf[:, kc, c0:c0 + 128],
                            start=(kc == 0), stop=(kc == KC - 1), reads=[wres, ("HT", kc, gi)], writes=[R(ps)])
            out, in_ = dst_fn(tb0, nb, ps)
            self.op("act", "activation", out=out, in_=in_, func=(func or AF.Copy), reads=[R(ps)], writes=[(dname, gi)])
            self.drain(4)

    def finish_chunk(self, l, ci, OC3, ocname, OCT, octname, need_ctx):
        self.drain()
        i = self.wo_i % 2
        self.wo_i += 1
        WO = self.WO[i]
        wres = ("WO", i)
        self.dma(WO[:], self.wout_d[l][ci * 128:(ci + 1) * 128, :], writes=[wres], key="wo%d" % i, eng="pool")
        identb = self.C["ident_b"]
        ntb = NTB if need_ctx else 16
        for tb0 in range(0, ntb, 4):
            nb = min(4, ntb - tb0)
            gi = min(tb0 // 4, 4)
            pk = self.gp()
            pkb = pk[:].bitcast(BF16)
            for j in range(nb):
                self.tr(pkb[:, j * 128:(j + 1) * 128], OC3[:, tb0 + j, :], identb[:],
                        reads=[(ocname, gi), R(identb)], writes=[R(pk)])
            self.op("dve", "tensor_copy", out=OCT[:, tb0 * 128:(tb0 + nb) * 128], in_=pkb[:, 0:nb * 128],
                    reads=[R(pk)], writes=[(octname, gi)])
        for gi, (t0, n) in enumerate(TGS):
            if gi == 4 and not need_ctx:
                continue
            s = 0 if gi < 4 else 1
            for dc in range(KC):
                def item(gi=gi, t0=t0, n=n, s=s, dc=dc):
                    ps, pres = self.ab()
                    self.mm(ps[:, :n], WO[:, dc * 128:(dc + 1) * 128], OCT[:, t0:t0 + n],
                            reads=[wres, (octname, gi)], writes=[pres])
                    self.op("dve", "scalar_tensor_tensor", out=self.XT[:, dc, t0:t0 + n], in0=ps[:, :n],
                            scalar=self.MOD[:, l, 16 + dc, s:s + 1], in1=self.XT[:, dc, t0:t0 + n], op0=ALU.mult, op1=ALU.add,
                            reads=[pres, ("MOD", l), ("XT", dc, gi)], writes=[("XT", dc, gi)])
                self.deferred.append(item)

    def drain(self, k=None):
        while self.deferred and (k is None or k > 0):
            self.deferred.pop(0)()
            if k is not None:
                k -= 1

    def chunk_A(self, l, h, need_ctx):
        src3 = self.win_d[l].rearrange("(kc p) n -> p kc n", p=128)
        wbuf, wres = self.wst_load(src3, [(h * 128, 0, 128), (512 + h * 128, 128, 128), (1024 + h * 128, 256, 128)])
        par = h % 2
        QT, KT, V = self.gb(par * 3), self.gb(par * 3 + 1), self.gb(par * 3 + 2)
        qn, kn, vn = ("GB", par * 3), ("GB", par * 3 + 1), ("GB", par * 3 + 2)
        OC, OCT = self.gb(6), self.gb(7)
        V3 = V.rearrange("p (t c) -> p t c", c=130)
        OC3 = OC[:, 0:NT].rearrange("p (t c) -> p t c", c=128)
        allg = list(range(5))
        self.op("pool", "memset", ap=V3[:, :, 128:129], constant=1.0, writes=[(vn, g) for g in allg])
        stage = self.cfg.get("a_stage", 9)
        if stage < 0:
            return
        self.proj_fm(wbuf, wres, 0, QT, qn, stage >= 0.5, need_ctx)
        if stage < 0.7:
            return
        self.proj_fm(wbuf, wres, 128, KT, kn, True, True)
        if stage < 0.8:
            return
        self.proj_tm(wbuf, wres, 256,
                     lambda tb0, nb, ps: (V3[:, tb0:tb0 + nb, 0:128], ps[:, 0:nb * 128].rearrange("p (j c) -> p j c", c=128)),
                     vn)
        if stage < 2:
            return
        self.drain()
        for gi, (t0, n) in enumerate(TGS):
            if gi == 4 and not need_ctx:
                continue
            kbs = list(range(NTB)) if gi < 4 else [16, 17]
            nqb = n // 128
            nbk = nqb // 2
            aress = [[("oacc", c, 0), ("oacc", c, 1)] for c in range(2)]

            def emit_pv(ki, kb, pts):
                kgi = min(kb // 4, 4)
                for c in range(2):
                    acc = self.OACC[c]
                    for qb in range(nqb):
                        off = (qb // 2) * 512 + (qb % 2) * 129
                        self.mm(acc[:, off:off + 129], pts[c][:, qb * 128:(qb + 1) * 128], V3[:, kb, 0:129],
                                start=(ki == 0 and qb % 2 == 0), stop=(ki == len(kbs) - 1),
                                reads=[R(pts[c]), (vn, kgi)], writes=[aress[c][qb // 2]], sgc=True)

            pending = None
            for ki, kb in enumerate(kbs):
                kgi = min(kb // 4, 4)
                sts = [self.gp(), self.gp()]
                for c in range(2):
                    hs = slice(c * 64, (c + 1) * 64)
                    self.mm(sts[c][:, :n], KT[hs, kb * 128:(kb + 1) * 128], QT[hs, t0:t0 + n],
                            reads=[(kn, kgi), (qn, gi)], writes=[R(sts[c])])
                pts = [self.ptb(), self.ptb()]
                for c in range(2):
                    self.op("act", "activation", out=pts[c][:, :n], in_=sts[c][:, :n], func=AF.Exp, scale=0.125,
                            reads=[R(sts[c])], writes=[R(pts[c])])
                if pending is not None:
                    emit_pv(*pending)
                pending = (ki, kb, pts)
            emit_pv(*pending)
            if stage < 3:
                continue
            for c in range(2):
                acc = self.OACC[c]
                ares = aress[c]
                accv = acc[:].rearrange("p (b x) -> p b x", b=2)[:, 0:nbk, 0:258].rearrange("p b (j c) -> p b j c", c=129)
                zv = accv[:, :, :, 128]
                ov = accv[:, :, :, 0:128]
                sm = self.small
                rz = sm[:, 0:nqb].rearrange("p (b j) -> p b j", j=2)
                ar = ares[0:nbk]
                self.op("dve", "reciprocal", out=rz, in_=zv, reads=ar, writes=[R(sm)])
                o0 = self.O0N[:, 0:nqb, :].rearrange("p (b j) c -> p b j c", j=2)
                oa = self.OA[:, 0:nqb, :].rearrange("p (b j) c -> p b j c", j=2)
                if c == 0:
                    self.op("dve", "tensor_tensor", out=o0, in0=ov, in1=rz.unsqueeze(3).to_broadcast([128, nbk, 2, 128]),
                            op=ALU.mult, reads=ar + [R(sm)], writes=[R(self.O0N)])
                else:
                    rz1 = sm[:, 4:4 + nqb].rearrange("p (b j) -> p b j", j=2)
                    self.op("dve", "tensor_scalar", out=rz1, in0=rz, scalar1=self.NEGLAM[:, l:l + 1], scalar2=None,
                            op0=ALU.mult, reads=[R(sm), R(self.NEGLAM)], writes=[R(sm)])
                    self.op("dve", "tensor_tensor", out=oa, in0=ov, in1=rz1.unsqueeze(3).to_broadcast([128, nbk, 2, 128]),
                            op=ALU.mult, reads=ar + [R(sm)], writes=[R(self.OA)])
                    oaf = self.OA[:, 0:nqb, :]
                    obf = self.OB[:, 0:nqb, :]
                    self.op("pool", "tensor_tensor", out=oaf, in0=oaf, in1=self.O0N[:, 0:nqb, :], op=ALU.add,
                            reads=[R(self.OA), R(self.O0N)], writes=[R(self.OA)])
                    self.op("pool", "tensor_tensor", out=obf, in0=oaf, in1=oaf, op=ALU.mult,
                            reads=[R(self.OA)], writes=[R(self.OB)])
                    ss = sm[:, 8:8 + nqb]
                    self.op("dve", "tensor_reduce", out=ss, in_=obf, axis=AX.X, op=ALU.add,
                            reads=[R(self.OB)], writes=[R(sm)])
                    l1 = sm[:, 12:12 + nqb]
                    rs = sm[:, 16:16 + nqb]
                    self.op("act", "activation", out=l1, in_=ss, func=AF.Ln, bias=self.epsb[:, 1:2], scale=1.0 / 128.0,
                            reads=[R(sm), R(self.epsb)], writes=[R(sm)])
                    self.op("act", "activation", out=rs, in_=l1, func=AF.Exp, scale=-0.5,
                            reads=[R(sm)], writes=[R(sm)])
                    self.op("dve", "tensor_tensor", out=obf, in0=oaf, in1=rs.unsqueeze(2).to_broadcast([128, nqb, 128]),
                            op=ALU.mult, reads=[R(self.OA), R(sm)], writes=[R(self.OB)])
                    tbq = t0 // 128
                    self.op("pool", "tensor_tensor", out=OC3[:, tbq:tbq + nqb, :], in0=obf,
                            in1=self.GA[:, l, :].unsqueeze(1).to_broadcast([128, nqb, 128]), op=ALU.mult,
                            reads=[R(self.OB), R(self.GA)], writes=[(("GB", 6), gi)])
        if stage < 4:
            return
        self.finish_chunk(l, h, OC3, ("GB", 6), OCT, ("GB", 7), need_ctx)

    def ret_tables(self, l, pp):
        C = self.C
        if True:
            lgf = self.LGC[:, l, 0, pp:pp + 1]
            lgb = self.LGC[:, l, 1, pp:pp + 1]
            self.op("act", "activation", out=self.TQF[pp][:], in_=C["i1"][:], func=AF.Exp, scale=lgf,
                    reads=[R(C["i1"]), R(self.LGC)], writes=[R(self.TQF[pp])])
            self.op("act", "activation", out=self.TQB[pp][:], in_=C["i2"][:], func=AF.Exp, scale=lgb,
                    reads=[R(C["i2"]), R(self.LGC)], writes=[R(self.TQB[pp])])
            self.op("act", "activation", out=self.CFB[pp][:, 0:1], in_=lgf, func=AF.Exp, scale=128.0,
                    reads=[R(self.LGC)], writes=[R(self.CFB[pp])])
            self.op("act", "activation", out=self.CFB[pp][:, 1:2], in_=lgb, func=AF.Exp, scale=128.0,
                    reads=[R(self.LGC)], writes=[R(self.CFB[pp])])
            for h2 in range(2):
                h = 2 * pp + h2
                e1 = self.tmpf[0][:, 0:128]
                e2 = self.tmpf[1][:, 0:128]
                self.op("dve", "tensor_scalar", out=e1, in0=C["r1"][:], scalar1=self.LG[:, l, h:h + 1], scalar2=None,
                        op0=ALU.mult, reads=[R(C["r1"]), R(self.LG)], writes=[R(self.tmpf[0])])
                self.op("dve", "scalar_tensor_tensor", out=e2, in0=C["r2"][:], scalar=self.LG[:, l, 4 + h:5 + h], in1=e1,
                        op0=ALU.mult, op1=ALU.add, reads=[R(C["r2"]), R(self.LG), R(self.tmpf[0])], writes=[R(self.tmpf[1])])
                self.op("act", "activation", out=self.DT[pp][:, h2 * 128:(h2 + 1) * 128], in_=e2, func=AF.Exp,
                        reads=[R(self.tmpf[1])], writes=[R(self.DT[pp])])
            sm = self.small
            self.op("act", "activation", out=sm[:, 20:22], in_=self.LG[:, l, 2 * pp:2 * pp + 2], func=AF.Exp,
                    scale=C["kcol"][:, 0:1], reads=[R(self.LG), R(C["kcol"])], writes=[R(sm)])
            self.op("act", "activation", out=sm[:, 22:24], in_=self.LG[:, l, 4 + 2 * pp:6 + 2 * pp], func=AF.Exp,
                    scale=C["kcol"][:, 1:2], reads=[R(self.LG), R(C["kcol"])], writes=[R(sm)])
            self.op("dve", "tensor_scalar", out=self.TK[pp][:], in0=sm[:, 20:24], scalar1=0.125, scalar2=None, op0=ALU.mult,
                    reads=[R(sm)], writes=[R(self.TK[pp])])

    def chunk_B(self, l, pp, need_ctx):
        src3 = self.win_d[l].rearrange("(kc p) n -> p kc n", p=128)
        wbuf, wres = self.wst_load(src3, [(1536 + pp * 128, 0, 128), (1792 + pp * 128, 128, 128),
                                          (2048 + pp * 128, 256, 128), (2304 + pp * 128, 384, 128)])
        names = [("GB", i) for i in range(8)]
        QT, KT, V, G, SF, SB, OC, OCT = [self.gb(i) for i in range(8)]
        qn, kn, vn, gn, sfn, sbn, ocn, octn = names
        V3 = V.rearrange("p (t c) -> p t c", c=130)
        G3 = G[:, 0:NT].rearrange("p (t c) -> p t c", c=128)
        SF3 = SF[:, 0:NT].rearrange("p (t c) -> p t c", c=128)
        SB3 = SB[:, 0:NT].rearrange("p (t c) -> p t c", c=128)
        OC3 = OC[:, 0:NT].rearrange("p (t c) -> p t c", c=128)
        identb = self.C["ident_b"]
        self.ret_tables(l, pp)
        self.proj_fm(wbuf, wres, 0, QT, qn, True, need_ctx)
        self.proj_fm(wbuf, wres, 128, KT, kn, True, True)
        self.proj_tm(wbuf, wres, 256,
                     lambda tb0, nb, ps: (V3[:, tb0:tb0 + nb, 0:128], ps[:, 0:nb * 128].rearrange("p (j c) -> p j c", c=128)),
                     vn)
        self.proj_tm(wbuf, wres, 384,
                     lambda tb0, nb, ps: (G3[:, tb0:tb0 + nb, :], ps[:, 0:nb * 128].rearrange("p (j c) -> p j c", c=128)),
                     gn, func=AF.Silu)
        self.drain()
        orders = [[16, 17] + list(range(16)), [17, 16] + list(range(15, -1, -1))]
        sts_ = [self.stf, self.stb]
        ST3s = [SF3, SB3]
        stns = [sfn, sbn]
        for d in range(2):
            self.op("pool", "memset", ap=sts_[d][:], constant=0.0, writes=[R(sts_[d])])
        NS = len(orders[0])
        pks = {}
        pus = {}
        for i in range(NS + 3):
            for d in range(2):
                if i < NS:
                    tb = orders[d][i]
                    gi = min(tb // 4, 4)
                    pk = self.gp()
                    pkb = pk[:].bitcast(BF16)
                    self.tr(pkb[:, 0:128], KT[:, tb * 128:(tb + 1) * 128], identb[:], reads=[(kn, gi), R(identb)], writes=[R(pk)])
                    pks[(d, i)] = (pk, pkb)
                if 0 <= i - 1 < NS:
                    pk, pkb = pks.pop((d, i - 1))
                    kf = self.kfb[2 * d + (i - 1) % 2]
                    self.op("dve", "tensor_tensor", out=kf[:].rearrange("p (h c) -> p h c", h=2),
                            in0=pkb[:, 0:128].rearrange("p (h c) -> p h c", h=2),
                            in1=self.TK[pp][:, 2 * d:2 * d + 2].unsqueeze(2).to_broadcast([128, 2, 64]), op=ALU.mult,
                            reads=[R(pk), R(self.TK[pp])], writes=[R(kf)])
                if 0 <= i - 2 < NS:
                    tb = orders[d][i - 2]
                    gi = min(tb // 4, 4)
                    kf = self.kfb[2 * d + (i - 2) % 2]
                    pu, pures = self.ab()
                    self.mm(pu[:, 0:128], kf[:], V3[:, tb, 0:128], reads=[R(kf), (vn, gi)], writes=[pures])
                    pus[(d, i - 2)] = (pu, pures)
                if 0 <= i - 3 < NS:
                    tb = orders[d][i - 3]
                    gi = min(tb // 4, 4)
                    st = sts_[d]
                    pu, pures = pus.pop((d, i - 3))
                    self.op("pool", "tensor_copy", out=ST3s[d][:, tb, :], in_=st[:], reads=[R(st)], writes=[(stns[d], gi)])
                    self.op("dve", "scalar_tensor_tensor", out=st[:], in0=st[:], scalar=self.CFB[pp][:, d:d + 1], in1=pu[:, 0:128],
                            op0=ALU.mult, op1=ALU.add, reads=[R(st), R(self.CFB[pp]), pures], writes=[R(st)])
        ntb = NTB if need_ctx else 16
        sm = self.small
        cur = {}

        def front(tb):
            gi = min(tb // 4, 4)
            ts_ = slice(tb * 128, (tb + 1) * 128)
            sts = [self.gp(), self.gp()]
            for h2 in range(2):
                hs = slice(h2 * 64, (h2 + 1) * 64)
                self.mm(sts[h2][:, 0:128], KT[hs, ts_], QT[hs, ts_], reads=[(kn, gi), (qn, gi)], writes=[R(sts[h2])])
            pt = self.ptb()
            for h2 in range(2):
                self.op("dve", "scalar_tensor_tensor", out=pt[:, h2 * 128:(h2 + 1) * 128], in0=sts[h2][:, 0:128], scalar=0.125,
                        in1=self.DT[pp][:, h2 * 128:(h2 + 1) * 128],
                        op0=ALU.mult, op1=ALU.mult, reads=[R(sts[h2]), R(self.DT[pp])], writes=[R(pt)])
            qf = self.qfb[(2 * tb) % 4]
            qb_ = self.qfb[(2 * tb + 1) % 4]
            self.op("pool", "tensor_tensor", out=qf[:], in0=QT[:, ts_], in1=self.TQF[pp][:], op=ALU.mult,
                    reads=[(qn, gi), R(self.TQF[pp])], writes=[R(qf)])
            self.op("pool", "tensor_tensor", out=qb_[:], in0=QT[:, ts_], in1=self.TQB[pp][:], op=ALU.mult,
                    reads=[(qn, gi), R(self.TQB[pp])], writes=[R(qb_)])
            return (pt, qf, qb_)

        def back(tb, pt, qf, qb_):
            gi = min(tb // 4, 4)
            jj = tb % 4
            if jj == 0:
                cur["po"], cur["pres"] = self.ab()
            po, pres = cur["po"], cur["pres"]
            for h2 in range(2):
                hs = slice(h2 * 64, (h2 + 1) * 64)
                oreg = po[:, jj * 128 + h2 * 64:jj * 128 + (h2 + 1) * 64]
                self.mm(oreg, qf[hs, :], SF3[hs, tb, hs], start=(jj == 0 and h2 == 0), stop=False,
                        reads=[R(qf), (sfn, gi)], writes=[pres], sgc=True)
                self.mm(oreg, qb_[hs, :], SB3[hs, tb, hs], start=False, stop=False,
                        reads=[R(qb_), (sbn, gi)], writes=[pres], sgc=True)
                self.mm(oreg, pt[:, h2 * 128:(h2 + 1) * 128], V3[:, tb, hs], start=False, stop=True,
                        reads=[R(pt), (vn, gi)], writes=[pres], sgc=True)
            last = (jj == 3) or (tb == ntb - 1)
            if not last:
                return
            tb0 = tb - jj
            nb = jj + 1
            ng = 2 * nb
            W = nb * 128
            v3 = lambda a: a[:, 0:W].rearrange("p (g c) -> p g c", c=64)
            osb = self.OA[:].rearrange("p a b -> p (a b)")
            ocn_ = self.OB[:].rearrange("p a b -> p (a b)")
            sqv = self.O0N[:].rearrange("p a b -> p (a b)")
            self.op("dve", "tensor_copy", out=osb[:, 0:W], in_=po[:, 0:W], reads=[pres], writes=[R(self.OA)])
            self.op("dve", "tensor_reduce", out=sm[:, 24:24 + ng], in_=v3(osb), axis=AX.X, op=ALU.add,
                    reads=[R(self.OA)], writes=[R(sm)])
            self.op("dve", "tensor_scalar", out=sm[:, 24:24 + ng], in0=sm[:, 24:24 + ng], scalar1=1.0 / 64.0, scalar2=None, op0=ALU.mult,
                    reads=[R(sm)], writes=[R(sm)])
            self.op("dve", "tensor_tensor", out=v3(ocn_), in0=v3(osb), in1=sm[:, 24:24 + ng].unsqueeze(2).to_broadcast([128, ng, 64]),
                    op=ALU.subtract, reads=[R(self.OA), R(sm)], writes=[R(self.OB)])
            self.op("pool", "tensor_tensor", out=sqv[:, 0:W], in0=ocn_[:, 0:W], in1=ocn_[:, 0:W], op=ALU.mult,
                    reads=[R(self.OB)], writes=[R(self.O0N)])
            self.op("dve", "tensor_reduce", out=sm[:, 44:44 + ng], in_=v3(sqv), axis=AX.X, op=ALU.add,
                    reads=[R(self.O0N)], writes=[R(sm)])
            self.op("act", "activation", out=sm[:, 52:52 + ng], in_=sm[:, 44:44 + ng], func=AF.Ln, bias=self.epsb[:, 1:2], scale=1.0 / 64.0,
                    reads=[R(sm), R(self.epsb)], writes=[R(sm)])
            self.op("act", "activation", out=sm[:, 24:24 + ng], in_=sm[:, 52:52 + ng], func=AF.Exp, scale=-0.5,
                    reads=[R(sm)], writes=[R(sm)])
            self.op("dve", "tensor_tensor", out=v3(osb), in0=v3(ocn_), in1=sm[:, 24:24 + ng].unsqueeze(2).to_broadcast([128, ng, 64]),
                    op=ALU.mult, reads=[R(self.OB), R(sm)], writes=[R(self.OA)])
            self.op("pool", "tensor_tensor", out=OC3[:, tb0:tb0 + nb, :], in0=osb[:, 0:W].rearrange("p (t c) -> p t c", c=128),
                    in1=G3[:, tb0:tb0 + nb, :], op=ALU.mult,
                    reads=[R(self.OA), (gn, gi)], writes=[(ocn, gi)])

        pending = None
        for tb in range(ntb):
            fr = front(tb)
            if pending is not None:
                back(*pending)
            pending = (tb,) + fr
        back(*pending)
        self.finish_chunk(l, 4 + pp, OC3, ocn, OCT, octn, need_ctx)

    def chunk_C(self, l, need_ctx):
        src3 = self.win_d[l].rearrange("(kc p) n -> p kc n", p=128)
        wbuf, wres = self.wst_load(src3, [(2560, 0, 256), (2816, 256, 64), (2816, 320, 64), (2880, 384, 64), (2880, 448, 64)])
        wbufv, wresv = self.wst_load(src3, [(2944, 0, 128)])
        QZ = self.GBT[:, 0:2 * NT].rearrange("p (g t) -> p g t", g=2)
        qzn = "QZ"
        gb01 = [(("GB", i), g_) for i in range(2) for g_ in range(5)]
        KT, V = self.gb(2), self.gb(3)
        kn, vn = ("GB", 2), ("GB", 3)
        OC, OCT = self.gb(6), self.gb(7)
        ocn, octn = ("GB", 6), ("GB", 7)
        V4 = V.rearrange("p (t j c) -> p t j c", j=2, c=65)
        OC3 = OC[:, 0:NT].rearrange("p (t c) -> p t c", c=128)
        allg = list(range(5))
        self.op("pool", "memset", ap=V4[:, :, :, 64:65], constant=1.0, writes=[(vn, g) for g in allg])
        self.op("pool", "memset", ap=QZ[64:128, 0, :], constant=0.0, writes=gb01 + [(qzn, g) for g in allg])
        self.op("pool", "memset", ap=QZ[0:64, 1, :], constant=0.0, writes=gb01 + [(qzn, g) for g in allg])
        self.proj_tm(wbufv, wresv, 0,
                     lambda tb0, nb, ps: (V4[:, tb0:tb0 + nb, :, 0:64],
                                          ps[:, 0:nb * 128].rearrange("p (t j c) -> p t j c", j=2, c=64)),
                     vn)
        ntb = NTB if need_ctx else 16
        sm = self.small
        qz_fn = lambda t0, n: [(QZ[0:64, 0, t0:t0 + n], slice(0, 64)), (QZ[64:128, 1, t0:t0 + n], slice(64, 128))]
        for j in range(2):
            self.proj_fm(wbuf, wres, j * 128, None, qzn, True, need_ctx, dst_fn=qz_fn)
            self.proj_fm(wbuf, wres, 256 + j * 128, KT, kn, True, True)
            self.drain()
            items = []
            for n_ in range(ntb):
                if n_ < 16:
                    kbs = ([n_ - 1] if n_ > 0 else []) + [n_] + ([n_ + 1] if n_ < 15 else []) + [16, 17]
                else:
                    kbs = [16, 17]
                for ki, m in enumerate(kbs):
                    items.append((n_, ki, m, len(kbs)))
            cur = {}

            def front(n_, ki, m, nk):
                gi = min(n_ // 4, 4)
                mgi = min(m // 4, 4)
                st = self.gp()
                self.mm(st[:, 0:256], KT[:, m * 128:(m + 1) * 128], QZ[:, :, n_ * 128:(n_ + 1) * 128],
                        reads=[(kn, mgi), (qzn, gi)], writes=[R(st)])
                pt = self.ptb()
                self.op("act", "activation", out=pt[:, 0:256], in_=st[:, 0:256], func=AF.Exp, scale=0.125,
                        reads=[R(st)], writes=[R(pt)])
                if n_ < 16 and m == n_ - 1:
                    self.op("pool", "tensor_tensor", out=pt[:, 0:256], in0=pt[:, 0:256], in1=self.C["maskp"][:], op=ALU.mult,
                            reads=[R(pt), R(self.C["maskp"])], writes=[R(pt)])
                if n_ < 15 and m == n_ + 1:
                    self.op("pool", "tensor_tensor", out=pt[:, 0:256], in0=pt[:, 0:256], in1=self.C["maskn"][:], op=ALU.mult,
                            reads=[R(pt), R(self.C["maskn"])], writes=[R(pt)])
                return pt

            def back(n_, ki, m, nk, pt):
                mgi = min(m // 4, 4)
                r3 = n_ % 3
                if r3 == 0 and ki == 0:
                    cur["po"], cur["pres"] = self.ab()
                    cur["n0"] = n_
                po, pres = cur["po"], cur["pres"]
                for g in range(2):
                    off = r3 * 130 + g * 65
                    self.mm(po[:, off:off + 65], pt[:, g * 128:(g + 1) * 128], V4[:, m, j, 0:65],
                            start=(r3 == 0 and ki == 0 and g == 0), stop=(ki == nk - 1), reads=[R(pt), (vn, mgi)], writes=[pres],
                            sgc=True)
                if ki == nk - 1 and (r3 == 2 or n_ == ntb - 1):
                    n0 = cur["n0"]
                    cnt = n_ - n0 + 1
                    pov = po[:, 0:cnt * 130].rearrange("p (n g c) -> p n g c", g=2, c=65)
                    den = sm[:, 34:34 + 2 * cnt].rearrange("p (n g) -> p n g", g=2)
                    rz = sm[:, 40:40 + 2 * cnt].rearrange("p (n g) -> p n g", g=2)
                    self.op("dve", "tensor_tensor", out=den, in0=pov[:, :, :, 64],
                            in1=self.ESINK[:, l, 2 * j:2 * j + 2].unsqueeze(1).to_broadcast([128, cnt, 2]), op=ALU.add,
                            reads=[pres, R(self.ESINK)], writes=[R(sm)])
                    self.op("dve", "reciprocal", out=rz, in_=den, reads=[R(sm)], writes=[R(sm)])
                    gis = sorted(set(min(q // 4, 4) for q in range(n0, n_ + 1)))
                    self.op("dve", "tensor_tensor", out=OC3[:, n0:n_ + 1, :].rearrange("p n (g c) -> p n g c", g=2),
                            in0=pov[:, :, :, 0:64], in1=rz.unsqueeze(3).to_broadcast([128, cnt, 2, 64]), op=ALU.mult,
                            reads=[pres, R(sm)], writes=[(ocn, g_) for g_ in gis])

            pending = None
            for it in items:
                pt = front(*it)
                if pending is not None:
                    back(*pending)
                pending = it + (pt,)
            back(*pending)
            self.finish_chunk(l, 6 + j, OC3, ocn, OCT, octn, need_ctx)

    def mlp(self, l, need_ctx):
        self.drain()
        self.norm(l, 1, skip_ctx=not need_ctx)
        w1v = self.w1_d[l].rearrange("(kc p) n -> p kc n", p=128)
        w2v = self.w2_d[l].rearrange("(fc p) n -> p fc n", p=128)
        pending = None
        ucount = 0

        def mlp2(W2, w2res, AT, ares, gi, t0, n, s):
            for dc in range(KC):
                ps2, pres = self.ab()
                for fc in range(4):
                    self.mm(ps2[:, :n], W2[:, fc, dc * 128:(dc + 1) * 128], AT[:, fc, :n], start=(fc == 0), stop=(fc == 3),
                            reads=[(w2res[0], 0), (w2res[1], 0), ares], writes=[pres])
                self.op("dve", "scalar_tensor_tensor", out=self.XT[:, dc, t0:t0 + n], in0=ps2[:, :n],
                        scalar=self.MOD[:, l, 40 + dc, s:s + 1], in1=self.XT[:, dc, t0:t0 + n], op0=ALU.mult, op1=ALU.add,
                        reads=[pres, ("MOD", l), ("XT", dc, gi)], writes=[("XT", dc, gi)])

        for fb in range(8):
            W1, w1res = self.wst_load(w1v, [(fb * 512, 0, 512)])
            i2 = fb % 2
            W2 = self.GBT[:, i2 * 2 * GBW:i2 * 2 * GBW + 4096].rearrange("p (f n) -> p f n", f=4)
            w2res = [("GB", 2 * i2), ("GB", 2 * i2 + 1)]
            self.dma(W2, w2v[:, fb * 4:(fb + 1) * 4, :], writes=[(r, g) for r in w2res for g in range(5)],
                     key="w2_%d" % i2, eng="pool")
            for gi, (t0, n) in enumerate(TGS):
                if gi == 4 and not need_ctx:
                    continue
                s = 0 if gi < 4 else 1
                ai = 4 + (ucount % 2)
                ucount += 1
                AT = self.gb(ai)[:, 0:2048].rearrange("p (f n) -> p f n", f=4)
                ares = (("GB", ai), 0)
                aresw = [(("GB", ai), g_) for g_ in range(5)]
                for fc in range(4):
                    ps = self.gp()
                    for kc in range(KC):
                        self.mm(ps[:, :n], W1[:, kc, fc * 128:(fc + 1) * 128], self.HT[:, kc, t0:t0 + n],
                                start=(kc == 0), stop=(kc == KC - 1), reads=[w1res, ("HT", kc, gi)], writes=[R(ps)])
                    rt = self.tmpf[fc % 4]
                    self.op("act", "activation", out=rt[:, :n], in_=ps[:, :n], func=AF.Relu, reads=[R(ps)], writes=[R(rt)])
                    self.op("pool", "tensor_tensor", out=AT[:, fc, :n], in0=rt[:, :n], in1=rt[:, :n], op=ALU.mult,
                            reads=[R(rt)], writes=aresw)
                if pending is not None:
                    mlp2(*pending)
                pending = (W2, w2res, AT, ares, gi, t0, n, s)
        mlp2(*pending)

    def final_norm(self):
        XT, identf = self.XT, self.C["ident_f"]
        gfin = self.P["g_final"]
        YT = self.HT[:].rearrange("p k t -> p (k t)")[:, 0:8192].bitcast(F32).rearrange("p (k t) -> p k t", k=KC)
        ytres = [("HT", kc, g) for kc in range(KC) for g in range(5)]
        for g in range(4):
            t0 = g * 512
            acc = self.gp()
            self.sumsq_rstd(g, acc)
            for kc in range(KC):
                self.op("dve", "scalar_tensor_tensor", out=YT[:, kc, :], in0=XT[:, kc, t0:t0 + 512], scalar=gfin[:, kc:kc + 1],
                        in1=self.rstd[:], op0=ALU.mult, op1=ALU.mult,
                        reads=[("XT", kc, g), R(self.rstd), R(gfin)], writes=(ytres if kc == 0 else []) + [("YT", kc)])
            for j in range(4):
                tb = g * 4 + j
                ob = self.iobuf[tb % 2]
                obres = self.iores[tb % 2]
                for half in range(2):
                    pt = self.gp()
                    for jj in range(4):
                        kc = half * 4 + jj
                        self.tr(pt[:, jj * 128:(jj + 1) * 128], YT[:, kc, j * 128:(j + 1) * 128], identf[:],
                                reads=[("YT", kc), ("HT", 0, 0), R(identf)], writes=[R(pt)])
                    self.op("act", "mul", out=ob[:, half * 512:(half + 1) * 512], in_=pt[:], mul=32.0,
                            reads=[R(pt)], writes=obres)
                self.dma(self.out_d[tb * 128:(tb + 1) * 128, :], ob, reads=obres, key="io%d" % (tb % 2))


_CACHE = {}


def kernel(**inputs):
    cfg = inputs.pop("_cfg", {}) if "_cfg" in inputs else {}
    inputs = {k: np.asarray(v) for k, v in inputs.items()}
    key = tuple(sorted(cfg.items()))
    if key not in _CACHE:
        b = Builder(cfg)
        _CACHE[key] = (b.build(), b)
    nc, b = _CACHE[key]
    in_maps = [prep_inputs(inputs, i, b.depth) for i in range(8)]
    res = run_bass_kernel_spmd(nc, in_maps, core_ids=list(range(8)))
    if cfg.get("_ret_all"):
        return res
    out = np.stack([np.asarray(r["out"]) for r in res.results], axis=0)
    return out.astype(np.float32)
```

```python
import contextlib
import math
import numpy as np
import ml_dtypes
import concourse.bass as bass
import concourse.mybir as mybir
from concourse.bass_utils import run_bass_kernel_spmd

F32 = mybir.dt.float32
BF16 = mybir.dt.bfloat16
ALU = mybir.AluOpType
AF = mybir.ActivationFunctionType
AX = mybir.AxisListType

D = 1024
T = 2048
L = 256
NT = T + L
DEPTH = 4
KC = D // 128
NTB = NT // 128
EPS = 1e-6
TGS = [(0, 512), (512, 512), (1024, 512), (1536, 512), (2048, 256)]


class Op:
    __slots__ = ("eng", "fn", "idx", "deps", "raw", "inc", "count", "dma", "is_dma")

    def __init__(self, eng, fn, idx):
        self.eng = eng
        self.fn = fn
        self.idx = idx
        self.deps = set()
        self.raw = set()
        self.inc = False
        self.count = 0
        self.dma = None
        self.is_dma = False


class Sched:
    ENGS = ["pe", "act", "dve", "pool", "sp"]

    def __init__(self):
        self.ops = {e: [] for e in self.ENGS}
        self.last_w = {}
        self.readers = {}
        self.dma_n = {}

    def add(self, eng, fn, reads=(), writes=(), dma=None):
        op = Op(eng, fn, len(self.ops[eng]))
        xr = [r for r in reads if isinstance(r, tuple) and str(r[0]).startswith(("ps", "oacc"))]
        if xr:
            for r in xr:
                w = self.last_w.get(r)
                if w is not None:
                    op.raw.add(w)
            writes = list(writes) + [r for r in xr if r not in writes]
        for r in reads:
            w = self.last_w.get(r)
            if w is not None:
                op.deps.add(w)
                op.raw.add(w)
        for r in writes:
            w = self.last_w.get(r)
            if w is not None:
                op.deps.add(w)
            rd = self.readers.get(r)
            if rd:
                for o in rd.values():
                    op.deps.add(o)
        for r in reads:
            d = self.readers.setdefault(r, {})
            if dma is not None:
                d[("dma", id(op))] = op
            else:
                d[eng] = op
        for r in writes:
            self.last_w[r] = op
            self.readers[r] = {}
        if dma is not None:
            n = self.dma_n.get(dma, 0) + 1
            self.dma_n[dma] = n
            op.dma = (dma, n)
            op.is_dma = True
        self.ops[eng].append(op)
        return op

    def _needs_wait(self, op, d):
        if d.is_dma:
            return True
        if d.eng != op.eng:
            return True
        if op.is_dma:
            return True
        if op.eng == "pe":
            return False
        return (d in op.raw) and (op.idx - d.idx <= 4)

    def emit(self, nc, stack):
        for e in self.ENGS:
            for op in self.ops[e]:
                for d in op.deps:
                    if not d.is_dma and self._needs_wait(op, d):
                        d.inc = True
        for e in self.ENGS:
            c = 0
            for op in self.ops[e]:
                if op.inc:
                    c += 1
                op.count = c
        sems = {e: stack.enter_context(nc.semaphore("s_" + e)) for e in self.ENGS}
        dsems = {k: stack.enter_context(nc.semaphore("d_%s" % (k,))) for k in self.dma_n}
        block = stack.enter_context(nc.Block())
        stats = {}

        def run(ename, engine):
            known = {}
            nw = 0
            for op in self.ops[ename]:
                need = {}
                for d in op.deps:
                    if not self._needs_wait(op, d):
                        continue
                    if d.is_dma:
                        key = ("d", d.dma[0])
                        val = 16 * d.dma[1]
                    else:
                        key = ("e", d.eng)
                        val = d.count
                    if val > need.get(key, 0):
                        need[key] = val
                for key, val in need.items():
                    if known.get(key, 0) >= val:
                        continue
                    known[key] = val
                    s = dsems[key[1]] if key[0] == "d" else sems[key[1]]
                    engine.wait_ge(s, val)
                    nw += 1
                ins = op.fn(engine)
                if op.is_dma:
                    ins.then_inc(dsems[op.dma[0]], 16)
                elif op.inc:
                    ins.then_inc(sems[ename], 1)
            for op in self.ops[ename]:
                if op.is_dma:
                    key = ("d", op.dma[0])
                    val = 16 * self.dma_n[op.dma[0]]
                    if known.get(key, 0) < val:
                        known[key] = val
                        engine.wait_ge(dsems[op.dma[0]], val)
            stats[ename] = (len(self.ops[ename]), nw)

        block.tensor(lambda e: run("pe", e))
        block.scalar(lambda e: run("act", e))
        block.vector(lambda e: run("dve", e))
        block.gpsimd(lambda e: run("pool", e))
        block.sync(lambda e: run("sp", e))
        self.stats = stats
        return stats


def R(t, *idx):
    return (t.name,) + idx


GBW = 2340
NGB = 8
LAM_INIT = [0.8 - 0.6 * math.exp(-0.3 * l) for l in range(DEPTH)]


def host_consts():
    c = {}
    bf = ml_dtypes.bfloat16
    c["ident_f"] = np.eye(128, dtype=np.float32)
    c["ident_b"] = np.eye(128, dtype=np.float32).astype(bf)
    c["ones_b"] = np.ones((128, 128), dtype=np.float32).astype(bf)
    pm = np.zeros((128, 128), np.float32)
    sign = np.zeros(128, np.float64)
    for f in range(128):
        fl = f % 64
        q = fl // 16
        src = f + 16 if q in (0, 2) else f - 16
        pm[src, f] = 1.0
        sign[f] = -1.0 if q in (0, 2) else 1.0
    c["perm_b"] = pm.astype(bf)
    t = np.arange(T)
    r = (t // 64).astype(np.float64)
    col = (t % 64).astype(np.float64)
    inv = 10000.0 ** (-np.arange(16, dtype=np.float64) / 16)
    ang64 = np.concatenate([r[:, None] * inv, r[:, None] * inv, col[:, None] * inv, col[:, None] * inv], axis=1)
    ang = np.concatenate([ang64, ang64], axis=1).T
    c["cosT"] = np.cos(ang).astype(np.float32).astype(bf)
    c["sinT"] = (np.sin(ang) * sign[:, None]).astype(np.float32).astype(bf)
    il = np.arange(128)
    mp = (il[None, :] <= il[:, None]).astype(np.float32)
    mn = (il[:, None] <= il[None, :]).astype(np.float32)
    c["maskp"] = np.concatenate([mp, mp], axis=1).astype(bf)
    c["maskn"] = np.concatenate([mn, mn], axis=1).astype(bf)
    s_ = il[:, None].astype(np.float32)
    t_ = il[None, :].astype(np.float32)
    c["r1"] = np.maximum(t_ - s_, 0.0).astype(np.float32)
    c["r2"] = np.maximum(s_ - t_, 0.0).astype(np.float32)
    c["i1"] = np.broadcast_to((il + 1.0)[None, :], (128, 128)).astype(np.float32).copy()
    c["i2"] = np.broadcast_to((128.0 - il)[None, :], (128, 128)).astype(np.float32).copy()
    c["kcol"] = np.stack([127.0 - il, il * 1.0], axis=1).astype(np.float32)
    return c


CONST_SPECS = [("ident_f", [128, 128], F32), ("ident_b", [128, 128], BF16), ("ones_b", [128, 128], BF16),
               ("perm_b", [128, 128], BF16), ("cosT", [128, T], BF16), ("sinT", [128, T], BF16),
               ("maskp", [128, 256], BF16), ("maskn", [128, 256], BF16),
               ("r1", [128, 128], F32), ("r2", [128, 128], F32), ("i1", [128, 128], F32), ("i2", [128, 128], F32),
               ("kcol", [128, 2], F32)]

PARAM_SPECS = [("g_final", [128, KC]), ("cc", [128, KC, 2]), ("bada", [128, DEPTH, 48]),
               ("gmix", [128, DEPTH, KC]), ("gmlp", [128, DEPTH, KC]),
               ("lamq", [128, DEPTH, 2, 64]), ("lamk", [128, DEPTH, 2, 64]), ("sublng", [128, DEPTH, 128]),
               ("decbc", [128, DEPTH, 8]), ("deccol", [128, DEPTH, 2, 2]), ("sinkbc", [128, DEPTH, 4])]


def prep_inputs(inputs, b, depth=DEPTH):
    f = np.float32
    m = {}
    m["x"] = np.ascontiguousarray(inputs["x"][b], dtype=f)
    m["ctx"] = np.ascontiguousarray(inputs["ctx"][b], dtype=f)
    m["g_final"] = np.ascontiguousarray(inputs["g_final"].reshape(KC, 128).T, dtype=f)
    cc = np.stack([inputs["c"][b].reshape(KC, 128).T, inputs["c_ctx"].reshape(KC, 128).T], axis=2)
    m["cc"] = np.ascontiguousarray(cc, dtype=f)
    m["bada"] = np.ascontiguousarray(inputs["b_ada"].reshape(DEPTH, 48, 128).transpose(2, 0, 1), dtype=f)
    m["gmix"] = np.ascontiguousarray(inputs["g_mix"].reshape(DEPTH, KC, 128).transpose(2, 0, 1), dtype=f)
    m["gmlp"] = np.ascontiguousarray(inputs["g_mlp"].reshape(DEPTH, KC, 128).transpose(2, 0, 1), dtype=f)
    lamq = np.stack([inputs["lam_q1"], inputs["lam_q2"]], axis=1)
    lamk = np.stack([inputs["lam_k1"], inputs["lam_k2"]], axis=1)
    m["lamq"] = np.ascontiguousarray(np.broadcast_to(lamq[None], (128, DEPTH, 2, 64)), dtype=f)
    m["lamk"] = np.ascontiguousarray(np.broadcast_to(lamk[None], (128, DEPTH, 2, 64)), dtype=f)
    m["sublng"] = np.ascontiguousarray(np.broadcast_to(inputs["subln_g"][None], (128, DEPTH, 128)), dtype=f)
    dec = np.concatenate([inputs["ret_decay_fwd"], inputs["ret_decay_bwd"]], axis=1)
    m["decbc"] = np.ascontiguousarray(np.broadcast_to(dec[None], (128, DEPTH, 8)), dtype=f)
    dcol = np.zeros((128, DEPTH, 2, 2), f)
    for di, nm in enumerate(["ret_decay_fwd", "ret_decay_bwd"]):
        for pp in range(2):
            dcol[0:64, :, di, pp] = inputs[nm][:, 2 * pp][None, :]
            dcol[64:128, :, di, pp] = inputs[nm][:, 2 * pp + 1][None, :]
    m["deccol"] = dcol
    m["sinkbc"] = np.ascontiguousarray(np.broadcast_to(inputs["sink_logit"][None], (128, DEPTH, 4)), dtype=f)
    for k in ["w_ada", "w_in", "w_out", "w_mlp1", "w_mlp2"]:
        m[k] = np.ascontiguousarray(inputs[k][:depth], dtype=f)
    m.update(host_consts())
    return m


class Builder:
    def __init__(self, cfg):
        self.cfg = cfg
        self.depth = cfg.get("depth", DEPTH)
        self.nc = bass.Bass("TRN2", target_bir_lowering=False)
        self.S = Sched()
        self.stack = contextlib.ExitStack()
        self.gp_i = 0
        self.ab_i = 0
        self.pt_i = 0
        self.wst_i = 0
        self.wo_i = 0
        self.oacc_i = 0
        self.rope_i = 0
        self.dbg_names = []
        self.deferred = []

    def dram_in(self, name, shape, dt=F32):
        return self.nc.dram_tensor(name, list(shape), dt, kind="ExternalInput").ap()

    def sb(self, name, shape, dt):
        return self.stack.enter_context(self.nc.sbuf_tensor(name, list(shape), dt))

    def ps(self, name, shape, dt=F32):
        return self.stack.enter_context(self.nc.psum_tensor(name, list(shape), dt))

    def dma(self, out, in_, reads=(), writes=(), key=None, eng="sp", **kw):
        self.S.add(eng, lambda e: e.dma_start(out=out, in_=in_, **kw), reads=reads, writes=writes, dma=key)

    def mm(self, out, lhsT, rhs, start=True, stop=True, reads=(), writes=(), sgc=False):
        if sgc:
            self.S.add("pe", lambda e: e.matmul(out, lhsT=lhsT, rhs=rhs, start=start, stop=stop, skip_group_check=True),
                       reads=reads, writes=writes)
        else:
            self.S.add("pe", lambda e: e.matmul(out, lhsT=lhsT, rhs=rhs, start=start, stop=stop),
                       reads=reads, writes=writes)

    def tr(self, out, in_, ident, reads=(), writes=()):
        self.S.add("pe", lambda e: e.transpose(out, in_, ident), reads=reads, writes=writes)

    def op(self, eng, meth, reads=(), writes=(), **kw):
        self.S.add(eng, lambda e: getattr(e, meth)(**kw), reads=reads, writes=writes)

    def gp(self):
        t = self.PSG[self.gp_i % len(self.PSG)]
        self.gp_i += 1
        return t

    def ab(self):
        i = self.ab_i % 4
        self.ab_i += 1
        return self.OACC[i // 2][:, (i % 2) * 512:(i % 2 + 1) * 512], ("oacc", i // 2, i % 2)

    def ptb(self):
        t = self.PT[self.pt_i % len(self.PT)]
        self.pt_i += 1
        return t

    def gb(self, i):
        return self.GBT[:, i * GBW:(i + 1) * GBW]

    def dbg(self, name, ap, shape, dt, reads):
        d = self.nc.dram_tensor(name, list(shape), dt, kind="ExternalOutput").ap()
        self.dma(d, ap, reads=reads, key="dbg_" + name)
        self.dbg_names.append(name)

    def build(self):
        nc, S = self.nc, self.S
        cfg = self.cfg
        self.x_d = self.dram_in("x", [T, D])
        self.ctx_d = self.dram_in("ctx", [L, D])
        self.out_d = nc.dram_tensor("out", [T, D], F32, kind="ExternalOutput").ap()
        self.cd = {n: self.dram_in(n, shp, dt) for n, shp, dt in CONST_SPECS}
        self.pd = {n: self.dram_in(n, shp, F32) for n, shp in PARAM_SPECS}
        self.wada_d = self.dram_in("w_ada", [self.depth, D, 6 * D])
        self.win_d = self.dram_in("w_in", [self.depth, D, 3 * D])
        self.wout_d = self.dram_in("w_out", [self.depth, D, D])
        self.w1_d = self.dram_in("w_mlp1", [self.depth, D, 4 * D])
        self.w2_d = self.dram_in("w_mlp2", [self.depth, 4 * D, D])

        self.XT = self.sb("XT", [128, KC, NT], F32)
        self.HT = self.sb("HT", [128, KC, NT], BF16)
        self.GBT = self.sb("GBT", [128, NGB * GBW], BF16)
        self.WST = [self.sb("WST%d" % i, [128, KC, 512], BF16) for i in range(2)]
        self.WO = [self.sb("WO%d" % i, [128, D], BF16) for i in range(2)]
        self.C = {}
        for n, shp, dt in CONST_SPECS:
            t = self.sb("c_" + n, shp, dt)
            self.C[n] = t
            self.dma(t[:], self.cd[n], writes=[R(t)], key="c_" + n)
        self.O0N = self.sb("o0n", [128, 4, 128], F32)
        self.OA = self.sb("oa", [128, 4, 128], F32)
        self.OB = self.sb("ob", [128, 4, 128], F32)
        self.GA = self.sb("GA", [128, DEPTH, 128], F32)
        self.P = {}
        alias = {"lamq": self.O0N, "lamk": self.OB, "sublng": self.GA}
        for n, shp in PARAM_SPECS:
            if n in alias:
                t = alias[n]
                self.dma(t[:].rearrange("p a b -> p (a b)"), self.pd[n].rearrange("p a b c -> p (a b c)") if len(shp) == 4 else self.pd[n].rearrange("p a b -> p (a b)"), writes=[R(t)], key="p_" + n)
            else:
                t = self.sb("p_" + n, shp, F32)
                self.dma(t[:], self.pd[n], writes=[R(t)], key="p_" + n)
            self.P[n] = t
        self.iobuf = [self.gb(i)[:, 0:2048].bitcast(F32) for i in range(2)]
        self.iores = [[(("GB", i), g) for g in range(5)] for i in range(2)]
        self.tmpf = [self.sb("tmpf%d" % i, [128, 512], F32) for i in range(4)]
        self.PT = [self.sb("pt%d" % i, [128, 512], BF16) for i in range(4)]
        self.sqb = [self.PT[0], self.PT[1]]
        self.ropeq = [self.PT[2], self.PT[3]]
        self.rstd = self.tmpf[2]
        self.rstd0 = self.tmpf[3]
        self.small = self.sb("small", [128, 64], F32)
        self.epsb = self.sb("epsb", [128, 4], F32)
        S.add("pool", lambda e: e.memset(self.epsb[:, 0:1], float(D * EPS)), writes=[R(self.epsb)])
        S.add("pool", lambda e: e.memset(self.epsb[:, 1:2], float(EPS)), writes=[R(self.epsb)])
        S.add("pool", lambda e: e.memset(self.epsb[:, 2:3], 1.0), writes=[R(self.epsb)])
        self.MOD = self.sb("MOD", [128, DEPTH, 48, 2], F32)
        self.GS = [self.sb("GS%d" % i, [128, DEPTH, KC, 2], F32) for i in range(2)]
        self.G32 = [self.sb("G32_%d" % i, [128, DEPTH, KC], F32) for i in range(2)]
        self.siluT = self.sb("siluT", [128, KC, 2], BF16)
        self.LG = self.sb("LG", [128, DEPTH, 8], F32)
        self.LGC = self.sb("LGC", [128, DEPTH, 2, 2], F32)
        self.ESINK = self.sb("ESINK", [128, DEPTH, 4], F32)
        self.NEGLAM = self.sb("NEGLAM", [128, DEPTH], F32)
        self.lame = self.sb("lame", [128, DEPTH, 2], F32)
        _tqf = self.sb("TQF", [128, 128], F32)
        _tqb = self.sb("TQB", [128, 128], F32)
        _dt = self.sb("DT", [128, 256], F32)
        _tk = self.sb("TK", [128, 4], F32)
        _cfb = self.sb("CFB", [128, 2], F32)
        self.TQF, self.TQB, self.DT, self.TK, self.CFB = [_tqf] * 2, [_tqb] * 2, [_dt] * 2, [_tk] * 2, [_cfb] * 2
        self.stf = self.sb("stf", [128, 128], F32)
        self.stb = self.sb("stb", [128, 128], F32)
        self.kfb = [self.sb("kfb%d" % i, [128, 128], BF16) for i in range(4)]
        self.qfb = [self.sb("qfb%d" % i, [128, 128], BF16) for i in range(4)]
        self.PSG = [self.ps("ps%d" % i, [128, 512], F32) for i in range(4)]
        self.OACC = [self.ps("oacc%d" % i, [128, 1024], F32) for i in range(2)]

        self.load_input()
        self.prologue_params()
        self.adaln_all()
        for l in range(self.depth):
            need_ctx = l < DEPTH - 1
            self.norm(l, 0, skip_ctx=False)
            if cfg.get("dbg_h") == l:
                self.dbg("dbg_h", self.HT[:], [128, KC, NT], BF16, [("HT", kc, g) for kc in range(KC) for g in range(5)])
            order = []
            mx = cfg.get("mixers", "ABC")
            if "A" in mx:
                order += [("A", h) for h in range(4)]
            if "B" in mx:
                order += [("B", 0), ("B", 1)]
            if "C" in mx:
                order += [("C", 0)]
            for kind, i in order:
                if kind == "A":
                    self.chunk_A(l, i, need_ctx)
                elif kind == "B":
                    self.chunk_B(l, i, need_ctx)
                else:
                    self.chunk_C(l, need_ctx)
            if cfg.get("mlp", True):
                self.mlp(l, need_ctx)
        if cfg.get("dbg_x"):
            self.dbg("dbg_x", self.XT[:], [128, KC, NT], F32, [("XT", kc, g) for kc in range(KC) for g in range(5)])
        self.drain()
        self.final_norm()
        S.emit(nc, self.stack)
        return nc

    def xt_res(self, kc, gi):
        return ("XT", kc, gi)

    def load_input(self):
        S = self.S
        XT, PS, identf = self.XT, self.PSG, self.C["ident_f"]
        for tb in range(NTB):
            buf = self.iobuf[tb % 2]
            bres = self.iores[tb % 2]
            gi = min(tb // 4, 4)
            src = self.x_d[tb * 128:(tb + 1) * 128, :] if tb < 16 else self.ctx_d[(tb - 16) * 128:(tb - 15) * 128, :]
            self.dma(buf, src, writes=bres, key="io%d" % (tb % 2))
            for half in range(2):
                pt = self.gp()
                for j in range(4):
                    kc = half * 4 + j
                    self.tr(pt[:, j * 128:(j + 1) * 128], buf[:, kc * 128:(kc + 1) * 128], identf[:],
                            reads=bres + [R(identf)], writes=[R(pt)])
                self.op("dve", "tensor_copy", out=XT[:, half * 4:half * 4 + 4, tb * 128:(tb + 1) * 128],
                        in_=pt[:].rearrange("p (j t) -> p j t", j=4),
                        reads=[R(pt)], writes=[("XT", k, gi) for k in range(half * 4, half * 4 + 4)])

    def prologue_params(self):
        P = self.P
        sm = self.small
        self.op("act", "activation", out=self.siluT[:], in_=P["cc"][:], func=AF.Silu,
                reads=[R(P["cc"])], writes=[R(self.siluT)])
        self.op("dve", "tensor_scalar", out=self.G32[0][:], in0=P["gmix"][:], scalar1=32.0, scalar2=None, op0=ALU.mult,
                reads=[R(P["gmix"])], writes=[R(self.G32[0])])
        self.op("dve", "tensor_scalar", out=self.G32[1][:], in0=P["gmlp"][:], scalar1=32.0, scalar2=None, op0=ALU.mult,
                reads=[R(P["gmlp"])], writes=[R(self.G32[1])])
        for src, dst, n in ((P["decbc"], self.LG, DEPTH * 8), (P["deccol"], self.LGC, DEPTH * 4)):
            sv = src[:].rearrange("p a b -> p (a b)") if len(src.shape) == 3 else src[:].rearrange("p a b c -> p (a b c)")
            dv = dst[:].rearrange("p a b -> p (a b)") if len(dst.shape) == 3 else dst[:].rearrange("p a b c -> p (a b c)")
            self.op("act", "activation", out=sm[:, 0:n], in_=sv, func=AF.Exp, scale=-1.0,
                    reads=[R(src)], writes=[R(sm)])
            self.op("act", "activation", out=sm[:, 32:32 + n], in_=sm[:, 0:n], func=AF.Ln, bias=self.epsb[:, 2:3], scale=1.0,
                    reads=[R(sm), R(self.epsb)], writes=[R(sm)])
            self.op("dve", "tensor_scalar", out=dv, in0=sm[:, 32:32 + n], scalar1=-1.0, scalar2=None, op0=ALU.mult,
                    reads=[R(sm)], writes=[R(dst)])
        self.op("act", "activation", out=self.ESINK[:], in_=P["sinkbc"][:], func=AF.Exp,
                reads=[R(P["sinkbc"])], writes=[R(self.ESINK)])
        fl = lambda t: t[:].rearrange("p a b -> p (a b)")
        self.op("dve", "tensor_tensor", out=fl(self.OA), in0=fl(P["lamq"]), in1=fl(P["lamk"]), op=ALU.mult,
                reads=[R(P["lamq"]), R(P["lamk"])], writes=[R(self.OA)])
        self.op("dve", "tensor_reduce", out=self.lame[:].rearrange("p a b -> p (a b)"),
                in_=fl(self.OA).rearrange("p (a c) -> p a c", c=64), axis=AX.X, op=ALU.add,
                reads=[R(self.OA)], writes=[R(self.lame)])
        self.op("act", "activation", out=self.lame[:], in_=self.lame[:], func=AF.Exp,
                reads=[R(self.lame)], writes=[R(self.lame)])
        for l in range(DEPTH):
            self.op("dve", "tensor_scalar", out=self.NEGLAM[:, l:l + 1], in0=self.lame[:, l, 1:2],
                    scalar1=self.lame[:, l, 0:1], scalar2=-LAM_INIT[l], op0=ALU.subtract, op1=ALU.add,
                    reads=[R(self.lame)], writes=[R(self.NEGLAM)])
            self.op("dve", "tensor_scalar", out=self.GA[:, l, :], in0=self.GA[:, l, :],
                    scalar1=1.0 - LAM_INIT[l], scalar2=None, op0=ALU.mult,
                    reads=[R(self.GA)], writes=[R(self.GA)])

    def wst_load(self, src3, pieces, keyname="wst"):
        i = self.wst_i % 2
        self.wst_i += 1
        buf = self.WST[i]
        res = ("WST", i)
        for (sc, dc, n) in pieces:
            self.dma(buf[:, :, dc:dc + n], src3[:, :, sc:sc + n], writes=[res], key="wst%d" % i, eng="pool")
        return buf, res

    def adaln_all(self):
        P = self.P
        for l in range(self.depth):
            src3 = self.wada_d[l].rearrange("(kc p) n -> p kc n", p=128)
            for piece in range(12):
                buf, res = self.wst_load(src3, [(piece * 512, 0, 512)])
                pst = self.gp()
                for jc in range(4):
                    for kc in range(KC):
                        self.mm(pst[:, jc * 2:jc * 2 + 2], buf[:, kc, jc * 128:(jc + 1) * 128], self.siluT[:, kc, :],
                                start=(kc == 0), stop=(kc == KC - 1), reads=[res, R(self.siluT)], writes=[R(pst)])
                j0 = piece * 4
                self.op("dve", "tensor_tensor", out=self.MOD[:, l, j0:j0 + 4, :],
                        in0=pst[:, 0:8].rearrange("p (j s) -> p j s", s=2),
                        in1=P["bada"][:, l, j0:j0 + 4].unsqueeze(2).to_broadcast([128, 4, 2]), op=ALU.add,
                        reads=[R(pst), R(P["bada"])], writes=[("MOD", l)])
            for w, base in ((0, 8), (1, 32)):
                self.op("dve", "scalar_tensor_tensor", out=self.GS[w][:, l, :, :], in0=self.MOD[:, l, base:base + 8, :],
                        scalar=1.0, in1=self.G32[w][:, l, :].unsqueeze(2).to_broadcast([128, KC, 2]),
                        op0=ALU.add, op1=ALU.mult,
                        reads=[("MOD", l), R(self.G32[w])], writes=[("GS", w, l)])

    def sumsq_rstd(self, gi, acc):
        t0, n = TGS[gi]
        XT, onesb = self.XT, self.C["ones_b"]
        for kc in range(KC):
            sq = self.sqb[kc % 2]
            self.op("act", "activation", out=sq[:, :n], in_=XT[:, kc, t0:t0 + n], func=AF.Square,
                    reads=[("XT", kc, gi)], writes=[R(sq)])
            self.mm(acc[:, :n], onesb[:], sq[:, :n], start=(kc == 0), stop=(kc == KC - 1),
                    reads=[R(sq), R(onesb)], writes=[R(acc)])
        self.op("act", "activation", out=self.rstd0[:, :n], in_=acc[:, :n], func=AF.Ln, bias=self.epsb[:, 0:1], scale=1.0,
                reads=[R(acc), R(self.epsb)], writes=[R(self.rstd0)])
        self.op("act", "activation", out=self.rstd[:, :n], in_=self.rstd0[:, :n], func=AF.Exp, scale=-0.5,
                reads=[R(self.rstd0)], writes=[R(self.rstd)])

    def norm(self, l, w, skip_ctx):
        shbase = 0 if w == 0 else 24
        for gi, (t0, n) in enumerate(TGS):
            if gi == 4 and skip_ctx:
                continue
            s = 0 if gi < 4 else 1
            acc = self.gp()
            self.sumsq_rstd(gi, acc)
            for kc in range(KC):
                tm = self.tmpf[kc % 2]
                self.op("dve", "scalar_tensor_tensor", out=tm[:, :n], in0=self.XT[:, kc, t0:t0 + n],
                        scalar=self.GS[w][:, l, kc, s:s + 1], in1=self.rstd[:, :n], op0=ALU.mult, op1=ALU.mult,
                        reads=[("XT", kc, gi), ("GS", w, l), R(self.rstd)], writes=[R(tm)])
                self.op("act", "activation", out=self.HT[:, kc, t0:t0 + n], in_=tm[:, :n], func=AF.Identity,
                        bias=self.MOD[:, l, shbase + kc, s:s + 1], scale=1.0,
                        reads=[R(tm), ("MOD", l)], writes=[("HT", kc, gi)])

    def proj_fm(self, wbuf, wres, c0, dst, dname, rope, with_ctx, dst_fn=None):
        if dst_fn is None:
            dst_fn = lambda t0, n: [(dst[:, t0:t0 + n], slice(0, 128))]
        pending = None
        for gi, (t0, n) in enumerate(TGS):
            if gi == 4 and not with_ctx:
                continue
            ps = self.gp()
            for kc in range(KC):
                self.mm(ps[:, :n], wbuf[:, kc, c0:c0 + 128], self.HT[:, kc, t0:t0 + n], start=(kc == 0), stop=(kc == KC - 1),
                        reads=[wres, ("HT", kc, gi)], writes=[R(ps)])
            if rope and gi < 4:
                tail = self.rope_head(ps, n)
                if pending is not None:
                    self.rope_tail(*pending)
                pending = (ps, dst_fn(t0, n), t0, n, (dname, gi)) + tail
            else:
                for (oap, psl) in dst_fn(t0, n):
                    self.op("act", "activation", out=oap, in_=ps[psl, :n], func=AF.Copy,
                            reads=[R(ps)], writes=[(dname, gi)])
        if pending is not None:
            self.rope_tail(*pending)

    def rope_head(self, ps, n):
        i = self.rope_i % 2
        self.rope_i += 1
        qb = self.ropeq[i]
        self.op("act", "activation", out=qb[:, :n], in_=ps[:, :n], func=AF.Copy, reads=[R(ps)], writes=[R(qb)])
        return (i, qb)

    def rope_tail(self, ps, dsts, t0, n, dres, i, qb):
        t1, t2 = self.tmpf[2 * i], self.tmpf[2 * i + 1]
        permb, cosT, sinT = self.C["perm_b"], self.C["cosT"], self.C["sinT"]
        ps2 = self.gp()
        self.mm(ps2[:, :n], permb[:], qb[:, :n], reads=[R(qb), R(permb)], writes=[R(ps2)])
        self.op("dve", "tensor_tensor", out=t1[:, :n], in0=ps[:, :n], in1=cosT[:, t0:t0 + n], op=ALU.mult,
                reads=[R(ps), R(cosT)], writes=[R(t1)])
        self.op("dve", "tensor_tensor", out=t2[:, :n], in0=ps2[:, :n], in1=sinT[:, t0:t0 + n], op=ALU.mult,
                reads=[R(ps2), R(sinT)], writes=[R(t2)])
        for (oap, psl) in dsts:
            self.op("pool", "tensor_tensor", out=oap, in0=t1[psl, :n], in1=t2[psl, :n], op=ALU.add,
                    reads=[R(t1), R(t2)], writes=[dres])

    def proj_tm(self, wbuf, wres, c0, dst_fn, dname, func=None):
        for tb0 in range(0, NTB, 4):
            nb = min(4, NTB - tb0)
            gi = min(tb0 // 4, 4)
            ps = self.gp()
            for j in range(nb):
                tb = tb0 + j
                for kc in range(KC):
                    self.mm(ps[:, j * 128:(j + 1) * 128], self.HT[:, kc, tb * 128:(tb + 1) * 128], wbuf[:, kc, c0:c0 + 128],
                            start=(kc == 0), stop=(kc == KC - 1), reads=[wres, ("HT", kc, gi)], writes=[R(ps)])
            out, in_ = dst_fn(tb0, nb, ps)
            self.op("act", "activation", out=out, in_=in_, func=(func or AF.Copy), reads=[R(ps)], writes=[(dname, gi)])

    def finish_chunk(self, l, ci, OC3, ocname, OCT, octname, need_ctx):
        self.drain()
        i = self.wo_i % 2
        self.wo_i += 1
        WO = self.WO[i]
        wres = ("WO", i)
        self.dma(WO[:], self.wout_d[l][ci * 128:(ci + 1) * 128, :], writes=[wres], key="wo%d" % i, eng="pool")
        identb = self.C["ident_b"]
        ntb = NTB if need_ctx else 16
        for tb0 in range(0, ntb, 4):
            nb = min(4, ntb - tb0)
            gi = min(tb0 // 4, 4)
            pk = self.gp()
            pkb = pk[:].bitcast(BF16)
            for j in range(nb):
                self.tr(pkb[:, j * 128:(j + 1) * 128], OC3[:, tb0 + j, :], identb[:],
                        reads=[(ocname, gi), R(identb)], writes=[R(pk)])
            self.op("dve", "tensor_copy", out=OCT[:, tb0 * 128:(tb0 + nb) * 128], in_=pkb[:, 0:nb * 128],
                    reads=[R(pk)], writes=[(octname, gi)])
        for gi, (t0, n) in enumerate(TGS):
            if gi == 4 and not need_ctx:
                continue
            s = 0 if gi < 4 else 1
            for dc in range(KC):
                def item(use_gp, gi=gi, t0=t0, n=n, s=s, dc=dc):
                    if use_gp:
                        ps = self.gp()
                        pres = R(ps)
                    else:
                        ps, pres = self.ab()
                    self.mm(ps[:, :n], WO[:, dc * 128:(dc + 1) * 128], OCT[:, t0:t0 + n],
                            reads=[wres, (octname, gi)], writes=[pres])
                    self.op("dve", "scalar_tensor_tensor", out=self.XT[:, dc, t0:t0 + n], in0=ps[:, :n],
                            scalar=self.MOD[:, l, 16 + dc, s:s + 1], in1=self.XT[:, dc, t0:t0 + n], op0=ALU.mult, op1=ALU.add,
                            reads=[pres, ("MOD", l), ("XT", dc, gi)], writes=[("XT", dc, gi)])
                self.deferred.append(item)

    def drain(self, k=None, use_gp=False):
        while self.deferred and (k is None or k > 0):
            self.deferred.pop(0)(use_gp)
            if k is not None:
                k -= 1

    def chunk_A(self, l, h, need_ctx):
        src3 = self.win_d[l].rearrange("(kc p) n -> p kc n", p=128)
        wbuf, wres = self.wst_load(src3, [(h * 128, 0, 128), (512 + h * 128, 128, 128), (1024 + h * 128, 256, 128)])
        par = h % 2
        QT, KT, V = self.gb(par * 3), self.gb(par * 3 + 1), self.gb(par * 3 + 2)
        qn, kn, vn = ("GB", par * 3), ("GB", par * 3 + 1), ("GB", par * 3 + 2)
        OC, OCT = self.gb(6), self.gb(7)
        V3 = V.rearrange("p (t c) -> p t c", c=130)
        OC3 = OC[:, 0:NT].rearrange("p (t c) -> p t c", c=128)
        allg = list(range(5))
        self.op("pool", "memset", ap=V3[:, :, 128:129], constant=1.0, writes=[(vn, g) for g in allg])
        stage = self.cfg.get("a_stage", 9)
        if stage < 0:
            return
        self.proj_fm(wbuf, wres, 0, QT, qn, stage >= 0.5, need_ctx)
        if stage < 0.7:
            return
        self.proj_fm(wbuf, wres, 128, KT, kn, True, True)
        if stage < 0.8:
            return
        self.proj_tm(wbuf, wres, 256,
                     lambda tb0, nb, ps: (V3[:, tb0:tb0 + nb, 0:128], ps[:, 0:nb * 128].rearrange("p (j c) -> p j c", c=128)),
                     vn)
        if stage < 2:
            return
        for gi, (t0, n) in enumerate(TGS):
            if gi == 4 and not need_ctx:
                continue
            kbs = list(range(NTB)) if gi < 4 else [16, 17]
            nqb = n // 128
            nbk = nqb // 2
            aress = [[("oacc", c, 0), ("oacc", c, 1)] for c in range(2)]

            def emit_pv(ki, kb, pts):
                kgi = min(kb // 4, 4)
                for c in range(2):
                    acc = self.OACC[c]
                    for qb in range(nqb):
                        off = (qb // 2) * 512 + (qb % 2) * 129
                        self.mm(acc[:, off:off + 129], pts[c][:, qb * 128:(qb + 1) * 128], V3[:, kb, 0:129],
                                start=(ki == 0 and qb % 2 == 0), stop=(ki == len(kbs) - 1),
                                reads=[R(pts[c]), (vn, kgi)], writes=[aress[c][qb // 2]], sgc=True)

            pending = None
            for ki, kb in enumerate(kbs):
                kgi = min(kb // 4, 4)
                sts = [self.gp(), self.gp()]
                for c in range(2):
                    hs = slice(c * 64, (c + 1) * 64)
                    self.mm(sts[c][:, :n], KT[hs, kb * 128:(kb + 1) * 128], QT[hs, t0:t0 + n],
                            reads=[(kn, kgi), (qn, gi)], writes=[R(sts[c])])
                pts = [self.ptb(), self.ptb()]
                for c in range(2):
                    self.op("act", "activation", out=pts[c][:, :n], in_=sts[c][:, :n], func=AF.Exp, scale=0.125,
                            reads=[R(sts[c])], writes=[R(pts[c])])
                if pending is not None:
                    emit_pv(*pending)
                pending = (ki, kb, pts)
                self.drain(1, use_gp=True)
            emit_pv(*pending)
            if stage < 3:
                continue
            for c in range(2):
                acc = self.OACC[c]
                ares = aress[c]
                accv = acc[:].rearrange("p (b x) -> p b x", b=2)[:, 0:nbk, 0:258].rearrange("p b (j c) -> p b j c", c=129)
                zv = accv[:, :, :, 128]
                ov = accv[:, :, :, 0:128]
                sm = self.small
                rz = sm[:, 0:nqb].rearrange("p (b j) -> p b j", j=2)
                ar = ares[0:nbk]
                self.op("dve", "reciprocal", out=rz, in_=zv, reads=ar, writes=[R(sm)])
                o0 = self.O0N[:, 0:nqb, :].rearrange("p (b j) c -> p b j c", j=2)
                oa = self.OA[:, 0:nqb, :].rearrange("p (b j) c -> p b j c", j=2)
                if c == 0:
                    self.op("dve", "tensor_tensor", out=o0, in0=ov, in1=rz.unsqueeze(3).to_broadcast([128, nbk, 2, 128]),
                            op=ALU.mult, reads=ar + [R(sm)], writes=[R(self.O0N)])
                else:
                    rz1 = sm[:, 4:4 + nqb].rearrange("p (b j) -> p b j", j=2)
                    self.op("dve", "tensor_scalar", out=rz1, in0=rz, scalar1=self.NEGLAM[:, l:l + 1], scalar2=None,
                            op0=ALU.mult, reads=[R(sm), R(self.NEGLAM)], writes=[R(sm)])
                    self.op("dve", "tensor_tensor", out=oa, in0=ov, in1=rz1.unsqueeze(3).to_broadcast([128, nbk, 2, 128]),
                            op=ALU.mult, reads=ar + [R(sm)], writes=[R(self.OA)])
                    oaf = self.OA[:, 0:nqb, :]
                    obf = self.OB[:, 0:nqb, :]
                    self.op("pool", "tensor_tensor", out=oaf, in0=oaf, in1=self.O0N[:, 0:nqb, :], op=ALU.add,
                            reads=[R(self.OA), R(self.O0N)], writes=[R(self.OA)])
                    self.op("pool", "tensor_tensor", out=obf, in0=oaf, in1=oaf, op=ALU.mult,
                            reads=[R(self.OA)], writes=[R(self.OB)])
                    ss = sm[:, 8:8 + nqb]
                    self.op("dve", "tensor_reduce", out=ss, in_=obf, axis=AX.X, op=ALU.add,
                            reads=[R(self.OB)], writes=[R(sm)])
                    l1 = sm[:, 12:12 + nqb]
                    rs = sm[:, 16:16 + nqb]
                    self.op("act", "activation", out=l1, in_=ss, func=AF.Ln, bias=self.epsb[:, 1:2], scale=1.0 / 128.0,
                            reads=[R(sm), R(self.epsb)], writes=[R(sm)])
                    self.op("act", "activation", out=rs, in_=l1, func=AF.Exp, scale=-0.5,
                            reads=[R(sm)], writes=[R(sm)])
                    self.op("dve", "tensor_tensor", out=obf, in0=oaf, in1=rs.unsqueeze(2).to_broadcast([128, nqb, 128]),
                            op=ALU.mult, reads=[R(self.OA), R(sm)], writes=[R(self.OB)])
                    tbq = t0 // 128
                    self.op("pool", "tensor_tensor", out=OC3[:, tbq:tbq + nqb, :], in0=obf,
                            in1=self.GA[:, l, :].unsqueeze(1).to_broadcast([128, nqb, 128]), op=ALU.mult,
                            reads=[R(self.OB), R(self.GA)], writes=[(("GB", 6), gi)])
        if stage < 4:
            return
        self.finish_chunk(l, h, OC3, ("GB", 6), OCT, ("GB", 7), need_ctx)

    def ret_tables(self, l, pp):
        C = self.C
        if True:
            lgf = self.LGC[:, l, 0, pp:pp + 1]
            lgb = self.LGC[:, l, 1, pp:pp + 1]
            self.op("act", "activation", out=self.TQF[pp][:], in_=C["i1"][:], func=AF.Exp, scale=lgf,
                    reads=[R(C["i1"]), R(self.LGC)], writes=[R(self.TQF[pp])])
            self.op("act", "activation", out=self.TQB[pp][:], in_=C["i2"][:], func=AF.Exp, scale=lgb,
                    reads=[R(C["i2"]), R(self.LGC)], writes=[R(self.TQB[pp])])
            self.op("act", "activation", out=self.CFB[pp][:, 0:1], in_=lgf, func=AF.Exp, scale=128.0,
                    reads=[R(self.LGC)], writes=[R(self.CFB[pp])])
            self.op("act", "activation", out=self.CFB[pp][:, 1:2], in_=lgb, func=AF.Exp, scale=128.0,
                    reads=[R(self.LGC)], writes=[R(self.CFB[pp])])
            for h2 in range(2):
                h = 2 * pp + h2
                e1 = self.tmpf[0][:, 0:128]
                e2 = self.tmpf[1][:, 0:128]
                self.op("dve", "tensor_scalar", out=e1, in0=C["r1"][:], scalar1=self.LG[:, l, h:h + 1], scalar2=None,
                        op0=ALU.mult, reads=[R(C["r1"]), R(self.LG)], writes=[R(self.tmpf[0])])
                self.op("dve", "scalar_tensor_tensor", out=e2, in0=C["r2"][:], scalar=self.LG[:, l, 4 + h:5 + h], in1=e1,
                        op0=ALU.mult, op1=ALU.add, reads=[R(C["r2"]), R(self.LG), R(self.tmpf[0])], writes=[R(self.tmpf[1])])
                self.op("act", "activation", out=self.DT[pp][:, h2 * 128:(h2 + 1) * 128], in_=e2, func=AF.Exp,
                        reads=[R(self.tmpf[1])], writes=[R(self.DT[pp])])
            sm = self.small
            self.op("act", "activation", out=sm[:, 20:22], in_=self.LG[:, l, 2 * pp:2 * pp + 2], func=AF.Exp,
                    scale=C["kcol"][:, 0:1], reads=[R(self.LG), R(C["kcol"])], writes=[R(sm)])
            self.op("act", "activation", out=sm[:, 22:24], in_=self.LG[:, l, 4 + 2 * pp:6 + 2 * pp], func=AF.Exp,
                    scale=C["kcol"][:, 1:2], reads=[R(self.LG), R(C["kcol"])], writes=[R(sm)])
            self.op("dve", "tensor_scalar", out=self.TK[pp][:], in0=sm[:, 20:24], scalar1=0.125, scalar2=None, op0=ALU.mult,
                    reads=[R(sm)], writes=[R(self.TK[pp])])

    def chunk_B(self, l, pp, need_ctx):
        src3 = self.win_d[l].rearrange("(kc p) n -> p kc n", p=128)
        wbuf, wres = self.wst_load(src3, [(1536 + pp * 128, 0, 128), (1792 + pp * 128, 128, 128),
                                          (2048 + pp * 128, 256, 128), (2304 + pp * 128, 384, 128)])
        names = [("GB", i) for i in range(8)]
        QT, KT, V, G, SF, SB, OC, OCT = [self.gb(i) for i in range(8)]
        qn, kn, vn, gn, sfn, sbn, ocn, octn = names
        V3 = V.rearrange("p (t c) -> p t c", c=130)
        G3 = G[:, 0:NT].rearrange("p (t c) -> p t c", c=128)
        SF3 = SF[:, 0:NT].rearrange("p (t c) -> p t c", c=128)
        SB3 = SB[:, 0:NT].rearrange("p (t c) -> p t c", c=128)
        OC3 = OC[:, 0:NT].rearrange("p (t c) -> p t c", c=128)
        identb = self.C["ident_b"]
        self.ret_tables(l, pp)
        self.proj_fm(wbuf, wres, 0, QT, qn, True, need_ctx)
        self.proj_fm(wbuf, wres, 128, KT, kn, True, True)
        self.proj_tm(wbuf, wres, 256,
                     lambda tb0, nb, ps: (V3[:, tb0:tb0 + nb, 0:128], ps[:, 0:nb * 128].rearrange("p (j c) -> p j c", c=128)),
                     vn)
        self.proj_tm(wbuf, wres, 384,
                     lambda tb0, nb, ps: (G3[:, tb0:tb0 + nb, :], ps[:, 0:nb * 128].rearrange("p (j c) -> p j c", c=128)),
                     gn, func=AF.Silu)
        orders = [[16, 17] + list(range(16)), [17, 16] + list(range(15, -1, -1))]
        sts_ = [self.stf, self.stb]
        ST3s = [SF3, SB3]
        stns = [sfn, sbn]
        for d in range(2):
            self.op("pool", "memset", ap=sts_[d][:], constant=0.0, writes=[R(sts_[d])])
        NS = len(orders[0])
        pks = {}
        pus = {}
        for i in range(NS + 3):
            for d in range(2):
                if i < NS:
                    tb = orders[d][i]
                    gi = min(tb // 4, 4)
                    pk = self.gp()
                    pkb = pk[:].bitcast(BF16)
                    self.tr(pkb[:, 0:128], KT[:, tb * 128:(tb + 1) * 128], identb[:], reads=[(kn, gi), R(identb)], writes=[R(pk)])
                    pks[(d, i)] = (pk, pkb)
                if 0 <= i - 1 < NS:
                    pk, pkb = pks.pop((d, i - 1))
                    kf = self.kfb[2 * d + (i - 1) % 2]
                    self.op("dve", "tensor_tensor", out=kf[:].rearrange("p (h c) -> p h c", h=2),
                            in0=pkb[:, 0:128].rearrange("p (h c) -> p h c", h=2),
                            in1=self.TK[pp][:, 2 * d:2 * d + 2].unsqueeze(2).to_broadcast([128, 2, 64]), op=ALU.mult,
                            reads=[R(pk), R(self.TK[pp])], writes=[R(kf)])
                if 0 <= i - 2 < NS:
                    tb = orders[d][i - 2]
                    gi = min(tb // 4, 4)
                    kf = self.kfb[2 * d + (i - 2) % 2]
                    pu, pures = self.ab()
                    self.mm(pu[:, 0:128], kf[:], V3[:, tb, 0:128], reads=[R(kf), (vn, gi)], writes=[pures])
                    pus[(d, i - 2)] = (pu, pures)
                if 0 <= i - 3 < NS:
                    tb = orders[d][i - 3]
                    gi = min(tb // 4, 4)
                    st = sts_[d]
                    pu, pures = pus.pop((d, i - 3))
                    self.op("pool", "tensor_copy", out=ST3s[d][:, tb, :], in_=st[:], reads=[R(st)], writes=[(stns[d], gi)])
                    self.op("dve", "scalar_tensor_tensor", out=st[:], in0=st[:], scalar=self.CFB[pp][:, d:d + 1], in1=pu[:, 0:128],
                            op0=ALU.mult, op1=ALU.add, reads=[R(st), R(self.CFB[pp]), pures], writes=[R(st)])
        ntb = NTB if need_ctx else 16
        sm = self.small
        cur = {}

        def front(tb):
            gi = min(tb // 4, 4)
            ts_ = slice(tb * 128, (tb + 1) * 128)
            sts = [self.gp(), self.gp()]
            for h2 in range(2):
                hs = slice(h2 * 64, (h2 + 1) * 64)
                self.mm(sts[h2][:, 0:128], KT[hs, ts_], QT[hs, ts_], reads=[(kn, gi), (qn, gi)], writes=[R(sts[h2])])
            pt = self.ptb()
            for h2 in range(2):
                self.op("dve", "scalar_tensor_tensor", out=pt[:, h2 * 128:(h2 + 1) * 128], in0=sts[h2][:, 0:128], scalar=0.125,
                        in1=self.DT[pp][:, h2 * 128:(h2 + 1) * 128],
                        op0=ALU.mult, op1=ALU.mult, reads=[R(sts[h2]), R(self.DT[pp])], writes=[R(pt)])
            qf = self.qfb[(2 * tb) % 4]
            qb_ = self.qfb[(2 * tb + 1) % 4]
            self.op("pool", "tensor_tensor", out=qf[:], in0=QT[:, ts_], in1=self.TQF[pp][:], op=ALU.mult,
                    reads=[(qn, gi), R(self.TQF[pp])], writes=[R(qf)])
            self.op("pool", "tensor_tensor", out=qb_[:], in0=QT[:, ts_], in1=self.TQB[pp][:], op=ALU.mult,
                    reads=[(qn, gi), R(self.TQB[pp])], writes=[R(qb_)])
            return (pt, qf, qb_)

        def back(tb, pt, qf, qb_):
            gi = min(tb // 4, 4)
            jj = tb % 4
            if jj == 0:
                cur["po"], cur["pres"] = self.ab()
            po, pres = cur["po"], cur["pres"]
            for h2 in range(2):
                hs = slice(h2 * 64, (h2 + 1) * 64)
                oreg = po[:, jj * 128 + h2 * 64:jj * 128 + (h2 + 1) * 64]
                self.mm(oreg, qf[hs, :], SF3[hs, tb, hs], start=(jj == 0 and h2 == 0), stop=False,
                        reads=[R(qf), (sfn, gi)], writes=[pres], sgc=True)
                self.mm(oreg, qb_[hs, :], SB3[hs, tb, hs], start=False, stop=False,
                        reads=[R(qb_), (sbn, gi)], writes=[pres], sgc=True)
                self.mm(oreg, pt[:, h2 * 128:(h2 + 1) * 128], V3[:, tb, hs], start=False, stop=True,
                        reads=[R(pt), (vn, gi)], writes=[pres], sgc=True)
            last = (jj == 3) or (tb == ntb - 1)
            if not last:
                return
            tb0 = tb - jj
            nb = jj + 1
            ng = 2 * nb
            W = nb * 128
            v3 = lambda a: a[:, 0:W].rearrange("p (g c) -> p g c", c=64)
            osb = self.OA[:].rearrange("p a b -> p (a b)")
            ocn_ = self.OB[:].rearrange("p a b -> p (a b)")
            sqv = self.O0N[:].rearrange("p a b -> p (a b)")
            self.op("dve", "tensor_copy", out=osb[:, 0:W], in_=po[:, 0:W], reads=[pres], writes=[R(self.OA)])
            self.op("dve", "tensor_reduce", out=sm[:, 24:24 + ng], in_=v3(osb), axis=AX.X, op=ALU.add,
                    reads=[R(self.OA)], writes=[R(sm)])
            self.op("dve", "tensor_scalar", out=sm[:, 24:24 + ng], in0=sm[:, 24:24 + ng], scalar1=1.0 / 64.0, scalar2=None, op0=ALU.mult,
                    reads=[R(sm)], writes=[R(sm)])
            self.op("dve", "tensor_tensor", out=v3(ocn_), in0=v3(osb), in1=sm[:, 24:24 + ng].unsqueeze(2).to_broadcast([128, ng, 64]),
                    op=ALU.subtract, reads=[R(self.OA), R(sm)], writes=[R(self.OB)])
            self.op("pool", "tensor_tensor", out=sqv[:, 0:W], in0=ocn_[:, 0:W], in1=ocn_[:, 0:W], op=ALU.mult,
                    reads=[R(self.OB)], writes=[R(self.O0N)])
            self.op("dve", "tensor_reduce", out=sm[:, 44:44 + ng], in_=v3(sqv), axis=AX.X, op=ALU.add,
                    reads=[R(self.O0N)], writes=[R(sm)])
            self.op("act", "activation", out=sm[:, 52:52 + ng], in_=sm[:, 44:44 + ng], func=AF.Ln, bias=self.epsb[:, 1:2], scale=1.0 / 64.0,
                    reads=[R(sm), R(self.epsb)], writes=[R(sm)])
            self.op("act", "activation", out=sm[:, 24:24 + ng], in_=sm[:, 52:52 + ng], func=AF.Exp, scale=-0.5,
                    reads=[R(sm)], writes=[R(sm)])
            self.op("dve", "tensor_tensor", out=v3(osb), in0=v3(ocn_), in1=sm[:, 24:24 + ng].unsqueeze(2).to_broadcast([128, ng, 64]),
                    op=ALU.mult, reads=[R(self.OB), R(sm)], writes=[R(self.OA)])
            self.op("pool", "tensor_tensor", out=OC3[:, tb0:tb0 + nb, :], in0=osb[:, 0:W].rearrange("p (t c) -> p t c", c=128),
                    in1=G3[:, tb0:tb0 + nb, :], op=ALU.mult,
                    reads=[R(self.OA), (gn, gi)], writes=[(ocn, gi)])

        pending = None
        for tb in range(ntb):
            fr = front(tb)
            if pending is not None:
                back(*pending)
            pending = (tb,) + fr
            self.drain(3, use_gp=True)
        back(*pending)
        self.finish_chunk(l, 4 + pp, OC3, ocn, OCT, octn, need_ctx)

    def chunk_C(self, l, need_ctx):
        src3 = self.win_d[l].rearrange("(kc p) n -> p kc n", p=128)
        wbuf, wres = self.wst_load(src3, [(2560, 0, 256), (2816, 256, 64), (2816, 320, 64), (2880, 384, 64), (2880, 448, 64)])
        wbufv, wresv = self.wst_load(src3, [(2944, 0, 128)])
        QZ = self.GBT[:, 0:2 * NT].rearrange("p (g t) -> p g t", g=2)
        qzn = "QZ"
        gb01 = [(("GB", i), g_) for i in range(2) for g_ in range(5)]
        KT, V = self.gb(2), self.gb(3)
        kn, vn = ("GB", 2), ("GB", 3)
        OC, OCT = self.gb(6), self.gb(7)
        ocn, octn = ("GB", 6), ("GB", 7)
        V4 = V.rearrange("p (t j c) -> p t j c", j=2, c=65)
        OC3 = OC[:, 0:NT].rearrange("p (t c) -> p t c", c=128)
        allg = list(range(5))
        self.op("pool", "memset", ap=V4[:, :, :, 64:65], constant=1.0, writes=[(vn, g) for g in allg])
        self.op("pool", "memset", ap=QZ[64:128, 0, :], constant=0.0, writes=gb01 + [(qzn, g) for g in allg])
        self.op("pool", "memset", ap=QZ[0:64, 1, :], constant=0.0, writes=gb01 + [(qzn, g) for g in allg])
        self.proj_tm(wbufv, wresv, 0,
                     lambda tb0, nb, ps: (V4[:, tb0:tb0 + nb, :, 0:64],
                                          ps[:, 0:nb * 128].rearrange("p (t j c) -> p t j c", j=2, c=64)),
                     vn)
        ntb = NTB if need_ctx else 16
        sm = self.small
        qz_fn = lambda t0, n: [(QZ[0:64, 0, t0:t0 + n], slice(0, 64)), (QZ[64:128, 1, t0:t0 + n], slice(64, 128))]
        for j in range(2):
            self.proj_fm(wbuf, wres, j * 128, None, qzn, True, need_ctx, dst_fn=qz_fn)
            self.proj_fm(wbuf, wres, 256 + j * 128, KT, kn, True, True)
            items = []
            for n_ in range(ntb):
                if n_ < 16:
                    kbs = ([n_ - 1] if n_ > 0 else []) + [n_] + ([n_ + 1] if n_ < 15 else []) + [16, 17]
                else:
                    kbs = [16, 17]
                for ki, m in enumerate(kbs):
                    items.append((n_, ki, m, len(kbs)))
            cur = {}

            def front(n_, ki, m, nk):
                gi = min(n_ // 4, 4)
                mgi = min(m // 4, 4)
                st = self.gp()
                self.mm(st[:, 0:256], KT[:, m * 128:(m + 1) * 128], QZ[:, :, n_ * 128:(n_ + 1) * 128],
                        reads=[(kn, mgi), (qzn, gi)], writes=[R(st)])
                pt = self.ptb()
                self.op("act", "activation", out=pt[:, 0:256], in_=st[:, 0:256], func=AF.Exp, scale=0.125,
                        reads=[R(st)], writes=[R(pt)])
                if n_ < 16 and m == n_ - 1:
                    self.op("pool", "tensor_tensor", out=pt[:, 0:256], in0=pt[:, 0:256], in1=self.C["maskp"][:], op=ALU.mult,
                            reads=[R(pt), R(self.C["maskp"])], writes=[R(pt)])
                if n_ < 15 and m == n_ + 1:
                    self.op("pool", "tensor_tensor", out=pt[:, 0:256], in0=pt[:, 0:256], in1=self.C["maskn"][:], op=ALU.mult,
                            reads=[R(pt), R(self.C["maskn"])], writes=[R(pt)])
                return pt

            def back(n_, ki, m, nk, pt):
                mgi = min(m // 4, 4)
                r3 = n_ % 3
                if r3 == 0 and ki == 0:
                    cur["po"], cur["pres"] = self.ab()
                    cur["n0"] = n_
                po, pres = cur["po"], cur["pres"]
                for g in range(2):
                    off = r3 * 130 + g * 65
                    self.mm(po[:, off:off + 65], pt[:, g * 128:(g + 1) * 128], V4[:, m, j, 0:65],
                            start=(r3 == 0 and ki == 0 and g == 0), stop=(ki == nk - 1), reads=[R(pt), (vn, mgi)], writes=[pres],
                            sgc=True)
                if ki == nk - 1 and (r3 == 2 or n_ == ntb - 1):
                    n0 = cur["n0"]
                    cnt = n_ - n0 + 1
                    pov = po[:, 0:cnt * 130].rearrange("p (n g c) -> p n g c", g=2, c=65)
                    den = sm[:, 34:34 + 2 * cnt].rearrange("p (n g) -> p n g", g=2)
                    rz = sm[:, 40:40 + 2 * cnt].rearrange("p (n g) -> p n g", g=2)
                    self.op("dve", "tensor_tensor", out=den, in0=pov[:, :, :, 64],
                            in1=self.ESINK[:, l, 2 * j:2 * j + 2].unsqueeze(1).to_broadcast([128, cnt, 2]), op=ALU.add,
                            reads=[pres, R(self.ESINK)], writes=[R(sm)])
                    self.op("dve", "reciprocal", out=rz, in_=den, reads=[R(sm)], writes=[R(sm)])
                    gis = sorted(set(min(q // 4, 4) for q in range(n0, n_ + 1)))
                    self.op("dve", "tensor_tensor", out=OC3[:, n0:n_ + 1, :].rearrange("p n (g c) -> p n g c", g=2),
                            in0=pov[:, :, :, 0:64], in1=rz.unsqueeze(3).to_broadcast([128, cnt, 2, 64]), op=ALU.mult,
                            reads=[pres, R(sm)], writes=[(ocn, g_) for g_ in gis])

            pending = None
            for it in items:
                pt = front(*it)
                if pending is not None:
                    back(*pending)
                pending = it + (pt,)
                self.drain(1, use_gp=True)
            back(*pending)
            self.finish_chunk(l, 6 + j, OC3, ocn, OCT, octn, need_ctx)

    def mlp(self, l, need_ctx):
        self.drain()
        self.norm(l, 1, skip_ctx=not need_ctx)
        w1v = self.w1_d[l].rearrange("(kc p) n -> p kc n", p=128)
        w2v = self.w2_d[l].rearrange("(fc p) n -> p fc n", p=128)
        pending = None
        ucount = 0

        def mlp2(W2, w2res, AT, ares, gi, t0, n, s):
            for dc in range(KC):
                ps2, pres = self.ab()
                for fc in range(4):
                    self.mm(ps2[:, :n], W2[:, fc, dc * 128:(dc + 1) * 128], AT[:, fc, :n], start=(fc == 0), stop=(fc == 3),
                            reads=[(w2res[0], 0), (w2res[1], 0), ares], writes=[pres])
                self.op("dve", "scalar_tensor_tensor", out=self.XT[:, dc, t0:t0 + n], in0=ps2[:, :n],
                        scalar=self.MOD[:, l, 40 + dc, s:s + 1], in1=self.XT[:, dc, t0:t0 + n], op0=ALU.mult, op1=ALU.add,
                        reads=[pres, ("MOD", l), ("XT", dc, gi)], writes=[("XT", dc, gi)])

        for fb in range(8):
            W1, w1res = self.wst_load(w1v, [(fb * 512, 0, 512)])
            i2 = fb % 2
            W2 = self.GBT[:, i2 * 2 * GBW:i2 * 2 * GBW + 4096].rearrange("p (f n) -> p f n", f=4)
            w2res = [("GB", 2 * i2), ("GB", 2 * i2 + 1)]
            self.dma(W2, w2v[:, fb * 4:(fb + 1) * 4, :], writes=[(r, g) for r in w2res for g in range(5)],
                     key="w2_%d" % i2, eng="pool")
            for gi, (t0, n) in enumerate(TGS):
                if gi == 4 and not need_ctx:
                    continue
                s = 0 if gi < 4 else 1
                ai = 4 + (ucount % 2)
                ucount += 1
                AT = self.gb(ai)[:, 0:2048].rearrange("p (f n) -> p f n", f=4)
                ares = (("GB", ai), 0)
                aresw = [(("GB", ai), g_) for g_ in range(5)]
                for fc in range(4):
                    ps = self.gp()
                    for kc in range(KC):
                        self.mm(ps[:, :n], W1[:, kc, fc * 128:(fc + 1) * 128], self.HT[:, kc, t0:t0 + n],
                                start=(kc == 0), stop=(kc == KC - 1), reads=[w1res, ("HT", kc, gi)], writes=[R(ps)])
                    rt = self.tmpf[fc % 4]
                    self.op("act", "activation", out=rt[:, :n], in_=ps[:, :n], func=AF.Relu, reads=[R(ps)], writes=[R(rt)])
                    self.op("pool", "tensor_tensor", out=AT[:, fc, :n], in0=rt[:, :n], in1=rt[:, :n], op=ALU.mult,
                            reads=[R(rt)], writes=aresw)
                if pending is not None:
                    mlp2(*pending)
                pending = (W2, w2res, AT, ares, gi, t0, n, s)
        mlp2(*pending)

    def final_norm(self):
        XT, identf = self.XT, self.C["ident_f"]
        gfin = self.P["g_final"]
        YT = self.HT[:].rearrange("p k t -> p (k t)")[:, 0:8192].bitcast(F32).rearrange("p (k t) -> p k t", k=KC)
        ytres = [("HT", kc, g) for kc in range(KC) for g in range(5)]
        for g in range(4):
            t0 = g * 512
            acc = self.gp()
            self.sumsq_rstd(g, acc)
            for kc in range(KC):
                self.op("dve", "scalar_tensor_tensor", out=YT[:, kc, :], in0=XT[:, kc, t0:t0 + 512], scalar=gfin[:, kc:kc + 1],
                        in1=self.rstd[:], op0=ALU.mult, op1=ALU.mult,
                        reads=[("XT", kc, g), R(self.rstd), R(gfin)], writes=(ytres if kc == 0 else []) + [("YT", kc)])
            for j in range(4):
                tb = g * 4 + j
                ob = self.iobuf[tb % 2]
                obres = self.iores[tb % 2]
                for half in range(2):
                    pt = self.gp()
                    for jj in range(4):
                        kc = half * 4 + jj
                        self.tr(pt[:, jj * 128:(jj + 1) * 128], YT[:, kc, j * 128:(j + 1) * 128], identf[:],
                                reads=[("YT", kc), ("HT", 0, 0), R(identf)], writes=[R(pt)])
                    self.op("act", "mul", out=ob[:, half * 512:(half + 1) * 512], in_=pt[:], mul=32.0,
                            reads=[R(pt)], writes=obres)
                self.dma(self.out_d[tb * 128:(tb + 1) * 128, :], ob, reads=obres, key="io%d" % (tb % 2))


_CACHE = {}


def kernel(**inputs):
    cfg = inputs.pop("_cfg", {}) if "_cfg" in inputs else {}
    inputs = {k: np.asarray(v) for k, v in inputs.items()}
    key = tuple(sorted(cfg.items()))
    if key not in _CACHE:
        b = Builder(cfg)
        _CACHE[key] = (b.build(), b)
    nc, b = _CACHE[key]
    in_maps = [prep_inputs(inputs, i, b.depth) for i in range(8)]
    res = run_bass_kernel_spmd(nc, in_maps, core_ids=list(range(8)))
    if cfg.get("_ret_all"):
        return res
    out = np.stack([np.asarray(r["out"]) for r in res.results], axis=0)
    return out.astype(np.float32)
```

```python
import contextlib
import math
import numpy as np
import ml_dtypes
import concourse.bass as bass
import concourse.mybir as mybir
from concourse.bass_utils import run_bass_kernel_spmd

F32 = mybir.dt.float32
BF16 = mybir.dt.bfloat16
ALU = mybir.AluOpType
AF = mybir.ActivationFunctionType
AX = mybir.AxisListType

D = 1024
T = 2048
L = 256
NT = T + L
DEPTH = 4
KC = D // 128
NTB = NT // 128
EPS = 1e-6
TGS = [(0, 512), (512, 512), (1024, 512), (1536, 512), (2048, 256)]


class Op:
    __slots__ = ("eng", "fn", "idx", "deps", "raw", "inc", "count", "dma", "is_dma")

    def __init__(self, eng, fn, idx):
        self.eng = eng
        self.fn = fn
        self.idx = idx
        self.deps = set()
        self.raw = set()
        self.inc = False
        self.count = 0
        self.dma = None
        self.is_dma = False


class Sched:
    ENGS = ["pe", "act", "dve", "pool", "sp"]

    def __init__(self):
        self.ops = {e: [] for e in self.ENGS}
        self.last_w = {}
        self.readers = {}
        self.dma_n = {}

    def add(self, eng, fn, reads=(), writes=(), dma=None):
        op = Op(eng, fn, len(self.ops[eng]))
        xr = [r for r in reads if isinstance(r, tuple) and str(r[0]).startswith(("ps", "oacc"))]
        if xr:
            for r in xr:
                w = self.last_w.get(r)
                if w is not None:
                    op.raw.add(w)
            writes = list(writes) + [r for r in xr if r not in writes]
        for r in reads:
            w = self.last_w.get(r)
            if w is not None:
                op.deps.add(w)
                op.raw.add(w)
        for r in writes:
            w = self.last_w.get(r)
            if w is not None:
                op.deps.add(w)
            rd = self.readers.get(r)
            if rd:
                for o in rd.values():
                    op.deps.add(o)
        for r in reads:
            d = self.readers.setdefault(r, {})
            if dma is not None:
                d[("dma", id(op))] = op
            else:
                d[eng] = op
        for r in writes:
            self.last_w[r] = op
            self.readers[r] = {}
        if dma is not None:
            n = self.dma_n.get(dma, 0) + 1
            self.dma_n[dma] = n
            op.dma = (dma, n)
            op.is_dma = True
        self.ops[eng].append(op)
        return op

    def _needs_wait(self, op, d):
        if d.is_dma:
            return True
        if d.eng != op.eng:
            return True
        if op.is_dma:
            return True
        if op.eng == "pe":
            return False
        return (d in op.raw) and (op.idx - d.idx <= 4)

    def emit(self, nc, stack):
        for e in self.ENGS:
            for op in self.ops[e]:
                for d in op.deps:
                    if not d.is_dma and self._needs_wait(op, d):
                        d.inc = True
        for e in self.ENGS:
            c = 0
            for op in self.ops[e]:
                if op.inc:
                    c += 1
                op.count = c
        sems = {e: stack.enter_context(nc.semaphore("s_" + e)) for e in self.ENGS}
        dsems = {k: stack.enter_context(nc.semaphore("d_%s" % (k,))) for k in self.dma_n}
        block = stack.enter_context(nc.Block())
        stats = {}

        def run(ename, engine):
            known = {}
            nw = 0
            for op in self.ops[ename]:
                need = {}
                for d in op.deps:
                    if not self._needs_wait(op, d):
                        continue
                    if d.is_dma:
                        key = ("d", d.dma[0])
                        val = 16 * d.dma[1]
                    else:
                        key = ("e", d.eng)
                        val = d.count
                    if val > need.get(key, 0):
                        need[key] = val
                for key, val in need.items():
                    if known.get(key, 0) >= val:
                        continue
                    known[key] = val
                    s = dsems[key[1]] if key[0] == "d" else sems[key[1]]
                    engine.wait_ge(s, val)
                    nw += 1
                ins = op.fn(engine)
                if op.is_dma:
                    ins.then_inc(dsems[op.dma[0]], 16)
                elif op.inc:
                    ins.then_inc(sems[ename], 1)
            for op in self.ops[ename]:
                if op.is_dma:
                    key = ("d", op.dma[0])
                    val = 16 * self.dma_n[op.dma[0]]
                    if known.get(key, 0) < val:
                        known[key] = val
                        engine.wait_ge(dsems[op.dma[0]], val)
            stats[ename] = (len(self.ops[ename]), nw)

        block.tensor(lambda e: run("pe", e))
        block.scalar(lambda e: run("act", e))
        block.vector(lambda e: run("dve", e))
        block.gpsimd(lambda e: run("pool", e))
        block.sync(lambda e: run("sp", e))
        self.stats = stats
        return stats


def R(t, *idx):
    return (t.name,) + idx


GBW = 2340
NGB = 8
LAM_INIT = [0.8 - 0.6 * math.exp(-0.3 * l) for l in range(DEPTH)]


def host_consts():
    c = {}
    bf = ml_dtypes.bfloat16
    c["ident_f"] = np.eye(128, dtype=np.float32)
    c["ident_b"] = np.eye(128, dtype=np.float32).astype(bf)
    c["ones_b"] = np.ones((128, 128), dtype=np.float32).astype(bf)
    pm = np.zeros((128, 128), np.float32)
    sign = np.zeros(128, np.float64)
    for f in range(128):
        fl = f % 64
        q = fl // 16
        src = f + 16 if q in (0, 2) else f - 16
        pm[src, f] = 1.0
        sign[f] = -1.0 if q in (0, 2) else 1.0
    c["perm_b"] = pm.astype(bf)
    t = np.arange(T)
    r = (t // 64).astype(np.float64)
    col = (t % 64).astype(np.float64)
    inv = 10000.0 ** (-np.arange(16, dtype=np.float64) / 16)
    ang64 = np.concatenate([r[:, None] * inv, r[:, None] * inv, col[:, None] * inv, col[:, None] * inv], axis=1)
    ang = np.concatenate([ang64, ang64], axis=1).T
    c["cosT"] = np.cos(ang).astype(np.float32).astype(bf)
    c["sinT"] = (np.sin(ang) * sign[:, None]).astype(np.float32).astype(bf)
    il = np.arange(128)
    mp = (il[None, :] <= il[:, None]).astype(np.float32)
    mn = (il[:, None] <= il[None, :]).astype(np.float32)
    c["maskp"] = np.concatenate([mp, mp], axis=1).astype(bf)
    c["maskn"] = np.concatenate([mn, mn], axis=1).astype(bf)
    s_ = il[:, None].astype(np.float32)
    t_ = il[None, :].astype(np.float32)
    c["r1"] = np.maximum(t_ - s_, 0.0).astype(np.float32)
    c["r2"] = np.maximum(s_ - t_, 0.0).astype(np.float32)
    c["i1"] = np.broadcast_to((il + 1.0)[None, :], (128, 128)).astype(np.float32).copy()
    c["i2"] = np.broadcast_to((128.0 - il)[None, :], (128, 128)).astype(np.float32).copy()
    c["kcol"] = np.stack([127.0 - il, il * 1.0], axis=1).astype(np.float32)
    return c


CONST_SPECS = [("ident_f", [128, 128], F32), ("ident_b", [128, 128], BF16), ("ones_b", [128, 128], BF16),
               ("perm_b", [128, 128], BF16), ("cosT", [128, T], BF16), ("sinT", [128, T], BF16),
               ("maskp", [128, 256], BF16), ("maskn", [128, 256], BF16),
               ("r1", [128, 128], F32), ("r2", [128, 128], F32), ("i1", [128, 128], F32), ("i2", [128, 128], F32),
               ("kcol", [128, 2], F32)]

PARAM_SPECS = [("g_final", [128, KC]), ("cc", [128, KC, 2]), ("bada", [128, DEPTH, 48]),
               ("gmix", [128, DEPTH, KC]), ("gmlp", [128, DEPTH, KC]),
               ("lamq", [128, DEPTH, 2, 64]), ("lamk", [128, DEPTH, 2, 64]), ("sublng", [128, DEPTH, 128]),
               ("decbc", [128, DEPTH, 8]), ("deccol", [128, DEPTH, 2, 2]), ("sinkbc", [128, DEPTH, 4])]


def prep_inputs(inputs, b, depth=DEPTH):
    f = np.float32
    m = {}
    m["x"] = np.ascontiguousarray(inputs["x"][b], dtype=f)
    m["ctx"] = np.ascontiguousarray(inputs["ctx"][b], dtype=f)
    m["g_final"] = np.ascontiguousarray(inputs["g_final"].reshape(KC, 128).T, dtype=f)
    cc = np.stack([inputs["c"][b].reshape(KC, 128).T, inputs["c_ctx"].reshape(KC, 128).T], axis=2)
    m["cc"] = np.ascontiguousarray(cc, dtype=f)
    m["bada"] = np.ascontiguousarray(inputs["b_ada"].reshape(DEPTH, 48, 128).transpose(2, 0, 1), dtype=f)
    m["gmix"] = np.ascontiguousarray(inputs["g_mix"].reshape(DEPTH, KC, 128).transpose(2, 0, 1), dtype=f)
    m["gmlp"] = np.ascontiguousarray(inputs["g_mlp"].reshape(DEPTH, KC, 128).transpose(2, 0, 1), dtype=f)
    lamq = np.stack([inputs["lam_q1"], inputs["lam_q2"]], axis=1)
    lamk = np.stack([inputs["lam_k1"], inputs["lam_k2"]], axis=1)
    m["lamq"] = np.ascontiguousarray(np.broadcast_to(lamq[None], (128, DEPTH, 2, 64)), dtype=f)
    m["lamk"] = np.ascontiguousarray(np.broadcast_to(lamk[None], (128, DEPTH, 2, 64)), dtype=f)
    m["sublng"] = np.ascontiguousarray(np.broadcast_to(inputs["subln_g"][None], (128, DEPTH, 128)), dtype=f)
    dec = np.concatenate([inputs["ret_decay_fwd"], inputs["ret_decay_bwd"]], axis=1)
    m["decbc"] = np.ascontiguousarray(np.broadcast_to(dec[None], (128, DEPTH, 8)), dtype=f)
    dcol = np.zeros((128, DEPTH, 2, 2), f)
    for di, nm in enumerate(["ret_decay_fwd", "ret_decay_bwd"]):
        for pp in range(2):
            dcol[0:64, :, di, pp] = inputs[nm][:, 2 * pp][None, :]
            dcol[64:128, :, di, pp] = inputs[nm][:, 2 * pp + 1][None, :]
    m["deccol"] = dcol
    m["sinkbc"] = np.ascontiguousarray(np.broadcast_to(inputs["sink_logit"][None], (128, DEPTH, 4)), dtype=f)
    for k in ["w_ada", "w_in", "w_out", "w_mlp1", "w_mlp2"]:
        m[k] = np.ascontiguousarray(inputs[k][:depth], dtype=f)
    m.update(host_consts())
    return m


class Builder:
    def __init__(self, cfg):
        self.cfg = cfg
        self.depth = cfg.get("depth", DEPTH)
        self.nc = bass.Bass("TRN2", target_bir_lowering=False)
        self.S = Sched()
        self.stack = contextlib.ExitStack()
        self.gp_i = 0
        self.ab_i = 0
        self.pt_i = 0
        self.wst_i = 0
        self.wo_i = 0
        self.oacc_i = 0
        self.rope_i = 0
        self.dbg_names = []
        self.deferred = []

    def dram_in(self, name, shape, dt=F32):
        return self.nc.dram_tensor(name, list(shape), dt, kind="ExternalInput").ap()

    def sb(self, name, shape, dt):
        return self.stack.enter_context(self.nc.sbuf_tensor(name, list(shape), dt))

    def ps(self, name, shape, dt=F32):
        return self.stack.enter_context(self.nc.psum_tensor(name, list(shape), dt))

    def dma(self, out, in_, reads=(), writes=(), key=None, eng="sp", **kw):
        self.S.add(eng, lambda e: e.dma_start(out=out, in_=in_, **kw), reads=reads, writes=writes, dma=key)

    def mm(self, out, lhsT, rhs, start=True, stop=True, reads=(), writes=(), sgc=False):
        if sgc:
            self.S.add("pe", lambda e: e.matmul(out, lhsT=lhsT, rhs=rhs, start=start, stop=stop, skip_group_check=True),
                       reads=reads, writes=writes)
        else:
            self.S.add("pe", lambda e: e.matmul(out, lhsT=lhsT, rhs=rhs, start=start, stop=stop),
                       reads=reads, writes=writes)

    def tr(self, out, in_, ident, reads=(), writes=()):
        self.S.add("pe", lambda e: e.transpose(out, in_, ident), reads=reads, writes=writes)

    def op(self, eng, meth, reads=(), writes=(), **kw):
        self.S.add(eng, lambda e: getattr(e, meth)(**kw), reads=reads, writes=writes)

    def gp(self):
        t = self.PSG[self.gp_i % len(self.PSG)]
        self.gp_i += 1
        return t

    def ab(self):
        i = self.ab_i % 4
        self.ab_i += 1
        return self.OACC[i // 2][:, (i % 2) * 512:(i % 2 + 1) * 512], ("oacc", i // 2, i % 2)

    def ptb(self):
        t = self.PT[self.pt_i % len(self.PT)]
        self.pt_i += 1
        return t

    def gb(self, i):
        return self.GBT[:, i * GBW:(i + 1) * GBW]

    def dbg(self, name, ap, shape, dt, reads):
        d = self.nc.dram_tensor(name, list(shape), dt, kind="ExternalOutput").ap()
        self.dma(d, ap, reads=reads, key="dbg_" + name)
        self.dbg_names.append(name)

    def build(self):
        nc, S = self.nc, self.S
        cfg = self.cfg
        self.x_d = self.dram_in("x", [T, D])
        self.ctx_d = self.dram_in("ctx", [L, D])
        self.out_d = nc.dram_tensor("out", [T, D], F32, kind="ExternalOutput").ap()
        self.cd = {n: self.dram_in(n, shp, dt) for n, shp, dt in CONST_SPECS}
        self.pd = {n: self.dram_in(n, shp, F32) for n, shp in PARAM_SPECS}
        self.wada_d = self.dram_in("w_ada", [self.depth, D, 6 * D])
        self.win_d = self.dram_in("w_in", [self.depth, D, 3 * D])
        self.wout_d = self.dram_in("w_out", [self.depth, D, D])
        self.w1_d = self.dram_in("w_mlp1", [self.depth, D, 4 * D])
        self.w2_d = self.dram_in("w_mlp2", [self.depth, 4 * D, D])

        self.XT = self.sb("XT", [128, KC, NT], F32)
        self.HT = self.sb("HT", [128, KC, NT], BF16)
        self.GBT = self.sb("GBT", [128, NGB * GBW], BF16)
        self.WST = [self.sb("WST%d" % i, [128, KC, 512], BF16) for i in range(2)]
        self.WO = [self.sb("WO%d" % i, [128, D], BF16) for i in range(2)]
        self.C = {}
        for n, shp, dt in CONST_SPECS:
            t = self.sb("c_" + n, shp, dt)
            self.C[n] = t
            self.dma(t[:], self.cd[n], writes=[R(t)], key="c_" + n)
        self.O0N = self.sb("o0n", [128, 4, 128], F32)
        self.OA = self.sb("oa", [128, 4, 128], F32)
        self.OB = self.sb("ob", [128, 4, 128], F32)
        self.GA = self.sb("GA", [128, DEPTH, 128], F32)
        self.P = {}
        alias = {"lamq": self.O0N, "lamk": self.OB, "sublng": self.GA}
        for n, shp in PARAM_SPECS:
            if n in alias:
                t = alias[n]
                self.dma(t[:].rearrange("p a b -> p (a b)"), self.pd[n].rearrange("p a b c -> p (a b c)") if len(shp) == 4 else self.pd[n].rearrange("p a b -> p (a b)"), writes=[R(t)], key="p_" + n)
            else:
                t = self.sb("p_" + n, shp, F32)
                self.dma(t[:], self.pd[n], writes=[R(t)], key="p_" + n)
            self.P[n] = t
        self.iobuf = [self.gb(i)[:, 0:2048].bitcast(F32) for i in range(2)]
        self.iores = [[(("GB", i), g) for g in range(5)] for i in range(2)]
        self.tmpf = [self.sb("tmpf%d" % i, [128, 512], F32) for i in range(4)]
        self.PT = [self.sb("pt%d" % i, [128, 512], BF16) for i in range(4)]
        self.sqb = [self.PT[0], self.PT[1]]
        self.ropeq = [self.PT[2], self.PT[3]]
        self.rstd = self.tmpf[2]
        self.rstd0 = self.tmpf[3]
        self.small = self.sb("small", [128, 64], F32)
        self.epsb = self.sb("epsb", [128, 4], F32)
        S.add("pool", lambda e: e.memset(self.epsb[:, 0:1], float(D * EPS)), writes=[R(self.epsb)])
        S.add("pool", lambda e: e.memset(self.epsb[:, 1:2], float(EPS)), writes=[R(self.epsb)])
        S.add("pool", lambda e: e.memset(self.epsb[:, 2:3], 1.0), writes=[R(self.epsb)])
        self.MOD = self.sb("MOD", [128, DEPTH, 48, 2], F32)
        self.GS = [self.sb("GS%d" % i, [128, DEPTH, KC, 2], F32) for i in range(2)]
        self.G32 = [self.sb("G32_%d" % i, [128, DEPTH, KC], F32) for i in range(2)]
        self.siluT = self.sb("siluT", [128, KC, 2], BF16)
        self.LG = self.sb("LG", [128, DEPTH, 8], F32)
        self.LGC = self.sb("LGC", [128, DEPTH, 2, 2], F32)
        self.ESINK = self.sb("ESINK", [128, DEPTH, 4], F32)
        self.NEGLAM = self.sb("NEGLAM", [128, DEPTH], F32)
        self.lame = self.sb("lame", [128, DEPTH, 2], F32)
        _tqf = self.sb("TQF", [128, 128], F32)
        _tqb = self.sb("TQB", [128, 128], F32)
        _dt = self.sb("DT", [128, 256], F32)
        _tk = self.sb("TK", [128, 4], F32)
        _cfb = self.sb("CFB", [128, 2], F32)
        self.TQF, self.TQB, self.DT, self.TK, self.CFB = [_tqf] * 2, [_tqb] * 2, [_dt] * 2, [_tk] * 2, [_cfb] * 2
        self.stf = self.sb("stf", [128, 128], F32)
        self.stb = self.sb("stb", [128, 128], F32)
        self.kfb = [self.sb("kfb%d" % i, [128, 128], BF16) for i in range(4)]
        self.qfb = [self.sb("qfb%d" % i, [128, 128], BF16) for i in range(4)]
        self.PSG = [self.ps("ps%d" % i, [128, 512], F32) for i in range(4)]
        self.OACC = [self.ps("oacc%d" % i, [128, 1024], F32) for i in range(2)]

        self.load_input()
        self.prologue_params()
        self.adaln_all()
        for l in range(self.depth):
            need_ctx = l < DEPTH - 1
            self.norm(l, 0, skip_ctx=False)
            if cfg.get("dbg_h") == l:
                self.dbg("dbg_h", self.HT[:], [128, KC, NT], BF16, [("HT", kc, g) for kc in range(KC) for g in range(5)])
            order = []
            mx = cfg.get("mixers", "ABC")
            if "A" in mx:
                order += [("A", h) for h in range(4)]
            if "B" in mx:
                order += [("B", 0), ("B", 1)]
            if "C" in mx:
                order += [("C", 0)]
            for kind, i in order:
                if kind == "A":
                    self.chunk_A(l, i, need_ctx)
                elif kind == "B":
                    self.chunk_B(l, i, need_ctx)
                else:
                    self.chunk_C(l, need_ctx)
            if cfg.get("mlp", True):
                self.mlp(l, need_ctx)
        if cfg.get("dbg_x"):
            self.dbg("dbg_x", self.XT[:], [128, KC, NT], F32, [("XT", kc, g) for kc in range(KC) for g in range(5)])
        self.drain()
        self.final_norm()
        S.emit(nc, self.stack)
        return nc

    def xt_res(self, kc, gi):
        return ("XT", kc, gi)

    def load_input(self):
        S = self.S
        XT, PS, identf = self.XT, self.PSG, self.C["ident_f"]
        for tb in range(NTB):
            buf = self.iobuf[tb % 2]
            bres = self.iores[tb % 2]
            gi = min(tb // 4, 4)
            src = self.x_d[tb * 128:(tb + 1) * 128, :] if tb < 16 else self.ctx_d[(tb - 16) * 128:(tb - 15) * 128, :]
            self.dma(buf, src, writes=bres, key="io%d" % (tb % 2))
            for half in range(2):
                pt = self.gp()
                for j in range(4):
                    kc = half * 4 + j
                    self.tr(pt[:, j * 128:(j + 1) * 128], buf[:, kc * 128:(kc + 1) * 128], identf[:],
                            reads=bres + [R(identf)], writes=[R(pt)])
                self.op("dve", "tensor_copy", out=XT[:, half * 4:half * 4 + 4, tb * 128:(tb + 1) * 128],
                        in_=pt[:].rearrange("p (j t) -> p j t", j=4),
                        reads=[R(pt)], writes=[("XT", k, gi) for k in range(half * 4, half * 4 + 4)])

    def prologue_params(self):
        P = self.P
        sm = self.small
        self.op("act", "activation", out=self.siluT[:], in_=P["cc"][:], func=AF.Silu,
                reads=[R(P["cc"])], writes=[R(self.siluT)])
        self.op("dve", "tensor_scalar", out=self.G32[0][:], in0=P["gmix"][:], scalar1=32.0, scalar2=None, op0=ALU.mult,
                reads=[R(P["gmix"])], writes=[R(self.G32[0])])
        self.op("dve", "tensor_scalar", out=self.G32[1][:], in0=P["gmlp"][:], scalar1=32.0, scalar2=None, op0=ALU.mult,
                reads=[R(P["gmlp"])], writes=[R(self.G32[1])])
        for src, dst, n in ((P["decbc"], self.LG, DEPTH * 8), (P["deccol"], self.LGC, DEPTH * 4)):
            sv = src[:].rearrange("p a b -> p (a b)") if len(src.shape) == 3 else src[:].rearrange("p a b c -> p (a b c)")
            dv = dst[:].rearrange("p a b -> p (a b)") if len(dst.shape) == 3 else dst[:].rearrange("p a b c -> p (a b c)")
            self.op("act", "activation", out=sm[:, 0:n], in_=sv, func=AF.Exp, scale=-1.0,
                    reads=[R(src)], writes=[R(sm)])
            self.op("act", "activation", out=sm[:, 32:32 + n], in_=sm[:, 0:n], func=AF.Ln, bias=self.epsb[:, 2:3], scale=1.0,
                    reads=[R(sm), R(self.epsb)], writes=[R(sm)])
            self.op("dve", "tensor_scalar", out=dv, in0=sm[:, 32:32 + n], scalar1=-1.0, scalar2=None, op0=ALU.mult,
                    reads=[R(sm)], writes=[R(dst)])
        self.op("act", "activation", out=self.ESINK[:], in_=P["sinkbc"][:], func=AF.Exp,
                reads=[R(P["sinkbc"])], writes=[R(self.ESINK)])
        fl = lambda t: t[:].rearrange("p a b -> p (a b)")
        self.op("dve", "tensor_tensor", out=fl(self.OA), in0=fl(P["lamq"]), in1=fl(P["lamk"]), op=ALU.mult,
                reads=[R(P["lamq"]), R(P["lamk"])], writes=[R(self.OA)])
        self.op("dve", "tensor_reduce", out=self.lame[:].rearrange("p a b -> p (a b)"),
                in_=fl(self.OA).rearrange("p (a c) -> p a c", c=64), axis=AX.X, op=ALU.add,
                reads=[R(self.OA)], writes=[R(self.lame)])
        self.op("act", "activation", out=self.lame[:], in_=self.lame[:], func=AF.Exp,
                reads=[R(self.lame)], writes=[R(self.lame)])
        for l in range(DEPTH):
            self.op("dve", "tensor_scalar", out=self.NEGLAM[:, l:l + 1], in0=self.lame[:, l, 1:2],
                    scalar1=self.lame[:, l, 0:1], scalar2=-LAM_INIT[l], op0=ALU.subtract, op1=ALU.add,
                    reads=[R(self.lame)], writes=[R(self.NEGLAM)])
            self.op("dve", "tensor_scalar", out=self.GA[:, l, :], in0=self.GA[:, l, :],
                    scalar1=1.0 - LAM_INIT[l], scalar2=None, op0=ALU.mult,
                    reads=[R(self.GA)], writes=[R(self.GA)])

    def wst_load(self, src3, pieces, keyname="wst"):
        i = self.wst_i % 2
        self.wst_i += 1
        buf = self.WST[i]
        res = ("WST", i)
        for (sc, dc, n) in pieces:
            self.dma(buf[:, :, dc:dc + n], src3[:, :, sc:sc + n], writes=[res], key="wst%d" % i, eng="pool")
        return buf, res

    def adaln_gs(self, l):
        for w, base in ((0, 8), (1, 32)):
            self.op("dve", "scalar_tensor_tensor", out=self.GS[w][:, l, :, :], in0=self.MOD[:, l, base:base + 8, :],
                    scalar=1.0, in1=self.G32[w][:, l, :].unsqueeze(2).to_broadcast([128, KC, 2]),
                    op0=ALU.add, op1=ALU.mult,
                    reads=[("MOD", l), R(self.G32[w])], writes=[("GS", w, l)])

    def adaln_unit_load(self, l, j, wbuf, i):
        src3 = self.wada_d[l].rearrange("(kc p) n -> p kc n", p=128)
        self.dma(wbuf[:, :, 384:512], src3[:, :, j * 128:(j + 1) * 128], writes=[("WSTx", i)], key="wax%d" % i, eng="pool")

    def adaln_unit_compute(self, l, j, wbuf, i):
        pst = self.gp()
        for kc in range(KC):
            self.mm(pst[:, 0:2], wbuf[:, kc, 384:512], self.siluT[:, kc, :], start=(kc == 0), stop=(kc == KC - 1),
                    reads=[("WSTx", i), R(self.siluT)], writes=[R(pst)])
        self.op("dve", "tensor_scalar", out=self.MOD[:, l, j, :], in0=pst[:, 0:2], scalar1=self.P["bada"][:, l, j:j + 1],
                scalar2=None, op0=ALU.add, reads=[R(pst), R(self.P["bada"])], writes=[("MOD", l)])

    def adaln_all(self):
        P = self.P
        for l in range(1):
            src3 = self.wada_d[l].rearrange("(kc p) n -> p kc n", p=128)
            for piece in range(12):
                buf, res = self.wst_load(src3, [(piece * 512, 0, 512)])
                pst = self.gp()
                for jc in range(4):
                    for kc in range(KC):
                        self.mm(pst[:, jc * 2:jc * 2 + 2], buf[:, kc, jc * 128:(jc + 1) * 128], self.siluT[:, kc, :],
                                start=(kc == 0), stop=(kc == KC - 1), reads=[res, R(self.siluT)], writes=[R(pst)])
                j0 = piece * 4
                self.op("dve", "tensor_tensor", out=self.MOD[:, l, j0:j0 + 4, :],
                        in0=pst[:, 0:8].rearrange("p (j s) -> p j s", s=2),
                        in1=P["bada"][:, l, j0:j0 + 4].unsqueeze(2).to_broadcast([128, 4, 2]), op=ALU.add,
                        reads=[R(pst), R(P["bada"])], writes=[("MOD", l)])
            for w, base in ((0, 8), (1, 32)):
                self.op("dve", "scalar_tensor_tensor", out=self.GS[w][:, l, :, :], in0=self.MOD[:, l, base:base + 8, :],
                        scalar=1.0, in1=self.G32[w][:, l, :].unsqueeze(2).to_broadcast([128, KC, 2]),
                        op0=ALU.add, op1=ALU.mult,
                        reads=[("MOD", l), R(self.G32[w])], writes=[("GS", w, l)])

    def sumsq_rstd(self, gi, acc):
        t0, n = TGS[gi]
        XT, onesb = self.XT, self.C["ones_b"]
        for kc in range(KC):
            sq = self.sqb[kc % 2]
            self.op("act", "activation", out=sq[:, :n], in_=XT[:, kc, t0:t0 + n], func=AF.Square,
                    reads=[("XT", kc, gi)], writes=[R(sq)])
            self.mm(acc[:, :n], onesb[:], sq[:, :n], start=(kc == 0), stop=(kc == KC - 1),
                    reads=[R(sq), R(onesb)], writes=[R(acc)])
        self.op("act", "activation", out=self.rstd0[:, :n], in_=acc[:, :n], func=AF.Ln, bias=self.epsb[:, 0:1], scale=1.0,
                reads=[R(acc), R(self.epsb)], writes=[R(self.rstd0)])
        self.op("act", "activation", out=self.rstd[:, :n], in_=self.rstd0[:, :n], func=AF.Exp, scale=-0.5,
                reads=[R(self.rstd0)], writes=[R(self.rstd)])

    def norm(self, l, w, skip_ctx):
        shbase = 0 if w == 0 else 24
        for gi, (t0, n) in enumerate(TGS):
            if gi == 4 and skip_ctx:
                continue
            s = 0 if gi < 4 else 1
            acc = self.gp()
            self.sumsq_rstd(gi, acc)
            for kc in range(KC):
                tm = self.tmpf[kc % 2]
                self.op("dve", "scalar_tensor_tensor", out=tm[:, :n], in0=self.XT[:, kc, t0:t0 + n],
                        scalar=self.GS[w][:, l, kc, s:s + 1], in1=self.rstd[:, :n], op0=ALU.mult, op1=ALU.mult,
                        reads=[("XT", kc, gi), ("GS", w, l), R(self.rstd)], writes=[R(tm)])
                self.op("act", "activation", out=self.HT[:, kc, t0:t0 + n], in_=tm[:, :n], func=AF.Identity,
                        bias=self.MOD[:, l, shbase + kc, s:s + 1], scale=1.0,
                        reads=[R(tm), ("MOD", l)], writes=[("HT", kc, gi)])

    def proj_fm(self, wbuf, wres, c0, dst, dname, rope, with_ctx, dst_fn=None):
        if dst_fn is None:
            dst_fn = lambda t0, n: [(dst[:, t0:t0 + n], slice(0, 128))]
        pending = None
        for gi, (t0, n) in enumerate(TGS):
            if gi == 4 and not with_ctx:
                continue
            ps = self.gp()
            for kc in range(KC):
                self.mm(ps[:, :n], wbuf[:, kc, c0:c0 + 128], self.HT[:, kc, t0:t0 + n], start=(kc == 0), stop=(kc == KC - 1),
                        reads=[wres, ("HT", kc, gi)], writes=[R(ps)])
            if rope and gi < 4:
                tail = self.rope_head(ps, n)
                if pending is not None:
                    self.rope_tail(*pending)
                pending = (ps, dst_fn(t0, n), t0, n, (dname, gi)) + tail
            else:
                for (oap, psl) in dst_fn(t0, n):
                    self.op("act", "activation", out=oap, in_=ps[psl, :n], func=AF.Copy,
                            reads=[R(ps)], writes=[(dname, gi)])
        if pending is not None:
            self.rope_tail(*pending)

    def rope_head(self, ps, n):
        i = self.rope_i % 2
        self.rope_i += 1
        qb = self.ropeq[i]
        self.op("act", "activation", out=qb[:, :n], in_=ps[:, :n], func=AF.Copy, reads=[R(ps)], writes=[R(qb)])
        return (i, qb)

    def rope_tail(self, ps, dsts, t0, n, dres, i, qb):
        t1, t2 = self.tmpf[2 * i], self.tmpf[2 * i + 1]
        permb, cosT, sinT = self.C["perm_b"], self.C["cosT"], self.C["sinT"]
        ps2 = self.gp()
        self.mm(ps2[:, :n], permb[:], qb[:, :n], reads=[R(qb), R(permb)], writes=[R(ps2)])
        self.op("dve", "tensor_tensor", out=t1[:, :n], in0=ps[:, :n], in1=cosT[:, t0:t0 + n], op=ALU.mult,
                reads=[R(ps), R(cosT)], writes=[R(t1)])
        self.op("dve", "tensor_tensor", out=t2[:, :n], in0=ps2[:, :n], in1=sinT[:, t0:t0 + n], op=ALU.mult,
                reads=[R(ps2), R(sinT)], writes=[R(t2)])
        for (oap, psl) in dsts:
            self.op("pool", "tensor_tensor", out=oap, in0=t1[psl, :n], in1=t2[psl, :n], op=ALU.add,
                    reads=[R(t1), R(t2)], writes=[dres])

    def proj_tm(self, wbuf, wres, c0, dst_fn, dname, func=None):
        for tb0 in range(0, NTB, 4):
            nb = min(4, NTB - tb0)
            gi = min(tb0 // 4, 4)
            ps = self.gp()
            for j in range(nb):
                tb = tb0 + j
                for kc in range(KC):
                    self.mm(ps[:, j * 128:(j + 1) * 128], self.HT[:, kc, tb * 128:(tb + 1) * 128], wbuf[:, kc, c0:c0 + 128],
                            start=(kc == 0), stop=(kc == KC - 1), reads=[wres, ("HT", kc, gi)], writes=[R(ps)])
            out, in_ = dst_fn(tb0, nb, ps)
            self.op("act", "activation", out=out, in_=in_, func=(func or AF.Copy), reads=[R(ps)], writes=[(dname, gi)])

    def finish_chunk(self, l, ci, OC3, ocname, OCT, octname, need_ctx):
        self.drain()
        i = self.wo_i % 2
        self.wo_i += 1
        WO = self.WO[i]
        wres = ("WO", i)
        self.dma(WO[:], self.wout_d[l][ci * 128:(ci + 1) * 128, :], writes=[wres], key="wo%d" % i, eng="pool")
        identb = self.C["ident_b"]
        ntb = NTB if need_ctx else 16
        for tb0 in range(0, ntb, 4):
            nb = min(4, ntb - tb0)
            gi = min(tb0 // 4, 4)
            pk = self.gp()
            pkb = pk[:].bitcast(BF16)
            for j in range(nb):
                self.tr(pkb[:, j * 128:(j + 1) * 128], OC3[:, tb0 + j, :], identb[:],
                        reads=[(ocname, gi), R(identb)], writes=[R(pk)])
            self.op("dve", "tensor_copy", out=OCT[:, tb0 * 128:(tb0 + nb) * 128], in_=pkb[:, 0:nb * 128],
                    reads=[R(pk)], writes=[(octname, gi)])
        for gi, (t0, n) in enumerate(TGS):
            if gi == 4 and not need_ctx:
                continue
            s = 0 if gi < 4 else 1
            for dc in range(KC):
                def item(use_gp, gi=gi, t0=t0, n=n, s=s, dc=dc):
                    if use_gp:
                        ps = self.gp()
                        pres = R(ps)
                    else:
                        ps, pres = self.ab()
                    self.mm(ps[:, :n], WO[:, dc * 128:(dc + 1) * 128], OCT[:, t0:t0 + n],
                            reads=[wres, (octname, gi)], writes=[pres])
                    self.op("dve", "scalar_tensor_tensor", out=self.XT[:, dc, t0:t0 + n], in0=ps[:, :n],
                            scalar=self.MOD[:, l, 16 + dc, s:s + 1], in1=self.XT[:, dc, t0:t0 + n], op0=ALU.mult, op1=ALU.add,
                            reads=[pres, ("MOD", l), ("XT", dc, gi)], writes=[("XT", dc, gi)])
                self.deferred.append(item)
        self.drain(use_gp=True)

    def drain(self, k=None, use_gp=False):
        while self.deferred and (k is None or k > 0):
            self.deferred.pop(0)(use_gp)
            if k is not None:
                k -= 1

    def chunk_A(self, l, h, need_ctx):
        src3 = self.win_d[l].rearrange("(kc p) n -> p kc n", p=128)
        wbuf, wres = self.wst_load(src3, [(h * 128, 0, 128), (512 + h * 128, 128, 128), (1024 + h * 128, 256, 128)])
        par = h % 2
        QT, KT, V = self.gb(par * 3), self.gb(par * 3 + 1), self.gb(par * 3 + 2)
        qn, kn, vn = ("GB", par * 3), ("GB", par * 3 + 1), ("GB", par * 3 + 2)
        OC, OCT = self.gb(6), self.gb(7)
        V3 = V.rearrange("p (t c) -> p t c", c=130)
        OC3 = OC[:, 0:NT].rearrange("p (t c) -> p t c", c=128)
        allg = list(range(5))
        self.op("pool", "memset", ap=V3[:, :, 128:129], constant=1.0, writes=[(vn, g) for g in allg])
        stage = self.cfg.get("a_stage", 9)
        if stage < 0:
            return
        self.proj_fm(wbuf, wres, 0, QT, qn, stage >= 0.5, need_ctx)
        if stage < 0.7:
            return
        self.proj_fm(wbuf, wres, 128, KT, kn, True, True)
        if stage < 0.8:
            return
        self.proj_tm(wbuf, wres, 256,
                     lambda tb0, nb, ps: (V3[:, tb0:tb0 + nb, 0:128], ps[:, 0:nb * 128].rearrange("p (j c) -> p j c", c=128)),
                     vn)
        if stage < 2:
            return
        itc = 0
        wi = wres[1]
        ada_l = l + 1 if (l + 1 < self.depth and self.cfg.get("ada_il", True)) else None
        ada_units = list(range(h * 12, h * 12 + 12)) if ada_l is not None else []
        ada_loaded = None
        for gi, (t0, n) in enumerate(TGS):
            if gi == 4 and not need_ctx:
                continue
            kbs = list(range(NTB)) if gi < 4 else [16, 17]
            nqb = n // 128
            nbk = nqb // 2
            aress = [[("oacc", c, 0), ("oacc", c, 1)] for c in range(2)]

            def emit_pv(ki, kb, pts):
                kgi = min(kb // 4, 4)
                for c in range(2):
                    acc = self.OACC[c]
                    for qb in range(nqb):
                        off = (qb // 2) * 512 + (qb % 2) * 129
                        self.mm(acc[:, off:off + 129], pts[c][:, qb * 128:(qb + 1) * 128], V3[:, kb, 0:129],
                                start=(ki == 0 and qb % 2 == 0), stop=(ki == len(kbs) - 1),
                                reads=[R(pts[c]), (vn, kgi)], writes=[aress[c][qb // 2]], sgc=True)

            pending = None
            for ki, kb in enumerate(kbs):
                kgi = min(kb // 4, 4)
                sts = [self.gp(), self.gp()]
                for c in range(2):
                    hs = slice(c * 64, (c + 1) * 64)
                    self.mm(sts[c][:, :n], KT[hs, kb * 128:(kb + 1) * 128], QT[hs, t0:t0 + n],
                            reads=[(kn, kgi), (qn, gi)], writes=[R(sts[c])])
                pts = [self.ptb(), self.ptb()]
                for c in range(2):
                    self.op("act", "activation", out=pts[c][:, :n], in_=sts[c][:, :n], func=AF.Exp, scale=0.125,
                            reads=[R(sts[c])], writes=[R(pts[c])])
                if pending is not None:
                    emit_pv(*pending)
                pending = (ki, kb, pts)
                if ada_l is not None and gi < 4:
                    if itc % 6 == 5 and ada_loaded is not None:
                        self.adaln_unit_compute(ada_l, ada_loaded, wbuf, wi)
                        ada_loaded = None
                    if itc % 6 == 0 and ada_units:
                        ada_loaded = ada_units.pop(0)
                        self.adaln_unit_load(ada_l, ada_loaded, wbuf, wi)
                    itc += 1
            emit_pv(*pending)
            if stage < 3:
                continue
            for c in range(2):
                acc = self.OACC[c]
                ares = aress[c]
                accv = acc[:].rearrange("p (b x) -> p b x", b=2)[:, 0:nbk, 0:258].rearrange("p b (j c) -> p b j c", c=129)
                zv = accv[:, :, :, 128]
                ov = accv[:, :, :, 0:128]
                sm = self.small
                rz = sm[:, 0:nqb].rearrange("p (b j) -> p b j", j=2)
                ar = ares[0:nbk]
                self.op("dve", "reciprocal", out=rz, in_=zv, reads=ar, writes=[R(sm)])
                o0 = self.O0N[:, 0:nqb, :].rearrange("p (b j) c -> p b j c", j=2)
                oa = self.OA[:, 0:nqb, :].rearrange("p (b j) c -> p b j c", j=2)
                if c == 0:
                    self.op("dve", "tensor_tensor", out=o0, in0=ov, in1=rz.unsqueeze(3).to_broadcast([128, nbk, 2, 128]),
                            op=ALU.mult, reads=ar + [R(sm)], writes=[R(self.O0N)])
                else:
                    rz1 = sm[:, 4:4 + nqb].rearrange("p (b j) -> p b j", j=2)
                    self.op("dve", "tensor_scalar", out=rz1, in0=rz, scalar1=self.NEGLAM[:, l:l + 1], scalar2=None,
                            op0=ALU.mult, reads=[R(sm), R(self.NEGLAM)], writes=[R(sm)])
                    self.op("dve", "tensor_tensor", out=oa, in0=ov, in1=rz1.unsqueeze(3).to_broadcast([128, nbk, 2, 128]),
                            op=ALU.mult, reads=ar + [R(sm)], writes=[R(self.OA)])
                    oaf = self.OA[:, 0:nqb, :]
                    obf = self.OB[:, 0:nqb, :]
                    self.op("pool", "tensor_tensor", out=oaf, in0=oaf, in1=self.O0N[:, 0:nqb, :], op=ALU.add,
                            reads=[R(self.OA), R(self.O0N)], writes=[R(self.OA)])
                    self.op("pool", "tensor_tensor", out=obf, in0=oaf, in1=oaf, op=ALU.mult,
                            reads=[R(self.OA)], writes=[R(self.OB)])
                    ss = sm[:, 8:8 + nqb]
                    self.op("dve", "tensor_reduce", out=ss, in_=obf, axis=AX.X, op=ALU.add,
                            reads=[R(self.OB)], writes=[R(sm)])
                    l1 = sm[:, 12:12 + nqb]
                    rs = sm[:, 16:16 + nqb]
                    self.op("act", "activation", out=l1, in_=ss, func=AF.Ln, bias=self.epsb[:, 1:2], scale=1.0 / 128.0,
                            reads=[R(sm), R(self.epsb)], writes=[R(sm)])
                    self.op("act", "activation", out=rs, in_=l1, func=AF.Exp, scale=-0.5,
                            reads=[R(sm)], writes=[R(sm)])
                    self.op("dve", "tensor_tensor", out=obf, in0=oaf, in1=rs.unsqueeze(2).to_broadcast([128, nqb, 128]),
                            op=ALU.mult, reads=[R(self.OA), R(sm)], writes=[R(self.OB)])
                    tbq = t0 // 128
                    self.op("pool", "tensor_tensor", out=OC3[:, tbq:tbq + nqb, :], in0=obf,
                            in1=self.GA[:, l, :].unsqueeze(1).to_broadcast([128, nqb, 128]), op=ALU.mult,
                            reads=[R(self.OB), R(self.GA)], writes=[(("GB", 6), gi)])
        if ada_l is not None:
            assert ada_loaded is None and not ada_units
            if h == 3:
                self.adaln_gs(ada_l)
        if stage < 4:
            return
        self.finish_chunk(l, h, OC3, ("GB", 6), OCT, ("GB", 7), need_ctx)

    def ret_tables(self, l, pp):
        C = self.C
        if True:
            lgf = self.LGC[:, l, 0, pp:pp + 1]
            lgb = self.LGC[:, l, 1, pp:pp + 1]
            self.op("act", "activation", out=self.TQF[pp][:], in_=C["i1"][:], func=AF.Exp, scale=lgf,
                    reads=[R(C["i1"]), R(self.LGC)], writes=[R(self.TQF[pp])])
            self.op("act", "activation", out=self.TQB[pp][:], in_=C["i2"][:], func=AF.Exp, scale=lgb,
                    reads=[R(C["i2"]), R(self.LGC)], writes=[R(self.TQB[pp])])
            self.op("act", "activation", out=self.CFB[pp][:, 0:1], in_=lgf, func=AF.Exp, scale=128.0,
                    reads=[R(self.LGC)], writes=[R(self.CFB[pp])])
            self.op("act", "activation", out=self.CFB[pp][:, 1:2], in_=lgb, func=AF.Exp, scale=128.0,
                    reads=[R(self.LGC)], writes=[R(self.CFB[pp])])
            for h2 in range(2):
                h = 2 * pp + h2
                e1 = self.tmpf[0][:, 0:128]
                e2 = self.tmpf[1][:, 0:128]
                self.op("dve", "tensor_scalar", out=e1, in0=C["r1"][:], scalar1=self.LG[:, l, h:h + 1], scalar2=None,
                        op0=ALU.mult, reads=[R(C["r1"]), R(self.LG)], writes=[R(self.tmpf[0])])
                self.op("dve", "scalar_tensor_tensor", out=e2, in0=C["r2"][:], scalar=self.LG[:, l, 4 + h:5 + h], in1=e1,
                        op0=ALU.mult, op1=ALU.add, reads=[R(C["r2"]), R(self.LG), R(self.tmpf[0])], writes=[R(self.tmpf[1])])
                self.op("act", "activation", out=self.DT[pp][:, h2 * 128:(h2 + 1) * 128], in_=e2, func=AF.Exp,
                        reads=[R(self.tmpf[1])], writes=[R(self.DT[pp])])
            sm = self.small
            self.op("act", "activation", out=sm[:, 20:22], in_=self.LG[:, l, 2 * pp:2 * pp + 2], func=AF.Exp,
                    scale=C["kcol"][:, 0:1], reads=[R(self.LG), R(C["kcol"])], writes=[R(sm)])
            self.op("act", "activation", out=sm[:, 22:24], in_=self.LG[:, l, 4 + 2 * pp:6 + 2 * pp], func=AF.Exp,
                    scale=C["kcol"][:, 1:2], reads=[R(self.LG), R(C["kcol"])], writes=[R(sm)])
            self.op("dve", "tensor_scalar", out=self.TK[pp][:], in0=sm[:, 20:24], scalar1=0.125, scalar2=None, op0=ALU.mult,
                    reads=[R(sm)], writes=[R(self.TK[pp])])

    def chunk_B(self, l, pp, need_ctx):
        src3 = self.win_d[l].rearrange("(kc p) n -> p kc n", p=128)
        wbuf, wres = self.wst_load(src3, [(1536 + pp * 128, 0, 128), (1792 + pp * 128, 128, 128),
                                          (2048 + pp * 128, 256, 128), (2304 + pp * 128, 384, 128)])
        names = [("GB", i) for i in range(8)]
        QT, KT, V, G, SF, SB, OC, OCT = [self.gb(i) for i in range(8)]
        qn, kn, vn, gn, sfn, sbn, ocn, octn = names
        V3 = V.rearrange("p (t c) -> p t c", c=130)
        G3 = G[:, 0:NT].rearrange("p (t c) -> p t c", c=128)
        SF3 = SF[:, 0:NT].rearrange("p (t c) -> p t c", c=128)
        SB3 = SB[:, 0:NT].rearrange("p (t c) -> p t c", c=128)
        OC3 = OC[:, 0:NT].rearrange("p (t c) -> p t c", c=128)
        identb = self.C["ident_b"]
        self.ret_tables(l, pp)
        self.proj_fm(wbuf, wres, 0, QT, qn, True, need_ctx)
        self.proj_fm(wbuf, wres, 128, KT, kn, True, True)
        self.proj_tm(wbuf, wres, 256,
                     lambda tb0, nb, ps: (V3[:, tb0:tb0 + nb, 0:128], ps[:, 0:nb * 128].rearrange("p (j c) -> p j c", c=128)),
                     vn)
        self.proj_tm(wbuf, wres, 384,
                     lambda tb0, nb, ps: (G3[:, tb0:tb0 + nb, :], ps[:, 0:nb * 128].rearrange("p (j c) -> p j c", c=128)),
                     gn, func=AF.Silu)
        orders = [[16, 17] + list(range(16)), [17, 16] + list(range(15, -1, -1))]
        sts_ = [self.stf, self.stb]
        ST3s = [SF3, SB3]
        stns = [sfn, sbn]
        for d in range(2):
            self.op("pool", "memset", ap=sts_[d][:], constant=0.0, writes=[R(sts_[d])])
        NS = len(orders[0])
        pks = {}
        pus = {}
        for i in range(NS + 3):
            for d in range(2):
                if i < NS:
                    tb = orders[d][i]
                    gi = min(tb // 4, 4)
                    pk = self.gp()
                    pkb = pk[:].bitcast(BF16)
                    self.tr(pkb[:, 0:128], KT[:, tb * 128:(tb + 1) * 128], identb[:], reads=[(kn, gi), R(identb)], writes=[R(pk)])
                    pks[(d, i)] = (pk, pkb)
                if 0 <= i - 1 < NS:
                    pk, pkb = pks.pop((d, i - 1))
                    kf = self.kfb[2 * d + (i - 1) % 2]
                    self.op("dve", "tensor_tensor", out=kf[:].rearrange("p (h c) -> p h c", h=2),
                            in0=pkb[:, 0:128].rearrange("p (h c) -> p h c", h=2),
                            in1=self.TK[pp][:, 2 * d:2 * d + 2].unsqueeze(2).to_broadcast([128, 2, 64]), op=ALU.mult,
                            reads=[R(pk), R(self.TK[pp])], writes=[R(kf)])
                if 0 <= i - 2 < NS:
                    tb = orders[d][i - 2]
                    gi = min(tb // 4, 4)
                    kf = self.kfb[2 * d + (i - 2) % 2]
                    pu, pures = self.ab()
                    self.mm(pu[:, 0:128], kf[:], V3[:, tb, 0:128], reads=[R(kf), (vn, gi)], writes=[pures])
                    pus[(d, i - 2)] = (pu, pures)
                if 0 <= i - 3 < NS:
                    tb = orders[d][i - 3]
                    gi = min(tb // 4, 4)
                    st = sts_[d]
                    pu, pures = pus.pop((d, i - 3))
                    self.op("pool", "tensor_copy", out=ST3s[d][:, tb, :], in_=st[:], reads=[R(st)], writes=[(stns[d], gi)])
                    self.op("dve", "scalar_tensor_tensor", out=st[:], in0=st[:], scalar=self.CFB[pp][:, d:d + 1], in1=pu[:, 0:128],
                            op0=ALU.mult, op1=ALU.add, reads=[R(st), R(self.CFB[pp]), pures], writes=[R(st)])
        ntb = NTB if need_ctx else 16
        sm = self.small
        cur = {}

        def front(tb):
            gi = min(tb // 4, 4)
            ts_ = slice(tb * 128, (tb + 1) * 128)
            sts = [self.gp(), self.gp()]
            for h2 in range(2):
                hs = slice(h2 * 64, (h2 + 1) * 64)
                self.mm(sts[h2][:, 0:128], KT[hs, ts_], QT[hs, ts_], reads=[(kn, gi), (qn, gi)], writes=[R(sts[h2])])
            pt = self.ptb()
            for h2 in range(2):
                self.op("dve", "scalar_tensor_tensor", out=pt[:, h2 * 128:(h2 + 1) * 128], in0=sts[h2][:, 0:128], scalar=0.125,
                        in1=self.DT[pp][:, h2 * 128:(h2 + 1) * 128],
                        op0=ALU.mult, op1=ALU.mult, reads=[R(sts[h2]), R(self.DT[pp])], writes=[R(pt)])
            qf = self.qfb[(2 * tb) % 4]
            qb_ = self.qfb[(2 * tb + 1) % 4]
            self.op("pool", "tensor_tensor", out=qf[:], in0=QT[:, ts_], in1=self.TQF[pp][:], op=ALU.mult,
                    reads=[(qn, gi), R(self.TQF[pp])], writes=[R(qf)])
            self.op("pool", "tensor_tensor", out=qb_[:], in0=QT[:, ts_], in1=self.TQB[pp][:], op=ALU.mult,
                    reads=[(qn, gi), R(self.TQB[pp])], writes=[R(qb_)])
            return (pt, qf, qb_)

        def back(tb, pt, qf, qb_):
            gi = min(tb // 4, 4)
            jj = tb % 4
            if jj == 0:
                cur["po"], cur["pres"] = self.ab()
            po, pres = cur["po"], cur["pres"]
            for h2 in range(2):
                hs = slice(h2 * 64, (h2 + 1) * 64)
                oreg = po[:, jj * 128 + h2 * 64:jj * 128 + (h2 + 1) * 64]
                self.mm(oreg, qf[hs, :], SF3[hs, tb, hs], start=(jj == 0 and h2 == 0), stop=False,
                        reads=[R(qf), (sfn, gi)], writes=[pres], sgc=True)
                self.mm(oreg, qb_[hs, :], SB3[hs, tb, hs], start=False, stop=False,
                        reads=[R(qb_), (sbn, gi)], writes=[pres], sgc=True)
                self.mm(oreg, pt[:, h2 * 128:(h2 + 1) * 128], V3[:, tb, hs], start=False, stop=True,
                        reads=[R(pt), (vn, gi)], writes=[pres], sgc=True)
            last = (jj == 3) or (tb == ntb - 1)
            if not last:
                return
            tb0 = tb - jj
            nb = jj + 1
            ng = 2 * nb
            W = nb * 128
            v3 = lambda a: a[:, 0:W].rearrange("p (g c) -> p g c", c=64)
            osb = self.OA[:].rearrange("p a b -> p (a b)")
            ocn_ = self.OB[:].rearrange("p a b -> p (a b)")
            sqv = self.O0N[:].rearrange("p a b -> p (a b)")
            self.op("dve", "tensor_copy", out=osb[:, 0:W], in_=po[:, 0:W], reads=[pres], writes=[R(self.OA)])
            self.op("dve", "tensor_reduce", out=sm[:, 24:24 + ng], in_=v3(osb), axis=AX.X, op=ALU.add,
                    reads=[R(self.OA)], writes=[R(sm)])
            self.op("dve", "tensor_scalar", out=sm[:, 24:24 + ng], in0=sm[:, 24:24 + ng], scalar1=1.0 / 64.0, scalar2=None, op0=ALU.mult,
                    reads=[R(sm)], writes=[R(sm)])
            self.op("dve", "tensor_tensor", out=v3(ocn_), in0=v3(osb), in1=sm[:, 24:24 + ng].unsqueeze(2).to_broadcast([128, ng, 64]),
                    op=ALU.subtract, reads=[R(self.OA), R(sm)], writes=[R(self.OB)])
            self.op("pool", "tensor_tensor", out=sqv[:, 0:W], in0=ocn_[:, 0:W], in1=ocn_[:, 0:W], op=ALU.mult,
                    reads=[R(self.OB)], writes=[R(self.O0N)])
            self.op("dve", "tensor_reduce", out=sm[:, 44:44 + ng], in_=v3(sqv), axis=AX.X, op=ALU.add,
                    reads=[R(self.O0N)], writes=[R(sm)])
            self.op("act", "activation", out=sm[:, 52:52 + ng], in_=sm[:, 44:44 + ng], func=AF.Ln, bias=self.epsb[:, 1:2], scale=1.0 / 64.0,
                    reads=[R(sm), R(self.epsb)], writes=[R(sm)])
            self.op("act", "activation", out=sm[:, 24:24 + ng], in_=sm[:, 52:52 + ng], func=AF.Exp, scale=-0.5,
                    reads=[R(sm)], writes=[R(sm)])
            self.op("dve", "tensor_tensor", out=v3(osb), in0=v3(ocn_), in1=sm[:, 24:24 + ng].unsqueeze(2).to_broadcast([128, ng, 64]),
                    op=ALU.mult, reads=[R(self.OB), R(sm)], writes=[R(self.OA)])
            self.op("pool", "tensor_tensor", out=OC3[:, tb0:tb0 + nb, :], in0=osb[:, 0:W].rearrange("p (t c) -> p t c", c=128),
                    in1=G3[:, tb0:tb0 + nb, :], op=ALU.mult,
                    reads=[R(self.OA), (gn, gi)], writes=[(ocn, gi)])

        pending = None
        for tb in range(ntb):
            fr = front(tb)
            if pending is not None:
                back(*pending)
            pending = (tb,) + fr
            self.drain(3, use_gp=True)
        back(*pending)
        self.finish_chunk(l, 4 + pp, OC3, ocn, OCT, octn, need_ctx)

    def chunk_C(self, l, need_ctx):
        src3 = self.win_d[l].rearrange("(kc p) n -> p kc n", p=128)
        wbuf, wres = self.wst_load(src3, [(2560, 0, 256), (2816, 256, 64), (2816, 320, 64), (2880, 384, 64), (2880, 448, 64)])
        wbufv, wresv = self.wst_load(src3, [(2944, 0, 128)])
        QZ = self.GBT[:, 0:2 * NT].rearrange("p (g t) -> p g t", g=2)
        qzn = "QZ"
        gb01 = [(("GB", i), g_) for i in range(2) for g_ in range(5)]
        KT, V = self.gb(2), self.gb(3)
        kn, vn = ("GB", 2), ("GB", 3)
        OC, OCT = self.gb(6), self.gb(7)
        ocn, octn = ("GB", 6), ("GB", 7)
        V4 = V.rearrange("p (t j c) -> p t j c", j=2, c=65)
        OC3 = OC[:, 0:NT].rearrange("p (t c) -> p t c", c=128)
        allg = list(range(5))
        self.op("pool", "memset", ap=V4[:, :, :, 64:65], constant=1.0, writes=[(vn, g) for g in allg])
        self.op("pool", "memset", ap=QZ[64:128, 0, :], constant=0.0, writes=gb01 + [(qzn, g) for g in allg])
        self.op("pool", "memset", ap=QZ[0:64, 1, :], constant=0.0, writes=gb01 + [(qzn, g) for g in allg])
        self.proj_tm(wbufv, wresv, 0,
                     lambda tb0, nb, ps: (V4[:, tb0:tb0 + nb, :, 0:64],
                                          ps[:, 0:nb * 128].rearrange("p (t j c) -> p t j c", j=2, c=64)),
                     vn)
        ntb = NTB if need_ctx else 16
        sm = self.small
        qz_fn = lambda t0, n: [(QZ[0:64, 0, t0:t0 + n], slice(0, 64)), (QZ[64:128, 1, t0:t0 + n], slice(64, 128))]
        for j in range(2):
            self.proj_fm(wbuf, wres, j * 128, None, qzn, True, need_ctx, dst_fn=qz_fn)
            self.proj_fm(wbuf, wres, 256 + j * 128, KT, kn, True, True)
            items = []
            for n_ in range(ntb):
                if n_ < 16:
                    kbs = ([n_ - 1] if n_ > 0 else []) + [n_] + ([n_ + 1] if n_ < 15 else []) + [16, 17]
                else:
                    kbs = [16, 17]
                for ki, m in enumerate(kbs):
                    items.append((n_, ki, m, len(kbs)))
            cur = {}

            def front(n_, ki, m, nk):
                gi = min(n_ // 4, 4)
                mgi = min(m // 4, 4)
                st = self.gp()
                self.mm(st[:, 0:256], KT[:, m * 128:(m + 1) * 128], QZ[:, :, n_ * 128:(n_ + 1) * 128],
                        reads=[(kn, mgi), (qzn, gi)], writes=[R(st)])
                pt = self.ptb()
                self.op("act", "activation", out=pt[:, 0:256], in_=st[:, 0:256], func=AF.Exp, scale=0.125,
                        reads=[R(st)], writes=[R(pt)])
                if n_ < 16 and m == n_ - 1:
                    self.op("pool", "tensor_tensor", out=pt[:, 0:256], in0=pt[:, 0:256], in1=self.C["maskp"][:], op=ALU.mult,
                            reads=[R(pt), R(self.C["maskp"])], writes=[R(pt)])
                if n_ < 15 and m == n_ + 1:
                    self.op("pool", "tensor_tensor", out=pt[:, 0:256], in0=pt[:, 0:256], in1=self.C["maskn"][:], op=ALU.mult,
                            reads=[R(pt), R(self.C["maskn"])], writes=[R(pt)])
                return pt

            def back(n_, ki, m, nk, pt):
                mgi = min(m // 4, 4)
                r3 = n_ % 3
                if r3 == 0 and ki == 0:
                    cur["po"], cur["pres"] = self.ab()
                    cur["n0"] = n_
                po, pres = cur["po"], cur["pres"]
                for g in range(2):
                    off = r3 * 130 + g * 65
                    self.mm(po[:, off:off + 65], pt[:, g * 128:(g + 1) * 128], V4[:, m, j, 0:65],
                            start=(r3 == 0 and ki == 0 and g == 0), stop=(ki == nk - 1), reads=[R(pt), (vn, mgi)], writes=[pres],
                            sgc=True)
                if ki == nk - 1 and (r3 == 2 or n_ == ntb - 1):
                    n0 = cur["n0"]
                    cnt = n_ - n0 + 1
                    pov = po[:, 0:cnt * 130].rearrange("p (n g c) -> p n g c", g=2, c=65)
                    den = sm[:, 34:34 + 2 * cnt].rearrange("p (n g) -> p n g", g=2)
                    rz = sm[:, 40:40 + 2 * cnt].rearrange("p (n g) -> p n g", g=2)
                    self.op("dve", "tensor_tensor", out=den, in0=pov[:, :, :, 64],
                            in1=self.ESINK[:, l, 2 * j:2 * j + 2].unsqueeze(1).to_broadcast([128, cnt, 2]), op=ALU.add,
                            reads=[pres, R(self.ESINK)], writes=[R(sm)])
                    self.op("dve", "reciprocal", out=rz, in_=den, reads=[R(sm)], writes=[R(sm)])
                    gis = sorted(set(min(q // 4, 4) for q in range(n0, n_ + 1)))
                    self.op("dve", "tensor_tensor", out=OC3[:, n0:n_ + 1, :].rearrange("p n (g c) -> p n g c", g=2),
                            in0=pov[:, :, :, 0:64], in1=rz.unsqueeze(3).to_broadcast([128, cnt, 2, 64]), op=ALU.mult,
                            reads=[pres, R(sm)], writes=[(ocn, g_) for g_ in gis])

            pend = []
            for it in items:
                pt = front(*it)
                pend.append(it + (pt,))
                if len(pend) > 3:
                    back(*pend.pop(0))
            while pend:
                back(*pend.pop(0))
            self.finish_chunk(l, 6 + j, OC3, ocn, OCT, octn, need_ctx)

    def mlp(self, l, need_ctx):
        self.drain()
        self.norm(l, 1, skip_ctx=not need_ctx)
        w1v = self.w1_d[l].rearrange("(kc p) n -> p kc n", p=128)
        w2v = self.w2_d[l].rearrange("(fc p) n -> p fc n", p=128)
        pending = None
        ucount = 0

        def mlp2(W2, w2res, AT, ares, gi, t0, n, s):
            for dc in range(KC):
                ps2, pres = self.ab()
                for fc in range(4):
                    self.mm(ps2[:, :n], W2[:, fc, dc * 128:(dc + 1) * 128], AT[:, fc, :n], start=(fc == 0), stop=(fc == 3),
                            reads=[(w2res[0], 0), (w2res[1], 0), ares], writes=[pres])
                self.op("dve", "scalar_tensor_tensor", out=self.XT[:, dc, t0:t0 + n], in0=ps2[:, :n],
                        scalar=self.MOD[:, l, 40 + dc, s:s + 1], in1=self.XT[:, dc, t0:t0 + n], op0=ALU.mult, op1=ALU.add,
                        reads=[pres, ("MOD", l), ("XT", dc, gi)], writes=[("XT", dc, gi)])

        for fb in range(8):
            W1, w1res = self.wst_load(w1v, [(fb * 512, 0, 512)])
            i2 = fb % 2
            W2 = self.GBT[:, i2 * 2 * GBW:i2 * 2 * GBW + 4096].rearrange("p (f n) -> p f n", f=4)
            w2res = [("GB", 2 * i2), ("GB", 2 * i2 + 1)]
            self.dma(W2, w2v[:, fb * 4:(fb + 1) * 4, :], writes=[(r, g) for r in w2res for g in range(5)],
                     key="w2_%d" % i2, eng="pool")
            for gi, (t0, n) in enumerate(TGS):
                if gi == 4 and not need_ctx:
                    continue
                s = 0 if gi < 4 else 1
                ai = 4 + (ucount % 2)
                ucount += 1
                AT = self.gb(ai)[:, 0:2048].rearrange("p (f n) -> p f n", f=4)
                ares = (("GB", ai), 0)
                aresw = [(("GB", ai), g_) for g_ in range(5)]
                for fc in range(4):
                    ps = self.gp()
                    for kc in range(KC):
                        self.mm(ps[:, :n], W1[:, kc, fc * 128:(fc + 1) * 128], self.HT[:, kc, t0:t0 + n],
                                start=(kc == 0), stop=(kc == KC - 1), reads=[w1res, ("HT", kc, gi)], writes=[R(ps)])
                    rt = self.tmpf[fc % 4]
                    self.op("act", "activation", out=rt[:, :n], in_=ps[:, :n], func=AF.Relu, reads=[R(ps)], writes=[R(rt)])
                    self.op("pool", "tensor_tensor", out=AT[:, fc, :n], in0=rt[:, :n], in1=rt[:, :n], op=ALU.mult,
                            reads=[R(rt)], writes=aresw)
                if pending is not None:
                    mlp2(*pending)
                pending = (W2, w2res, AT, ares, gi, t0, n, s)
        mlp2(*pending)

    def final_norm(self):
        XT, identf = self.XT, self.C["ident_f"]
        gfin = self.P["g_final"]
        YT = self.HT[:].rearrange("p k t -> p (k t)")[:, 0:8192].bitcast(F32).rearrange("p (k t) -> p k t", k=KC)
        ytres = [("HT", kc, g) for kc in range(KC) for g in range(5)]
        for g in range(4):
            t0 = g * 512
            acc = self.gp()
            self.sumsq_rstd(g, acc)
            for kc in range(KC):
                self.op("dve", "scalar_tensor_tensor", out=YT[:, kc, :], in0=XT[:, kc, t0:t0 + 512], scalar=gfin[:, kc:kc + 1],
                        in1=self.rstd[:], op0=ALU.mult, op1=ALU.mult,
                        reads=[("XT", kc, g), R(self.rstd), R(gfin)], writes=(ytres if kc == 0 else []) + [("YT", kc)])
            for j in range(4):
                tb = g * 4 + j
                ob = self.iobuf[tb % 2]
                obres = self.iores[tb % 2]
                for half in range(2):
                    pt = self.gp()
                    for jj in range(4):
                        kc = half * 4 + jj
                        self.tr(pt[:, jj * 128:(jj + 1) * 128], YT[:, kc, j * 128:(j + 1) * 128], identf[:],
                                reads=[("YT", kc), ("HT", 0, 0), R(identf)], writes=[R(pt)])
                    self.op("act", "mul", out=ob[:, half * 512:(half + 1) * 512], in_=pt[:], mul=32.0,
                            reads=[R(pt)], writes=obres)
                self.dma(self.out_d[tb * 128:(tb + 1) * 128, :], ob, reads=obres, key="io%d" % (tb % 2))


_CACHE = {}


def kernel(**inputs):
    cfg = inputs.pop("_cfg", {}) if "_cfg" in inputs else {}
    inputs = {k: np.asarray(v) for k, v in inputs.items()}
    key = tuple(sorted(cfg.items()))
    if key not in _CACHE:
        b = Builder(cfg)
        _CACHE[key] = (b.build(), b)
    nc, b = _CACHE[key]
    in_maps = [prep_inputs(inputs, i, b.depth) for i in range(8)]
    res = run_bass_kernel_spmd(nc, in_maps, core_ids=list(range(8)))
    if cfg.get("_ret_all"):
        return res
    out = np.stack([np.asarray(r["out"]) for r in res.results], axis=0)
    return out.astype(np.float32)
```

```python
import contextlib
import math
import numpy as np
import ml_dtypes
import concourse.bass as bass
import concourse.mybir as mybir
from concourse.bass_utils import run_bass_kernel_spmd

F32 = mybir.dt.float32
BF16 = mybir.dt.bfloat16
ALU = mybir.AluOpType
AF = mybir.ActivationFunctionType
AX = mybir.AxisListType

D = 1024
T = 2048
L = 256
NT = T + L
DEPTH = 4
KC = D // 128
NTB = NT // 128
EPS = 1e-6
TGS = [(0, 512), (512, 512), (1024, 512), (1536, 512), (2048, 256)]


class Op:
    __slots__ = ("eng", "fn", "idx", "deps", "raw", "inc", "count", "dma", "is_dma")

    def __init__(self, eng, fn, idx):
        self.eng = eng
        self.fn = fn
        self.idx = idx
        self.deps = set()
        self.raw = set()
        self.inc = False
        self.count = 0
        self.dma = None
        self.is_dma = False


class Sched:
    ENGS = ["pe", "act", "dve", "pool", "sp"]

    def __init__(self):
        self.ops = {e: [] for e in self.ENGS}
        self.last_w = {}
        self.readers = {}
        self.dma_n = {}

    def add(self, eng, fn, reads=(), writes=(), dma=None):
        op = Op(eng, fn, len(self.ops[eng]))
        xr = [r for r in reads if isinstance(r, tuple) and str(r[0]).startswith(("ps", "oacc"))]
        if xr:
            for r in xr:
                w = self.last_w.get(r)
                if w is not None:
                    op.raw.add(w)
            writes = list(writes) + [r for r in xr if r not in writes]
        for r in reads:
            w = self.last_w.get(r)
            if w is not None:
                op.deps.add(w)
                op.raw.add(w)
        for r in writes:
            w = self.last_w.get(r)
            if w is not None:
                op.deps.add(w)
            rd = self.readers.get(r)
            if rd:
                for o in rd.values():
                    op.deps.add(o)
        for r in reads:
            d = self.readers.setdefault(r, {})
            if dma is not None:
                d[("dma", id(op))] = op
            else:
                d[eng] = op
        for r in writes:
            self.last_w[r] = op
            self.readers[r] = {}
        if dma is not None:
            n = self.dma_n.get(dma, 0) + 1
            self.dma_n[dma] = n
            op.dma = (dma, n)
            op.is_dma = True
        self.ops[eng].append(op)
        return op

    def _needs_wait(self, op, d):
        if d.is_dma:
            return True
        if d.eng != op.eng:
            return True
        if op.is_dma:
            return True
        if op.eng == "pe":
            return False
        return (d in op.raw) and (op.idx - d.idx <= 4)

    def emit(self, nc, stack):
        for e in self.ENGS:
            for op in self.ops[e]:
                for d in op.deps:
                    if not d.is_dma and self._needs_wait(op, d):
                        d.inc = True
        for e in self.ENGS:
            c = 0
            for op in self.ops[e]:
                if op.inc:
                    c += 1
                op.count = c
        sems = {e: stack.enter_context(nc.semaphore("s_" + e)) for e in self.ENGS}
        dsems = {k: stack.enter_context(nc.semaphore("d_%s" % (k,))) for k in self.dma_n}
        block = stack.enter_context(nc.Block())
        stats = {}

        def run(ename, engine):
            known = {}
            nw = 0
            for op in self.ops[ename]:
                need = {}
                for d in op.deps:
                    if not self._needs_wait(op, d):
                        continue
                    if d.is_dma:
                        key = ("d", d.dma[0])
                        val = 16 * d.dma[1]
                    else:
                        key = ("e", d.eng)
                        val = d.count
                    if val > need.get(key, 0):
                        need[key] = val
                for key, val in need.items():
                    if known.get(key, 0) >= val:
                        continue
                    known[key] = val
                    s = dsems[key[1]] if key[0] == "d" else sems[key[1]]
                    engine.wait_ge(s, val)
                    nw += 1
                ins = op.fn(engine)
                if op.is_dma:
                    ins.then_inc(dsems[op.dma[0]], 16)
                elif op.inc:
                    ins.then_inc(sems[ename], 1)
            for op in self.ops[ename]:
                if op.is_dma:
                    key = ("d", op.dma[0])
                    val = 16 * self.dma_n[op.dma[0]]
                    if known.get(key, 0) < val:
                        known[key] = val
                        engine.wait_ge(dsems[op.dma[0]], val)
            stats[ename] = (len(self.ops[ename]), nw)

        block.tensor(lambda e: run("pe", e))
        block.scalar(lambda e: run("act", e))
        block.vector(lambda e: run("dve", e))
        block.gpsimd(lambda e: run("pool", e))
        block.sync(lambda e: run("sp", e))
        self.stats = stats
        return stats


def R(t, *idx):
    return (t.name,) + idx


GBW = 2340
NGB = 8
LAM_INIT = [0.8 - 0.6 * math.exp(-0.3 * l) for l in range(DEPTH)]


def host_consts():
    c = {}
    bf = ml_dtypes.bfloat16
    c["ident_f"] = np.eye(128, dtype=np.float32)
    c["ident_b"] = np.eye(128, dtype=np.float32).astype(bf)
    c["ones_b"] = np.ones((128, 128), dtype=np.float32).astype(bf)
    pm = np.zeros((128, 128), np.float32)
    sign = np.zeros(128, np.float64)
    for f in range(128):
        fl = f % 64
        q = fl // 16
        src = f + 16 if q in (0, 2) else f - 16
        pm[src, f] = 1.0
        sign[f] = -1.0 if q in (0, 2) else 1.0
    c["perm_b"] = pm.astype(bf)
    t = np.arange(T)
    r = (t // 64).astype(np.float64)
    col = (t % 64).astype(np.float64)
    inv = 10000.0 ** (-np.arange(16, dtype=np.float64) / 16)
    ang64 = np.concatenate([r[:, None] * inv, r[:, None] * inv, col[:, None] * inv, col[:, None] * inv], axis=1)
    ang = np.concatenate([ang64, ang64], axis=1).T
    c["cosT"] = np.cos(ang).astype(np.float32).astype(bf)
    c["sinT"] = (np.sin(ang) * sign[:, None]).astype(np.float32).astype(bf)
    il = np.arange(128)
    mp = (il[None, :] <= il[:, None]).astype(np.float32)
    mn = (il[:, None] <= il[None, :]).astype(np.float32)
    c["maskp"] = np.concatenate([mp, mp], axis=1).astype(bf)
    c["maskn"] = np.concatenate([mn, mn], axis=1).astype(bf)
    s_ = il[:, None].astype(np.float32)
    t_ = il[None, :].astype(np.float32)
    c["r1"] = np.maximum(t_ - s_, 0.0).astype(np.float32)
    c["r2"] = np.maximum(s_ - t_, 0.0).astype(np.float32)
    c["i1"] = np.broadcast_to((il + 1.0)[None, :], (128, 128)).astype(np.float32).copy()
    c["i2"] = np.broadcast_to((128.0 - il)[None, :], (128, 128)).astype(np.float32).copy()
    c["kcol"] = np.stack([127.0 - il, il * 1.0], axis=1).astype(np.float32)
    return c


CONST_SPECS = [("ident_f", [128, 128], F32), ("ident_b", [128, 128], BF16), ("ones_b", [128, 128], BF16),
               ("perm_b", [128, 128], BF16), ("cosT", [128, T], BF16), ("sinT", [128, T], BF16),
               ("maskp", [128, 256], BF16), ("maskn", [128, 256], BF16),
               ("r1", [128, 128], F32), ("r2", [128, 128], F32), ("i1", [128, 128], F32), ("i2", [128, 128], F32),
               ("kcol", [128, 2], F32)]

PARAM_SPECS = [("g_final", [128, KC]), ("cc", [128, KC, 2]), ("bada", [128, DEPTH, 48]),
               ("gmix", [128, DEPTH, KC]), ("gmlp", [128, DEPTH, KC]),
               ("lamq", [128, DEPTH, 2, 64]), ("lamk", [128, DEPTH, 2, 64]), ("sublng", [128, DEPTH, 128]),
               ("decbc", [128, DEPTH, 8]), ("deccol", [128, DEPTH, 2, 2]), ("sinkbc", [128, DEPTH, 4])]


def prep_inputs(inputs, b, depth=DEPTH):
    f = np.float32
    m = {}
    m["x"] = np.ascontiguousarray(inputs["x"][b], dtype=f)
    m["ctx"] = np.ascontiguousarray(inputs["ctx"][b], dtype=f)
    m["g_final"] = np.ascontiguousarray(inputs["g_final"].reshape(KC, 128).T, dtype=f)
    cc = np.stack([inputs["c"][b].reshape(KC, 128).T, inputs["c_ctx"].reshape(KC, 128).T], axis=2)
    m["cc"] = np.ascontiguousarray(cc, dtype=f)
    m["bada"] = np.ascontiguousarray(inputs["b_ada"].reshape(DEPTH, 48, 128).transpose(2, 0, 1), dtype=f)
    m["gmix"] = np.ascontiguousarray(inputs["g_mix"].reshape(DEPTH, KC, 128).transpose(2, 0, 1), dtype=f)
    m["gmlp"] = np.ascontiguousarray(inputs["g_mlp"].reshape(DEPTH, KC, 128).transpose(2, 0, 1), dtype=f)
    lamq = np.stack([inputs["lam_q1"], inputs["lam_q2"]], axis=1)
    lamk = np.stack([inputs["lam_k1"], inputs["lam_k2"]], axis=1)
    m["lamq"] = np.ascontiguousarray(np.broadcast_to(lamq[None], (128, DEPTH, 2, 64)), dtype=f)
    m["lamk"] = np.ascontiguousarray(np.broadcast_to(lamk[None], (128, DEPTH, 2, 64)), dtype=f)
    m["sublng"] = np.ascontiguousarray(np.broadcast_to(inputs["subln_g"][None], (128, DEPTH, 128)), dtype=f)
    dec = np.concatenate([inputs["ret_decay_fwd"], inputs["ret_decay_bwd"]], axis=1)
    m["decbc"] = np.ascontiguousarray(np.broadcast_to(dec[None], (128, DEPTH, 8)), dtype=f)
    dcol = np.zeros((128, DEPTH, 2, 2), f)
    for di, nm in enumerate(["ret_decay_fwd", "ret_decay_bwd"]):
        for pp in range(2):
            dcol[0:64, :, di, pp] = inputs[nm][:, 2 * pp][None, :]
            dcol[64:128, :, di, pp] = inputs[nm][:, 2 * pp + 1][None, :]
    m["deccol"] = dcol
    m["sinkbc"] = np.ascontiguousarray(np.broadcast_to(inputs["sink_logit"][None], (128, DEPTH, 4)), dtype=f)
    for k in ["w_ada", "w_in", "w_out", "w_mlp1", "w_mlp2"]:
        m[k] = np.ascontiguousarray(inputs[k][:depth], dtype=f)
    m.update(host_consts())
    return m


class Builder:
    def __init__(self, cfg):
        self.cfg = cfg
        self.depth = cfg.get("depth", DEPTH)
        self.nc = bass.Bass("TRN2", target_bir_lowering=False)
        self.S = Sched()
        self.stack = contextlib.ExitStack()
        self.gp_i = 0
        self.ab_i = 0
        self.pt_i = 0
        self.wst_i = 0
        self.wo_i = 0
        self.oacc_i = 0
        self.rope_i = 0
        self.dbg_names = []
        self.deferred = []

    def dram_in(self, name, shape, dt=F32):
        return self.nc.dram_tensor(name, list(shape), dt, kind="ExternalInput").ap()

    def sb(self, name, shape, dt):
        return self.stack.enter_context(self.nc.sbuf_tensor(name, list(shape), dt))

    def ps(self, name, shape, dt=F32):
        return self.stack.enter_context(self.nc.psum_tensor(name, list(shape), dt))

    def dma(self, out, in_, reads=(), writes=(), key=None, eng="sp", **kw):
        self.S.add(eng, lambda e: e.dma_start(out=out, in_=in_, **kw), reads=reads, writes=writes, dma=key)

    def mm(self, out, lhsT, rhs, start=True, stop=True, reads=(), writes=(), sgc=False):
        if sgc:
            self.S.add("pe", lambda e: e.matmul(out, lhsT=lhsT, rhs=rhs, start=start, stop=stop, skip_group_check=True),
                       reads=reads, writes=writes)
        else:
            self.S.add("pe", lambda e: e.matmul(out, lhsT=lhsT, rhs=rhs, start=start, stop=stop),
                       reads=reads, writes=writes)

    def tr(self, out, in_, ident, reads=(), writes=()):
        self.S.add("pe", lambda e: e.transpose(out, in_, ident), reads=reads, writes=writes)

    def op(self, eng, meth, reads=(), writes=(), **kw):
        self.S.add(eng, lambda e: getattr(e, meth)(**kw), reads=reads, writes=writes)

    def gp(self):
        t = self.PSG[self.gp_i % len(self.PSG)]
        self.gp_i += 1
        return t

    def ab(self):
        i = self.ab_i % 4
        self.ab_i += 1
        return self.OACC[i // 2][:, (i % 2) * 512:(i % 2 + 1) * 512], ("oacc", i // 2, i % 2)

    def ptb(self):
        t = self.PT[self.pt_i % len(self.PT)]
        self.pt_i += 1
        return t

    def gb(self, i):
        return self.GBT[:, i * GBW:(i + 1) * GBW]

    def dbg(self, name, ap, shape, dt, reads):
        d = self.nc.dram_tensor(name, list(shape), dt, kind="ExternalOutput").ap()
        self.dma(d, ap, reads=reads, key="dbg_" + name)
        self.dbg_names.append(name)

    def build(self):
        nc, S = self.nc, self.S
        cfg = self.cfg
        self.x_d = self.dram_in("x", [T, D])
        self.ctx_d = self.dram_in("ctx", [L, D])
        self.out_d = nc.dram_tensor("out", [T, D], F32, kind="ExternalOutput").ap()
        self.cd = {n: self.dram_in(n, shp, dt) for n, shp, dt in CONST_SPECS}
        self.pd = {n: self.dram_in(n, shp, F32) for n, shp in PARAM_SPECS}
        self.wada_d = self.dram_in("w_ada", [self.depth, D, 6 * D])
        self.win_d = self.dram_in("w_in", [self.depth, D, 3 * D])
        self.wout_d = self.dram_in("w_out", [self.depth, D, D])
        self.w1_d = self.dram_in("w_mlp1", [self.depth, D, 4 * D])
        self.w2_d = self.dram_in("w_mlp2", [self.depth, 4 * D, D])

        self.XT = self.sb("XT", [128, KC, NT], F32)
        self.HT = self.sb("HT", [128, KC, NT], BF16)
        self.GBT = self.sb("GBT", [128, NGB * GBW], BF16)
        self.WST = [self.sb("WST%d" % i, [128, KC, 512], BF16) for i in range(2)]
        self.WO = [self.sb("WO%d" % i, [128, D], BF16) for i in range(2)]
        self.C = {}
        for n, shp, dt in CONST_SPECS:
            t = self.sb("c_" + n, shp, dt)
            self.C[n] = t
            self.dma(t[:], self.cd[n], writes=[R(t)], key="c_" + n)
        self.O0N = self.sb("o0n", [128, 4, 128], F32)
        self.OA = self.sb("oa", [128, 4, 128], F32)
        self.OB = self.sb("ob", [128, 4, 128], F32)
        self.GA = self.sb("GA", [128, DEPTH, 128], F32)
        self.P = {}
        alias = {"lamq": self.O0N, "lamk": self.OB, "sublng": self.GA}
        for n, shp in PARAM_SPECS:
            if n in alias:
                t = alias[n]
                self.dma(t[:].rearrange("p a b -> p (a b)"), self.pd[n].rearrange("p a b c -> p (a b c)") if len(shp) == 4 else self.pd[n].rearrange("p a b -> p (a b)"), writes=[R(t)], key="p_" + n)
            else:
                t = self.sb("p_" + n, shp, F32)
                self.dma(t[:], self.pd[n], writes=[R(t)], key="p_" + n)
            self.P[n] = t
        self.iobuf = [self.gb(i)[:, 0:2048].bitcast(F32) for i in range(2)]
        self.iores = [[(("GB", i), g) for g in range(5)] for i in range(2)]
        self.tmpf = [self.sb("tmpf%d" % i, [128, 512], F32) for i in range(4)]
        self.PT = [self.sb("pt%d" % i, [128, 512], BF16) for i in range(4)]
        self.sqb = [self.PT[0], self.PT[1]]
        self.ropeq = [self.PT[2], self.PT[3]]
        self.rstd = self.tmpf[2]
        self.rstd0 = self.tmpf[3]
        self.small = self.sb("small", [128, 64], F32)
        self.epsb = self.sb("epsb", [128, 4], F32)
        S.add("pool", lambda e: e.memset(self.epsb[:, 0:1], float(D * EPS)), writes=[R(self.epsb)])
        S.add("pool", lambda e: e.memset(self.epsb[:, 1:2], float(EPS)), writes=[R(self.epsb)])
        S.add("pool", lambda e: e.memset(self.epsb[:, 2:3], 1.0), writes=[R(self.epsb)])
        self.MOD = self.sb("MOD", [128, DEPTH, 48, 2], F32)
        self.GS = [self.sb("GS%d" % i, [128, DEPTH, KC, 2], F32) for i in range(2)]
        self.G32 = [self.sb("G32_%d" % i, [128, DEPTH, KC], F32) for i in range(2)]
        self.siluT = self.sb("siluT", [128, KC, 2], BF16)
        self.LG = self.sb("LG", [128, DEPTH, 8], F32)
        self.LGC = self.sb("LGC", [128, DEPTH, 2, 2], F32)
        self.ESINK = self.sb("ESINK", [128, DEPTH, 4], F32)
        self.NEGLAM = self.sb("NEGLAM", [128, DEPTH], F32)
        self.lame = self.sb("lame", [128, DEPTH, 2], F32)
        _tqf = self.sb("TQF", [128, 128], F32)
        _tqb = self.sb("TQB", [128, 128], F32)
        _dt = self.sb("DT", [128, 256], F32)
        _tk = self.sb("TK", [128, 4], F32)
        _cfb = self.sb("CFB", [128, 2], F32)
        self.TQF, self.TQB, self.DT, self.TK, self.CFB = [_tqf] * 2, [_tqb] * 2, [_dt] * 2, [_tk] * 2, [_cfb] * 2
        self.stf = self.sb("stf", [128, 128], F32)
        self.stb = self.sb("stb", [128, 128], F32)
        self.kfb = [self.sb("kfb%d" % i, [128, 128], BF16) for i in range(4)]
        self.qfb = [self.sb("qfb%d" % i, [128, 128], BF16) for i in range(4)]
        self.PSG = [self.ps("ps%d" % i, [128, 512], F32) for i in range(4)]
        self.OACC = [self.ps("oacc%d" % i, [128, 1024], F32) for i in range(2)]

        self.load_input()
        self.prologue_params()
        self.adaln_all()
        for l in range(self.depth):
            need_ctx = l < DEPTH - 1
            self.norm(l, 0, skip_ctx=False)
            if cfg.get("dbg_h") == l:
                self.dbg("dbg_h", self.HT[:], [128, KC, NT], BF16, [("HT", kc, g) for kc in range(KC) for g in range(5)])
            order = []
            mx = cfg.get("mixers", "ABC")
            if "A" in mx:
                order += [("A", h) for h in range(4)]
            if "B" in mx:
                order += [("B", 0), ("B", 1)]
            if "C" in mx:
                order += [("C", 0)]
            for kind, i in order:
                if kind == "A":
                    self.chunk_A(l, i, need_ctx)
                elif kind == "B":
                    self.chunk_B(l, i, need_ctx)
                else:
                    self.chunk_C(l, need_ctx)
            if cfg.get("mlp", True):
                self.mlp(l, need_ctx)
        if cfg.get("dbg_x"):
            self.dbg("dbg_x", self.XT[:], [128, KC, NT], F32, [("XT", kc, g) for kc in range(KC) for g in range(5)])
        self.drain()
        self.final_norm()
        S.emit(nc, self.stack)
        return nc

    def xt_res(self, kc, gi):
        return ("XT", kc, gi)

    def load_input(self):
        S = self.S
        XT, PS, identf = self.XT, self.PSG, self.C["ident_f"]
        for tb in range(NTB):
            buf = self.iobuf[tb % 2]
            bres = self.iores[tb % 2]
            gi = min(tb // 4, 4)
            src = self.x_d[tb * 128:(tb + 1) * 128, :] if tb < 16 else self.ctx_d[(tb - 16) * 128:(tb - 15) * 128, :]
            self.dma(buf, src, writes=bres, key="io%d" % (tb % 2))
            for half in range(2):
                pt = self.gp()
                for j in range(4):
                    kc = half * 4 + j
                    self.tr(pt[:, j * 128:(j + 1) * 128], buf[:, kc * 128:(kc + 1) * 128], identf[:],
                            reads=bres + [R(identf)], writes=[R(pt)])
                self.op("dve", "tensor_copy", out=XT[:, half * 4:half * 4 + 4, tb * 128:(tb + 1) * 128],
                        in_=pt[:].rearrange("p (j t) -> p j t", j=4),
                        reads=[R(pt)], writes=[("XT", k, gi) for k in range(half * 4, half * 4 + 4)])

    def prologue_params(self):
        P = self.P
        sm = self.small
        self.op("act", "activation", out=self.siluT[:], in_=P["cc"][:], func=AF.Silu,
                reads=[R(P["cc"])], writes=[R(self.siluT)])
        self.op("dve", "tensor_scalar", out=self.G32[0][:], in0=P["gmix"][:], scalar1=32.0, scalar2=None, op0=ALU.mult,
                reads=[R(P["gmix"])], writes=[R(self.G32[0])])
        self.op("dve", "tensor_scalar", out=self.G32[1][:], in0=P["gmlp"][:], scalar1=32.0, scalar2=None, op0=ALU.mult,
                reads=[R(P["gmlp"])], writes=[R(self.G32[1])])
        for src, dst, n in ((P["decbc"], self.LG, DEPTH * 8), (P["deccol"], self.LGC, DEPTH * 4)):
            sv = src[:].rearrange("p a b -> p (a b)") if len(src.shape) == 3 else src[:].rearrange("p a b c -> p (a b c)")
            dv = dst[:].rearrange("p a b -> p (a b)") if len(dst.shape) == 3 else dst[:].rearrange("p a b c -> p (a b c)")
            self.op("act", "activation", out=sm[:, 0:n], in_=sv, func=AF.Exp, scale=-1.0,
                    reads=[R(src)], writes=[R(sm)])
            self.op("act", "activation", out=sm[:, 32:32 + n], in_=sm[:, 0:n], func=AF.Ln, bias=self.epsb[:, 2:3], scale=1.0,
                    reads=[R(sm), R(self.epsb)], writes=[R(sm)])
            self.op("dve", "tensor_scalar", out=dv, in0=sm[:, 32:32 + n], scalar1=-1.0, scalar2=None, op0=ALU.mult,
                    reads=[R(sm)], writes=[R(dst)])
        self.op("act", "activation", out=self.ESINK[:], in_=P["sinkbc"][:], func=AF.Exp,
                reads=[R(P["sinkbc"])], writes=[R(self.ESINK)])
        fl = lambda t: t[:].rearrange("p a b -> p (a b)")
        self.op("dve", "tensor_tensor", out=fl(self.OA), in0=fl(P["lamq"]), in1=fl(P["lamk"]), op=ALU.mult,
                reads=[R(P["lamq"]), R(P["lamk"])], writes=[R(self.OA)])
        self.op("dve", "tensor_reduce", out=self.lame[:].rearrange("p a b -> p (a b)"),
                in_=fl(self.OA).rearrange("p (a c) -> p a c", c=64), axis=AX.X, op=ALU.add,
                reads=[R(self.OA)], writes=[R(self.lame)])
        self.op("act", "activation", out=self.lame[:], in_=self.lame[:], func=AF.Exp,
                reads=[R(self.lame)], writes=[R(self.lame)])
        for l in range(DEPTH):
            self.op("dve", "tensor_scalar", out=self.NEGLAM[:, l:l + 1], in0=self.lame[:, l, 1:2],
                    scalar1=self.lame[:, l, 0:1], scalar2=-LAM_INIT[l], op0=ALU.subtract, op1=ALU.add,
                    reads=[R(self.lame)], writes=[R(self.NEGLAM)])
            self.op("dve", "tensor_scalar", out=self.GA[:, l, :], in0=self.GA[:, l, :],
                    scalar1=1.0 - LAM_INIT[l], scalar2=None, op0=ALU.mult,
                    reads=[R(self.GA)], writes=[R(self.GA)])

    def wst_load(self, src3, pieces, keyname="wst"):
        i = self.wst_i % 2
        self.wst_i += 1
        buf = self.WST[i]
        res = ("WST", i)
        for (sc, dc, n) in pieces:
            self.dma(buf[:, :, dc:dc + n], src3[:, :, sc:sc + n], writes=[res], key="wst%d" % i, eng="pool")
        return buf, res

    def adaln_gs(self, l):
        for w, base in ((0, 8), (1, 32)):
            self.op("dve", "scalar_tensor_tensor", out=self.GS[w][:, l, :, :], in0=self.MOD[:, l, base:base + 8, :],
                    scalar=1.0, in1=self.G32[w][:, l, :].unsqueeze(2).to_broadcast([128, KC, 2]),
                    op0=ALU.add, op1=ALU.mult,
                    reads=[("MOD", l), R(self.G32[w])], writes=[("GS", w, l)])

    def adaln_unit_load(self, l, j, wbuf, i):
        src3 = self.wada_d[l].rearrange("(kc p) n -> p kc n", p=128)
        self.dma(wbuf[:, :, 384:512], src3[:, :, j * 128:(j + 1) * 128], writes=[("WSTx", i)], key="wax%d" % i, eng="pool")

    def adaln_unit_compute(self, l, j, wbuf, i):
        pst = self.gp()
        for kc in range(KC):
            self.mm(pst[:, 0:2], wbuf[:, kc, 384:512], self.siluT[:, kc, :], start=(kc == 0), stop=(kc == KC - 1),
                    reads=[("WSTx", i), R(self.siluT)], writes=[R(pst)])
        self.op("dve", "tensor_scalar", out=self.MOD[:, l, j, :], in0=pst[:, 0:2], scalar1=self.P["bada"][:, l, j:j + 1],
                scalar2=None, op0=ALU.add, reads=[R(pst), R(self.P["bada"])], writes=[("MOD", l)])

    def adaln_all(self):
        P = self.P
        for l in range(1):
            src3 = self.wada_d[l].rearrange("(kc p) n -> p kc n", p=128)
            for piece in range(12):
                buf, res = self.wst_load(src3, [(piece * 512, 0, 512)])
                pst = self.gp()
                for jc in range(4):
                    for kc in range(KC):
                        self.mm(pst[:, jc * 2:jc * 2 + 2], buf[:, kc, jc * 128:(jc + 1) * 128], self.siluT[:, kc, :],
                                start=(kc == 0), stop=(kc == KC - 1), reads=[res, R(self.siluT)], writes=[R(pst)])
                j0 = piece * 4
                self.op("dve", "tensor_tensor", out=self.MOD[:, l, j0:j0 + 4, :],
                        in0=pst[:, 0:8].rearrange("p (j s) -> p j s", s=2),
                        in1=P["bada"][:, l, j0:j0 + 4].unsqueeze(2).to_broadcast([128, 4, 2]), op=ALU.add,
                        reads=[R(pst), R(P["bada"])], writes=[("MOD", l)])
            for w, base in ((0, 8), (1, 32)):
                self.op("dve", "scalar_tensor_tensor", out=self.GS[w][:, l, :, :], in0=self.MOD[:, l, base:base + 8, :],
                        scalar=1.0, in1=self.G32[w][:, l, :].unsqueeze(2).to_broadcast([128, KC, 2]),
                        op0=ALU.add, op1=ALU.mult,
                        reads=[("MOD", l), R(self.G32[w])], writes=[("GS", w, l)])

    def sumsq_rstd(self, gi, acc):
        t0, n = TGS[gi]
        XT, onesb = self.XT, self.C["ones_b"]
        for kc in range(KC):
            sq = self.sqb[kc % 2]
            self.op("act", "activation", out=sq[:, :n], in_=XT[:, kc, t0:t0 + n], func=AF.Square,
                    reads=[("XT", kc, gi)], writes=[R(sq)])
            self.mm(acc[:, :n], onesb[:], sq[:, :n], start=(kc == 0), stop=(kc == KC - 1),
                    reads=[R(sq), R(onesb)], writes=[R(acc)])
        self.op("act", "activation", out=self.rstd0[:, :n], in_=acc[:, :n], func=AF.Ln, bias=self.epsb[:, 0:1], scale=1.0,
                reads=[R(acc), R(self.epsb)], writes=[R(self.rstd0)])
        self.op("act", "activation", out=self.rstd[:, :n], in_=self.rstd0[:, :n], func=AF.Exp, scale=-0.5,
                reads=[R(self.rstd0)], writes=[R(self.rstd)])

    def norm(self, l, w, skip_ctx):
        shbase = 0 if w == 0 else 24
        for gi, (t0, n) in enumerate(TGS):
            if gi == 4 and skip_ctx:
                continue
            s = 0 if gi < 4 else 1
            acc = self.gp()
            self.sumsq_rstd(gi, acc)
            for kc in range(KC):
                tm = self.tmpf[kc % 2]
                self.op("dve", "scalar_tensor_tensor", out=tm[:, :n], in0=self.XT[:, kc, t0:t0 + n],
                        scalar=self.GS[w][:, l, kc, s:s + 1], in1=self.rstd[:, :n], op0=ALU.mult, op1=ALU.mult,
                        reads=[("XT", kc, gi), ("GS", w, l), R(self.rstd)], writes=[R(tm)])
                self.op("act", "activation", out=self.HT[:, kc, t0:t0 + n], in_=tm[:, :n], func=AF.Identity,
                        bias=self.MOD[:, l, shbase + kc, s:s + 1], scale=1.0,
                        reads=[R(tm), ("MOD", l)], writes=[("HT", kc, gi)])

    def proj_fm(self, wbuf, wres, c0, dst, dname, rope, with_ctx, dst_fn=None):
        if dst_fn is None:
            dst_fn = lambda t0, n: [(dst[:, t0:t0 + n], slice(0, 128))]
        pending = None
        for gi, (t0, n) in enumerate(TGS):
            if gi == 4 and not with_ctx:
                continue
            ps = self.gp()
            for kc in range(KC):
                self.mm(ps[:, :n], wbuf[:, kc, c0:c0 + 128], self.HT[:, kc, t0:t0 + n], start=(kc == 0), stop=(kc == KC - 1),
                        reads=[wres, ("HT", kc, gi)], writes=[R(ps)])
            if rope and gi < 4:
                tail = self.rope_head(ps, n)
                if pending is not None:
                    self.rope_tail(*pending)
                pending = (ps, dst_fn(t0, n), t0, n, (dname, gi)) + tail
            else:
                for (oap, psl) in dst_fn(t0, n):
                    self.op("act", "activation", out=oap, in_=ps[psl, :n], func=AF.Copy,
                            reads=[R(ps)], writes=[(dname, gi)])
        if pending is not None:
            self.rope_tail(*pending)

    def rope_head(self, ps, n):
        i = self.rope_i % 2
        self.rope_i += 1
        qb = self.ropeq[i]
        self.op("act", "activation", out=qb[:, :n], in_=ps[:, :n], func=AF.Copy, reads=[R(ps)], writes=[R(qb)])
        return (i, qb)

    def rope_tail(self, ps, dsts, t0, n, dres, i, qb):
        t1, t2 = self.tmpf[2 * i], self.tmpf[2 * i + 1]
        permb, cosT, sinT = self.C["perm_b"], self.C["cosT"], self.C["sinT"]
        ps2 = self.gp()
        self.mm(ps2[:, :n], permb[:], qb[:, :n], reads=[R(qb), R(permb)], writes=[R(ps2)])
        self.op("dve", "tensor_tensor", out=t1[:, :n], in0=ps[:, :n], in1=cosT[:, t0:t0 + n], op=ALU.mult,
                reads=[R(ps), R(cosT)], writes=[R(t1)])
        self.op("dve", "tensor_tensor", out=t2[:, :n], in0=ps2[:, :n], in1=sinT[:, t0:t0 + n], op=ALU.mult,
                reads=[R(ps2), R(sinT)], writes=[R(t2)])
        for (oap, psl) in dsts:
            self.op("pool", "tensor_tensor", out=oap, in0=t1[psl, :n], in1=t2[psl, :n], op=ALU.add,
                    reads=[R(t1), R(t2)], writes=[dres])

    def proj_tm(self, wbuf, wres, c0, dst_fn, dname, func=None):
        for tb0 in range(0, NTB, 4):
            nb = min(4, NTB - tb0)
            gi = min(tb0 // 4, 4)
            ps = self.gp()
            for j in range(nb):
                tb = tb0 + j
                for kc in range(KC):
                    self.mm(ps[:, j * 128:(j + 1) * 128], self.HT[:, kc, tb * 128:(tb + 1) * 128], wbuf[:, kc, c0:c0 + 128],
                            start=(kc == 0), stop=(kc == KC - 1), reads=[wres, ("HT", kc, gi)], writes=[R(ps)])
            out, in_ = dst_fn(tb0, nb, ps)
            self.op("act", "activation", out=out, in_=in_, func=(func or AF.Copy), reads=[R(ps)], writes=[(dname, gi)])

    def finish_chunk(self, l, ci, OC3, ocname, OCT, octname, need_ctx):
        self.drain()
        i = self.wo_i % 2
        self.wo_i += 1
        WO = self.WO[i]
        wres = ("WO", i)
        self.dma(WO[:], self.wout_d[l][ci * 128:(ci + 1) * 128, :], writes=[wres], key="wo%d" % i, eng="pool")
        identb = self.C["ident_b"]
        ntb = NTB if need_ctx else 16
        for tb0 in range(0, ntb, 4):
            nb = min(4, ntb - tb0)
            gi = min(tb0 // 4, 4)
            pk = self.gp()
            pkb = pk[:].bitcast(BF16)
            for j in range(nb):
                self.tr(pkb[:, j * 128:(j + 1) * 128], OC3[:, tb0 + j, :], identb[:],
                        reads=[(ocname, gi), R(identb)], writes=[R(pk)])
            self.op("dve", "tensor_copy", out=OCT[:, tb0 * 128:(tb0 + nb) * 128], in_=pkb[:, 0:nb * 128],
                    reads=[R(pk)], writes=[(octname, gi)])
        for gi, (t0, n) in enumerate(TGS):
            if gi == 4 and not need_ctx:
                continue
            s = 0 if gi < 4 else 1
            for dc in range(KC):
                def item(use_gp, gi=gi, t0=t0, n=n, s=s, dc=dc):
                    if use_gp:
                        ps = self.gp()
                        pres = R(ps)
                    else:
                        ps, pres = self.ab()
                    self.mm(ps[:, :n], WO[:, dc * 128:(dc + 1) * 128], OCT[:, t0:t0 + n],
                            reads=[wres, (octname, gi)], writes=[pres])
                    self.op("dve", "scalar_tensor_tensor", out=self.XT[:, dc, t0:t0 + n], in0=ps[:, :n],
                            scalar=self.MOD[:, l, 16 + dc, s:s + 1], in1=self.XT[:, dc, t0:t0 + n], op0=ALU.mult, op1=ALU.add,
                            reads=[pres, ("MOD", l), ("XT", dc, gi)], writes=[("XT", dc, gi)])
                self.deferred.append(item)
        self.drain(use_gp=True)

    def drain(self, k=None, use_gp=False):
        while self.deferred and (k is None or k > 0):
            self.deferred.pop(0)(use_gp)
            if k is not None:
                k -= 1

    def chunk_A(self, l, h, need_ctx):
        src3 = self.win_d[l].rearrange("(kc p) n -> p kc n", p=128)
        wbuf, wres = self.wst_load(src3, [(h * 128, 0, 128), (512 + h * 128, 128, 128), (1024 + h * 128, 256, 128)])
        par = h % 2
        QT, KT, V = self.gb(par * 3), self.gb(par * 3 + 1), self.gb(par * 3 + 2)
        qn, kn, vn = ("GB", par * 3), ("GB", par * 3 + 1), ("GB", par * 3 + 2)
        OC, OCT = self.gb(6), self.gb(7)
        V3 = V.rearrange("p (t c) -> p t c", c=130)
        OC3 = OC[:, 0:NT].rearrange("p (t c) -> p t c", c=128)
        allg = list(range(5))
        self.op("pool", "memset", ap=V3[:, :, 128:129], constant=1.0, writes=[(vn, g) for g in allg])
        stage = self.cfg.get("a_stage", 9)
        if stage < 0:
            return
        self.proj_fm(wbuf, wres, 0, QT, qn, stage >= 0.5, need_ctx)
        if stage < 0.7:
            return
        self.proj_fm(wbuf, wres, 128, KT, kn, True, True)
        if stage < 0.8:
            return
        self.proj_tm(wbuf, wres, 256,
                     lambda tb0, nb, ps: (V3[:, tb0:tb0 + nb, 0:128], ps[:, 0:nb * 128].rearrange("p (j c) -> p j c", c=128)),
                     vn)
        if stage < 2:
            return
        itc = 0
        wi = wres[1]
        ada_l = l + 1 if (l + 1 < self.depth and self.cfg.get("ada_il", True)) else None
        ada_units = list(range(h * 12, h * 12 + 12)) if ada_l is not None else []
        ada_loaded = None
        for gi, (t0, n) in enumerate(TGS):
            if gi == 4 and not need_ctx:
                continue
            kbs = list(range(NTB)) if gi < 4 else [16, 17]
            nqb = n // 128
            nbk = nqb // 2
            aress = [[("oacc", c, 0), ("oacc", c, 1)] for c in range(2)]

            def emit_pv(ki, kb, pts):
                kgi = min(kb // 4, 4)
                for c in range(2):
                    acc = self.OACC[c]
                    for qb in range(nqb):
                        off = (qb // 2) * 512 + (qb % 2) * 129
                        self.mm(acc[:, off:off + 129], pts[c][:, qb * 128:(qb + 1) * 128], V3[:, kb, 0:129],
                                start=(ki == 0 and qb % 2 == 0), stop=(ki == len(kbs) - 1),
                                reads=[R(pts[c]), (vn, kgi)], writes=[aress[c][qb // 2]], sgc=True)

            pending = None
            for ki, kb in enumerate(kbs):
                kgi = min(kb // 4, 4)
                sts = [self.gp(), self.gp()]
                for c in range(2):
                    hs = slice(c * 64, (c + 1) * 64)
                    self.mm(sts[c][:, :n], KT[hs, kb * 128:(kb + 1) * 128], QT[hs, t0:t0 + n],
                            reads=[(kn, kgi), (qn, gi)], writes=[R(sts[c])])
                pts = [self.ptb(), self.ptb()]
                for c in range(2):
                    self.op("act", "activation", out=pts[c][:, :n], in_=sts[c][:, :n], func=AF.Exp, scale=0.125,
                            reads=[R(sts[c])], writes=[R(pts[c])])
                if pending is not None:
                    emit_pv(*pending)
                pending = (ki, kb, pts)
                if ada_l is not None and gi < 4:
                    if itc % 6 == 5 and ada_loaded is not None:
                        self.adaln_unit_compute(ada_l, ada_loaded, wbuf, wi)
                        ada_loaded = None
                    if itc % 6 == 0 and ada_units:
                        ada_loaded = ada_units.pop(0)
                        self.adaln_unit_load(ada_l, ada_loaded, wbuf, wi)
                    itc += 1
            emit_pv(*pending)
            if stage < 3:
                continue
            for c in range(2):
                acc = self.OACC[c]
                ares = aress[c]
                accv = acc[:].rearrange("p (b x) -> p b x", b=2)[:, 0:nbk, 0:258].rearrange("p b (j c) -> p b j c", c=129)
                zv = accv[:, :, :, 128]
                ov = accv[:, :, :, 0:128]
                sm = self.small
                rz = sm[:, 0:nqb].rearrange("p (b j) -> p b j", j=2)
                ar = ares[0:nbk]
                self.op("dve", "reciprocal", out=rz, in_=zv, reads=ar, writes=[R(sm)])
                o0 = self.O0N[:, 0:nqb, :].rearrange("p (b j) c -> p b j c", j=2)
                oa = self.OA[:, 0:nqb, :].rearrange("p (b j) c -> p b j c", j=2)
                if c == 0:
                    self.op("dve", "tensor_tensor", out=o0, in0=ov, in1=rz.unsqueeze(3).to_broadcast([128, nbk, 2, 128]),
                            op=ALU.mult, reads=ar + [R(sm)], writes=[R(self.O0N)])
                else:
                    rz1 = sm[:, 4:4 + nqb].rearrange("p (b j) -> p b j", j=2)
                    self.op("dve", "tensor_scalar", out=rz1, in0=rz, scalar1=self.NEGLAM[:, l:l + 1], scalar2=None,
                            op0=ALU.mult, reads=[R(sm), R(self.NEGLAM)], writes=[R(sm)])
                    self.op("dve", "tensor_tensor", out=oa, in0=ov, in1=rz1.unsqueeze(3).to_broadcast([128, nbk, 2, 128]),
                            op=ALU.mult, reads=ar + [R(sm)], writes=[R(self.OA)])
                    oaf = self.OA[:, 0:nqb, :]
                    obf = self.OB[:, 0:nqb, :]
                    self.op("pool", "tensor_tensor", out=oaf, in0=oaf, in1=self.O0N[:, 0:nqb, :], op=ALU.add,
                            reads=[R(self.OA), R(self.O0N)], writes=[R(self.OA)])
                    self.op("pool", "tensor_tensor", out=obf, in0=oaf, in1=oaf, op=ALU.mult,
                            reads=[R(self.OA)], writes=[R(self.OB)])
                    ss = sm[:, 8:8 + nqb]
                    self.op("dve", "tensor_reduce", out=ss, in_=obf, axis=AX.X, op=ALU.add,
                            reads=[R(self.OB)], writes=[R(sm)])
                    l1 = sm[:, 12:12 + nqb]
                    rs = sm[:, 16:16 + nqb]
                    self.op("act", "activation", out=l1, in_=ss, func=AF.Ln, bias=self.epsb[:, 1:2], scale=1.0 / 128.0,
                            reads=[R(sm), R(self.epsb)], writes=[R(sm)])
                    self.op("act", "activation", out=rs, in_=l1, func=AF.Exp, scale=-0.5,
                            reads=[R(sm)], writes=[R(sm)])
                    self.op("dve", "tensor_tensor", out=obf, in0=oaf, in1=rs.unsqueeze(2).to_broadcast([128, nqb, 128]),
                            op=ALU.mult, reads=[R(self.OA), R(sm)], writes=[R(self.OB)])
                    tbq = t0 // 128
                    self.op("pool", "tensor_tensor", out=OC3[:, tbq:tbq + nqb, :], in0=obf,
                            in1=self.GA[:, l, :].unsqueeze(1).to_broadcast([128, nqb, 128]), op=ALU.mult,
                            reads=[R(self.OB), R(self.GA)], writes=[(("GB", 6), gi)])
        if ada_l is not None:
            assert ada_loaded is None and not ada_units
            if h == 3:
                self.adaln_gs(ada_l)
        if stage < 4:
            return
        self.finish_chunk(l, h, OC3, ("GB", 6), OCT, ("GB", 7), need_ctx)

    def ret_tables(self, l, pp):
        C = self.C
        if True:
            lgf = self.LGC[:, l, 0, pp:pp + 1]
            lgb = self.LGC[:, l, 1, pp:pp + 1]
            self.op("act", "activation", out=self.TQF[pp][:], in_=C["i1"][:], func=AF.Exp, scale=lgf,
                    reads=[R(C["i1"]), R(self.LGC)], writes=[R(self.TQF[pp])])
            self.op("act", "activation", out=self.TQB[pp][:], in_=C["i2"][:], func=AF.Exp, scale=lgb,
                    reads=[R(C["i2"]), R(self.LGC)], writes=[R(self.TQB[pp])])
            self.op("act", "activation", out=self.CFB[pp][:, 0:1], in_=lgf, func=AF.Exp, scale=128.0,
                    reads=[R(self.LGC)], writes=[R(self.CFB[pp])])
            self.op("act", "activation", out=self.CFB[pp][:, 1:2], in_=lgb, func=AF.Exp, scale=128.0,
                    reads=[R(self.LGC)], writes=[R(self.CFB[pp])])
            for h2 in range(2):
                h = 2 * pp + h2
                e1 = self.tmpf[0][:, 0:128]
                e2 = self.tmpf[1][:, 0:128]
                self.op("dve", "tensor_scalar", out=e1, in0=C["r1"][:], scalar1=self.LG[:, l, h:h + 1], scalar2=None,
                        op0=ALU.mult, reads=[R(C["r1"]), R(self.LG)], writes=[R(self.tmpf[0])])
                self.op("dve", "scalar_tensor_tensor", out=e2, in0=C["r2"][:], scalar=self.LG[:, l, 4 + h:5 + h], in1=e1,
                        op0=ALU.mult, op1=ALU.add, reads=[R(C["r2"]), R(self.LG), R(self.tmpf[0])], writes=[R(self.tmpf[1])])
                self.op("act", "activation", out=self.DT[pp][:, h2 * 128:(h2 + 1) * 128], in_=e2, func=AF.Exp,
                        reads=[R(self.tmpf[1])], writes=[R(self.DT[pp])])
            sm = self.small
            self.op("act", "activation", out=sm[:, 20:22], in_=self.LG[:, l, 2 * pp:2 * pp + 2], func=AF.Exp,
                    scale=C["kcol"][:, 0:1], reads=[R(self.LG), R(C["kcol"])], writes=[R(sm)])
            self.op("act", "activation", out=sm[:, 22:24], in_=self.LG[:, l, 4 + 2 * pp:6 + 2 * pp], func=AF.Exp,
                    scale=C["kcol"][:, 1:2], reads=[R(self.LG), R(C["kcol"])], writes=[R(sm)])
            self.op("dve", "tensor_scalar", out=self.TK[pp][:], in0=sm[:, 20:24], scalar1=0.125, scalar2=None, op0=ALU.mult,
                    reads=[R(sm)], writes=[R(self.TK[pp])])

    def chunk_B(self, l, pp, need_ctx):
        src3 = self.win_d[l].rearrange("(kc p) n -> p kc n", p=128)
        wbuf, wres = self.wst_load(src3, [(1536 + pp * 128, 0, 128), (1792 + pp * 128, 128, 128),
                                          (2048 + pp * 128, 256, 128), (2304 + pp * 128, 384, 128)])
        names = [("GB", i) for i in range(8)]
        QT, KT, V, G, SF, SB, OC, OCT = [self.gb(i) for i in range(8)]
        qn, kn, vn, gn, sfn, sbn, ocn, octn = names
        V3 = V.rearrange("p (t c) -> p t c", c=130)
        G3 = G[:, 0:NT].rearrange("p (t c) -> p t c", c=128)
        SF3 = SF[:, 0:NT].rearrange("p (t c) -> p t c", c=128)
        SB3 = SB[:, 0:NT].rearrange("p (t c) -> p t c", c=128)
        OC3 = OC[:, 0:NT].rearrange("p (t c) -> p t c", c=128)
        identb = self.C["ident_b"]
        self.ret_tables(l, pp)
        self.proj_fm(wbuf, wres, 0, QT, qn, True, need_ctx)
        self.proj_fm(wbuf, wres, 128, KT, kn, True, True)
        self.proj_tm(wbuf, wres, 256,
                     lambda tb0, nb, ps: (V3[:, tb0:tb0 + nb, 0:128], ps[:, 0:nb * 128].rearrange("p (j c) -> p j c", c=128)),
                     vn)
        self.proj_tm(wbuf, wres, 384,
                     lambda tb0, nb, ps: (G3[:, tb0:tb0 + nb, :], ps[:, 0:nb * 128].rearrange("p (j c) -> p j c", c=128)),
                     gn, func=AF.Silu)
        orders = [[16, 17] + list(range(16)), [17, 16] + list(range(15, -1, -1))]
        sts_ = [self.stf, self.stb]
        ST3s = [SF3, SB3]
        stns = [sfn, sbn]
        for d in range(2):
            self.op("pool", "memset", ap=sts_[d][:], constant=0.0, writes=[R(sts_[d])])
        NS = len(orders[0])
        pks = {}
        pus = {}
        for i in range(NS + 3):
            for d in range(2):
                if i < NS:
                    tb = orders[d][i]
                    gi = min(tb // 4, 4)
                    pk = self.gp()
                    pkb = pk[:].bitcast(BF16)
                    self.tr(pkb[:, 0:128], KT[:, tb * 128:(tb + 1) * 128], identb[:], reads=[(kn, gi), R(identb)], writes=[R(pk)])
                    pks[(d, i)] = (pk, pkb)
                if 0 <= i - 1 < NS:
                    pk, pkb = pks.pop((d, i - 1))
                    kf = self.kfb[2 * d + (i - 1) % 2]
                    self.op("dve", "tensor_tensor", out=kf[:].rearrange("p (h c) -> p h c", h=2),
                            in0=pkb[:, 0:128].rearrange("p (h c) -> p h c", h=2),
                            in1=self.TK[pp][:, 2 * d:2 * d + 2].unsqueeze(2).to_broadcast([128, 2, 64]), op=ALU.mult,
                            reads=[R(pk), R(self.TK[pp])], writes=[R(kf)])
                if 0 <= i - 2 < NS:
                    tb = orders[d][i - 2]
                    gi = min(tb // 4, 4)
                    kf = self.kfb[2 * d + (i - 2) % 2]
                    pu, pures = self.ab()
                    self.mm(pu[:, 0:128], kf[:], V3[:, tb, 0:128], reads=[R(kf), (vn, gi)], writes=[pures])
                    pus[(d, i - 2)] = (pu, pures)
                if 0 <= i - 3 < NS:
                    tb = orders[d][i - 3]
                    gi = min(tb // 4, 4)
                    st = sts_[d]
                    pu, pures = pus.pop((d, i - 3))
                    self.op("pool", "tensor_copy", out=ST3s[d][:, tb, :], in_=st[:], reads=[R(st)], writes=[(stns[d], gi)])
                    self.op("dve", "scalar_tensor_tensor", out=st[:], in0=st[:], scalar=self.CFB[pp][:, d:d + 1], in1=pu[:, 0:128],
                            op0=ALU.mult, op1=ALU.add, reads=[R(st), R(self.CFB[pp]), pures], writes=[R(st)])
        ntb = NTB if need_ctx else 16
        sm = self.small
        cur = {}

        def front(tb):
            gi = min(tb // 4, 4)
            ts_ = slice(tb * 128, (tb + 1) * 128)
            sts = [self.gp(), self.gp()]
            for h2 in range(2):
                hs = slice(h2 * 64, (h2 + 1) * 64)
                self.mm(sts[h2][:, 0:128], KT[hs, ts_], QT[hs, ts_], reads=[(kn, gi), (qn, gi)], writes=[R(sts[h2])])
            pt = self.ptb()
            for h2 in range(2):
                self.op("dve", "scalar_tensor_tensor", out=pt[:, h2 * 128:(h2 + 1) * 128], in0=sts[h2][:, 0:128], scalar=0.125,
                        in1=self.DT[pp][:, h2 * 128:(h2 + 1) * 128],
                        op0=ALU.mult, op1=ALU.mult, reads=[R(sts[h2]), R(self.DT[pp])], writes=[R(pt)])
            qpool = self.qfb + self.kfb
            qf = qpool[(2 * tb) % 8]
            qb_ = qpool[(2 * tb + 1) % 8]
            self.op("pool", "tensor_tensor", out=qf[:], in0=QT[:, ts_], in1=self.TQF[pp][:], op=ALU.mult,
                    reads=[(qn, gi), R(self.TQF[pp])], writes=[R(qf)])
            self.op("pool", "tensor_tensor", out=qb_[:], in0=QT[:, ts_], in1=self.TQB[pp][:], op=ALU.mult,
                    reads=[(qn, gi), R(self.TQB[pp])], writes=[R(qb_)])
            return (pt, qf, qb_)

        def back(tb, pt, qf, qb_):
            gi = min(tb // 4, 4)
            jj = tb % 4
            if jj == 0:
                cur["po"], cur["pres"] = self.ab()
            po, pres = cur["po"], cur["pres"]
            for h2 in range(2):
                hs = slice(h2 * 64, (h2 + 1) * 64)
                oreg = po[:, jj * 128 + h2 * 64:jj * 128 + (h2 + 1) * 64]
                self.mm(oreg, qf[hs, :], SF3[hs, tb, hs], start=(jj == 0 and h2 == 0), stop=False,
                        reads=[R(qf), (sfn, gi)], writes=[pres], sgc=True)
                self.mm(oreg, qb_[hs, :], SB3[hs, tb, hs], start=False, stop=False,
                        reads=[R(qb_), (sbn, gi)], writes=[pres], sgc=True)
                self.mm(oreg, pt[:, h2 * 128:(h2 + 1) * 128], V3[:, tb, hs], start=False, stop=True,
                        reads=[R(pt), (vn, gi)], writes=[pres], sgc=True)
            last = (jj == 3) or (tb == ntb - 1)
            if not last:
                return
            tb0 = tb - jj
            nb = jj + 1
            ng = 2 * nb
            W = nb * 128
            v3 = lambda a: a[:, 0:W].rearrange("p (g c) -> p g c", c=64)
            osb = self.OA[:].rearrange("p a b -> p (a b)")
            ocn_ = self.OB[:].rearrange("p a b -> p (a b)")
            sqv = self.O0N[:].rearrange("p a b -> p (a b)")
            self.op("dve", "tensor_copy", out=osb[:, 0:W], in_=po[:, 0:W], reads=[pres], writes=[R(self.OA)])
            self.op("dve", "tensor_reduce", out=sm[:, 24:24 + ng], in_=v3(osb), axis=AX.X, op=ALU.add,
                    reads=[R(self.OA)], writes=[R(sm)])
            self.op("dve", "tensor_scalar", out=sm[:, 24:24 + ng], in0=sm[:, 24:24 + ng], scalar1=1.0 / 64.0, scalar2=None, op0=ALU.mult,
                    reads=[R(sm)], writes=[R(sm)])
            self.op("dve", "tensor_tensor", out=v3(ocn_), in0=v3(osb), in1=sm[:, 24:24 + ng].unsqueeze(2).to_broadcast([128, ng, 64]),
                    op=ALU.subtract, reads=[R(self.OA), R(sm)], writes=[R(self.OB)])
            self.op("pool", "tensor_tensor", out=sqv[:, 0:W], in0=ocn_[:, 0:W], in1=ocn_[:, 0:W], op=ALU.mult,
                    reads=[R(self.OB)], writes=[R(self.O0N)])
            self.op("dve", "tensor_reduce", out=sm[:, 44:44 + ng], in_=v3(sqv), axis=AX.X, op=ALU.add,
                    reads=[R(self.O0N)], writes=[R(sm)])
            self.op("act", "activation", out=sm[:, 52:52 + ng], in_=sm[:, 44:44 + ng], func=AF.Ln, bias=self.epsb[:, 1:2], scale=1.0 / 64.0,
                    reads=[R(sm), R(self.epsb)], writes=[R(sm)])
            self.op("act", "activation", out=sm[:, 24:24 + ng], in_=sm[:, 52:52 + ng], func=AF.Exp, scale=-0.5,
                    reads=[R(sm)], writes=[R(sm)])
            self.op("dve", "tensor_tensor", out=v3(osb), in0=v3(ocn_), in1=sm[:, 24:24 + ng].unsqueeze(2).to_broadcast([128, ng, 64]),
                    op=ALU.mult, reads=[R(self.OB), R(sm)], writes=[R(self.OA)])
            self.op("pool", "tensor_tensor", out=OC3[:, tb0:tb0 + nb, :], in0=osb[:, 0:W].rearrange("p (t c) -> p t c", c=128),
                    in1=G3[:, tb0:tb0 + nb, :], op=ALU.mult,
                    reads=[R(self.OA), (gn, gi)], writes=[(ocn, gi)])

        pend = []
        for tb in range(ntb):
            fr = front(tb)
            pend.append((tb,) + fr)
            if len(pend) > 3:
                back(*pend.pop(0))
        while pend:
            back(*pend.pop(0))
        self.finish_chunk(l, 4 + pp, OC3, ocn, OCT, octn, need_ctx)

    def chunk_C(self, l, need_ctx):
        src3 = self.win_d[l].rearrange("(kc p) n -> p kc n", p=128)
        wbuf, wres = self.wst_load(src3, [(2560, 0, 256), (2816, 256, 64), (2816, 320, 64), (2880, 384, 64), (2880, 448, 64)])
        wbufv, wresv = self.wst_load(src3, [(2944, 0, 128)])
        QZ = self.GBT[:, 0:2 * NT].rearrange("p (g t) -> p g t", g=2)
        qzn = "QZ"
        gb01 = [(("GB", i), g_) for i in range(2) for g_ in range(5)]
        KT, V = self.gb(2), self.gb(3)
        kn, vn = ("GB", 2), ("GB", 3)
        OC, OCT = self.gb(6), self.gb(7)
        ocn, octn = ("GB", 6), ("GB", 7)
        V4 = V.rearrange("p (t j c) -> p t j c", j=2, c=65)
        OC3 = OC[:, 0:NT].rearrange("p (t c) -> p t c", c=128)
        allg = list(range(5))
        self.op("pool", "memset", ap=V4[:, :, :, 64:65], constant=1.0, writes=[(vn, g) for g in allg])
        self.op("pool", "memset", ap=QZ[64:128, 0, :], constant=0.0, writes=gb01 + [(qzn, g) for g in allg])
        self.op("pool", "memset", ap=QZ[0:64, 1, :], constant=0.0, writes=gb01 + [(qzn, g) for g in allg])
        self.proj_tm(wbufv, wresv, 0,
                     lambda tb0, nb, ps: (V4[:, tb0:tb0 + nb, :, 0:64],
                                          ps[:, 0:nb * 128].rearrange("p (t j c) -> p t j c", j=2, c=64)),
                     vn)
        ntb = NTB if need_ctx else 16
        sm = self.small
        qz_fn = lambda t0, n: [(QZ[0:64, 0, t0:t0 + n], slice(0, 64)), (QZ[64:128, 1, t0:t0 + n], slice(64, 128))]
        for j in range(2):
            self.proj_fm(wbuf, wres, j * 128, None, qzn, True, need_ctx, dst_fn=qz_fn)
            self.proj_fm(wbuf, wres, 256 + j * 128, KT, kn, True, True)
            items = []
            for n_ in range(ntb):
                if n_ < 16:
                    kbs = ([n_ - 1] if n_ > 0 else []) + [n_] + ([n_ + 1] if n_ < 15 else []) + [16, 17]
                else:
                    kbs = [16, 17]
                for ki, m in enumerate(kbs):
                    items.append((n_, ki, m, len(kbs)))
            cur = {}

            def front(n_, ki, m, nk):
                gi = min(n_ // 4, 4)
                mgi = min(m // 4, 4)
                st = self.gp()
                self.mm(st[:, 0:256], KT[:, m * 128:(m + 1) * 128], QZ[:, :, n_ * 128:(n_ + 1) * 128],
                        reads=[(kn, mgi), (qzn, gi)], writes=[R(st)])
                pt = self.ptb()
                self.op("act", "activation", out=pt[:, 0:256], in_=st[:, 0:256], func=AF.Exp, scale=0.125,
                        reads=[R(st)], writes=[R(pt)])
                if n_ < 16 and m == n_ - 1:
                    self.op("pool", "tensor_tensor", out=pt[:, 0:256], in0=pt[:, 0:256], in1=self.C["maskp"][:], op=ALU.mult,
                            reads=[R(pt), R(self.C["maskp"])], writes=[R(pt)])
                if n_ < 15 and m == n_ + 1:
                    self.op("pool", "tensor_tensor", out=pt[:, 0:256], in0=pt[:, 0:256], in1=self.C["maskn"][:], op=ALU.mult,
                            reads=[R(pt), R(self.C["maskn"])], writes=[R(pt)])
                return pt

            def back(n_, ki, m, nk, pt):
                mgi = min(m // 4, 4)
                r3 = n_ % 3
                if r3 == 0 and ki == 0:
                    cur["po"], cur["pres"] = self.ab()
                    cur["n0"] = n_
                po, pres = cur["po"], cur["pres"]
                for g in range(2):
                    off = r3 * 130 + g * 65
                    self.mm(po[:, off:off + 65], pt[:, g * 128:(g + 1) * 128], V4[:, m, j, 0:65],
                            start=(r3 == 0 and ki == 0 and g == 0), stop=(ki == nk - 1), reads=[R(pt), (vn, mgi)], writes=[pres],
                            sgc=True)
                if ki == nk - 1 and (r3 == 2 or n_ == ntb - 1):
                    n0 = cur["n0"]
                    cnt = n_ - n0 + 1
                    pov = po[:, 0:cnt * 130].rearrange("p (n g c) -> p n g c", g=2, c=65)
                    den = sm[:, 34:34 + 2 * cnt].rearrange("p (n g) -> p n g", g=2)
                    rz = sm[:, 40:40 + 2 * cnt].rearrange("p (n g) -> p n g", g=2)
                    self.op("dve", "tensor_tensor", out=den, in0=pov[:, :, :, 64],
                            in1=self.ESINK[:, l, 2 * j:2 * j + 2].unsqueeze(1).to_broadcast([128, cnt, 2]), op=ALU.add,
                            reads=[pres, R(self.ESINK)], writes=[R(sm)])
                    self.op("dve", "reciprocal", out=rz, in_=den, reads=[R(sm)], writes=[R(sm)])
                    gis = sorted(set(min(q // 4, 4) for q in range(n0, n_ + 1)))
                    self.op("dve", "tensor_tensor", out=OC3[:, n0:n_ + 1, :].rearrange("p n (g c) -> p n g c", g=2),
                            in0=pov[:, :, :, 0:64], in1=rz.unsqueeze(3).to_broadcast([128, cnt, 2, 64]), op=ALU.mult,
                            reads=[pres, R(sm)], writes=[(ocn, g_) for g_ in gis])

            pend = []
            for it in items:
                pt = front(*it)
                pend.append(it + (pt,))
                if len(pend) > 3:
                    back(*pend.pop(0))
            while pend:
                back(*pend.pop(0))
            self.finish_chunk(l, 6 + j, OC3, ocn, OCT, octn, need_ctx)

    def mlp(self, l, need_ctx):
        self.drain()
        self.norm(l, 1, skip_ctx=not need_ctx)
        w1v = self.w1_d[l].rearrange("(kc p) n -> p kc n", p=128)
        w2v = self.w2_d[l].rearrange("(fc p) n -> p fc n", p=128)
        pending = None
        ucount = 0

        def mlp2(W2, w2res, AT, ares, gi, t0, n, s):
            for dc in range(KC):
                ps2, pres = self.ab()
                for fc in range(4):
                    self.mm(ps2[:, :n], W2[:, fc, dc * 128:(dc + 1) * 128], AT[:, fc, :n], start=(fc == 0), stop=(fc == 3),
                            reads=[(w2res[0], 0), (w2res[1], 0), ares], writes=[pres])
                self.op("dve", "scalar_tensor_tensor", out=self.XT[:, dc, t0:t0 + n], in0=ps2[:, :n],
                        scalar=self.MOD[:, l, 40 + dc, s:s + 1], in1=self.XT[:, dc, t0:t0 + n], op0=ALU.mult, op1=ALU.add,
                        reads=[pres, ("MOD", l), ("XT", dc, gi)], writes=[("XT", dc, gi)])

        for fb in range(8):
            W1, w1res = self.wst_load(w1v, [(fb * 512, 0, 512)])
            i2 = fb % 2
            W2 = self.GBT[:, i2 * 2 * GBW:i2 * 2 * GBW + 4096].rearrange("p (f n) -> p f n", f=4)
            w2res = [("GB", 2 * i2), ("GB", 2 * i2 + 1)]
            self.dma(W2, w2v[:, fb * 4:(fb + 1) * 4, :], writes=[(r, g) for r in w2res for g in range(5)],
                     key="w2_%d" % i2, eng="pool")
            for gi, (t0, n) in enumerate(TGS):
                if gi == 4 and not need_ctx:
                    continue
                s = 0 if gi < 4 else 1
                ai = 4 + (ucount % 2)
                ucount += 1
                AT = self.gb(ai)[:, 0:2048].rearrange("p (f n) -> p f n", f=4)
                ares = (("GB", ai), 0)
                aresw = [(("GB", ai), g_) for g_ in range(5)]
                for fc in range(4):
                    ps = self.gp()
                    for kc in range(KC):
                        self.mm(ps[:, :n], W1[:, kc, fc * 128:(fc + 1) * 128], self.HT[:, kc, t0:t0 + n],
                                start=(kc == 0), stop=(kc == KC - 1), reads=[w1res, ("HT", kc, gi)], writes=[R(ps)])
                    rt = self.tmpf[fc % 4]
                    self.op("act", "activation", out=rt[:, :n], in_=ps[:, :n], func=AF.Relu, reads=[R(ps)], writes=[R(rt)])
                    self.op("pool", "tensor_tensor", out=AT[:, fc, :n], in0=rt[:, :n], in1=rt[:, :n], op=ALU.mult,
                            reads=[R(rt)], writes=aresw)
                if pending is not None:
                    mlp2(*pending)
                pending = (W2, w2res, AT, ares, gi, t0, n, s)
        mlp2(*pending)

    def final_norm(self):
        XT, identf = self.XT, self.C["ident_f"]
        gfin = self.P["g_final"]
        YT = self.HT[:].rearrange("p k t -> p (k t)")[:, 0:8192].bitcast(F32).rearrange("p (k t) -> p k t", k=KC)
        ytres = [("HT", kc, g) for kc in range(KC) for g in range(5)]
        for g in range(4):
            t0 = g * 512
            acc = self.gp()
            self.sumsq_rstd(g, acc)
            for kc in range(KC):
                self.op("dve", "scalar_tensor_tensor", out=YT[:, kc, :], in0=XT[:, kc, t0:t0 + 512], scalar=gfin[:, kc:kc + 1],
                        in1=self.rstd[:], op0=ALU.mult, op1=ALU.mult,
                        reads=[("XT", kc, g), R(self.rstd), R(gfin)], writes=(ytres if kc == 0 else []) + [("YT", kc)])
            for j in range(4):
                tb = g * 4 + j
                ob = self.iobuf[tb % 2]
                obres = self.iores[tb % 2]
                for half in range(2):
                    pt = self.gp()
                    for jj in range(4):
                        kc = half * 4 + jj
                        self.tr(pt[:, jj * 128:(jj + 1) * 128], YT[:, kc, j * 128:(j + 1) * 128], identf[:],
                                reads=[("YT", kc), ("HT", 0, 0), R(identf)], writes=[R(pt)])
                    self.op("act", "mul", out=ob[:, half * 512:(half + 1) * 512], in_=pt[:], mul=32.0,
                            reads=[R(pt)], writes=obres)
                self.dma(self.out_d[tb * 128:(tb + 1) * 128, :], ob, reads=obres, key="io%d" % (tb % 2))


_CACHE = {}


def kernel(**inputs):
    cfg = inputs.pop("_cfg", {}) if "_cfg" in inputs else {}
    inputs = {k: np.asarray(v) for k, v in inputs.items()}
    key = tuple(sorted(cfg.items()))
    if key not in _CACHE:
        b = Builder(cfg)
        _CACHE[key] = (b.build(), b)
    nc, b = _CACHE[key]
    in_maps = [prep_inputs(inputs, i, b.depth) for i in range(8)]
    res = run_bass_kernel_spmd(nc, in_maps, core_ids=list(range(8)))
    if cfg.get("_ret_all"):
        return res
    out = np.stack([np.asarray(r["out"]) for r in res.results], axis=0)
    return out.astype(np.float32)
```

```python
import contextlib
import math
import numpy as np
import ml_dtypes
import concourse.bass as bass
import concourse.mybir as mybir
from concourse.bass_utils import run_bass_kernel_spmd

F32 = mybir.dt.float32
BF16 = mybir.dt.bfloat16
ALU = mybir.AluOpType
AF = mybir.ActivationFunctionType
AX = mybir.AxisListType

D = 1024
T = 2048
L = 256
NT = T + L
DEPTH = 4
KC = D // 128
NTB = NT // 128
EPS = 1e-6
TGS = [(0, 512), (512, 512), (1024, 512), (1536, 512), (2048, 256)]


class Op:
    __slots__ = ("eng", "fn", "idx", "deps", "raw", "inc", "count", "dma", "is_dma", "embed")

    def __init__(self, eng, fn, idx):
        self.eng = eng
        self.fn = fn
        self.idx = idx
        self.deps = set()
        self.raw = set()
        self.inc = False
        self.count = 0
        self.dma = None
        self.is_dma = False
        self.embed = False


EMBED_WAITS = True


class Sched:
    ENGS = ["pe", "act", "dve", "pool", "sp"]

    def __init__(self):
        self.ops = {e: [] for e in self.ENGS}
        self.last_w = {}
        self.readers = {}
        self.dma_n = {}

    def add(self, eng, fn, reads=(), writes=(), dma=None):
        op = Op(eng, fn, len(self.ops[eng]))
        xr = [r for r in reads if isinstance(r, tuple) and str(r[0]).startswith(("ps", "oacc"))]
        if xr:
            for r in xr:
                w = self.last_w.get(r)
                if w is not None:
                    op.raw.add(w)
            writes = list(writes) + [r for r in xr if r not in writes]
        for r in reads:
            w = self.last_w.get(r)
            if w is not None:
                op.deps.add(w)
                op.raw.add(w)
        for r in writes:
            w = self.last_w.get(r)
            if w is not None:
                op.deps.add(w)
            rd = self.readers.get(r)
            if rd:
                for o in rd.values():
                    op.deps.add(o)
        for r in reads:
            d = self.readers.setdefault(r, {})
            if dma is not None:
                d[("dma", id(op))] = op
            else:
                d[eng] = op
        for r in writes:
            self.last_w[r] = op
            self.readers[r] = {}
        if dma is not None:
            n = self.dma_n.get(dma, 0) + 1
            self.dma_n[dma] = n
            op.dma = (dma, n)
            op.is_dma = True
        self.ops[eng].append(op)
        return op

    def _needs_wait(self, op, d):
        if d.is_dma:
            return True
        if d.eng != op.eng:
            return True
        if op.is_dma:
            return True
        if op.eng == "pe":
            return False
        return (d in op.raw) and (op.idx - d.idx <= 4)

    def emit(self, nc, stack):
        for e in self.ENGS:
            for op in self.ops[e]:
                for d in op.deps:
                    if not d.is_dma and self._needs_wait(op, d):
                        d.inc = True
        for e in self.ENGS:
            c = 0
            for op in self.ops[e]:
                if op.inc:
                    c += 1
                op.count = c
        sems = {e: stack.enter_context(nc.semaphore("s_" + e)) for e in self.ENGS}
        dsems = {k: stack.enter_context(nc.semaphore("d_%s" % (k,))) for k in self.dma_n}
        block = stack.enter_context(nc.Block())
        stats = {}

        def run(ename, engine):
            known = {}
            nw = 0
            for op in self.ops[ename]:
                need = {}
                for d in op.deps:
                    if not self._needs_wait(op, d):
                        continue
                    if d.is_dma:
                        key = ("d", d.dma[0])
                        val = 16 * d.dma[1]
                    else:
                        key = ("e", d.eng)
                        val = d.count
                    if val > need.get(key, 0):
                        need[key] = val
                todo = []
                for key, val in need.items():
                    if known.get(key, 0) >= val:
                        continue
                    known[key] = val
                    todo.append((dsems[key[1]] if key[0] == "d" else sems[key[1]], val))
                last = todo.pop() if (todo and op.embed and EMBED_WAITS) else None
                for s, val in todo:
                    engine.wait_ge(s, val)
                    nw += 1
                ins = op.fn(engine)
                if last is not None:
                    ins._wait_ge(last[0], last[1])
                if op.is_dma:
                    ins.then_inc(dsems[op.dma[0]], 16)
                elif op.inc:
                    ins.then_inc(sems[ename], 1)
            for op in self.ops[ename]:
                if op.is_dma:
                    key = ("d", op.dma[0])
                    val = 16 * self.dma_n[op.dma[0]]
                    if known.get(key, 0) < val:
                        known[key] = val
                        engine.wait_ge(dsems[op.dma[0]], val)
            stats[ename] = (len(self.ops[ename]), nw)

        block.tensor(lambda e: run("pe", e))
        block.scalar(lambda e: run("act", e))
        block.vector(lambda e: run("dve", e))
        block.gpsimd(lambda e: run("pool", e))
        block.sync(lambda e: run("sp", e))
        self.stats = stats
        return stats


def R(t, *idx):
    return (t.name,) + idx


GBW = 2340
NGB = 8
LAM_INIT = [0.8 - 0.6 * math.exp(-0.3 * l) for l in range(DEPTH)]


def host_consts():
    c = {}
    bf = ml_dtypes.bfloat16
    c["ident_f"] = np.eye(128, dtype=np.float32)
    c["ident_b"] = np.eye(128, dtype=np.float32).astype(bf)
    c["ones_b"] = np.ones((128, 128), dtype=np.float32).astype(bf)
    pm = np.zeros((128, 128), np.float32)
    sign = np.zeros(128, np.float64)
    for f in range(128):
        fl = f % 64
        q = fl // 16
        src = f + 16 if q in (0, 2) else f - 16
        pm[src, f] = 1.0
        sign[f] = -1.0 if q in (0, 2) else 1.0
    c["perm_b"] = pm.astype(bf)
    t = np.arange(T)
    r = (t // 64).astype(np.float64)
    col = (t % 64).astype(np.float64)
    inv = 10000.0 ** (-np.arange(16, dtype=np.float64) / 16)
    ang64 = np.concatenate([r[:, None] * inv, r[:, None] * inv, col[:, None] * inv, col[:, None] * inv], axis=1)
    ang = np.concatenate([ang64, ang64], axis=1).T
    c["cosT"] = np.cos(ang).astype(np.float32).astype(bf)
    c["sinT"] = (np.sin(ang) * sign[:, None]).astype(np.float32).astype(bf)
    il = np.arange(128)
    mp = (il[None, :] <= il[:, None]).astype(np.float32)
    mn = (il[:, None] <= il[None, :]).astype(np.float32)
    c["maskp"] = np.concatenate([mp, mp], axis=1).astype(bf)
    c["maskn"] = np.concatenate([mn, mn], axis=1).astype(bf)
    s_ = il[:, None].astype(np.float32)
    t_ = il[None, :].astype(np.float32)
    c["r1"] = np.maximum(t_ - s_, 0.0).astype(np.float32)
    c["r2"] = np.maximum(s_ - t_, 0.0).astype(np.float32)
    c["i1"] = np.broadcast_to((il + 1.0)[None, :], (128, 128)).astype(np.float32).copy()
    c["i2"] = np.broadcast_to((128.0 - il)[None, :], (128, 128)).astype(np.float32).copy()
    c["kcol"] = np.stack([127.0 - il, il * 1.0], axis=1).astype(np.float32)
    return c


CONST_SPECS = [("ident_f", [128, 128], F32), ("ident_b", [128, 128], BF16), ("ones_b", [128, 128], BF16),
               ("perm_b", [128, 128], BF16), ("cosT", [128, T], BF16), ("sinT", [128, T], BF16),
               ("maskp", [128, 256], BF16), ("maskn", [128, 256], BF16),
               ("r1", [128, 128], F32), ("r2", [128, 128], F32), ("i1", [128, 128], F32), ("i2", [128, 128], F32),
               ("kcol", [128, 2], F32)]

PARAM_SPECS = [("g_final", [128, KC]), ("cc", [128, KC, 2]), ("bada", [128, DEPTH, 48]),
               ("gmix", [128, DEPTH, KC]), ("gmlp", [128, DEPTH, KC]),
               ("lamq", [128, DEPTH, 2, 64]), ("lamk", [128, DEPTH, 2, 64]), ("sublng", [128, DEPTH, 128]),
               ("decbc", [128, DEPTH, 8]), ("deccol", [128, DEPTH, 2, 2]), ("sinkbc", [128, DEPTH, 4])]


def prep_inputs(inputs, b, depth=DEPTH):
    f = np.float32
    m = {}
    m["x"] = np.ascontiguousarray(inputs["x"][b], dtype=f)
    m["ctx"] = np.ascontiguousarray(inputs["ctx"][b], dtype=f)
    m["g_final"] = np.ascontiguousarray(inputs["g_final"].reshape(KC, 128).T, dtype=f)
    cc = np.stack([inputs["c"][b].reshape(KC, 128).T, inputs["c_ctx"].reshape(KC, 128).T], axis=2)
    m["cc"] = np.ascontiguousarray(cc, dtype=f)
    m["bada"] = np.ascontiguousarray(inputs["b_ada"].reshape(DEPTH, 48, 128).transpose(2, 0, 1), dtype=f)
    m["gmix"] = np.ascontiguousarray(inputs["g_mix"].reshape(DEPTH, KC, 128).transpose(2, 0, 1), dtype=f)
    m["gmlp"] = np.ascontiguousarray(inputs["g_mlp"].reshape(DEPTH, KC, 128).transpose(2, 0, 1), dtype=f)
    lamq = np.stack([inputs["lam_q1"], inputs["lam_q2"]], axis=1)
    lamk = np.stack([inputs["lam_k1"], inputs["lam_k2"]], axis=1)
    m["lamq"] = np.ascontiguousarray(np.broadcast_to(lamq[None], (128, DEPTH, 2, 64)), dtype=f)
    m["lamk"] = np.ascontiguousarray(np.broadcast_to(lamk[None], (128, DEPTH, 2, 64)), dtype=f)
    m["sublng"] = np.ascontiguousarray(np.broadcast_to(inputs["subln_g"][None], (128, DEPTH, 128)), dtype=f)
    dec = np.concatenate([inputs["ret_decay_fwd"], inputs["ret_decay_bwd"]], axis=1)
    m["decbc"] = np.ascontiguousarray(np.broadcast_to(dec[None], (128, DEPTH, 8)), dtype=f)
    dcol = np.zeros((128, DEPTH, 2, 2), f)
    for di, nm in enumerate(["ret_decay_fwd", "ret_decay_bwd"]):
        for pp in range(2):
            dcol[0:64, :, di, pp] = inputs[nm][:, 2 * pp][None, :]
            dcol[64:128, :, di, pp] = inputs[nm][:, 2 * pp + 1][None, :]
    m["deccol"] = dcol
    m["sinkbc"] = np.ascontiguousarray(np.broadcast_to(inputs["sink_logit"][None], (128, DEPTH, 4)), dtype=f)
    for k in ["w_ada", "w_in", "w_out", "w_mlp1", "w_mlp2"]:
        m[k] = np.ascontiguousarray(inputs[k][:depth], dtype=f)
    m.update(host_consts())
    return m


class Builder:
    def __init__(self, cfg):
        self.cfg = cfg
        self.depth = cfg.get("depth", DEPTH)
        self.nc = bass.Bass("TRN2", target_bir_lowering=False)
        self.S = Sched()
        self.stack = contextlib.ExitStack()
        self.gp_i = 0
        self.ab_i = 0
        self.pt_i = 0
        self.wst_i = 0
        self.wo_i = 0
        self.oacc_i = 0
        self.rope_i = 0
        self.dbg_names = []
        self.deferred = []

    def dram_in(self, name, shape, dt=F32):
        return self.nc.dram_tensor(name, list(shape), dt, kind="ExternalInput").ap()

    def sb(self, name, shape, dt):
        return self.stack.enter_context(self.nc.sbuf_tensor(name, list(shape), dt))

    def ps(self, name, shape, dt=F32):
        return self.stack.enter_context(self.nc.psum_tensor(name, list(shape), dt))

    def dma(self, out, in_, reads=(), writes=(), key=None, eng="sp", **kw):
        self.S.add(eng, lambda e: e.dma_start(out=out, in_=in_, **kw), reads=reads, writes=writes, dma=key)

    def mm(self, out, lhsT, rhs, start=True, stop=True, reads=(), writes=(), sgc=False):
        if sgc:
            o = self.S.add("pe", lambda e: e.matmul(out, lhsT=lhsT, rhs=rhs, start=start, stop=stop, skip_group_check=True),
                           reads=reads, writes=writes)
        else:
            o = self.S.add("pe", lambda e: e.matmul(out, lhsT=lhsT, rhs=rhs, start=start, stop=stop),
                           reads=reads, writes=writes)
        o.embed = True

    def tr(self, out, in_, ident, reads=(), writes=()):
        o = self.S.add("pe", lambda e: e.transpose(out, in_, ident), reads=reads, writes=writes)
        o.embed = (ident.dtype == BF16)

    def op(self, eng, meth, reads=(), writes=(), **kw):
        o = self.S.add(eng, lambda e: getattr(e, meth)(**kw), reads=reads, writes=writes)
        o.embed = eng in ("act", "dve")

    def gp(self):
        t = self.PSG[self.gp_i % len(self.PSG)]
        self.gp_i += 1
        return t

    def ab(self):
        i = self.ab_i % 4
        self.ab_i += 1
        return self.OACC[i // 2][:, (i % 2) * 512:(i % 2 + 1) * 512], ("oacc", i // 2, i % 2)

    def ptb(self):
        t = self.PT[self.pt_i % len(self.PT)]
        self.pt_i += 1
        return t

    def gb(self, i):
        return self.GBT[:, i * GBW:(i + 1) * GBW]

    def dbg(self, name, ap, shape, dt, reads):
        d = self.nc.dram_tensor(name, list(shape), dt, kind="ExternalOutput").ap()
        self.dma(d, ap, reads=reads, key="dbg_" + name)
        self.dbg_names.append(name)

    def build(self):
        nc, S = self.nc, self.S
        cfg = self.cfg
        self.x_d = self.dram_in("x", [T, D])
        self.ctx_d = self.dram_in("ctx", [L, D])
        self.out_d = nc.dram_tensor("out", [T, D], F32, kind="ExternalOutput").ap()
        self.cd = {n: self.dram_in(n, shp, dt) for n, shp, dt in CONST_SPECS}
        self.pd = {n: self.dram_in(n, shp, F32) for n, shp in PARAM_SPECS}
        self.wada_d = self.dram_in("w_ada", [self.depth, D, 6 * D])
        self.win_d = self.dram_in("w_in", [self.depth, D, 3 * D])
        self.wout_d = self.dram_in("w_out", [self.depth, D, D])
        self.w1_d = self.dram_in("w_mlp1", [self.depth, D, 4 * D])
        self.w2_d = self.dram_in("w_mlp2", [self.depth, 4 * D, D])

        self.XT = self.sb("XT", [128, KC, NT], F32)
        self.HT = self.sb("HT", [128, KC, NT], BF16)
        self.GBT = self.sb("GBT", [128, NGB * GBW], BF16)
        self.WST = [self.sb("WST%d" % i, [128, KC, 512], BF16) for i in range(2)]
        self.WO = [self.sb("WO%d" % i, [128, D], BF16) for i in range(2)]
        self.C = {}
        for n, shp, dt in CONST_SPECS:
            t = self.sb("c_" + n, shp, dt)
            self.C[n] = t
            self.dma(t[:], self.cd[n], writes=[R(t)], key="c_" + n)
        self.O0N = self.sb("o0n", [128, 4, 128], F32)
        self.OA = self.sb("oa", [128, 4, 128], F32)
        self.OB = self.sb("ob", [128, 4, 128], F32)
        self.GA = self.sb("GA", [128, DEPTH, 128], F32)
        self.P = {}
        alias = {"lamq": self.O0N, "lamk": self.OB, "sublng": self.GA}
        for n, shp in PARAM_SPECS:
            if n in alias:
                t = alias[n]
                self.dma(t[:].rearrange("p a b -> p (a b)"), self.pd[n].rearrange("p a b c -> p (a b c)") if len(shp) == 4 else self.pd[n].rearrange("p a b -> p (a b)"), writes=[R(t)], key="p_" + n)
            else:
                t = self.sb("p_" + n, shp, F32)
                self.dma(t[:], self.pd[n], writes=[R(t)], key="p_" + n)
            self.P[n] = t
        self.iobuf = [self.gb(i)[:, 0:2048].bitcast(F32) for i in range(2)]
        self.iores = [[(("GB", i), g) for g in range(5)] for i in range(2)]
        self.tmpf = [self.sb("tmpf%d" % i, [128, 512], F32) for i in range(4)]
        self.PT = [self.sb("pt%d" % i, [128, 512], BF16) for i in range(4)]
        self.sqb = [self.PT[0], self.PT[1]]
        self.ropeq = [self.PT[2], self.PT[3]]
        self.rstd = self.tmpf[2]
        self.rstd0 = self.tmpf[3]
        self.small = self.sb("small", [128, 64], F32)
        self.epsb = self.sb("epsb", [128, 4], F32)
        S.add("pool", lambda e: e.memset(self.epsb[:, 0:1], float(D * EPS)), writes=[R(self.epsb)])
        S.add("pool", lambda e: e.memset(self.epsb[:, 1:2], float(EPS)), writes=[R(self.epsb)])
        S.add("pool", lambda e: e.memset(self.epsb[:, 2:3], 1.0), writes=[R(self.epsb)])
        self.MOD = self.sb("MOD", [128, DEPTH, 48, 2], F32)
        self.GS = [self.sb("GS%d" % i, [128, DEPTH, KC, 2], F32) for i in range(2)]
        self.G32 = [self.sb("G32_%d" % i, [128, DEPTH, KC], F32) for i in range(2)]
        self.siluT = self.sb("siluT", [128, KC, 2], BF16)
        self.LG = self.sb("LG", [128, DEPTH, 8], F32)
        self.LGC = self.sb("LGC", [128, DEPTH, 2, 2], F32)
        self.ESINK = self.sb("ESINK", [128, DEPTH, 4], F32)
        self.NEGLAM = self.sb("NEGLAM", [128, DEPTH], F32)
        self.lame = self.sb("lame", [128, DEPTH, 2], F32)
        _tqf = self.sb("TQF", [128, 128], F32)
        _tqb = self.sb("TQB", [128, 128], F32)
        _dt = self.sb("DT", [128, 256], F32)
        _tk = self.sb("TK", [128, 4], F32)
        _cfb = self.sb("CFB", [128, 2], F32)
        self.TQF, self.TQB, self.DT, self.TK, self.CFB = [_tqf] * 2, [_tqb] * 2, [_dt] * 2, [_tk] * 2, [_cfb] * 2
        self.stf = self.sb("stf", [128, 128], F32)
        self.stb = self.sb("stb", [128, 128], F32)
        self.kfb = [self.sb("kfb%d" % i, [128, 128], BF16) for i in range(4)]
        self.qfb = [self.sb("qfb%d" % i, [128, 128], BF16) for i in range(4)]
        self.PSG = [self.ps("ps%d" % i, [128, 512], F32) for i in range(4)]
        self.OACC = [self.ps("oacc%d" % i, [128, 1024], F32) for i in range(2)]

        self.load_input()
        self.prologue_params()
        self.adaln_all()
        for l in range(self.depth):
            need_ctx = l < DEPTH - 1
            self.norm(l, 0, skip_ctx=False)
            if cfg.get("dbg_h") == l:
                self.dbg("dbg_h", self.HT[:], [128, KC, NT], BF16, [("HT", kc, g) for kc in range(KC) for g in range(5)])
            order = []
            mx = cfg.get("mixers", "ABC")
            if "A" in mx:
                order += [("A", h) for h in range(4)]
            if "B" in mx:
                order += [("B", 0), ("B", 1)]
            if "C" in mx:
                order += [("C", 0)]
            for kind, i in order:
                if kind == "A":
                    self.chunk_A(l, i, need_ctx)
                elif kind == "B":
                    self.chunk_B(l, i, need_ctx)
                else:
                    self.chunk_C(l, need_ctx)
            if cfg.get("mlp", True):
                self.mlp(l, need_ctx)
        if cfg.get("dbg_x"):
            self.dbg("dbg_x", self.XT[:], [128, KC, NT], F32, [("XT", kc, g) for kc in range(KC) for g in range(5)])
        self.drain()
        self.final_norm()
        S.emit(nc, self.stack)
        return nc

    def xt_res(self, kc, gi):
        return ("XT", kc, gi)

    def load_input(self):
        S = self.S
        XT, PS, identf = self.XT, self.PSG, self.C["ident_f"]
        for tb in range(NTB):
            buf = self.iobuf[tb % 2]
            bres = self.iores[tb % 2]
            gi = min(tb // 4, 4)
            src = self.x_d[tb * 128:(tb + 1) * 128, :] if tb < 16 else self.ctx_d[(tb - 16) * 128:(tb - 15) * 128, :]
            self.dma(buf, src, writes=bres, key="io%d" % (tb % 2))
            for half in range(2):
                pt = self.gp()
                for j in range(4):
                    kc = half * 4 + j
                    self.tr(pt[:, j * 128:(j + 1) * 128], buf[:, kc * 128:(kc + 1) * 128], identf[:],
                            reads=bres + [R(identf)], writes=[R(pt)])
                self.op("dve", "tensor_copy", out=XT[:, half * 4:half * 4 + 4, tb * 128:(tb + 1) * 128],
                        in_=pt[:].rearrange("p (j t) -> p j t", j=4),
                        reads=[R(pt)], writes=[("XT", k, gi) for k in range(half * 4, half * 4 + 4)])

    def prologue_params(self):
        P = self.P
        sm = self.small
        self.op("act", "activation", out=self.siluT[:], in_=P["cc"][:], func=AF.Silu,
                reads=[R(P["cc"])], writes=[R(self.siluT)])
        self.op("dve", "tensor_scalar", out=self.G32[0][:], in0=P["gmix"][:], scalar1=32.0, scalar2=None, op0=ALU.mult,
                reads=[R(P["gmix"])], writes=[R(self.G32[0])])
        self.op("dve", "tensor_scalar", out=self.G32[1][:], in0=P["gmlp"][:], scalar1=32.0, scalar2=None, op0=ALU.mult,
                reads=[R(P["gmlp"])], writes=[R(self.G32[1])])
        for src, dst, n in ((P["decbc"], self.LG, DEPTH * 8), (P["deccol"], self.LGC, DEPTH * 4)):
            sv = src[:].rearrange("p a b -> p (a b)") if len(src.shape) == 3 else src[:].rearrange("p a b c -> p (a b c)")
            dv = dst[:].rearrange("p a b -> p (a b)") if len(dst.shape) == 3 else dst[:].rearrange("p a b c -> p (a b c)")
            self.op("act", "activation", out=sm[:, 0:n], in_=sv, func=AF.Exp, scale=-1.0,
                    reads=[R(src)], writes=[R(sm)])
            self.op("act", "activation", out=sm[:, 32:32 + n], in_=sm[:, 0:n], func=AF.Ln, bias=self.epsb[:, 2:3], scale=1.0,
                    reads=[R(sm), R(self.epsb)], writes=[R(sm)])
            self.op("dve", "tensor_scalar", out=dv, in0=sm[:, 32:32 + n], scalar1=-1.0, scalar2=None, op0=ALU.mult,
                    reads=[R(sm)], writes=[R(dst)])
        self.op("act", "activation", out=self.ESINK[:], in_=P["sinkbc"][:], func=AF.Exp,
                reads=[R(P["sinkbc"])], writes=[R(self.ESINK)])
        fl = lambda t: t[:].rearrange("p a b -> p (a b)")
        self.op("dve", "tensor_tensor", out=fl(self.OA), in0=fl(P["lamq"]), in1=fl(P["lamk"]), op=ALU.mult,
                reads=[R(P["lamq"]), R(P["lamk"])], writes=[R(self.OA)])
        self.op("dve", "tensor_reduce", out=self.lame[:].rearrange("p a b -> p (a b)"),
                in_=fl(self.OA).rearrange("p (a c) -> p a c", c=64), axis=AX.X, op=ALU.add,
                reads=[R(self.OA)], writes=[R(self.lame)])
        self.op("act", "activation", out=self.lame[:], in_=self.lame[:], func=AF.Exp,
                reads=[R(self.lame)], writes=[R(self.lame)])
        for l in range(DEPTH):
            self.op("dve", "tensor_scalar", out=self.NEGLAM[:, l:l + 1], in0=self.lame[:, l, 1:2],
                    scalar1=self.lame[:, l, 0:1], scalar2=-LAM_INIT[l], op0=ALU.subtract, op1=ALU.add,
                    reads=[R(self.lame)], writes=[R(self.NEGLAM)])
            self.op("dve", "tensor_scalar", out=self.GA[:, l, :], in0=self.GA[:, l, :],
                    scalar1=1.0 - LAM_INIT[l], scalar2=None, op0=ALU.mult,
                    reads=[R(self.GA)], writes=[R(self.GA)])

    def wst_load(self, src3, pieces, keyname="wst"):
        i = self.wst_i % 2
        self.wst_i += 1
        buf = self.WST[i]
        res = ("WST", i)
        for (sc, dc, n) in pieces:
            self.dma(buf[:, :, dc:dc + n], src3[:, :, sc:sc + n], writes=[res], key="wst%d" % i, eng="pool")
        return buf, res

    def adaln_gs(self, l):
        for w, base in ((0, 8), (1, 32)):
            self.op("dve", "scalar_tensor_tensor", out=self.GS[w][:, l, :, :], in0=self.MOD[:, l, base:base + 8, :],
                    scalar=1.0, in1=self.G32[w][:, l, :].unsqueeze(2).to_broadcast([128, KC, 2]),
                    op0=ALU.add, op1=ALU.mult,
                    reads=[("MOD", l), R(self.G32[w])], writes=[("GS", w, l)])

    def adaln_unit_load(self, l, j, wbuf, i):
        src3 = self.wada_d[l].rearrange("(kc p) n -> p kc n", p=128)
        self.dma(wbuf[:, :, 384:512], src3[:, :, j * 128:(j + 1) * 128], writes=[("WSTx", i)], key="wax%d" % i, eng="pool")

    def adaln_unit_compute(self, l, j, wbuf, i):
        pst = self.gp()
        for kc in range(KC):
            self.mm(pst[:, 0:2], wbuf[:, kc, 384:512], self.siluT[:, kc, :], start=(kc == 0), stop=(kc == KC - 1),
                    reads=[("WSTx", i), R(self.siluT)], writes=[R(pst)])
        self.op("dve", "tensor_scalar", out=self.MOD[:, l, j, :], in0=pst[:, 0:2], scalar1=self.P["bada"][:, l, j:j + 1],
                scalar2=None, op0=ALU.add, reads=[R(pst), R(self.P["bada"])], writes=[("MOD", l)])

    def adaln_all(self):
        P = self.P
        for l in range(1):
            src3 = self.wada_d[l].rearrange("(kc p) n -> p kc n", p=128)
            for piece in range(12):
                buf, res = self.wst_load(src3, [(piece * 512, 0, 512)])
                pst = self.gp()
                for jc in range(4):
                    for kc in range(KC):
                        self.mm(pst[:, jc * 2:jc * 2 + 2], buf[:, kc, jc * 128:(jc + 1) * 128], self.siluT[:, kc, :],
                                start=(kc == 0), stop=(kc == KC - 1), reads=[res, R(self.siluT)], writes=[R(pst)])
                j0 = piece * 4
                self.op("dve", "tensor_tensor", out=self.MOD[:, l, j0:j0 + 4, :],
                        in0=pst[:, 0:8].rearrange("p (j s) -> p j s", s=2),
                        in1=P["bada"][:, l, j0:j0 + 4].unsqueeze(2).to_broadcast([128, 4, 2]), op=ALU.add,
                        reads=[R(pst), R(P["bada"])], writes=[("MOD", l)])
            for w, base in ((0, 8), (1, 32)):
                self.op("dve", "scalar_tensor_tensor", out=self.GS[w][:, l, :, :], in0=self.MOD[:, l, base:base + 8, :],
                        scalar=1.0, in1=self.G32[w][:, l, :].unsqueeze(2).to_broadcast([128, KC, 2]),
                        op0=ALU.add, op1=ALU.mult,
                        reads=[("MOD", l), R(self.G32[w])], writes=[("GS", w, l)])

    def sumsq_rstd(self, gi, acc):
        t0, n = TGS[gi]
        XT, onesb = self.XT, self.C["ones_b"]
        for kc in range(KC):
            sq = self.sqb[kc % 2]
            self.op("act", "activation", out=sq[:, :n], in_=XT[:, kc, t0:t0 + n], func=AF.Square,
                    reads=[("XT", kc, gi)], writes=[R(sq)])
            self.mm(acc[:, :n], onesb[:], sq[:, :n], start=(kc == 0), stop=(kc == KC - 1),
                    reads=[R(sq), R(onesb)], writes=[R(acc)])
        self.op("act", "activation", out=self.rstd0[:, :n], in_=acc[:, :n], func=AF.Ln, bias=self.epsb[:, 0:1], scale=1.0,
                reads=[R(acc), R(self.epsb)], writes=[R(self.rstd0)])
        self.op("act", "activation", out=self.rstd[:, :n], in_=self.rstd0[:, :n], func=AF.Exp, scale=-0.5,
                reads=[R(self.rstd0)], writes=[R(self.rstd)])

    def norm(self, l, w, skip_ctx):
        shbase = 0 if w == 0 else 24
        for gi, (t0, n) in enumerate(TGS):
            if gi == 4 and skip_ctx:
                continue
            s = 0 if gi < 4 else 1
            acc = self.gp()
            self.sumsq_rstd(gi, acc)
            for kc in range(KC):
                tm = self.tmpf[kc % 2]
                self.op("dve", "scalar_tensor_tensor", out=tm[:, :n], in0=self.XT[:, kc, t0:t0 + n],
                        scalar=self.GS[w][:, l, kc, s:s + 1], in1=self.rstd[:, :n], op0=ALU.mult, op1=ALU.mult,
                        reads=[("XT", kc, gi), ("GS", w, l), R(self.rstd)], writes=[R(tm)])
                if kc % 2 == 0:
                    self.op("act", "activation", out=self.HT[:, kc, t0:t0 + n], in_=tm[:, :n], func=AF.Identity,
                            bias=self.MOD[:, l, shbase + kc, s:s + 1], scale=1.0,
                            reads=[R(tm), ("MOD", l)], writes=[("HT", kc, gi)])
                else:
                    self.op("dve", "tensor_scalar", out=self.HT[:, kc, t0:t0 + n], in0=tm[:, :n],
                            scalar1=self.MOD[:, l, shbase + kc, s:s + 1], scalar2=None, op0=ALU.add,
                            reads=[R(tm), ("MOD", l)], writes=[("HT", kc, gi)])

    def proj_fm(self, wbuf, wres, c0, dst, dname, rope, with_ctx, dst_fn=None):
        if dst_fn is None:
            dst_fn = lambda t0, n: [(dst[:, t0:t0 + n], slice(0, 128))]
        pending = None
        for gi, (t0, n) in enumerate(TGS):
            if gi == 4 and not with_ctx:
                continue
            ps = self.gp()
            for kc in range(KC):
                self.mm(ps[:, :n], wbuf[:, kc, c0:c0 + 128], self.HT[:, kc, t0:t0 + n], start=(kc == 0), stop=(kc == KC - 1),
                        reads=[wres, ("HT", kc, gi)], writes=[R(ps)])
            if rope and gi < 4:
                tail = self.rope_head(ps, n)
                if pending is not None:
                    self.rope_tail(*pending)
                pending = (ps, dst_fn(t0, n), t0, n, (dname, gi)) + tail
            else:
                for (oap, psl) in dst_fn(t0, n):
                    self.op("act", "activation", out=oap, in_=ps[psl, :n], func=AF.Copy,
                            reads=[R(ps)], writes=[(dname, gi)])
        if pending is not None:
            self.rope_tail(*pending)

    def rope_head(self, ps, n):
        i = self.rope_i % 2
        self.rope_i += 1
        qb = self.ropeq[i]
        self.op("act", "activation", out=qb[:, :n], in_=ps[:, :n], func=AF.Copy, reads=[R(ps)], writes=[R(qb)])
        return (i, qb)

    def rope_tail(self, ps, dsts, t0, n, dres, i, qb):
        t1, t2 = self.tmpf[2 * i], self.tmpf[2 * i + 1]
        permb, cosT, sinT = self.C["perm_b"], self.C["cosT"], self.C["sinT"]
        ps2 = self.gp()
        self.mm(ps2[:, :n], permb[:], qb[:, :n], reads=[R(qb), R(permb)], writes=[R(ps2)])
        self.op("dve", "tensor_tensor", out=t1[:, :n], in0=ps[:, :n], in1=cosT[:, t0:t0 + n], op=ALU.mult,
                reads=[R(ps), R(cosT)], writes=[R(t1)])
        self.op("dve", "tensor_tensor", out=t2[:, :n], in0=ps2[:, :n], in1=sinT[:, t0:t0 + n], op=ALU.mult,
                reads=[R(ps2), R(sinT)], writes=[R(t2)])
        for (oap, psl) in dsts:
            self.op("pool", "tensor_tensor", out=oap, in0=t1[psl, :n], in1=t2[psl, :n], op=ALU.add,
                    reads=[R(t1), R(t2)], writes=[dres])

    def proj_tm(self, wbuf, wres, c0, dst_fn, dname, func=None):
        for tb0 in range(0, NTB, 4):
            nb = min(4, NTB - tb0)
            gi = min(tb0 // 4, 4)
            ps = self.gp()
            for j in range(nb):
                tb = tb0 + j
                for kc in range(KC):
                    self.mm(ps[:, j * 128:(j + 1) * 128], self.HT[:, kc, tb * 128:(tb + 1) * 128], wbuf[:, kc, c0:c0 + 128],
                            start=(kc == 0), stop=(kc == KC - 1), reads=[wres, ("HT", kc, gi)], writes=[R(ps)])
            out, in_ = dst_fn(tb0, nb, ps)
            self.op("act", "activation", out=out, in_=in_, func=(func or AF.Copy), reads=[R(ps)], writes=[(dname, gi)])

    def finish_chunk(self, l, ci, OC3, ocname, OCT, octname, need_ctx):
        self.drain()
        i = self.wo_i % 2
        self.wo_i += 1
        WO = self.WO[i]
        wres = ("WO", i)
        self.dma(WO[:], self.wout_d[l][ci * 128:(ci + 1) * 128, :], writes=[wres], key="wo%d" % i, eng="pool")
        identb = self.C["ident_b"]
        ntb = NTB if need_ctx else 16
        for tb0 in range(0, ntb, 4):
            nb = min(4, ntb - tb0)
            gi = min(tb0 // 4, 4)
            pk = self.gp()
            pkb = pk[:].bitcast(BF16)
            for j in range(nb):
                self.tr(pkb[:, j * 128:(j + 1) * 128], OC3[:, tb0 + j, :], identb[:],
                        reads=[(ocname, gi), R(identb)], writes=[R(pk)])
            self.op("dve", "tensor_copy", out=OCT[:, tb0 * 128:(tb0 + nb) * 128], in_=pkb[:, 0:nb * 128],
                    reads=[R(pk)], writes=[(octname, gi)])
        for gi, (t0, n) in enumerate(TGS):
            if gi == 4 and not need_ctx:
                continue
            s = 0 if gi < 4 else 1
            for dc in range(KC):
                def item(use_gp, gi=gi, t0=t0, n=n, s=s, dc=dc):
                    if use_gp:
                        ps = self.gp()
                        pres = R(ps)
                    else:
                        ps, pres = self.ab()
                    self.mm(ps[:, :n], WO[:, dc * 128:(dc + 1) * 128], OCT[:, t0:t0 + n],
                            reads=[wres, (octname, gi)], writes=[pres])
                    self.op("dve", "scalar_tensor_tensor", out=self.XT[:, dc, t0:t0 + n], in0=ps[:, :n],
                            scalar=self.MOD[:, l, 16 + dc, s:s + 1], in1=self.XT[:, dc, t0:t0 + n], op0=ALU.mult, op1=ALU.add,
                            reads=[pres, ("MOD", l), ("XT", dc, gi)], writes=[("XT", dc, gi)])
                self.deferred.append(item)
        self.drain(use_gp=True)

    def drain(self, k=None, use_gp=False):
        while self.deferred and (k is None or k > 0):
            self.deferred.pop(0)(use_gp)
            if k is not None:
                k -= 1

    def chunk_A(self, l, h, need_ctx):
        src3 = self.win_d[l].rearrange("(kc p) n -> p kc n", p=128)
        wbuf, wres = self.wst_load(src3, [(h * 128, 0, 128), (512 + h * 128, 128, 128), (1024 + h * 128, 256, 128)])
        par = h % 2
        QT, KT, V = self.gb(par * 3), self.gb(par * 3 + 1), self.gb(par * 3 + 2)
        qn, kn, vn = ("GB", par * 3), ("GB", par * 3 + 1), ("GB", par * 3 + 2)
        OC, OCT = self.gb(6), self.gb(7)
        V3 = V.rearrange("p (t c) -> p t c", c=130)
        OC3 = OC[:, 0:NT].rearrange("p (t c) -> p t c", c=128)
        allg = list(range(5))
        self.op("pool", "memset", ap=V3[:, :, 128:129], constant=1.0, writes=[(vn, g) for g in allg])
        stage = self.cfg.get("a_stage", 9)
        if stage < 0:
            return
        self.proj_fm(wbuf, wres, 0, QT, qn, stage >= 0.5, need_ctx)
        if stage < 0.7:
            return
        self.proj_fm(wbuf, wres, 128, KT, kn, True, True)
        if stage < 0.8:
            return
        self.proj_tm(wbuf, wres, 256,
                     lambda tb0, nb, ps: (V3[:, tb0:tb0 + nb, 0:128], ps[:, 0:nb * 128].rearrange("p (j c) -> p j c", c=128)),
                     vn)
        if stage < 2:
            return
        itc = 0
        wi = wres[1]
        ada_l = l + 1 if (l + 1 < self.depth and self.cfg.get("ada_il", True)) else None
        ada_units = list(range(h * 12, h * 12 + 12)) if ada_l is not None else []
        ada_loaded = None
        for gi, (t0, n) in enumerate(TGS):
            if gi == 4 and not need_ctx:
                continue
            kbs = list(range(NTB)) if gi < 4 else [16, 17]
            nqb = n // 128
            nbk = nqb // 2
            aress = [[("oacc", c, 0), ("oacc", c, 1)] for c in range(2)]

            def emit_pv(ki, kb, pts):
                kgi = min(kb // 4, 4)
                for c in range(2):
                    acc = self.OACC[c]
                    for qb in range(nqb):
                        off = (qb // 2) * 512 + (qb % 2) * 129
                        self.mm(acc[:, off:off + 129], pts[c][:, qb * 128:(qb + 1) * 128], V3[:, kb, 0:129],
                                start=(ki == 0 and qb % 2 == 0), stop=(ki == len(kbs) - 1),
                                reads=[R(pts[c]), (vn, kgi)], writes=[aress[c][qb // 2]], sgc=True)

            pending = None
            for ki, kb in enumerate(kbs):
                kgi = min(kb // 4, 4)
                sts = [self.gp(), self.gp()]
                for c in range(2):
                    hs = slice(c * 64, (c + 1) * 64)
                    self.mm(sts[c][:, :n], KT[hs, kb * 128:(kb + 1) * 128], QT[hs, t0:t0 + n],
                            reads=[(kn, kgi), (qn, gi)], writes=[R(sts[c])])
                pts = [self.ptb(), self.ptb()]
                for c in range(2):
                    self.op("act", "activation", out=pts[c][:, :n], in_=sts[c][:, :n], func=AF.Exp, scale=0.125,
                            reads=[R(sts[c])], writes=[R(pts[c])])
                if pending is not None:
                    emit_pv(*pending)
                pending = (ki, kb, pts)
                if ada_l is not None and gi < 4:
                    if itc % 6 == 5 and ada_loaded is not None:
                        self.adaln_unit_compute(ada_l, ada_loaded, wbuf, wi)
                        ada_loaded = None
                    if itc % 6 == 0 and ada_units:
                        ada_loaded = ada_units.pop(0)
                        self.adaln_unit_load(ada_l, ada_loaded, wbuf, wi)
                    itc += 1
            emit_pv(*pending)
            if stage < 3:
                continue
            for c in range(2):
                acc = self.OACC[c]
                ares = aress[c]
                accv = acc[:].rearrange("p (b x) -> p b x", b=2)[:, 0:nbk, 0:258].rearrange("p b (j c) -> p b j c", c=129)
                zv = accv[:, :, :, 128]
                ov = accv[:, :, :, 0:128]
                sm = self.small
                rz = sm[:, 0:nqb].rearrange("p (b j) -> p b j", j=2)
                ar = ares[0:nbk]
                self.op("dve", "reciprocal", out=rz, in_=zv, reads=ar, writes=[R(sm)])
                o0 = self.O0N[:, 0:nqb, :].rearrange("p (b j) c -> p b j c", j=2)
                oa = self.OA[:, 0:nqb, :].rearrange("p (b j) c -> p b j c", j=2)
                if c == 0:
                    self.op("dve", "tensor_tensor", out=o0, in0=ov, in1=rz.unsqueeze(3).to_broadcast([128, nbk, 2, 128]),
                            op=ALU.mult, reads=ar + [R(sm)], writes=[R(self.O0N)])
                else:
                    rz1 = sm[:, 4:4 + nqb].rearrange("p (b j) -> p b j", j=2)
                    self.op("dve", "tensor_scalar", out=rz1, in0=rz, scalar1=self.NEGLAM[:, l:l + 1], scalar2=None,
                            op0=ALU.mult, reads=[R(sm), R(self.NEGLAM)], writes=[R(sm)])
                    self.op("dve", "tensor_tensor", out=oa, in0=ov, in1=rz1.unsqueeze(3).to_broadcast([128, nbk, 2, 128]),
                            op=ALU.mult, reads=ar + [R(sm)], writes=[R(self.OA)])
                    oaf = self.OA[:, 0:nqb, :]
                    obf = self.OB[:, 0:nqb, :]
                    self.op("pool", "tensor_tensor", out=oaf, in0=oaf, in1=self.O0N[:, 0:nqb, :], op=ALU.add,
                            reads=[R(self.OA), R(self.O0N)], writes=[R(self.OA)])
                    self.op("pool", "tensor_tensor", out=obf, in0=oaf, in1=oaf, op=ALU.mult,
                            reads=[R(self.OA)], writes=[R(self.OB)])
                    ss = sm[:, 8:8 + nqb]
                    self.op("dve", "tensor_reduce", out=ss, in_=obf, axis=AX.X, op=ALU.add,
                            reads=[R(self.OB)], writes=[R(sm)])
                    l1 = sm[:, 12:12 + nqb]
                    rs = sm[:, 16:16 + nqb]
                    self.op("act", "activation", out=l1, in_=ss, func=AF.Ln, bias=self.epsb[:, 1:2], scale=1.0 / 128.0,
                            reads=[R(sm), R(self.epsb)], writes=[R(sm)])
                    self.op("act", "activation", out=rs, in_=l1, func=AF.Exp, scale=-0.5,
                            reads=[R(sm)], writes=[R(sm)])
                    self.op("dve", "tensor_tensor", out=obf, in0=oaf, in1=rs.unsqueeze(2).to_broadcast([128, nqb, 128]),
                            op=ALU.mult, reads=[R(self.OA), R(sm)], writes=[R(self.OB)])
                    tbq = t0 // 128
                    self.op("pool", "tensor_tensor", out=OC3[:, tbq:tbq + nqb, :], in0=obf,
                            in1=self.GA[:, l, :].unsqueeze(1).to_broadcast([128, nqb, 128]), op=ALU.mult,
                            reads=[R(self.OB), R(self.GA)], writes=[(("GB", 6), gi)])
        if ada_l is not None:
            assert ada_loaded is None and not ada_units
            if h == 3:
                self.adaln_gs(ada_l)
        if stage < 4:
            return
        self.finish_chunk(l, h, OC3, ("GB", 6), OCT, ("GB", 7), need_ctx)

    def ret_tables(self, l, pp):
        C = self.C
        if True:
            lgf = self.LGC[:, l, 0, pp:pp + 1]
            lgb = self.LGC[:, l, 1, pp:pp + 1]
            self.op("act", "activation", out=self.TQF[pp][:], in_=C["i1"][:], func=AF.Exp, scale=lgf,
                    reads=[R(C["i1"]), R(self.LGC)], writes=[R(self.TQF[pp])])
            self.op("act", "activation", out=self.TQB[pp][:], in_=C["i2"][:], func=AF.Exp, scale=lgb,
                    reads=[R(C["i2"]), R(self.LGC)], writes=[R(self.TQB[pp])])
            self.op("act", "activation", out=self.CFB[pp][:, 0:1], in_=lgf, func=AF.Exp, scale=128.0,
                    reads=[R(self.LGC)], writes=[R(self.CFB[pp])])
            self.op("act", "activation", out=self.CFB[pp][:, 1:2], in_=lgb, func=AF.Exp, scale=128.0,
                    reads=[R(self.LGC)], writes=[R(self.CFB[pp])])
            for h2 in range(2):
                h = 2 * pp + h2
                e1 = self.tmpf[0][:, 0:128]
                e2 = self.tmpf[1][:, 0:128]
                self.op("dve", "tensor_scalar", out=e1, in0=C["r1"][:], scalar1=self.LG[:, l, h:h + 1], scalar2=None,
                        op0=ALU.mult, reads=[R(C["r1"]), R(self.LG)], writes=[R(self.tmpf[0])])
                self.op("dve", "scalar_tensor_tensor", out=e2, in0=C["r2"][:], scalar=self.LG[:, l, 4 + h:5 + h], in1=e1,
                        op0=ALU.mult, op1=ALU.add, reads=[R(C["r2"]), R(self.LG), R(self.tmpf[0])], writes=[R(self.tmpf[1])])
                self.op("act", "activation", out=self.DT[pp][:, h2 * 128:(h2 + 1) * 128], in_=e2, func=AF.Exp,
                        reads=[R(self.tmpf[1])], writes=[R(self.DT[pp])])
            sm = self.small
            self.op("act", "activation", out=sm[:, 20:22], in_=self.LG[:, l, 2 * pp:2 * pp + 2], func=AF.Exp,
                    scale=C["kcol"][:, 0:1], reads=[R(self.LG), R(C["kcol"])], writes=[R(sm)])
            self.op("act", "activation", out=sm[:, 22:24], in_=self.LG[:, l, 4 + 2 * pp:6 + 2 * pp], func=AF.Exp,
                    scale=C["kcol"][:, 1:2], reads=[R(self.LG), R(C["kcol"])], writes=[R(sm)])
            self.op("dve", "tensor_scalar", out=self.TK[pp][:], in0=sm[:, 20:24], scalar1=0.125, scalar2=None, op0=ALU.mult,
                    reads=[R(sm)], writes=[R(self.TK[pp])])

    def chunk_B(self, l, pp, need_ctx):
        src3 = self.win_d[l].rearrange("(kc p) n -> p kc n", p=128)
        wbuf, wres = self.wst_load(src3, [(1536 + pp * 128, 0, 128), (1792 + pp * 128, 128, 128),
                                          (2048 + pp * 128, 256, 128), (2304 + pp * 128, 384, 128)])
        names = [("GB", i) for i in range(8)]
        QT, KT, V, G, SF, SB, OC, OCT = [self.gb(i) for i in range(8)]
        qn, kn, vn, gn, sfn, sbn, ocn, octn = names
        V3 = V.rearrange("p (t c) -> p t c", c=130)
        G3 = G[:, 0:NT].rearrange("p (t c) -> p t c", c=128)
        SF3 = SF[:, 0:NT].rearrange("p (t c) -> p t c", c=128)
        SB3 = SB[:, 0:NT].rearrange("p (t c) -> p t c", c=128)
        OC3 = OC[:, 0:NT].rearrange("p (t c) -> p t c", c=128)
        identb = self.C["ident_b"]
        self.ret_tables(l, pp)
        self.proj_fm(wbuf, wres, 0, QT, qn, True, need_ctx)
        self.proj_fm(wbuf, wres, 128, KT, kn, True, True)
        self.proj_tm(wbuf, wres, 256,
                     lambda tb0, nb, ps: (V3[:, tb0:tb0 + nb, 0:128], ps[:, 0:nb * 128].rearrange("p (j c) -> p j c", c=128)),
                     vn)
        self.proj_tm(wbuf, wres, 384,
                     lambda tb0, nb, ps: (G3[:, tb0:tb0 + nb, :], ps[:, 0:nb * 128].rearrange("p (j c) -> p j c", c=128)),
                     gn, func=AF.Silu)
        orders = [[16, 17] + list(range(16)), [17, 16] + list(range(15, -1, -1))]
        sts_ = [self.stf, self.stb]
        ST3s = [SF3, SB3]
        stns = [sfn, sbn]
        for d in range(2):
            self.op("pool", "memset", ap=sts_[d][:], constant=0.0, writes=[R(sts_[d])])
        NS = len(orders[0])
        pks = {}
        pus = {}
        for i in range(NS + 3):
            for d in range(2):
                if i < NS:
                    tb = orders[d][i]
                    gi = min(tb // 4, 4)
                    pk = self.gp()
                    pkb = pk[:].bitcast(BF16)
                    self.tr(pkb[:, 0:128], KT[:, tb * 128:(tb + 1) * 128], identb[:], reads=[(kn, gi), R(identb)], writes=[R(pk)])
                    pks[(d, i)] = (pk, pkb)
                if 0 <= i - 1 < NS:
                    pk, pkb = pks.pop((d, i - 1))
                    kf = self.kfb[2 * d + (i - 1) % 2]
                    self.op("dve", "tensor_tensor", out=kf[:].rearrange("p (h c) -> p h c", h=2),
                            in0=pkb[:, 0:128].rearrange("p (h c) -> p h c", h=2),
                            in1=self.TK[pp][:, 2 * d:2 * d + 2].unsqueeze(2).to_broadcast([128, 2, 64]), op=ALU.mult,
                            reads=[R(pk), R(self.TK[pp])], writes=[R(kf)])
                if 0 <= i - 2 < NS:
                    tb = orders[d][i - 2]
                    gi = min(tb // 4, 4)
                    kf = self.kfb[2 * d + (i - 2) % 2]
                    pu, pures = self.ab()
                    self.mm(pu[:, 0:128], kf[:], V3[:, tb, 0:128], reads=[R(kf), (vn, gi)], writes=[pures])
                    pus[(d, i - 2)] = (pu, pures)
                if 0 <= i - 3 < NS:
                    tb = orders[d][i - 3]
                    gi = min(tb // 4, 4)
                    st = sts_[d]
                    pu, pures = pus.pop((d, i - 3))
                    self.op("pool", "tensor_copy", out=ST3s[d][:, tb, :], in_=st[:], reads=[R(st)], writes=[(stns[d], gi)])
                    self.op("dve", "scalar_tensor_tensor", out=st[:], in0=st[:], scalar=self.CFB[pp][:, d:d + 1], in1=pu[:, 0:128],
                            op0=ALU.mult, op1=ALU.add, reads=[R(st), R(self.CFB[pp]), pures], writes=[R(st)])
        ntb = NTB if need_ctx else 16
        sm = self.small
        cur = {}

        def front(tb):
            gi = min(tb // 4, 4)
            ts_ = slice(tb * 128, (tb + 1) * 128)
            sts = [self.gp(), self.gp()]
            for h2 in range(2):
                hs = slice(h2 * 64, (h2 + 1) * 64)
                self.mm(sts[h2][:, 0:128], KT[hs, ts_], QT[hs, ts_], reads=[(kn, gi), (qn, gi)], writes=[R(sts[h2])])
            pt = self.ptb()
            for h2 in range(2):
                self.op("dve", "scalar_tensor_tensor", out=pt[:, h2 * 128:(h2 + 1) * 128], in0=sts[h2][:, 0:128], scalar=0.125,
                        in1=self.DT[pp][:, h2 * 128:(h2 + 1) * 128],
                        op0=ALU.mult, op1=ALU.mult, reads=[R(sts[h2]), R(self.DT[pp])], writes=[R(pt)])
            qpool = self.qfb + self.kfb
            qf = qpool[(2 * tb) % 8]
            qb_ = qpool[(2 * tb + 1) % 8]
            self.op("pool", "tensor_tensor", out=qf[:], in0=QT[:, ts_], in1=self.TQF[pp][:], op=ALU.mult,
                    reads=[(qn, gi), R(self.TQF[pp])], writes=[R(qf)])
            self.op("pool", "tensor_tensor", out=qb_[:], in0=QT[:, ts_], in1=self.TQB[pp][:], op=ALU.mult,
                    reads=[(qn, gi), R(self.TQB[pp])], writes=[R(qb_)])
            return (pt, qf, qb_)

        def back(tb, pt, qf, qb_):
            gi = min(tb // 4, 4)
            jj = tb % 4
            if jj == 0:
                cur["po"], cur["pres"] = self.ab()
            po, pres = cur["po"], cur["pres"]
            for h2 in range(2):
                hs = slice(h2 * 64, (h2 + 1) * 64)
                oreg = po[:, jj * 128 + h2 * 64:jj * 128 + (h2 + 1) * 64]
                self.mm(oreg, qf[hs, :], SF3[hs, tb, hs], start=(jj == 0 and h2 == 0), stop=False,
                        reads=[R(qf), (sfn, gi)], writes=[pres], sgc=True)
                self.mm(oreg, qb_[hs, :], SB3[hs, tb, hs], start=False, stop=False,
                        reads=[R(qb_), (sbn, gi)], writes=[pres], sgc=True)
                self.mm(oreg, pt[:, h2 * 128:(h2 + 1) * 128], V3[:, tb, hs], start=False, stop=True,
                        reads=[R(pt), (vn, gi)], writes=[pres], sgc=True)
            last = (jj == 3) or (tb == ntb - 1)
            if not last:
                return
            tb0 = tb - jj
            nb = jj + 1
            ng = 2 * nb
            W = nb * 128
            v3 = lambda a: a[:, 0:W].rearrange("p (g c) -> p g c", c=64)
            osb = self.OA[:].rearrange("p a b -> p (a b)")
            ocn_ = self.OB[:].rearrange("p a b -> p (a b)")
            sqv = self.O0N[:].rearrange("p a b -> p (a b)")
            self.op("dve", "tensor_copy", out=osb[:, 0:W], in_=po[:, 0:W], reads=[pres], writes=[R(self.OA)])
            self.op("dve", "tensor_reduce", out=sm[:, 24:24 + ng], in_=v3(osb), axis=AX.X, op=ALU.add,
                    reads=[R(self.OA)], writes=[R(sm)])
            self.op("dve", "tensor_scalar", out=sm[:, 24:24 + ng], in0=sm[:, 24:24 + ng], scalar1=1.0 / 64.0, scalar2=None, op0=ALU.mult,
                    reads=[R(sm)], writes=[R(sm)])
            self.op("dve", "tensor_tensor", out=v3(ocn_), in0=v3(osb), in1=sm[:, 24:24 + ng].unsqueeze(2).to_broadcast([128, ng, 64]),
                    op=ALU.subtract, reads=[R(self.OA), R(sm)], writes=[R(self.OB)])
            self.op("pool", "tensor_tensor", out=sqv[:, 0:W], in0=ocn_[:, 0:W], in1=ocn_[:, 0:W], op=ALU.mult,
                    reads=[R(self.OB)], writes=[R(self.O0N)])
            self.op("dve", "tensor_reduce", out=sm[:, 44:44 + ng], in_=v3(sqv), axis=AX.X, op=ALU.add,
                    reads=[R(self.O0N)], writes=[R(sm)])
            self.op("act", "activation", out=sm[:, 52:52 + ng], in_=sm[:, 44:44 + ng], func=AF.Ln, bias=self.epsb[:, 1:2], scale=1.0 / 64.0,
                    reads=[R(sm), R(self.epsb)], writes=[R(sm)])
            self.op("act", "activation", out=sm[:, 24:24 + ng], in_=sm[:, 52:52 + ng], func=AF.Exp, scale=-0.5,
                    reads=[R(sm)], writes=[R(sm)])
            self.op("dve", "tensor_tensor", out=v3(osb), in0=v3(ocn_), in1=sm[:, 24:24 + ng].unsqueeze(2).to_broadcast([128, ng, 64]),
                    op=ALU.mult, reads=[R(self.OB), R(sm)], writes=[R(self.OA)])
            self.op("pool", "tensor_tensor", out=OC3[:, tb0:tb0 + nb, :], in0=osb[:, 0:W].rearrange("p (t c) -> p t c", c=128),
                    in1=G3[:, tb0:tb0 + nb, :], op=ALU.mult,
                    reads=[R(self.OA), (gn, gi)], writes=[(ocn, gi)])

        pend = []
        for tb in range(ntb):
            fr = front(tb)
            pend.append((tb,) + fr)
            if len(pend) > 3:
                back(*pend.pop(0))
        while pend:
            back(*pend.pop(0))
        self.finish_chunk(l, 4 + pp, OC3, ocn, OCT, octn, need_ctx)

    def chunk_C(self, l, need_ctx):
        src3 = self.win_d[l].rearrange("(kc p) n -> p kc n", p=128)
        wbuf, wres = self.wst_load(src3, [(2560, 0, 256), (2816, 256, 64), (2816, 320, 64), (2880, 384, 64), (2880, 448, 64)])
        wbufv, wresv = self.wst_load(src3, [(2944, 0, 128)])
        QZ = self.GBT[:, 0:2 * NT].rearrange("p (g t) -> p g t", g=2)
        qzn = "QZ"
        gb01 = [(("GB", i), g_) for i in range(2) for g_ in range(5)]
        KT, V = self.gb(2), self.gb(3)
        kn, vn = ("GB", 2), ("GB", 3)
        OC, OCT = self.gb(6), self.gb(7)
        ocn, octn = ("GB", 6), ("GB", 7)
        V4 = V.rearrange("p (t j c) -> p t j c", j=2, c=65)
        OC3 = OC[:, 0:NT].rearrange("p (t c) -> p t c", c=128)
        allg = list(range(5))
        self.op("pool", "memset", ap=V4[:, :, :, 64:65], constant=1.0, writes=[(vn, g) for g in allg])
        self.op("pool", "memset", ap=QZ[64:128, 0, :], constant=0.0, writes=gb01 + [(qzn, g) for g in allg])
        self.op("pool", "memset", ap=QZ[0:64, 1, :], constant=0.0, writes=gb01 + [(qzn, g) for g in allg])
        self.proj_tm(wbufv, wresv, 0,
                     lambda tb0, nb, ps: (V4[:, tb0:tb0 + nb, :, 0:64],
                                          ps[:, 0:nb * 128].rearrange("p (t j c) -> p t j c", j=2, c=64)),
                     vn)
        ntb = NTB if need_ctx else 16
        sm = self.small
        qz_fn = lambda t0, n: [(QZ[0:64, 0, t0:t0 + n], slice(0, 64)), (QZ[64:128, 1, t0:t0 + n], slice(64, 128))]
        for j in range(2):
            self.proj_fm(wbuf, wres, j * 128, None, qzn, True, need_ctx, dst_fn=qz_fn)
            self.proj_fm(wbuf, wres, 256 + j * 128, KT, kn, True, True)
            items = []
            for n_ in range(ntb):
                if n_ < 16:
                    kbs = ([n_ - 1] if n_ > 0 else []) + [n_] + ([n_ + 1] if n_ < 15 else []) + [16, 17]
                else:
                    kbs = [16, 17]
                for ki, m in enumerate(kbs):
                    items.append((n_, ki, m, len(kbs)))
            cur = {}

            def front(n_, ki, m, nk):
                gi = min(n_ // 4, 4)
                mgi = min(m // 4, 4)
                st = self.gp()
                self.mm(st[:, 0:256], KT[:, m * 128:(m + 1) * 128], QZ[:, :, n_ * 128:(n_ + 1) * 128],
                        reads=[(kn, mgi), (qzn, gi)], writes=[R(st)])
                pt = self.ptb()
                self.op("act", "activation", out=pt[:, 0:256], in_=st[:, 0:256], func=AF.Exp, scale=0.125,
                        reads=[R(st)], writes=[R(pt)])
                if n_ < 16 and m == n_ - 1:
                    self.op("pool", "tensor_tensor", out=pt[:, 0:256], in0=pt[:, 0:256], in1=self.C["maskp"][:], op=ALU.mult,
                            reads=[R(pt), R(self.C["maskp"])], writes=[R(pt)])
                if n_ < 15 and m == n_ + 1:
                    self.op("pool", "tensor_tensor", out=pt[:, 0:256], in0=pt[:, 0:256], in1=self.C["maskn"][:], op=ALU.mult,
                            reads=[R(pt), R(self.C["maskn"])], writes=[R(pt)])
                return pt

            def back(n_, ki, m, nk, pt):
                mgi = min(m // 4, 4)
                r3 = n_ % 3
                if r3 == 0 and ki == 0:
                    cur["po"], cur["pres"] = self.ab()
                    cur["n0"] = n_
                po, pres = cur["po"], cur["pres"]
                for g in range(2):
                    off = r3 * 130 + g * 65
                    self.mm(po[:, off:off + 65], pt[:, g * 128:(g + 1) * 128], V4[:, m, j, 0:65],
                            start=(r3 == 0 and ki == 0 and g == 0), stop=(ki == nk - 1), reads=[R(pt), (vn, mgi)], writes=[pres],
                            sgc=True)
                if ki == nk - 1 and (r3 == 2 or n_ == ntb - 1):
                    n0 = cur["n0"]
                    cnt = n_ - n0 + 1
                    pov = po[:, 0:cnt * 130].rearrange("p (n g c) -> p n g c", g=2, c=65)
                    den = sm[:, 34:34 + 2 * cnt].rearrange("p (n g) -> p n g", g=2)
                    rz = sm[:, 40:40 + 2 * cnt].rearrange("p (n g) -> p n g", g=2)
                    self.op("dve", "tensor_tensor", out=den, in0=pov[:, :, :, 64],
                            in1=self.ESINK[:, l, 2 * j:2 * j + 2].unsqueeze(1).to_broadcast([128, cnt, 2]), op=ALU.add,
                            reads=[pres, R(self.ESINK)], writes=[R(sm)])
                    self.op("dve", "reciprocal", out=rz, in_=den, reads=[R(sm)], writes=[R(sm)])
                    gis = sorted(set(min(q // 4, 4) for q in range(n0, n_ + 1)))
                    self.op("dve", "tensor_tensor", out=OC3[:, n0:n_ + 1, :].rearrange("p n (g c) -> p n g c", g=2),
                            in0=pov[:, :, :, 0:64], in1=rz.unsqueeze(3).to_broadcast([128, cnt, 2, 64]), op=ALU.mult,
                            reads=[pres, R(sm)], writes=[(ocn, g_) for g_ in gis])

            pend = []
            for it in items:
                pt = front(*it)
                pend.append(it + (pt,))
                if len(pend) > 3:
                    back(*pend.pop(0))
            while pend:
                back(*pend.pop(0))
            self.finish_chunk(l, 6 + j, OC3, ocn, OCT, octn, need_ctx)

    def mlp(self, l, need_ctx):
        self.drain()
        self.norm(l, 1, skip_ctx=not need_ctx)
        w1v = self.w1_d[l].rearrange("(kc p) n -> p kc n", p=128)
        w2v = self.w2_d[l].rearrange("(fc p) n -> p fc n", p=128)
        pending = None
        ucount = 0

        def mlp2(W2, w2res, AT, ares, gi, t0, n, s):
            for dc in range(KC):
                ps2, pres = self.ab()
                for fc in range(4):
                    self.mm(ps2[:, :n], W2[:, fc, dc * 128:(dc + 1) * 128], AT[:, fc, :n], start=(fc == 0), stop=(fc == 3),
                            reads=[(w2res[0], 0), (w2res[1], 0), ares], writes=[pres])
                self.op("dve", "scalar_tensor_tensor", out=self.XT[:, dc, t0:t0 + n], in0=ps2[:, :n],
                        scalar=self.MOD[:, l, 40 + dc, s:s + 1], in1=self.XT[:, dc, t0:t0 + n], op0=ALU.mult, op1=ALU.add,
                        reads=[pres, ("MOD", l), ("XT", dc, gi)], writes=[("XT", dc, gi)])

        for fb in range(8):
            W1, w1res = self.wst_load(w1v, [(fb * 512, 0, 512)])
            i2 = fb % 2
            W2 = self.GBT[:, i2 * 2 * GBW:i2 * 2 * GBW + 4096].rearrange("p (f n) -> p f n", f=4)
            w2res = [("GB", 2 * i2), ("GB", 2 * i2 + 1)]
            self.dma(W2, w2v[:, fb * 4:(fb + 1) * 4, :], writes=[(r, g) for r in w2res for g in range(5)],
                     key="w2_%d" % i2, eng="pool")
            for gi, (t0, n) in enumerate(TGS):
                if gi == 4 and not need_ctx:
                    continue
                s = 0 if gi < 4 else 1
                ai = 4 + (ucount % 2)
                ucount += 1
                AT = self.gb(ai)[:, 0:2048].rearrange("p (f n) -> p f n", f=4)
                ares = (("GB", ai), 0)
                aresw = [(("GB", ai), g_) for g_ in range(5)]
                for fc in range(4):
                    ps = self.gp()
                    for kc in range(KC):
                        self.mm(ps[:, :n], W1[:, kc, fc * 128:(fc + 1) * 128], self.HT[:, kc, t0:t0 + n],
                                start=(kc == 0), stop=(kc == KC - 1), reads=[w1res, ("HT", kc, gi)], writes=[R(ps)])
                    rt = self.tmpf[fc % 4]
                    self.op("act", "activation", out=rt[:, :n], in_=ps[:, :n], func=AF.Relu, reads=[R(ps)], writes=[R(rt)])
                    self.op("pool", "tensor_tensor", out=AT[:, fc, :n], in0=rt[:, :n], in1=rt[:, :n], op=ALU.mult,
                            reads=[R(rt)], writes=aresw)
                if pending is not None:
                    mlp2(*pending)
                pending = (W2, w2res, AT, ares, gi, t0, n, s)
        mlp2(*pending)

    def final_norm(self):
        XT, identf = self.XT, self.C["ident_f"]
        gfin = self.P["g_final"]
        YT = self.HT[:].rearrange("p k t -> p (k t)")[:, 0:8192].bitcast(F32).rearrange("p (k t) -> p k t", k=KC)
        ytres = [("HT", kc, g) for kc in range(KC) for g in range(5)]
        for g in range(4):
            t0 = g * 512
            acc = self.gp()
            self.sumsq_rstd(g, acc)
            for kc in range(KC):
                self.op("dve", "scalar_tensor_tensor", out=YT[:, kc, :], in0=XT[:, kc, t0:t0 + 512], scalar=gfin[:, kc:kc + 1],
                        in1=self.rstd[:], op0=ALU.mult, op1=ALU.mult,
                        reads=[("XT", kc, g), R(self.rstd), R(gfin)], writes=(ytres if kc == 0 else []) + [("YT", kc)])
            for j in range(4):
                tb = g * 4 + j
                ob = self.iobuf[tb % 2]
                obres = self.iores[tb % 2]
                for half in range(2):
                    pt = self.gp()
                    for jj in range(4):
                        kc = half * 4 + jj
                        self.tr(pt[:, jj * 128:(jj + 1) * 128], YT[:, kc, j * 128:(j + 1) * 128], identf[:],
                                reads=[("YT", kc), ("HT", 0, 0), R(identf)], writes=[R(pt)])
                    self.op("act", "mul", out=ob[:, half * 512:(half + 1) * 512], in_=pt[:], mul=32.0,
                            reads=[R(pt)], writes=obres)
                self.dma(self.out_d[tb * 128:(tb + 1) * 128, :], ob, reads=obres, key="io%d" % (tb % 2))


_CACHE = {}


def kernel(**inputs):
    cfg = inputs.pop("_cfg", {}) if "_cfg" in inputs else {}
    inputs = {k: np.asarray(v) for k, v in inputs.items()}
    key = tuple(sorted(cfg.items()))
    if key not in _CACHE:
        b = Builder(cfg)
        _CACHE[key] = (b.build(), b)
    nc, b = _CACHE[key]
    in_maps = [prep_inputs(inputs, i, b.depth) for i in range(8)]
    res = run_bass_kernel_spmd(nc, in_maps, core_ids=list(range(8)))
    if cfg.get("_ret_all"):
        return res
    out = np.stack([np.asarray(r["out"]) for r in res.results], axis=0)
    return out.astype(np.float32)
```

```python
import contextlib
import math
import numpy as np
import ml_dtypes
import concourse.bass as bass
import concourse.mybir as mybir
from concourse.bass_utils import run_bass_kernel_spmd

F32 = mybir.dt.float32
BF16 = mybir.dt.bfloat16
ALU = mybir.AluOpType
AF = mybir.ActivationFunctionType
AX = mybir.AxisListType

D = 1024
T = 2048
L = 256
NT = T + L
DEPTH = 4
KC = D // 128
NTB = NT // 128
EPS = 1e-6
TGS = [(0, 512), (512, 512), (1024, 512), (1536, 512), (2048, 256)]


class Op:
    __slots__ = ("eng", "fn", "idx", "deps", "raw", "inc", "count", "dma", "is_dma", "embed")

    def __init__(self, eng, fn, idx):
        self.eng = eng
        self.fn = fn
        self.idx = idx
        self.deps = set()
        self.raw = set()
        self.inc = False
        self.count = 0
        self.dma = None
        self.is_dma = False
        self.embed = False


EMBED_WAITS = True


class Sched:
    ENGS = ["pe", "act", "dve", "pool", "sp"]

    def __init__(self):
        self.ops = {e: [] for e in self.ENGS}
        self.last_w = {}
        self.readers = {}
        self.dma_n = {}

    def add(self, eng, fn, reads=(), writes=(), dma=None):
        op = Op(eng, fn, len(self.ops[eng]))
        xr = [r for r in reads if isinstance(r, tuple) and str(r[0]).startswith(("ps", "oacc"))]
        if xr:
            for r in xr:
                w = self.last_w.get(r)
                if w is not None:
                    op.raw.add(w)
            writes = list(writes) + [r for r in xr if r not in writes]
        for r in reads:
            w = self.last_w.get(r)
            if w is not None:
                op.deps.add(w)
                op.raw.add(w)
        for r in writes:
            w = self.last_w.get(r)
            if w is not None:
                op.deps.add(w)
            rd = self.readers.get(r)
            if rd:
                for o in rd.values():
                    op.deps.add(o)
        for r in reads:
            d = self.readers.setdefault(r, {})
            if dma is not None:
                d[("dma", id(op))] = op
            else:
                d[eng] = op
        for r in writes:
            self.last_w[r] = op
            self.readers[r] = {}
        if dma is not None:
            n = self.dma_n.get(dma, 0) + 1
            self.dma_n[dma] = n
            op.dma = (dma, n)
            op.is_dma = True
        self.ops[eng].append(op)
        return op

    def _needs_wait(self, op, d):
        if d.is_dma:
            return True
        if d.eng != op.eng:
            return True
        if op.is_dma:
            return True
        if op.eng == "pe":
            return False
        return (d in op.raw) and (op.idx - d.idx <= 4)

    def emit(self, nc, stack):
        for e in self.ENGS:
            for op in self.ops[e]:
                for d in op.deps:
                    if not d.is_dma and self._needs_wait(op, d):
                        d.inc = True
        for e in self.ENGS:
            c = 0
            for op in self.ops[e]:
                if op.inc:
                    c += 1
                op.count = c
        sems = {e: stack.enter_context(nc.semaphore("s_" + e)) for e in self.ENGS}
        dsems = {k: stack.enter_context(nc.semaphore("d_%s" % (k,))) for k in self.dma_n}
        block = stack.enter_context(nc.Block())
        stats = {}

        def run(ename, engine):
            known = {}
            nw = 0
            for op in self.ops[ename]:
                need = {}
                for d in op.deps:
                    if not self._needs_wait(op, d):
                        continue
                    if d.is_dma:
                        key = ("d", d.dma[0])
                        val = 16 * d.dma[1]
                    else:
                        key = ("e", d.eng)
                        val = d.count
                    if val > need.get(key, 0):
                        need[key] = val
                todo = []
                for key, val in need.items():
                    if known.get(key, 0) >= val:
                        continue
                    known[key] = val
                    todo.append((dsems[key[1]] if key[0] == "d" else sems[key[1]], val))
                last = todo.pop() if (todo and op.embed and EMBED_WAITS) else None
                for s, val in todo:
                    engine.wait_ge(s, val)
                    nw += 1
                ins = op.fn(engine)
                if last is not None:
                    ins._wait_ge(last[0], last[1])
                if op.is_dma:
                    ins.then_inc(dsems[op.dma[0]], 16)
                elif op.inc:
                    ins.then_inc(sems[ename], 1)
            for op in self.ops[ename]:
                if op.is_dma:
                    key = ("d", op.dma[0])
                    val = 16 * self.dma_n[op.dma[0]]
                    if known.get(key, 0) < val:
                        known[key] = val
                        engine.wait_ge(dsems[op.dma[0]], val)
            stats[ename] = (len(self.ops[ename]), nw)

        block.tensor(lambda e: run("pe", e))
        block.scalar(lambda e: run("act", e))
        block.vector(lambda e: run("dve", e))
        block.gpsimd(lambda e: run("pool", e))
        block.sync(lambda e: run("sp", e))
        self.stats = stats
        return stats


def R(t, *idx):
    return (t.name,) + idx


GBW = 2340
NGB = 8
LAM_INIT = [0.8 - 0.6 * math.exp(-0.3 * l) for l in range(DEPTH)]


def host_consts():
    c = {}
    bf = ml_dtypes.bfloat16
    c["ident_f"] = np.eye(128, dtype=np.float32)
    c["ident_b"] = np.eye(128, dtype=np.float32).astype(bf)
    c["ones_b"] = np.ones((128, 128), dtype=np.float32).astype(bf)
    pm = np.zeros((128, 128), np.float32)
    sign = np.zeros(128, np.float64)
    for f in range(128):
        fl = f % 64
        q = fl // 16
        src = f + 16 if q in (0, 2) else f - 16
        pm[src, f] = 1.0
        sign[f] = -1.0 if q in (0, 2) else 1.0
    c["perm_b"] = pm.astype(bf)
    t = np.arange(T)
    r = (t // 64).astype(np.float64)
    col = (t % 64).astype(np.float64)
    inv = 10000.0 ** (-np.arange(16, dtype=np.float64) / 16)
    ang64 = np.concatenate([r[:, None] * inv, r[:, None] * inv, col[:, None] * inv, col[:, None] * inv], axis=1)
    ang = np.concatenate([ang64, ang64], axis=1).T
    c["cosT"] = np.cos(ang).astype(np.float32).astype(bf)
    c["sinT"] = (np.sin(ang) * sign[:, None]).astype(np.float32).astype(bf)
    il = np.arange(128)
    mp = (il[None, :] <= il[:, None]).astype(np.float32)
    mn = (il[:, None] <= il[None, :]).astype(np.float32)
    c["maskp"] = np.concatenate([mp, mp], axis=1).astype(bf)
    c["maskn"] = np.concatenate([mn, mn], axis=1).astype(bf)
    s_ = il[:, None].astype(np.float32)
    t_ = il[None, :].astype(np.float32)
    c["r1"] = np.maximum(t_ - s_, 0.0).astype(np.float32)
    c["r2"] = np.maximum(s_ - t_, 0.0).astype(np.float32)
    c["i1"] = np.broadcast_to((il + 1.0)[None, :], (128, 128)).astype(np.float32).copy()
    c["i2"] = np.broadcast_to((128.0 - il)[None, :], (128, 128)).astype(np.float32).copy()
    c["kcol"] = np.stack([127.0 - il, il * 1.0], axis=1).astype(np.float32)
    return c


CONST_SPECS = [("ident_f", [128, 128], F32), ("ident_b", [128, 128], BF16), ("ones_b", [128, 128], BF16),
               ("perm_b", [128, 128], BF16), ("cosT", [128, T], BF16), ("sinT", [128, T], BF16),
               ("maskp", [128, 256], BF16), ("maskn", [128, 256], BF16),
               ("r1", [128, 128], F32), ("r2", [128, 128], F32), ("i1", [128, 128], F32), ("i2", [128, 128], F32),
               ("kcol", [128, 2], F32)]

PARAM_SPECS = [("g_final", [128, KC]), ("cc", [128, KC, 2]), ("bada", [128, DEPTH, 48]),
               ("gmix", [128, DEPTH, KC]), ("gmlp", [128, DEPTH, KC]),
               ("lamq", [128, DEPTH, 2, 64]), ("lamk", [128, DEPTH, 2, 64]), ("sublng", [128, DEPTH, 128]),
               ("decbc", [128, DEPTH, 8]), ("deccol", [128, DEPTH, 2, 2]), ("sinkbc", [128, DEPTH, 4])]


def prep_inputs(inputs, b, depth=DEPTH):
    f = np.float32
    m = {}
    m["x"] = np.ascontiguousarray(inputs["x"][b], dtype=f)
    m["ctx"] = np.ascontiguousarray(inputs["ctx"][b], dtype=f)
    m["g_final"] = np.ascontiguousarray(inputs["g_final"].reshape(KC, 128).T, dtype=f)
    cc = np.stack([inputs["c"][b].reshape(KC, 128).T, inputs["c_ctx"].reshape(KC, 128).T], axis=2)
    m["cc"] = np.ascontiguousarray(cc, dtype=f)
    m["bada"] = np.ascontiguousarray(inputs["b_ada"].reshape(DEPTH, 48, 128).transpose(2, 0, 1), dtype=f)
    m["gmix"] = np.ascontiguousarray(inputs["g_mix"].reshape(DEPTH, KC, 128).transpose(2, 0, 1), dtype=f)
    m["gmlp"] = np.ascontiguousarray(inputs["g_mlp"].reshape(DEPTH, KC, 128).transpose(2, 0, 1), dtype=f)
    lamq = np.stack([inputs["lam_q1"], inputs["lam_q2"]], axis=1)
    lamk = np.stack([inputs["lam_k1"], inputs["lam_k2"]], axis=1)
    m["lamq"] = np.ascontiguousarray(np.broadcast_to(lamq[None], (128, DEPTH, 2, 64)), dtype=f)
    m["lamk"] = np.ascontiguousarray(np.broadcast_to(lamk[None], (128, DEPTH, 2, 64)), dtype=f)
    m["sublng"] = np.ascontiguousarray(np.broadcast_to(inputs["subln_g"][None], (128, DEPTH, 128)), dtype=f)
    dec = np.concatenate([inputs["ret_decay_fwd"], inputs["ret_decay_bwd"]], axis=1)
    m["decbc"] = np.ascontiguousarray(np.broadcast_to(dec[None], (128, DEPTH, 8)), dtype=f)
    dcol = np.zeros((128, DEPTH, 2, 2), f)
    for di, nm in enumerate(["ret_decay_fwd", "ret_decay_bwd"]):
        for pp in range(2):
            dcol[0:64, :, di, pp] = inputs[nm][:, 2 * pp][None, :]
            dcol[64:128, :, di, pp] = inputs[nm][:, 2 * pp + 1][None, :]
    m["deccol"] = dcol
    m["sinkbc"] = np.ascontiguousarray(np.broadcast_to(inputs["sink_logit"][None], (128, DEPTH, 4)), dtype=f)
    for k in ["w_ada", "w_in", "w_out", "w_mlp1", "w_mlp2"]:
        m[k] = np.ascontiguousarray(inputs[k][:depth], dtype=f)
    m.update(host_consts())
    return m


class Builder:
    def __init__(self, cfg):
        self.cfg = cfg
        self.depth = cfg.get("depth", DEPTH)
        self.nc = bass.Bass("TRN2", target_bir_lowering=False)
        self.S = Sched()
        self.stack = contextlib.ExitStack()
        self.gp_i = 0
        self.ab_i = 0
        self.pt_i = 0
        self.wst_i = 0
        self.wo_i = 0
        self.oacc_i = 0
        self.rope_i = 0
        self.dbg_names = []
        self.deferred = []

    def dram_in(self, name, shape, dt=F32):
        return self.nc.dram_tensor(name, list(shape), dt, kind="ExternalInput").ap()

    def sb(self, name, shape, dt):
        return self.stack.enter_context(self.nc.sbuf_tensor(name, list(shape), dt))

    def ps(self, name, shape, dt=F32):
        return self.stack.enter_context(self.nc.psum_tensor(name, list(shape), dt))

    def dma(self, out, in_, reads=(), writes=(), key=None, eng="sp", **kw):
        self.S.add(eng, lambda e: e.dma_start(out=out, in_=in_, **kw), reads=reads, writes=writes, dma=key)

    def mm(self, out, lhsT, rhs, start=True, stop=True, reads=(), writes=(), sgc=False):
        if sgc:
            o = self.S.add("pe", lambda e: e.matmul(out, lhsT=lhsT, rhs=rhs, start=start, stop=stop, skip_group_check=True),
                           reads=reads, writes=writes)
        else:
            o = self.S.add("pe", lambda e: e.matmul(out, lhsT=lhsT, rhs=rhs, start=start, stop=stop),
                           reads=reads, writes=writes)
        o.embed = True

    def tr(self, out, in_, ident, reads=(), writes=()):
        o = self.S.add("pe", lambda e: e.transpose(out, in_, ident), reads=reads, writes=writes)
        o.embed = (ident.dtype == BF16)

    def op(self, eng, meth, reads=(), writes=(), **kw):
        o = self.S.add(eng, lambda e: getattr(e, meth)(**kw), reads=reads, writes=writes)
        o.embed = eng in ("act", "dve") or (eng == "pool" and meth in ("tensor_tensor", "tensor_copy"))

    def gp(self):
        t = self.PSG[self.gp_i % len(self.PSG)]
        self.gp_i += 1
        return t

    def ab(self):
        i = self.ab_i % 4
        self.ab_i += 1
        return self.OACC[i // 2][:, (i % 2) * 512:(i % 2 + 1) * 512], ("oacc", i // 2, i % 2)

    def ptb(self):
        t = self.PT[self.pt_i % len(self.PT)]
        self.pt_i += 1
        return t

    def gb(self, i):
        return self.GBT[:, i * GBW:(i + 1) * GBW]

    def dbg(self, name, ap, shape, dt, reads):
        d = self.nc.dram_tensor(name, list(shape), dt, kind="ExternalOutput").ap()
        self.dma(d, ap, reads=reads, key="dbg_" + name)
        self.dbg_names.append(name)

    def build(self):
        nc, S = self.nc, self.S
        cfg = self.cfg
        self.x_d = self.dram_in("x", [T, D])
        self.ctx_d = self.dram_in("ctx", [L, D])
        self.out_d = nc.dram_tensor("out", [T, D], F32, kind="ExternalOutput").ap()
        self.cd = {n: self.dram_in(n, shp, dt) for n, shp, dt in CONST_SPECS}
        self.pd = {n: self.dram_in(n, shp, F32) for n, shp in PARAM_SPECS}
        self.wada_d = self.dram_in("w_ada", [self.depth, D, 6 * D])
        self.win_d = self.dram_in("w_in", [self.depth, D, 3 * D])
        self.wout_d = self.dram_in("w_out", [self.depth, D, D])
        self.w1_d = self.dram_in("w_mlp1", [self.depth, D, 4 * D])
        self.w2_d = self.dram_in("w_mlp2", [self.depth, 4 * D, D])

        self.XT = self.sb("XT", [128, KC, NT], F32)
        self.HT = self.sb("HT", [128, KC, NT], BF16)
        self.GBT = self.sb("GBT", [128, NGB * GBW], BF16)
        self.WST = [self.sb("WST%d" % i, [128, KC, 512], BF16) for i in range(2)]
        self.WO = [self.sb("WO%d" % i, [128, D], BF16) for i in range(2)]
        self.C = {}
        for n, shp, dt in CONST_SPECS:
            t = self.sb("c_" + n, shp, dt)
            self.C[n] = t
            self.dma(t[:], self.cd[n], writes=[R(t)], key="c_" + n)
        self.O0N = self.sb("o0n", [128, 4, 128], F32)
        self.OA = self.sb("oa", [128, 4, 128], F32)
        self.OB = self.sb("ob", [128, 4, 128], F32)
        self.GA = self.sb("GA", [128, DEPTH, 128], F32)
        self.P = {}
        alias = {"lamq": self.O0N, "lamk": self.OB, "sublng": self.GA}
        for n, shp in PARAM_SPECS:
            if n in alias:
                t = alias[n]
                self.dma(t[:].rearrange("p a b -> p (a b)"), self.pd[n].rearrange("p a b c -> p (a b c)") if len(shp) == 4 else self.pd[n].rearrange("p a b -> p (a b)"), writes=[R(t)], key="p_" + n)
            else:
                t = self.sb("p_" + n, shp, F32)
                self.dma(t[:], self.pd[n], writes=[R(t)], key="p_" + n)
            self.P[n] = t
        self.iobuf = [self.gb(i)[:, 0:2048].bitcast(F32) for i in range(2)]
        self.iores = [[(("GB", i), g) for g in range(5)] for i in range(2)]
        self.tmpf = [self.sb("tmpf%d" % i, [128, 512], F32) for i in range(4)]
        self.PT = [self.sb("pt%d" % i, [128, 512], BF16) for i in range(4)]
        self.sqb = [self.PT[0], self.PT[1]]
        self.ropeq = [self.PT[2], self.PT[3]]
        self.rstd = self.tmpf[2]
        self.rstd0 = self.tmpf[3]
        self.small = self.sb("small", [128, 64], F32)
        self.epsb = self.sb("epsb", [128, 4], F32)
        S.add("pool", lambda e: e.memset(self.epsb[:, 0:1], float(D * EPS)), writes=[R(self.epsb)])
        S.add("pool", lambda e: e.memset(self.epsb[:, 1:2], float(EPS)), writes=[R(self.epsb)])
        S.add("pool", lambda e: e.memset(self.epsb[:, 2:3], 1.0), writes=[R(self.epsb)])
        self.MOD = self.sb("MOD", [128, DEPTH, 48, 2], F32)
        self.GS = [self.sb("GS%d" % i, [128, DEPTH, KC, 2], F32) for i in range(2)]
        self.G32 = [self.sb("G32_%d" % i, [128, DEPTH, KC], F32) for i in range(2)]
        self.siluT = self.sb("siluT", [128, KC, 2], BF16)
        self.LG = self.sb("LG", [128, DEPTH, 8], F32)
        self.LGC = self.sb("LGC", [128, DEPTH, 2, 2], F32)
        self.ESINK = self.sb("ESINK", [128, DEPTH, 4], F32)
        self.NEGLAM = self.sb("NEGLAM", [128, DEPTH], F32)
        self.lame = self.sb("lame", [128, DEPTH, 2], F32)
        _tqf = self.sb("TQF", [128, 128], F32)
        _tqb = self.sb("TQB", [128, 128], F32)
        _dt = self.sb("DT", [128, 256], F32)
        _tk = self.sb("TK", [128, 4], F32)
        _cfb = self.sb("CFB", [128, 2], F32)
        self.TQF, self.TQB, self.DT, self.TK, self.CFB = [_tqf] * 2, [_tqb] * 2, [_dt] * 2, [_tk] * 2, [_cfb] * 2
        self.stf = self.sb("stf", [128, 128], F32)
        self.stb = self.sb("stb", [128, 128], F32)
        self.kfb = [self.sb("kfb%d" % i, [128, 128], BF16) for i in range(4)]
        self.qfb = [self.sb("qfb%d" % i, [128, 128], BF16) for i in range(4)]
        self.PSG = [self.ps("ps%d" % i, [128, 512], F32) for i in range(4)]
        self.OACC = [self.ps("oacc%d" % i, [128, 1024], F32) for i in range(2)]

        self.load_input()
        self.prologue_params()
        self.adaln_all()
        for l in range(self.depth):
            need_ctx = l < DEPTH - 1
            self.norm(l, 0, skip_ctx=False)
            if cfg.get("dbg_h") == l:
                self.dbg("dbg_h", self.HT[:], [128, KC, NT], BF16, [("HT", kc, g) for kc in range(KC) for g in range(5)])
            order = []
            mx = cfg.get("mixers", "ABC")
            if "A" in mx:
                order += [("A", h) for h in range(4)]
            if "B" in mx:
                order += [("B", 0), ("B", 1)]
            if "C" in mx:
                order += [("C", 0)]
            for kind, i in order:
                if kind == "A":
                    self.chunk_A(l, i, need_ctx)
                elif kind == "B":
                    self.chunk_B(l, i, need_ctx)
                else:
                    self.chunk_C(l, need_ctx)
            if cfg.get("mlp", True):
                self.mlp(l, need_ctx)
        if cfg.get("dbg_x"):
            self.dbg("dbg_x", self.XT[:], [128, KC, NT], F32, [("XT", kc, g) for kc in range(KC) for g in range(5)])
        self.drain()
        self.final_norm()
        S.emit(nc, self.stack)
        return nc

    def xt_res(self, kc, gi):
        return ("XT", kc, gi)

    def load_input(self):
        S = self.S
        XT, PS, identf = self.XT, self.PSG, self.C["ident_f"]
        for tb in range(NTB):
            buf = self.iobuf[tb % 2]
            bres = self.iores[tb % 2]
            gi = min(tb // 4, 4)
            src = self.x_d[tb * 128:(tb + 1) * 128, :] if tb < 16 else self.ctx_d[(tb - 16) * 128:(tb - 15) * 128, :]
            self.dma(buf, src, writes=bres, key="io%d" % (tb % 2))
            for half in range(2):
                pt = self.gp()
                for j in range(4):
                    kc = half * 4 + j
                    self.tr(pt[:, j * 128:(j + 1) * 128], buf[:, kc * 128:(kc + 1) * 128], identf[:],
                            reads=bres + [R(identf)], writes=[R(pt)])
                self.op("dve", "tensor_copy", out=XT[:, half * 4:half * 4 + 4, tb * 128:(tb + 1) * 128],
                        in_=pt[:].rearrange("p (j t) -> p j t", j=4),
                        reads=[R(pt)], writes=[("XT", k, gi) for k in range(half * 4, half * 4 + 4)])

    def prologue_params(self):
        P = self.P
        sm = self.small
        self.op("act", "activation", out=self.siluT[:], in_=P["cc"][:], func=AF.Silu,
                reads=[R(P["cc"])], writes=[R(self.siluT)])
        self.op("dve", "tensor_scalar", out=self.G32[0][:], in0=P["gmix"][:], scalar1=32.0, scalar2=None, op0=ALU.mult,
                reads=[R(P["gmix"])], writes=[R(self.G32[0])])
        self.op("dve", "tensor_scalar", out=self.G32[1][:], in0=P["gmlp"][:], scalar1=32.0, scalar2=None, op0=ALU.mult,
                reads=[R(P["gmlp"])], writes=[R(self.G32[1])])
        for src, dst, n in ((P["decbc"], self.LG, DEPTH * 8), (P["deccol"], self.LGC, DEPTH * 4)):
            sv = src[:].rearrange("p a b -> p (a b)") if len(src.shape) == 3 else src[:].rearrange("p a b c -> p (a b c)")
            dv = dst[:].rearrange("p a b -> p (a b)") if len(dst.shape) == 3 else dst[:].rearrange("p a b c -> p (a b c)")
            self.op("act", "activation", out=sm[:, 0:n], in_=sv, func=AF.Exp, scale=-1.0,
                    reads=[R(src)], writes=[R(sm)])
            self.op("act", "activation", out=sm[:, 32:32 + n], in_=sm[:, 0:n], func=AF.Ln, bias=self.epsb[:, 2:3], scale=1.0,
                    reads=[R(sm), R(self.epsb)], writes=[R(sm)])
            self.op("dve", "tensor_scalar", out=dv, in0=sm[:, 32:32 + n], scalar1=-1.0, scalar2=None, op0=ALU.mult,
                    reads=[R(sm)], writes=[R(dst)])
        self.op("act", "activation", out=self.ESINK[:], in_=P["sinkbc"][:], func=AF.Exp,
                reads=[R(P["sinkbc"])], writes=[R(self.ESINK)])
        fl = lambda t: t[:].rearrange("p a b -> p (a b)")
        self.op("dve", "tensor_tensor", out=fl(self.OA), in0=fl(P["lamq"]), in1=fl(P["lamk"]), op=ALU.mult,
                reads=[R(P["lamq"]), R(P["lamk"])], writes=[R(self.OA)])
        self.op("dve", "tensor_reduce", out=self.lame[:].rearrange("p a b -> p (a b)"),
                in_=fl(self.OA).rearrange("p (a c) -> p a c", c=64), axis=AX.X, op=ALU.add,
                reads=[R(self.OA)], writes=[R(self.lame)])
        self.op("act", "activation", out=self.lame[:], in_=self.lame[:], func=AF.Exp,
                reads=[R(self.lame)], writes=[R(self.lame)])
        for l in range(DEPTH):
            self.op("dve", "tensor_scalar", out=self.NEGLAM[:, l:l + 1], in0=self.lame[:, l, 1:2],
                    scalar1=self.lame[:, l, 0:1], scalar2=-LAM_INIT[l], op0=ALU.subtract, op1=ALU.add,
                    reads=[R(self.lame)], writes=[R(self.NEGLAM)])
            self.op("dve", "tensor_scalar", out=self.GA[:, l, :], in0=self.GA[:, l, :],
                    scalar1=1.0 - LAM_INIT[l], scalar2=None, op0=ALU.mult,
                    reads=[R(self.GA)], writes=[R(self.GA)])

    def wst_load(self, src3, pieces, keyname="wst"):
        i = self.wst_i % 2
        self.wst_i += 1
        buf = self.WST[i]
        res = ("WST", i)
        for (sc, dc, n) in pieces:
            self.dma(buf[:, :, dc:dc + n], src3[:, :, sc:sc + n], writes=[res], key="wst%d" % i, eng="pool")
        return buf, res

    def adaln_gs(self, l):
        for w, base in ((0, 8), (1, 32)):
            self.op("dve", "scalar_tensor_tensor", out=self.GS[w][:, l, :, :], in0=self.MOD[:, l, base:base + 8, :],
                    scalar=1.0, in1=self.G32[w][:, l, :].unsqueeze(2).to_broadcast([128, KC, 2]),
                    op0=ALU.add, op1=ALU.mult,
                    reads=[("MOD", l), R(self.G32[w])], writes=[("GS", w, l)])

    def adaln_unit_load(self, l, j, wbuf, i):
        src3 = self.wada_d[l].rearrange("(kc p) n -> p kc n", p=128)
        self.dma(wbuf[:, :, 384:512], src3[:, :, j * 128:(j + 1) * 128], writes=[("WSTx", i)], key="wax%d" % i, eng="pool")

    def adaln_unit_compute(self, l, j, wbuf, i):
        pst = self.gp()
        for kc in range(KC):
            self.mm(pst[:, 0:2], wbuf[:, kc, 384:512], self.siluT[:, kc, :], start=(kc == 0), stop=(kc == KC - 1),
                    reads=[("WSTx", i), R(self.siluT)], writes=[R(pst)])
        self.op("dve", "tensor_scalar", out=self.MOD[:, l, j, :], in0=pst[:, 0:2], scalar1=self.P["bada"][:, l, j:j + 1],
                scalar2=None, op0=ALU.add, reads=[R(pst), R(self.P["bada"])], writes=[("MOD", l)])

    def adaln_all(self):
        P = self.P
        for l in range(1):
            src3 = self.wada_d[l].rearrange("(kc p) n -> p kc n", p=128)
            for piece in range(12):
                buf, res = self.wst_load(src3, [(piece * 512, 0, 512)])
                pst = self.gp()
                for jc in range(4):
                    for kc in range(KC):
                        self.mm(pst[:, jc * 2:jc * 2 + 2], buf[:, kc, jc * 128:(jc + 1) * 128], self.siluT[:, kc, :],
                                start=(kc == 0), stop=(kc == KC - 1), reads=[res, R(self.siluT)], writes=[R(pst)])
                j0 = piece * 4
                self.op("dve", "tensor_tensor", out=self.MOD[:, l, j0:j0 + 4, :],
                        in0=pst[:, 0:8].rearrange("p (j s) -> p j s", s=2),
                        in1=P["bada"][:, l, j0:j0 + 4].unsqueeze(2).to_broadcast([128, 4, 2]), op=ALU.add,
                        reads=[R(pst), R(P["bada"])], writes=[("MOD", l)])
            for w, base in ((0, 8), (1, 32)):
                self.op("dve", "scalar_tensor_tensor", out=self.GS[w][:, l, :, :], in0=self.MOD[:, l, base:base + 8, :],
                        scalar=1.0, in1=self.G32[w][:, l, :].unsqueeze(2).to_broadcast([128, KC, 2]),
                        op0=ALU.add, op1=ALU.mult,
                        reads=[("MOD", l), R(self.G32[w])], writes=[("GS", w, l)])

    def sumsq_rstd(self, gi, acc):
        t0, n = TGS[gi]
        XT, onesb = self.XT, self.C["ones_b"]
        for kc in range(KC):
            sq = self.sqb[kc % 2]
            self.op("act", "activation", out=sq[:, :n], in_=XT[:, kc, t0:t0 + n], func=AF.Square,
                    reads=[("XT", kc, gi)], writes=[R(sq)])
            self.mm(acc[:, :n], onesb[:], sq[:, :n], start=(kc == 0), stop=(kc == KC - 1),
                    reads=[R(sq), R(onesb)], writes=[R(acc)])
        self.op("act", "activation", out=self.rstd0[:, :n], in_=acc[:, :n], func=AF.Ln, bias=self.epsb[:, 0:1], scale=1.0,
                reads=[R(acc), R(self.epsb)], writes=[R(self.rstd0)])
        self.op("act", "activation", out=self.rstd[:, :n], in_=self.rstd0[:, :n], func=AF.Exp, scale=-0.5,
                reads=[R(self.rstd0)], writes=[R(self.rstd)])

    def norm(self, l, w, skip_ctx):
        shbase = 0 if w == 0 else 24
        for gi, (t0, n) in enumerate(TGS):
            if gi == 4 and skip_ctx:
                continue
            s = 0 if gi < 4 else 1
            acc = self.gp()
            self.sumsq_rstd(gi, acc)
            for kc in range(KC):
                tm = self.tmpf[kc % 2]
                self.op("dve", "scalar_tensor_tensor", out=tm[:, :n], in0=self.XT[:, kc, t0:t0 + n],
                        scalar=self.GS[w][:, l, kc, s:s + 1], in1=self.rstd[:, :n], op0=ALU.mult, op1=ALU.mult,
                        reads=[("XT", kc, gi), ("GS", w, l), R(self.rstd)], writes=[R(tm)])
                if kc % 2 == 0:
                    self.op("act", "activation", out=self.HT[:, kc, t0:t0 + n], in_=tm[:, :n], func=AF.Identity,
                            bias=self.MOD[:, l, shbase + kc, s:s + 1], scale=1.0,
                            reads=[R(tm), ("MOD", l)], writes=[("HT", kc, gi)])
                else:
                    self.op("dve", "tensor_scalar", out=self.HT[:, kc, t0:t0 + n], in0=tm[:, :n],
                            scalar1=self.MOD[:, l, shbase + kc, s:s + 1], scalar2=None, op0=ALU.add,
                            reads=[R(tm), ("MOD", l)], writes=[("HT", kc, gi)])

    def proj_fm(self, wbuf, wres, c0, dst, dname, rope, with_ctx, dst_fn=None):
        if dst_fn is None:
            dst_fn = lambda t0, n: [(dst[:, t0:t0 + n], slice(0, 128))]
        pending = None
        for gi, (t0, n) in enumerate(TGS):
            if gi == 4 and not with_ctx:
                continue
            ps = self.gp()
            for kc in range(KC):
                self.mm(ps[:, :n], wbuf[:, kc, c0:c0 + 128], self.HT[:, kc, t0:t0 + n], start=(kc == 0), stop=(kc == KC - 1),
                        reads=[wres, ("HT", kc, gi)], writes=[R(ps)])
            if rope and gi < 4:
                tail = self.rope_head(ps, n)
                if pending is not None:
                    self.rope_tail(*pending)
                pending = (ps, dst_fn(t0, n), t0, n, (dname, gi)) + tail
            else:
                for (oap, psl) in dst_fn(t0, n):
                    self.op("act", "activation", out=oap, in_=ps[psl, :n], func=AF.Copy,
                            reads=[R(ps)], writes=[(dname, gi)])
        if pending is not None:
            self.rope_tail(*pending)

    def rope_head(self, ps, n):
        i = self.rope_i % 2
        self.rope_i += 1
        qb = self.ropeq[i]
        self.op("act", "activation", out=qb[:, :n], in_=ps[:, :n], func=AF.Copy, reads=[R(ps)], writes=[R(qb)])
        return (i, qb)

    def rope_tail(self, ps, dsts, t0, n, dres, i, qb):
        t1, t2 = self.tmpf[2 * i], self.tmpf[2 * i + 1]
        permb, cosT, sinT = self.C["perm_b"], self.C["cosT"], self.C["sinT"]
        ps2 = self.gp()
        self.mm(ps2[:, :n], permb[:], qb[:, :n], reads=[R(qb), R(permb)], writes=[R(ps2)])
        self.op("dve", "tensor_tensor", out=t1[:, :n], in0=ps[:, :n], in1=cosT[:, t0:t0 + n], op=ALU.mult,
                reads=[R(ps), R(cosT)], writes=[R(t1)])
        self.op("dve", "tensor_tensor", out=t2[:, :n], in0=ps2[:, :n], in1=sinT[:, t0:t0 + n], op=ALU.mult,
                reads=[R(ps2), R(sinT)], writes=[R(t2)])
        for (oap, psl) in dsts:
            self.op("pool", "tensor_tensor", out=oap, in0=t1[psl, :n], in1=t2[psl, :n], op=ALU.add,
                    reads=[R(t1), R(t2)], writes=[dres])

    def proj_tm(self, wbuf, wres, c0, dst_fn, dname, func=None):
        for tb0 in range(0, NTB, 4):
            nb = min(4, NTB - tb0)
            gi = min(tb0 // 4, 4)
            ps = self.gp()
            for j in range(nb):
                tb = tb0 + j
                for kc in range(KC):
                    self.mm(ps[:, j * 128:(j + 1) * 128], self.HT[:, kc, tb * 128:(tb + 1) * 128], wbuf[:, kc, c0:c0 + 128],
                            start=(kc == 0), stop=(kc == KC - 1), reads=[wres, ("HT", kc, gi)], writes=[R(ps)])
            out, in_ = dst_fn(tb0, nb, ps)
            self.op("act", "activation", out=out, in_=in_, func=(func or AF.Copy), reads=[R(ps)], writes=[(dname, gi)])

    def finish_chunk(self, l, ci, OC3, ocname, OCT, octname, need_ctx):
        self.drain()
        i = self.wo_i % 2
        self.wo_i += 1
        WO = self.WO[i]
        wres = ("WO", i)
        self.dma(WO[:], self.wout_d[l][ci * 128:(ci + 1) * 128, :], writes=[wres], key="wo%d" % i, eng="pool")
        identb = self.C["ident_b"]
        ntb = NTB if need_ctx else 16
        for tb0 in range(0, ntb, 4):
            nb = min(4, ntb - tb0)
            gi = min(tb0 // 4, 4)
            pk = self.gp()
            pkb = pk[:].bitcast(BF16)
            for j in range(nb):
                self.tr(pkb[:, j * 128:(j + 1) * 128], OC3[:, tb0 + j, :], identb[:],
                        reads=[(ocname, gi), R(identb)], writes=[R(pk)])
            self.op("dve", "tensor_copy", out=OCT[:, tb0 * 128:(tb0 + nb) * 128], in_=pkb[:, 0:nb * 128],
                    reads=[R(pk)], writes=[(octname, gi)])
        for gi, (t0, n) in enumerate(TGS):
            if gi == 4 and not need_ctx:
                continue
            s = 0 if gi < 4 else 1
            for dc in range(KC):
                def item(use_gp, gi=gi, t0=t0, n=n, s=s, dc=dc):
                    if use_gp:
                        ps = self.gp()
                        pres = R(ps)
                    else:
                        ps, pres = self.ab()
                    self.mm(ps[:, :n], WO[:, dc * 128:(dc + 1) * 128], OCT[:, t0:t0 + n],
                            reads=[wres, (octname, gi)], writes=[pres])
                    self.op("dve", "scalar_tensor_tensor", out=self.XT[:, dc, t0:t0 + n], in0=ps[:, :n],
                            scalar=self.MOD[:, l, 16 + dc, s:s + 1], in1=self.XT[:, dc, t0:t0 + n], op0=ALU.mult, op1=ALU.add,
                            reads=[pres, ("MOD", l), ("XT", dc, gi)], writes=[("XT", dc, gi)])
                self.deferred.append(item)
        self.drain(use_gp=True)

    def drain(self, k=None, use_gp=False):
        while self.deferred and (k is None or k > 0):
            self.deferred.pop(0)(use_gp)
            if k is not None:
                k -= 1

    def chunk_A(self, l, h, need_ctx):
        src3 = self.win_d[l].rearrange("(kc p) n -> p kc n", p=128)
        wbuf, wres = self.wst_load(src3, [(h * 128, 0, 128), (512 + h * 128, 128, 128), (1024 + h * 128, 256, 128)])
        par = h % 2
        QT, KT, V = self.gb(par * 3), self.gb(par * 3 + 1), self.gb(par * 3 + 2)
        qn, kn, vn = ("GB", par * 3), ("GB", par * 3 + 1), ("GB", par * 3 + 2)
        OC, OCT = self.gb(6), self.gb(7)
        V3 = V.rearrange("p (t c) -> p t c", c=130)
        OC3 = OC[:, 0:NT].rearrange("p (t c) -> p t c", c=128)
        allg = list(range(5))
        self.op("pool", "memset", ap=V3[:, :, 128:129], constant=1.0, writes=[(vn, g) for g in allg])
        stage = self.cfg.get("a_stage", 9)
        if stage < 0:
            return
        self.proj_fm(wbuf, wres, 0, QT, qn, stage >= 0.5, need_ctx)
        if stage < 0.7:
            return
        self.proj_fm(wbuf, wres, 128, KT, kn, True, True)
        if stage < 0.8:
            return
        self.proj_tm(wbuf, wres, 256,
                     lambda tb0, nb, ps: (V3[:, tb0:tb0 + nb, 0:128], ps[:, 0:nb * 128].rearrange("p (j c) -> p j c", c=128)),
                     vn)
        if stage < 2:
            return
        itc = 0
        wi = wres[1]
        ada_l = l + 1 if (l + 1 < self.depth and self.cfg.get("ada_il", True)) else None
        ada_units = list(range(h * 12, h * 12 + 12)) if ada_l is not None else []
        ada_loaded = None
        for gi, (t0, n) in enumerate(TGS):
            if gi == 4 and not need_ctx:
                continue
            kbs = list(range(NTB)) if gi < 4 else [16, 17]
            nqb = n // 128
            nbk = nqb // 2
            aress = [[("oacc", c, 0), ("oacc", c, 1)] for c in range(2)]

            def emit_pv(ki, kb, pts):
                kgi = min(kb // 4, 4)
                for c in range(2):
                    acc = self.OACC[c]
                    for qb in range(nqb):
                        off = (qb // 2) * 512 + (qb % 2) * 129
                        self.mm(acc[:, off:off + 129], pts[c][:, qb * 128:(qb + 1) * 128], V3[:, kb, 0:129],
                                start=(ki == 0 and qb % 2 == 0), stop=(ki == len(kbs) - 1),
                                reads=[R(pts[c]), (vn, kgi)], writes=[aress[c][qb // 2]], sgc=True)

            pending = None
            for ki, kb in enumerate(kbs):
                kgi = min(kb // 4, 4)
                sts = [self.gp(), self.gp()]
                for c in range(2):
                    hs = slice(c * 64, (c + 1) * 64)
                    self.mm(sts[c][:, :n], KT[hs, kb * 128:(kb + 1) * 128], QT[hs, t0:t0 + n],
                            reads=[(kn, kgi), (qn, gi)], writes=[R(sts[c])])
                pts = [self.ptb(), self.ptb()]
                for c in range(2):
                    self.op("act", "activation", out=pts[c][:, :n], in_=sts[c][:, :n], func=AF.Exp, scale=0.125,
                            reads=[R(sts[c])], writes=[R(pts[c])])
                if pending is not None:
                    emit_pv(*pending)
                pending = (ki, kb, pts)
                if ada_l is not None and gi < 4:
                    if itc % 6 == 5 and ada_loaded is not None:
                        self.adaln_unit_compute(ada_l, ada_loaded, wbuf, wi)
                        ada_loaded = None
                    if itc % 6 == 0 and ada_units:
                        ada_loaded = ada_units.pop(0)
                        self.adaln_unit_load(ada_l, ada_loaded, wbuf, wi)
                    itc += 1
            emit_pv(*pending)
            if stage < 3:
                continue
            for c in range(2):
                acc = self.OACC[c]
                ares = aress[c]
                accv = acc[:].rearrange("p (b x) -> p b x", b=2)[:, 0:nbk, 0:258].rearrange("p b (j c) -> p b j c", c=129)
                zv = accv[:, :, :, 128]
                ov = accv[:, :, :, 0:128]
                sm = self.small
                rz = sm[:, 0:nqb].rearrange("p (b j) -> p b j", j=2)
                ar = ares[0:nbk]
                self.op("dve", "reciprocal", out=rz, in_=zv, reads=ar, writes=[R(sm)])
                o0 = self.O0N[:, 0:nqb, :].rearrange("p (b j) c -> p b j c", j=2)
                oa = self.OA[:, 0:nqb, :].rearrange("p (b j) c -> p b j c", j=2)
                if c == 0:
                    self.op("dve", "tensor_tensor", out=o0, in0=ov, in1=rz.unsqueeze(3).to_broadcast([128, nbk, 2, 128]),
                            op=ALU.mult, reads=ar + [R(sm)], writes=[R(self.O0N)])
                else:
                    rz1 = sm[:, 4:4 + nqb].rearrange("p (b j) -> p b j", j=2)
                    self.op("dve", "tensor_scalar", out=rz1, in0=rz, scalar1=self.NEGLAM[:, l:l + 1], scalar2=None,
                            op0=ALU.mult, reads=[R(sm), R(self.NEGLAM)], writes=[R(sm)])
                    self.op("dve", "tensor_tensor", out=oa, in0=ov, in1=rz1.unsqueeze(3).to_broadcast([128, nbk, 2, 128]),
                            op=ALU.mult, reads=ar + [R(sm)], writes=[R(self.OA)])
                    oaf = self.OA[:, 0:nqb, :]
                    obf = self.OB[:, 0:nqb, :]
                    self.op("pool", "tensor_tensor", out=oaf, in0=oaf, in1=self.O0N[:, 0:nqb, :], op=ALU.add,
                            reads=[R(self.OA), R(self.O0N)], writes=[R(self.OA)])
                    self.op("pool", "tensor_tensor", out=obf, in0=oaf, in1=oaf, op=ALU.mult,
                            reads=[R(self.OA)], writes=[R(self.OB)])
                    ss = sm[:, 8:8 + nqb]
                    self.op("dve", "tensor_reduce", out=ss, in_=obf, axis=AX.X, op=ALU.add,
                            reads=[R(self.OB)], writes=[R(sm)])
                    l1 = sm[:, 12:12 + nqb]
                    rs = sm[:, 16:16 + nqb]
                    self.op("act", "activation", out=l1, in_=ss, func=AF.Ln, bias=self.epsb[:, 1:2], scale=1.0 / 128.0,
                            reads=[R(sm), R(self.epsb)], writes=[R(sm)])
                    self.op("act", "activation", out=rs, in_=l1, func=AF.Exp, scale=-0.5,
                            reads=[R(sm)], writes=[R(sm)])
                    self.op("dve", "tensor_tensor", out=obf, in0=oaf, in1=rs.unsqueeze(2).to_broadcast([128, nqb, 128]),
                            op=ALU.mult, reads=[R(self.OA), R(sm)], writes=[R(self.OB)])
                    tbq = t0 // 128
                    self.op("pool", "tensor_tensor", out=OC3[:, tbq:tbq + nqb, :], in0=obf,
                            in1=self.GA[:, l, :].unsqueeze(1).to_broadcast([128, nqb, 128]), op=ALU.mult,
                            reads=[R(self.OB), R(self.GA)], writes=[(("GB", 6), gi)])
        if ada_l is not None:
            assert ada_loaded is None and not ada_units
            if h == 3:
                self.adaln_gs(ada_l)
        if stage < 4:
            return
        self.finish_chunk(l, h, OC3, ("GB", 6), OCT, ("GB", 7), need_ctx)

    def ret_tables(self, l, pp):
        C = self.C
        if True:
            lgf = self.LGC[:, l, 0, pp:pp + 1]
            lgb = self.LGC[:, l, 1, pp:pp + 1]
            self.op("act", "activation", out=self.TQF[pp][:], in_=C["i1"][:], func=AF.Exp, scale=lgf,
                    reads=[R(C["i1"]), R(self.LGC)], writes=[R(self.TQF[pp])])
            self.op("act", "activation", out=self.TQB[pp][:], in_=C["i2"][:], func=AF.Exp, scale=lgb,
                    reads=[R(C["i2"]), R(self.LGC)], writes=[R(self.TQB[pp])])
            self.op("act", "activation", out=self.CFB[pp][:, 0:1], in_=lgf, func=AF.Exp, scale=128.0,
                    reads=[R(self.LGC)], writes=[R(self.CFB[pp])])
            self.op("act", "activation", out=self.CFB[pp][:, 1:2], in_=lgb, func=AF.Exp, scale=128.0,
                    reads=[R(self.LGC)], writes=[R(self.CFB[pp])])
            for h2 in range(2):
                h = 2 * pp + h2
                e1 = self.tmpf[0][:, 0:128]
                e2 = self.tmpf[1][:, 0:128]
                self.op("dve", "tensor_scalar", out=e1, in0=C["r1"][:], scalar1=self.LG[:, l, h:h + 1], scalar2=None,
                        op0=ALU.mult, reads=[R(C["r1"]), R(self.LG)], writes=[R(self.tmpf[0])])
                self.op("dve", "scalar_tensor_tensor", out=e2, in0=C["r2"][:], scalar=self.LG[:, l, 4 + h:5 + h], in1=e1,
                        op0=ALU.mult, op1=ALU.add, reads=[R(C["r2"]), R(self.LG), R(self.tmpf[0])], writes=[R(self.tmpf[1])])
                self.op("act", "activation", out=self.DT[pp][:, h2 * 128:(h2 + 1) * 128], in_=e2, func=AF.Exp,
                        reads=[R(self.tmpf[1])], writes=[R(self.DT[pp])])
            sm = self.small
            self.op("act", "activation", out=sm[:, 20:22], in_=self.LG[:, l, 2 * pp:2 * pp + 2], func=AF.Exp,
                    scale=C["kcol"][:, 0:1], reads=[R(self.LG), R(C["kcol"])], writes=[R(sm)])
            self.op("act", "activation", out=sm[:, 22:24], in_=self.LG[:, l, 4 + 2 * pp:6 + 2 * pp], func=AF.Exp,
                    scale=C["kcol"][:, 1:2], reads=[R(self.LG), R(C["kcol"])], writes=[R(sm)])
            self.op("dve", "tensor_scalar", out=self.TK[pp][:], in0=sm[:, 20:24], scalar1=0.125, scalar2=None, op0=ALU.mult,
                    reads=[R(sm)], writes=[R(self.TK[pp])])

    def chunk_B(self, l, pp, need_ctx):
        src3 = self.win_d[l].rearrange("(kc p) n -> p kc n", p=128)
        wbuf, wres = self.wst_load(src3, [(1536 + pp * 128, 0, 128), (1792 + pp * 128, 128, 128),
                                          (2048 + pp * 128, 256, 128), (2304 + pp * 128, 384, 128)])
        names = [("GB", i) for i in range(8)]
        QT, KT, V, G, SF, SB, OC, OCT = [self.gb(i) for i in range(8)]
        qn, kn, vn, gn, sfn, sbn, ocn, octn = names
        V3 = V.rearrange("p (t c) -> p t c", c=130)
        G3 = G[:, 0:NT].rearrange("p (t c) -> p t c", c=128)
        SF3 = SF[:, 0:NT].rearrange("p (t c) -> p t c", c=128)
        SB3 = SB[:, 0:NT].rearrange("p (t c) -> p t c", c=128)
        OC3 = OC[:, 0:NT].rearrange("p (t c) -> p t c", c=128)
        identb = self.C["ident_b"]
        self.ret_tables(l, pp)
        self.proj_fm(wbuf, wres, 0, QT, qn, True, need_ctx)
        self.proj_fm(wbuf, wres, 128, KT, kn, True, True)
        self.proj_tm(wbuf, wres, 256,
                     lambda tb0, nb, ps: (V3[:, tb0:tb0 + nb, 0:128], ps[:, 0:nb * 128].rearrange("p (j c) -> p j c", c=128)),
                     vn)
        self.proj_tm(wbuf, wres, 384,
                     lambda tb0, nb, ps: (G3[:, tb0:tb0 + nb, :], ps[:, 0:nb * 128].rearrange("p (j c) -> p j c", c=128)),
                     gn, func=AF.Silu)
        orders = [[16, 17] + list(range(16)), [17, 16] + list(range(15, -1, -1))]
        sts_ = [self.stf, self.stb]
        ST3s = [SF3, SB3]
        stns = [sfn, sbn]
        for d in range(2):
            self.op("pool", "memset", ap=sts_[d][:], constant=0.0, writes=[R(sts_[d])])
        NS = len(orders[0])
        pks = {}
        pus = {}
        for i in range(NS + 3):
            for d in range(2):
                if i < NS:
                    tb = orders[d][i]
                    gi = min(tb // 4, 4)
                    pk = self.gp()
                    pkb = pk[:].bitcast(BF16)
                    self.tr(pkb[:, 0:128], KT[:, tb * 128:(tb + 1) * 128], identb[:], reads=[(kn, gi), R(identb)], writes=[R(pk)])
                    pks[(d, i)] = (pk, pkb)
                if 0 <= i - 1 < NS:
                    pk, pkb = pks.pop((d, i - 1))
                    kf = self.kfb[2 * d + (i - 1) % 2]
                    self.op("dve", "tensor_tensor", out=kf[:].rearrange("p (h c) -> p h c", h=2),
                            in0=pkb[:, 0:128].rearrange("p (h c) -> p h c", h=2),
                            in1=self.TK[pp][:, 2 * d:2 * d + 2].unsqueeze(2).to_broadcast([128, 2, 64]), op=ALU.mult,
                            reads=[R(pk), R(self.TK[pp])], writes=[R(kf)])
                if 0 <= i - 2 < NS:
                    tb = orders[d][i - 2]
                    gi = min(tb // 4, 4)
                    kf = self.kfb[2 * d + (i - 2) % 2]
                    pu, pures = self.ab()
                    self.mm(pu[:, 0:128], kf[:], V3[:, tb, 0:128], reads=[R(kf), (vn, gi)], writes=[pures])
                    pus[(d, i - 2)] = (pu, pures)
                if 0 <= i - 3 < NS:
                    tb = orders[d][i - 3]
                    gi = min(tb // 4, 4)
                    st = sts_[d]
                    pu, pures = pus.pop((d, i - 3))
                    self.op("pool", "tensor_copy", out=ST3s[d][:, tb, :], in_=st[:], reads=[R(st)], writes=[(stns[d], gi)])
                    self.op("dve", "scalar_tensor_tensor", out=st[:], in0=st[:], scalar=self.CFB[pp][:, d:d + 1], in1=pu[:, 0:128],
                            op0=ALU.mult, op1=ALU.add, reads=[R(st), R(self.CFB[pp]), pures], writes=[R(st)])
        ntb = NTB if need_ctx else 16
        sm = self.small
        cur = {}

        def front(tb):
            gi = min(tb // 4, 4)
            ts_ = slice(tb * 128, (tb + 1) * 128)
            sts = [self.gp(), self.gp()]
            for h2 in range(2):
                hs = slice(h2 * 64, (h2 + 1) * 64)
                self.mm(sts[h2][:, 0:128], KT[hs, ts_], QT[hs, ts_], reads=[(kn, gi), (qn, gi)], writes=[R(sts[h2])])
            pt = self.ptb()
            for h2 in range(2):
                self.op("dve", "scalar_tensor_tensor", out=pt[:, h2 * 128:(h2 + 1) * 128], in0=sts[h2][:, 0:128], scalar=0.125,
                        in1=self.DT[pp][:, h2 * 128:(h2 + 1) * 128],
                        op0=ALU.mult, op1=ALU.mult, reads=[R(sts[h2]), R(self.DT[pp])], writes=[R(pt)])
            qpool = self.qfb + self.kfb
            qf = qpool[(2 * tb) % 8]
            qb_ = qpool[(2 * tb + 1) % 8]
            self.op("pool", "tensor_tensor", out=qf[:], in0=QT[:, ts_], in1=self.TQF[pp][:], op=ALU.mult,
                    reads=[(qn, gi), R(self.TQF[pp])], writes=[R(qf)])
            self.op("pool", "tensor_tensor", out=qb_[:], in0=QT[:, ts_], in1=self.TQB[pp][:], op=ALU.mult,
                    reads=[(qn, gi), R(self.TQB[pp])], writes=[R(qb_)])
            return (pt, qf, qb_)

        def back(tb, pt, qf, qb_):
            gi = min(tb // 4, 4)
            jj = tb % 4
            if jj == 0:
                cur["po"], cur["pres"] = self.ab()
            po, pres = cur["po"], cur["pres"]
            for h2 in range(2):
                hs = slice(h2 * 64, (h2 + 1) * 64)
                oreg = po[:, jj * 128 + h2 * 64:jj * 128 + (h2 + 1) * 64]
                self.mm(oreg, qf[hs, :], SF3[hs, tb, hs], start=(jj == 0 and h2 == 0), stop=False,
                        reads=[R(qf), (sfn, gi)], writes=[pres], sgc=True)
                self.mm(oreg, qb_[hs, :], SB3[hs, tb, hs], start=False, stop=False,
                        reads=[R(qb_), (sbn, gi)], writes=[pres], sgc=True)
                self.mm(oreg, pt[:, h2 * 128:(h2 + 1) * 128], V3[:, tb, hs], start=False, stop=True,
                        reads=[R(pt), (vn, gi)], writes=[pres], sgc=True)
            last = (jj == 3) or (tb == ntb - 1)
            if not last:
                return
            tb0 = tb - jj
            nb = jj + 1
            ng = 2 * nb
            W = nb * 128
            v3 = lambda a: a[:, 0:W].rearrange("p (g c) -> p g c", c=64)
            osb = self.OA[:].rearrange("p a b -> p (a b)")
            ocn_ = self.OB[:].rearrange("p a b -> p (a b)")
            sqv = self.O0N[:].rearrange("p a b -> p (a b)")
            self.op("dve", "tensor_copy", out=osb[:, 0:W], in_=po[:, 0:W], reads=[pres], writes=[R(self.OA)])
            self.op("dve", "tensor_reduce", out=sm[:, 24:24 + ng], in_=v3(osb), axis=AX.X, op=ALU.add,
                    reads=[R(self.OA)], writes=[R(sm)])
            self.op("dve", "tensor_scalar", out=sm[:, 24:24 + ng], in0=sm[:, 24:24 + ng], scalar1=1.0 / 64.0, scalar2=None, op0=ALU.mult,
                    reads=[R(sm)], writes=[R(sm)])
            self.op("dve", "tensor_tensor", out=v3(ocn_), in0=v3(osb), in1=sm[:, 24:24 + ng].unsqueeze(2).to_broadcast([128, ng, 64]),
                    op=ALU.subtract, reads=[R(self.OA), R(sm)], writes=[R(self.OB)])
            self.op("pool", "tensor_tensor", out=sqv[:, 0:W], in0=ocn_[:, 0:W], in1=ocn_[:, 0:W], op=ALU.mult,
                    reads=[R(self.OB)], writes=[R(self.O0N)])
            self.op("dve", "tensor_reduce", out=sm[:, 44:44 + ng], in_=v3(sqv), axis=AX.X, op=ALU.add,
                    reads=[R(self.O0N)], writes=[R(sm)])
            self.op("act", "activation", out=sm[:, 52:52 + ng], in_=sm[:, 44:44 + ng], func=AF.Ln, bias=self.epsb[:, 1:2], scale=1.0 / 64.0,
                    reads=[R(sm), R(self.epsb)], writes=[R(sm)])
            self.op("act", "activation", out=sm[:, 24:24 + ng], in_=sm[:, 52:52 + ng], func=AF.Exp, scale=-0.5,
                    reads=[R(sm)], writes=[R(sm)])
            self.op("dve", "tensor_tensor", out=v3(osb), in0=v3(ocn_), in1=sm[:, 24:24 + ng].unsqueeze(2).to_broadcast([128, ng, 64]),
                    op=ALU.mult, reads=[R(self.OB), R(sm)], writes=[R(self.OA)])
            self.op("pool", "tensor_tensor", out=OC3[:, tb0:tb0 + nb, :], in0=osb[:, 0:W].rearrange("p (t c) -> p t c", c=128),
                    in1=G3[:, tb0:tb0 + nb, :], op=ALU.mult,
                    reads=[R(self.OA), (gn, gi)], writes=[(ocn, gi)])

        pend = []
        for tb in range(ntb):
            fr = front(tb)
            pend.append((tb,) + fr)
            if len(pend) > 3:
                back(*pend.pop(0))
        while pend:
            back(*pend.pop(0))
        self.finish_chunk(l, 4 + pp, OC3, ocn, OCT, octn, need_ctx)

    def chunk_C(self, l, need_ctx):
        src3 = self.win_d[l].rearrange("(kc p) n -> p kc n", p=128)
        wbuf, wres = self.wst_load(src3, [(2560, 0, 256), (2816, 256, 64), (2816, 320, 64), (2880, 384, 64), (2880, 448, 64)])
        wbufv, wresv = self.wst_load(src3, [(2944, 0, 128)])
        QZ = self.GBT[:, 0:2 * NT].rearrange("p (g t) -> p g t", g=2)
        qzn = "QZ"
        gb01 = [(("GB", i), g_) for i in range(2) for g_ in range(5)]
        KT, V = self.gb(2), self.gb(3)
        kn, vn = ("GB", 2), ("GB", 3)
        OC, OCT = self.gb(6), self.gb(7)
        ocn, octn = ("GB", 6), ("GB", 7)
        V4 = V.rearrange("p (t j c) -> p t j c", j=2, c=65)
        OC3 = OC[:, 0:NT].rearrange("p (t c) -> p t c", c=128)
        allg = list(range(5))
        self.op("pool", "memset", ap=V4[:, :, :, 64:65], constant=1.0, writes=[(vn, g) for g in allg])
        self.op("pool", "memset", ap=QZ[64:128, 0, :], constant=0.0, writes=gb01 + [(qzn, g) for g in allg])
        self.op("pool", "memset", ap=QZ[0:64, 1, :], constant=0.0, writes=gb01 + [(qzn, g) for g in allg])
        self.proj_tm(wbufv, wresv, 0,
                     lambda tb0, nb, ps: (V4[:, tb0:tb0 + nb, :, 0:64],
                                          ps[:, 0:nb * 128].rearrange("p (t j c) -> p t j c", j=2, c=64)),
                     vn)
        ntb = NTB if need_ctx else 16
        sm = self.small
        qz_fn = lambda t0, n: [(QZ[0:64, 0, t0:t0 + n], slice(0, 64)), (QZ[64:128, 1, t0:t0 + n], slice(64, 128))]
        for j in range(2):
            self.proj_fm(wbuf, wres, j * 128, None, qzn, True, need_ctx, dst_fn=qz_fn)
            self.proj_fm(wbuf, wres, 256 + j * 128, KT, kn, True, True)
            items = []
            for n_ in range(ntb):
                if n_ < 16:
                    kbs = ([n_ - 1] if n_ > 0 else []) + [n_] + ([n_ + 1] if n_ < 15 else []) + [16, 17]
                else:
                    kbs = [16, 17]
                for ki, m in enumerate(kbs):
                    items.append((n_, ki, m, len(kbs)))
            cur = {}

            def front(n_, ki, m, nk):
                gi = min(n_ // 4, 4)
                mgi = min(m // 4, 4)
                st = self.gp()
                self.mm(st[:, 0:256], KT[:, m * 128:(m + 1) * 128], QZ[:, :, n_ * 128:(n_ + 1) * 128],
                        reads=[(kn, mgi), (qzn, gi)], writes=[R(st)])
                pt = self.ptb()
                self.op("act", "activation", out=pt[:, 0:256], in_=st[:, 0:256], func=AF.Exp, scale=0.125,
                        reads=[R(st)], writes=[R(pt)])
                if n_ < 16 and m == n_ - 1:
                    self.op("pool", "tensor_tensor", out=pt[:, 0:256], in0=pt[:, 0:256], in1=self.C["maskp"][:], op=ALU.mult,
                            reads=[R(pt), R(self.C["maskp"])], writes=[R(pt)])
                if n_ < 15 and m == n_ + 1:
                    self.op("pool", "tensor_tensor", out=pt[:, 0:256], in0=pt[:, 0:256], in1=self.C["maskn"][:], op=ALU.mult,
                            reads=[R(pt), R(self.C["maskn"])], writes=[R(pt)])
                return pt

            def back(n_, ki, m, nk, pt):
                mgi = min(m // 4, 4)
                r3 = n_ % 3
                if r3 == 0 and ki == 0:
                    cur["po"], cur["pres"] = self.ab()
                    cur["n0"] = n_
                po, pres = cur["po"], cur["pres"]
                for g in range(2):
                    off = r3 * 130 + g * 65
                    self.mm(po[:, off:off + 65], pt[:, g * 128:(g + 1) * 128], V4[:, m, j, 0:65],
                            start=(r3 == 0 and ki == 0 and g == 0), stop=(ki == nk - 1), reads=[R(pt), (vn, mgi)], writes=[pres],
                            sgc=True)
                if ki == nk - 1 and (r3 == 2 or n_ == ntb - 1):
                    n0 = cur["n0"]
                    cnt = n_ - n0 + 1
                    pov = po[:, 0:cnt * 130].rearrange("p (n g c) -> p n g c", g=2, c=65)
                    den = sm[:, 34:34 + 2 * cnt].rearrange("p (n g) -> p n g", g=2)
                    rz = sm[:, 40:40 + 2 * cnt].rearrange("p (n g) -> p n g", g=2)
                    self.op("dve", "tensor_tensor", out=den, in0=pov[:, :, :, 64],
                            in1=self.ESINK[:, l, 2 * j:2 * j + 2].unsqueeze(1).to_broadcast([128, cnt, 2]), op=ALU.add,
                            reads=[pres, R(self.ESINK)], writes=[R(sm)])
                    self.op("dve", "reciprocal", out=rz, in_=den, reads=[R(sm)], writes=[R(sm)])
                    gis = sorted(set(min(q // 4, 4) for q in range(n0, n_ + 1)))
                    self.op("dve", "tensor_tensor", out=OC3[:, n0:n_ + 1, :].rearrange("p n (g c) -> p n g c", g=2),
                            in0=pov[:, :, :, 0:64], in1=rz.unsqueeze(3).to_broadcast([128, cnt, 2, 64]), op=ALU.mult,
                            reads=[pres, R(sm)], writes=[(ocn, g_) for g_ in gis])

            pend = []
            for it in items:
                pt = front(*it)
                pend.append(it + (pt,))
                if len(pend) > 3:
                    back(*pend.pop(0))
            while pend:
                back(*pend.pop(0))
            self.finish_chunk(l, 6 + j, OC3, ocn, OCT, octn, need_ctx)

    def mlp(self, l, need_ctx):
        self.drain()
        self.norm(l, 1, skip_ctx=not need_ctx)
        w1v = self.w1_d[l].rearrange("(kc p) n -> p kc n", p=128)
        w2v = self.w2_d[l].rearrange("(fc p) n -> p fc n", p=128)
        pending = None
        ucount = 0

        def mlp2(W2, w2res, AT, ares, gi, t0, n, s):
            for dc in range(KC):
                ps2, pres = self.ab()
                for fc in range(4):
                    self.mm(ps2[:, :n], W2[:, fc, dc * 128:(dc + 1) * 128], AT[:, fc, :n], start=(fc == 0), stop=(fc == 3),
                            reads=[(w2res[0], 0), (w2res[1], 0), ares], writes=[pres])
                self.op("dve", "scalar_tensor_tensor", out=self.XT[:, dc, t0:t0 + n], in0=ps2[:, :n],
                        scalar=self.MOD[:, l, 40 + dc, s:s + 1], in1=self.XT[:, dc, t0:t0 + n], op0=ALU.mult, op1=ALU.add,
                        reads=[pres, ("MOD", l), ("XT", dc, gi)], writes=[("XT", dc, gi)])

        for fb in range(8):
            W1, w1res = self.wst_load(w1v, [(fb * 512, 0, 512)])
            i2 = fb % 2
            W2 = self.GBT[:, i2 * 2 * GBW:i2 * 2 * GBW + 4096].rearrange("p (f n) -> p f n", f=4)
            w2res = [("GB", 2 * i2), ("GB", 2 * i2 + 1)]
            self.dma(W2, w2v[:, fb * 4:(fb + 1) * 4, :], writes=[(r, g) for r in w2res for g in range(5)],
                     key="w2_%d" % i2, eng="pool")
            for gi, (t0, n) in enumerate(TGS):
                if gi == 4 and not need_ctx:
                    continue
                s = 0 if gi < 4 else 1
                ai = 4 + (ucount % 2)
                ucount += 1
                AT = self.gb(ai)[:, 0:2048].rearrange("p (f n) -> p f n", f=4)
                ares = (("GB", ai), 0)
                aresw = [(("GB", ai), g_) for g_ in range(5)]
                for fc in range(4):
                    ps = self.gp()
                    for kc in range(KC):
                        self.mm(ps[:, :n], W1[:, kc, fc * 128:(fc + 1) * 128], self.HT[:, kc, t0:t0 + n],
                                start=(kc == 0), stop=(kc == KC - 1), reads=[w1res, ("HT", kc, gi)], writes=[R(ps)])
                    rt = self.tmpf[fc % 4]
                    self.op("act", "activation", out=rt[:, :n], in_=ps[:, :n], func=AF.Relu, reads=[R(ps)], writes=[R(rt)])
                    self.op("pool", "tensor_tensor", out=AT[:, fc, :n], in0=rt[:, :n], in1=rt[:, :n], op=ALU.mult,
                            reads=[R(rt)], writes=aresw)
                if pending is not None:
                    mlp2(*pending)
                pending = (W2, w2res, AT, ares, gi, t0, n, s)
        mlp2(*pending)

    def final_norm(self):
        XT, identf = self.XT, self.C["ident_f"]
        gfin = self.P["g_final"]
        YT = self.HT[:].rearrange("p k t -> p (k t)")[:, 0:8192].bitcast(F32).rearrange("p (k t) -> p k t", k=KC)
        ytres = [("HT", kc, g) for kc in range(KC) for g in range(5)]
        for g in range(4):
            t0 = g * 512
            acc = self.gp()
            self.sumsq_rstd(g, acc)
            for kc in range(KC):
                self.op("dve", "scalar_tensor_tensor", out=YT[:, kc, :], in0=XT[:, kc, t0:t0 + 512], scalar=gfin[:, kc:kc + 1],
                        in1=self.rstd[:], op0=ALU.mult, op1=ALU.mult,
                        reads=[("XT", kc, g), R(self.rstd), R(gfin)], writes=(ytres if kc == 0 else []) + [("YT", kc)])
            for j in range(4):
                tb = g * 4 + j
                ob = self.iobuf[tb % 2]
                obres = self.iores[tb % 2]
                for half in range(2):
                    pt = self.gp()
                    for jj in range(4):
                        kc = half * 4 + jj
                        self.tr(pt[:, jj * 128:(jj + 1) * 128], YT[:, kc, j * 128:(j + 1) * 128], identf[:],
                                reads=[("YT", kc), ("HT", 0, 0), R(identf)], writes=[R(pt)])
                    self.op("act", "mul", out=ob[:, half * 512:(half + 1) * 512], in_=pt[:], mul=32.0,
                            reads=[R(pt)], writes=obres)
                self.dma(self.out_d[tb * 128:(tb + 1) * 128, :], ob, reads=obres, key="io%d" % (tb % 2))


_CACHE = {}


def kernel(**inputs):
    cfg = inputs.pop("_cfg", {}) if "_cfg" in inputs else {}
    inputs = {k: np.asarray(v) for k, v in inputs.items()}
    key = tuple(sorted(cfg.items()))
    if key not in _CACHE:
        b = Builder(cfg)
        _CACHE[key] = (b.build(), b)
    nc, b = _CACHE[key]
    in_maps = [prep_inputs(inputs, i, b.depth) for i in range(8)]
    res = run_bass_kernel_spmd(nc, in_maps, core_ids=list(range(8)))
    if cfg.get("_ret_all"):
        return res
    out = np.stack([np.asarray(r["out"]) for r in res.results], axis=0)
    return out.astype(np.float32)
```

```python
import contextlib
import math
import numpy as np
import ml_dtypes
import concourse.bass as bass
import concourse.mybir as mybir
from concourse.bass_utils import run_bass_kernel_spmd

F32 = mybir.dt.float32
BF16 = mybir.dt.bfloat16
ALU = mybir.AluOpType
AF = mybir.ActivationFunctionType
AX = mybir.AxisListType

D = 1024
T = 2048
L = 256
NT = T + L
DEPTH = 4
KC = D // 128
NTB = NT // 128
EPS = 1e-6
TGS = [(0, 512), (512, 512), (1024, 512), (1536, 512), (2048, 256)]


class Op:
    __slots__ = ("eng", "fn", "idx", "deps", "raw", "inc", "count", "dma", "is_dma", "embed")

    def __init__(self, eng, fn, idx):
        self.eng = eng
        self.fn = fn
        self.idx = idx
        self.deps = set()
        self.raw = set()
        self.inc = False
        self.count = 0
        self.dma = None
        self.is_dma = False
        self.embed = False


EMBED_WAITS = True


class Sched:
    ENGS = ["pe", "act", "dve", "pool", "sp"]

    def __init__(self):
        self.ops = {e: [] for e in self.ENGS}
        self.last_w = {}
        self.readers = {}
        self.dma_n = {}

    def add(self, eng, fn, reads=(), writes=(), dma=None):
        op = Op(eng, fn, len(self.ops[eng]))
        xr = [r for r in reads if isinstance(r, tuple) and str(r[0]).startswith(("ps", "oacc"))]
        if xr:
            for r in xr:
                w = self.last_w.get(r)
                if w is not None:
                    op.raw.add(w)
            writes = list(writes) + [r for r in xr if r not in writes]
        for r in reads:
            w = self.last_w.get(r)
            if w is not None:
                op.deps.add(w)
                op.raw.add(w)
        for r in writes:
            w = self.last_w.get(r)
            if w is not None:
                op.deps.add(w)
            rd = self.readers.get(r)
            if rd:
                for o in rd.values():
                    op.deps.add(o)
        for r in reads:
            d = self.readers.setdefault(r, {})
            if dma is not None:
                d[("dma", id(op))] = op
            else:
                d[eng] = op
        for r in writes:
            self.last_w[r] = op
            self.readers[r] = {}
        if dma is not None:
            n = self.dma_n.get(dma, 0) + 1
            self.dma_n[dma] = n
            op.dma = (dma, n)
            op.is_dma = True
        self.ops[eng].append(op)
        return op

    def _needs_wait(self, op, d):
        if d.is_dma:
            return True
        if d.eng != op.eng:
            return True
        if op.is_dma:
            return True
        if op.eng == "pe":
            return False
        return (d in op.raw) and (op.idx - d.idx <= 4)

    def emit(self, nc, stack):
        for e in self.ENGS:
            for op in self.ops[e]:
                for d in op.deps:
                    if not d.is_dma and self._needs_wait(op, d):
                        d.inc = True
        for e in self.ENGS:
            c = 0
            for op in self.ops[e]:
                if op.inc:
                    c += 1
                op.count = c
        sems = {e: stack.enter_context(nc.semaphore("s_" + e)) for e in self.ENGS}
        dsems = {k: stack.enter_context(nc.semaphore("d_%s" % (k,))) for k in self.dma_n}
        block = stack.enter_context(nc.Block())
        stats = {}

        def run(ename, engine):
            known = {}
            nw = 0
            for op in self.ops[ename]:
                need = {}
                for d in op.deps:
                    if not self._needs_wait(op, d):
                        continue
                    if d.is_dma:
                        key = ("d", d.dma[0])
                        val = 16 * d.dma[1]
                    else:
                        key = ("e", d.eng)
                        val = d.count
                    if val > need.get(key, 0):
                        need[key] = val
                todo = []
                for key, val in need.items():
                    if known.get(key, 0) >= val:
                        continue
                    known[key] = val
                    todo.append((dsems[key[1]] if key[0] == "d" else sems[key[1]], val))
                last = todo.pop() if (todo and op.embed and EMBED_WAITS) else None
                for s, val in todo:
                    engine.wait_ge(s, val)
                    nw += 1
                ins = op.fn(engine)
                if last is not None:
                    ins._wait_ge(last[0], last[1])
                if op.is_dma:
                    ins.then_inc(dsems[op.dma[0]], 16)
                elif op.inc:
                    ins.then_inc(sems[ename], 1)
            for op in self.ops[ename]:
                if op.is_dma:
                    key = ("d", op.dma[0])
                    val = 16 * self.dma_n[op.dma[0]]
                    if known.get(key, 0) < val:
                        known[key] = val
                        engine.wait_ge(dsems[op.dma[0]], val)
            stats[ename] = (len(self.ops[ename]), nw)

        block.tensor(lambda e: run("pe", e))
        block.scalar(lambda e: run("act", e))
        block.vector(lambda e: run("dve", e))
        block.gpsimd(lambda e: run("pool", e))
        block.sync(lambda e: run("sp", e))
        self.stats = stats
        return stats


def R(t, *idx):
    return (t.name,) + idx


GBW = 2340
NGB = 8
LAM_INIT = [0.8 - 0.6 * math.exp(-0.3 * l) for l in range(DEPTH)]


def host_consts():
    c = {}
    bf = ml_dtypes.bfloat16
    c["ident_f"] = np.eye(128, dtype=np.float32)
    c["ident_b"] = np.eye(128, dtype=np.float32).astype(bf)
    c["ones_b"] = np.ones((128, 128), dtype=np.float32).astype(bf)
    pm = np.zeros((128, 128), np.float32)
    sign = np.zeros(128, np.float64)
    for f in range(128):
        fl = f % 64
        q = fl // 16
        src = f + 16 if q in (0, 2) else f - 16
        pm[src, f] = 1.0
        sign[f] = -1.0 if q in (0, 2) else 1.0
    c["perm_b"] = pm.astype(bf)
    t = np.arange(T)
    r = (t // 64).astype(np.float64)
    col = (t % 64).astype(np.float64)
    inv = 10000.0 ** (-np.arange(16, dtype=np.float64) / 16)
    ang64 = np.concatenate([r[:, None] * inv, r[:, None] * inv, col[:, None] * inv, col[:, None] * inv], axis=1)
    ang = np.concatenate([ang64, ang64], axis=1).T
    c["cosT"] = np.cos(ang).astype(np.float32).astype(bf)
    c["sinT"] = (np.sin(ang) * sign[:, None]).astype(np.float32).astype(bf)
    il = np.arange(128)
    mp = (il[None, :] <= il[:, None]).astype(np.float32)
    mn = (il[:, None] <= il[None, :]).astype(np.float32)
    c["maskp"] = np.concatenate([mp, mp], axis=1).astype(bf)
    c["maskn"] = np.concatenate([mn, mn], axis=1).astype(bf)
    s_ = il[:, None].astype(np.float32)
    t_ = il[None, :].astype(np.float32)
    c["r1"] = np.maximum(t_ - s_, 0.0).astype(np.float32)
    c["r2"] = np.maximum(s_ - t_, 0.0).astype(np.float32)
    c["i1"] = np.broadcast_to((il + 1.0)[None, :], (128, 128)).astype(np.float32).copy()
    c["i2"] = np.broadcast_to((128.0 - il)[None, :], (128, 128)).astype(np.float32).copy()
    c["kcol"] = np.stack([127.0 - il, il * 1.0], axis=1).astype(np.float32)
    return c


CONST_SPECS = [("ident_f", [128, 128], F32), ("ident_b", [128, 128], BF16), ("ones_b", [128, 128], BF16),
               ("perm_b", [128, 128], BF16), ("cosT", [128, T], BF16), ("sinT", [128, T], BF16),
               ("maskp", [128, 256], BF16), ("maskn", [128, 256], BF16),
               ("r1", [128, 128], F32), ("r2", [128, 128], F32), ("i1", [128, 128], F32), ("i2", [128, 128], F32),
               ("kcol", [128, 2], F32)]

PARAM_SPECS = [("g_final", [128, KC]), ("cc", [128, KC, 2]), ("bada", [128, DEPTH, 48]),
               ("gmix", [128, DEPTH, KC]), ("gmlp", [128, DEPTH, KC]),
               ("lamq", [128, DEPTH, 2, 64]), ("lamk", [128, DEPTH, 2, 64]), ("sublng", [128, DEPTH, 128]),
               ("decbc", [128, DEPTH, 8]), ("deccol", [128, DEPTH, 2, 2]), ("sinkbc", [128, DEPTH, 4])]


def prep_inputs(inputs, b, depth=DEPTH):
    f = np.float32
    m = {}
    m["x"] = np.ascontiguousarray(inputs["x"][b], dtype=f)
    m["ctx"] = np.ascontiguousarray(inputs["ctx"][b], dtype=f)
    m["g_final"] = np.ascontiguousarray(inputs["g_final"].reshape(KC, 128).T, dtype=f)
    cc = np.stack([inputs["c"][b].reshape(KC, 128).T, inputs["c_ctx"].reshape(KC, 128).T], axis=2)
    m["cc"] = np.ascontiguousarray(cc, dtype=f)
    m["bada"] = np.ascontiguousarray(inputs["b_ada"].reshape(DEPTH, 48, 128).transpose(2, 0, 1), dtype=f)
    m["gmix"] = np.ascontiguousarray(inputs["g_mix"].reshape(DEPTH, KC, 128).transpose(2, 0, 1), dtype=f)
    m["gmlp"] = np.ascontiguousarray(inputs["g_mlp"].reshape(DEPTH, KC, 128).transpose(2, 0, 1), dtype=f)
    lamq = np.stack([inputs["lam_q1"], inputs["lam_q2"]], axis=1)
    lamk = np.stack([inputs["lam_k1"], inputs["lam_k2"]], axis=1)
    m["lamq"] = np.ascontiguousarray(np.broadcast_to(lamq[None], (128, DEPTH, 2, 64)), dtype=f)
    m["lamk"] = np.ascontiguousarray(np.broadcast_to(lamk[None], (128, DEPTH, 2, 64)), dtype=f)
    m["sublng"] = np.ascontiguousarray(np.broadcast_to(inputs["subln_g"][None], (128, DEPTH, 128)), dtype=f)
    dec = np.concatenate([inputs["ret_decay_fwd"], inputs["ret_decay_bwd"]], axis=1)
    m["decbc"] = np.ascontiguousarray(np.broadcast_to(dec[None], (128, DEPTH, 8)), dtype=f)
    dcol = np.zeros((128, DEPTH, 2, 2), f)
    for di, nm in enumerate(["ret_decay_fwd", "ret_decay_bwd"]):
        for pp in range(2):
            dcol[0:64, :, di, pp] = inputs[nm][:, 2 * pp][None, :]
            dcol[64:128, :, di, pp] = inputs[nm][:, 2 * pp + 1][None, :]
    m["deccol"] = dcol
    m["sinkbc"] = np.ascontiguousarray(np.broadcast_to(inputs["sink_logit"][None], (128, DEPTH, 4)), dtype=f)
    for k in ["w_ada", "w_in", "w_out", "w_mlp1", "w_mlp2"]:
        m[k] = np.ascontiguousarray(inputs[k][:depth], dtype=f)
    m.update(host_consts())
    return m


class Builder:
    def __init__(self, cfg):
        self.cfg = cfg
        self.depth = cfg.get("depth", DEPTH)
        self.nc = bass.Bass("TRN2", target_bir_lowering=False)
        self.S = Sched()
        self.stack = contextlib.ExitStack()
        self.gp_i = 0
        self.ab_i = 0
        self.pt_i = 0
        self.wst_i = 0
        self.wo_i = 0
        self.oacc_i = 0
        self.rope_i = 0
        self.dbg_names = []
        self.deferred = []

    def dram_in(self, name, shape, dt=F32):
        return self.nc.dram_tensor(name, list(shape), dt, kind="ExternalInput").ap()

    def sb(self, name, shape, dt):
        return self.stack.enter_context(self.nc.sbuf_tensor(name, list(shape), dt))

    def ps(self, name, shape, dt=F32):
        return self.stack.enter_context(self.nc.psum_tensor(name, list(shape), dt))

    def dma(self, out, in_, reads=(), writes=(), key=None, eng="sp", **kw):
        self.S.add(eng, lambda e: e.dma_start(out=out, in_=in_, **kw), reads=reads, writes=writes, dma=key)

    def mm(self, out, lhsT, rhs, start=True, stop=True, reads=(), writes=(), sgc=False):
        if sgc:
            o = self.S.add("pe", lambda e: e.matmul(out, lhsT=lhsT, rhs=rhs, start=start, stop=stop, skip_group_check=True),
                           reads=reads, writes=writes)
        else:
            o = self.S.add("pe", lambda e: e.matmul(out, lhsT=lhsT, rhs=rhs, start=start, stop=stop),
                           reads=reads, writes=writes)
        o.embed = True

    def tr(self, out, in_, ident, reads=(), writes=()):
        o = self.S.add("pe", lambda e: e.transpose(out, in_, ident), reads=reads, writes=writes)
        o.embed = (ident.dtype == BF16)

    def op(self, eng, meth, reads=(), writes=(), **kw):
        o = self.S.add(eng, lambda e: getattr(e, meth)(**kw), reads=reads, writes=writes)
        o.embed = eng in ("act", "dve") or (eng == "pool" and meth in ("tensor_tensor", "tensor_copy"))

    def gp(self):
        t = self.PSG[self.gp_i % len(self.PSG)]
        self.gp_i += 1
        return t

    def ab(self):
        i = self.ab_i % 4
        self.ab_i += 1
        return self.OACC[i // 2][:, (i % 2) * 512:(i % 2 + 1) * 512], ("oacc", i // 2, i % 2)

    def ptb(self):
        t = self.PT[self.pt_i % len(self.PT)]
        self.pt_i += 1
        return t

    def gb(self, i):
        return self.GBT[:, i * GBW:(i + 1) * GBW]

    def dbg(self, name, ap, shape, dt, reads):
        d = self.nc.dram_tensor(name, list(shape), dt, kind="ExternalOutput").ap()
        self.dma(d, ap, reads=reads, key="dbg_" + name)
        self.dbg_names.append(name)

    def build(self):
        nc, S = self.nc, self.S
        cfg = self.cfg
        self.x_d = self.dram_in("x", [T, D])
        self.ctx_d = self.dram_in("ctx", [L, D])
        self.out_d = nc.dram_tensor("out", [T, D], F32, kind="ExternalOutput").ap()
        self.cd = {n: self.dram_in(n, shp, dt) for n, shp, dt in CONST_SPECS}
        self.pd = {n: self.dram_in(n, shp, F32) for n, shp in PARAM_SPECS}
        self.wada_d = self.dram_in("w_ada", [self.depth, D, 6 * D])
        self.win_d = self.dram_in("w_in", [self.depth, D, 3 * D])
        self.wout_d = self.dram_in("w_out", [self.depth, D, D])
        self.w1_d = self.dram_in("w_mlp1", [self.depth, D, 4 * D])
        self.w2_d = self.dram_in("w_mlp2", [self.depth, 4 * D, D])

        self.XT = self.sb("XT", [128, KC, NT], F32)
        self.HT = self.sb("HT", [128, KC, NT], BF16)
        self.GBT = self.sb("GBT", [128, NGB * GBW], BF16)
        self.WST = [self.sb("WST%d" % i, [128, KC, 512], BF16) for i in range(2)]
        self.WO = [self.sb("WO%d" % i, [128, D], BF16) for i in range(2)]
        self.C = {}
        for n, shp, dt in CONST_SPECS:
            t = self.sb("c_" + n, shp, dt)
            self.C[n] = t
            self.dma(t[:], self.cd[n], writes=[R(t)], key="c_" + n)
        self.O0N = self.sb("o0n", [128, 4, 128], F32)
        self.OA = self.sb("oa", [128, 4, 128], F32)
        self.OB = self.sb("ob", [128, 4, 128], F32)
        self.GA = self.sb("GA", [128, DEPTH, 128], F32)
        self.P = {}
        alias = {"lamq": self.O0N, "lamk": self.OB, "sublng": self.GA}
        for n, shp in PARAM_SPECS:
            if n in alias:
                t = alias[n]
                self.dma(t[:].rearrange("p a b -> p (a b)"), self.pd[n].rearrange("p a b c -> p (a b c)") if len(shp) == 4 else self.pd[n].rearrange("p a b -> p (a b)"), writes=[R(t)], key="p_" + n)
            else:
                t = self.sb("p_" + n, shp, F32)
                self.dma(t[:], self.pd[n], writes=[R(t)], key="p_" + n)
            self.P[n] = t
        self.iobuf = [self.gb(i)[:, 0:2048].bitcast(F32) for i in range(2)]
        self.iores = [[(("GB", i), g) for g in range(5)] for i in range(2)]
        self.tmpf = [self.sb("tmpf%d" % i, [128, 512], F32) for i in range(4)]
        self.PT = [self.sb("pt%d" % i, [128, 512], BF16) for i in range(4)]
        self.sqb = [self.PT[0], self.PT[1]]
        self.ropeq = [self.PT[2], self.PT[3]]
        self.rstd = self.tmpf[2]
        self.rstd0 = self.tmpf[3]
        self.small = self.sb("small", [128, 64], F32)
        self.epsb = self.sb("epsb", [128, 4], F32)
        S.add("pool", lambda e: e.memset(self.epsb[:, 0:1], float(D * EPS)), writes=[R(self.epsb)])
        S.add("pool", lambda e: e.memset(self.epsb[:, 1:2], float(EPS)), writes=[R(self.epsb)])
        S.add("pool", lambda e: e.memset(self.epsb[:, 2:3], 1.0), writes=[R(self.epsb)])
        self.MOD = self.sb("MOD", [128, DEPTH, 48, 2], F32)
        self.GS = [self.sb("GS%d" % i, [128, DEPTH, KC, 2], F32) for i in range(2)]
        self.G32 = [self.sb("G32_%d" % i, [128, DEPTH, KC], F32) for i in range(2)]
        self.siluT = self.sb("siluT", [128, KC, 2], BF16)
        self.LG = self.sb("LG", [128, DEPTH, 8], F32)
        self.LGC = self.sb("LGC", [128, DEPTH, 2, 2], F32)
        self.ESINK = self.sb("ESINK", [128, DEPTH, 4], F32)
        self.NEGLAM = self.sb("NEGLAM", [128, DEPTH], F32)
        self.lame = self.sb("lame", [128, DEPTH, 2], F32)
        _tqf = self.sb("TQF", [128, 128], F32)
        _tqb = self.sb("TQB", [128, 128], F32)
        _dt = self.sb("DT", [128, 256], F32)
        _tk = self.sb("TK", [128, 4], F32)
        _cfb = self.sb("CFB", [128, 2], F32)
        self.TQF, self.TQB, self.DT, self.TK, self.CFB = [_tqf] * 2, [_tqb] * 2, [_dt] * 2, [_tk] * 2, [_cfb] * 2
        self.stf = self.sb("stf", [128, 128], F32)
        self.stb = self.sb("stb", [128, 128], F32)
        self.kfb = [self.sb("kfb%d" % i, [128, 128], BF16) for i in range(4)]
        self.qfb = [self.sb("qfb%d" % i, [128, 128], BF16) for i in range(4)]
        self.PSG = [self.ps("ps%d" % i, [128, 512], F32) for i in range(4)]
        self.OACC = [self.ps("oacc%d" % i, [128, 1024], F32) for i in range(2)]

        self.load_input()
        self.prologue_params()
        self.adaln_all()
        for l in range(self.depth):
            need_ctx = l < DEPTH - 1
            self.norm(l, 0, skip_ctx=False)
            if cfg.get("dbg_h") == l:
                self.dbg("dbg_h", self.HT[:], [128, KC, NT], BF16, [("HT", kc, g) for kc in range(KC) for g in range(5)])
            order = []
            mx = cfg.get("mixers", "ABC")
            if "A" in mx:
                order += [("A", h) for h in range(4)]
            if "B" in mx:
                order += [("B", 0), ("B", 1)]
            if "C" in mx:
                order += [("C", 0)]
            if "A" in mx:
                g = [self.chunk_A(l, h, need_ctx) for h in range(4)]
                if cfg.get("a_overlap", True) and cfg.get("a_stage", 9) >= 4:
                    next(g[0])
                    next(g[0])
                    for h in range(4):
                        if h < 3:
                            next(g[h + 1])
                        next(g[h])
                        if h < 3:
                            next(g[h + 1])
                        for _ in g[h]:
                            pass
                else:
                    for h in range(4):
                        for _ in g[h]:
                            pass
            for kind, i in order:
                if kind == "A":
                    continue
                elif kind == "B":
                    self.chunk_B(l, i, need_ctx)
                else:
                    self.chunk_C(l, need_ctx)
            if cfg.get("mlp", True):
                self.mlp(l, need_ctx)
        if cfg.get("dbg_x"):
            self.dbg("dbg_x", self.XT[:], [128, KC, NT], F32, [("XT", kc, g) for kc in range(KC) for g in range(5)])
        self.drain()
        self.final_norm()
        S.emit(nc, self.stack)
        return nc

    def xt_res(self, kc, gi):
        return ("XT", kc, gi)

    def load_input(self):
        S = self.S
        XT, PS, identf = self.XT, self.PSG, self.C["ident_f"]
        for tb in range(NTB):
            buf = self.iobuf[tb % 2]
            bres = self.iores[tb % 2]
            gi = min(tb // 4, 4)
            src = self.x_d[tb * 128:(tb + 1) * 128, :] if tb < 16 else self.ctx_d[(tb - 16) * 128:(tb - 15) * 128, :]
            self.dma(buf, src, writes=bres, key="io%d" % (tb % 2))
            for half in range(2):
                pt = self.gp()
                for j in range(4):
                    kc = half * 4 + j
                    self.tr(pt[:, j * 128:(j + 1) * 128], buf[:, kc * 128:(kc + 1) * 128], identf[:],
                            reads=bres + [R(identf)], writes=[R(pt)])
                self.op("dve", "tensor_copy", out=XT[:, half * 4:half * 4 + 4, tb * 128:(tb + 1) * 128],
                        in_=pt[:].rearrange("p (j t) -> p j t", j=4),
                        reads=[R(pt)], writes=[("XT", k, gi) for k in range(half * 4, half * 4 + 4)])

    def prologue_params(self):
        P = self.P
        sm = self.small
        self.op("act", "activation", out=self.siluT[:], in_=P["cc"][:], func=AF.Silu,
                reads=[R(P["cc"])], writes=[R(self.siluT)])
        self.op("dve", "tensor_scalar", out=self.G32[0][:], in0=P["gmix"][:], scalar1=32.0, scalar2=None, op0=ALU.mult,
                reads=[R(P["gmix"])], writes=[R(self.G32[0])])
        self.op("dve", "tensor_scalar", out=self.G32[1][:], in0=P["gmlp"][:], scalar1=32.0, scalar2=None, op0=ALU.mult,
                reads=[R(P["gmlp"])], writes=[R(self.G32[1])])
        for src, dst, n in ((P["decbc"], self.LG, DEPTH * 8), (P["deccol"], self.LGC, DEPTH * 4)):
            sv = src[:].rearrange("p a b -> p (a b)") if len(src.shape) == 3 else src[:].rearrange("p a b c -> p (a b c)")
            dv = dst[:].rearrange("p a b -> p (a b)") if len(dst.shape) == 3 else dst[:].rearrange("p a b c -> p (a b c)")
            self.op("act", "activation", out=sm[:, 0:n], in_=sv, func=AF.Exp, scale=-1.0,
                    reads=[R(src)], writes=[R(sm)])
            self.op("act", "activation", out=sm[:, 32:32 + n], in_=sm[:, 0:n], func=AF.Ln, bias=self.epsb[:, 2:3], scale=1.0,
                    reads=[R(sm), R(self.epsb)], writes=[R(sm)])
            self.op("dve", "tensor_scalar", out=dv, in0=sm[:, 32:32 + n], scalar1=-1.0, scalar2=None, op0=ALU.mult,
                    reads=[R(sm)], writes=[R(dst)])
        self.op("act", "activation", out=self.ESINK[:], in_=P["sinkbc"][:], func=AF.Exp,
                reads=[R(P["sinkbc"])], writes=[R(self.ESINK)])
        fl = lambda t: t[:].rearrange("p a b -> p (a b)")
        self.op("dve", "tensor_tensor", out=fl(self.OA), in0=fl(P["lamq"]), in1=fl(P["lamk"]), op=ALU.mult,
                reads=[R(P["lamq"]), R(P["lamk"])], writes=[R(self.OA)])
        self.op("dve", "tensor_reduce", out=self.lame[:].rearrange("p a b -> p (a b)"),
                in_=fl(self.OA).rearrange("p (a c) -> p a c", c=64), axis=AX.X, op=ALU.add,
                reads=[R(self.OA)], writes=[R(self.lame)])
        self.op("act", "activation", out=self.lame[:], in_=self.lame[:], func=AF.Exp,
                reads=[R(self.lame)], writes=[R(self.lame)])
        for l in range(DEPTH):
            self.op("dve", "tensor_scalar", out=self.NEGLAM[:, l:l + 1], in0=self.lame[:, l, 1:2],
                    scalar1=self.lame[:, l, 0:1], scalar2=-LAM_INIT[l], op0=ALU.subtract, op1=ALU.add,
                    reads=[R(self.lame)], writes=[R(self.NEGLAM)])
            self.op("dve", "tensor_scalar", out=self.GA[:, l, :], in0=self.GA[:, l, :],
                    scalar1=1.0 - LAM_INIT[l], scalar2=None, op0=ALU.mult,
                    reads=[R(self.GA)], writes=[R(self.GA)])

    def wst_load(self, src3, pieces, keyname="wst"):
        i = self.wst_i % 2
        self.wst_i += 1
        buf = self.WST[i]
        res = ("WST", i)
        for (sc, dc, n) in pieces:
            self.dma(buf[:, :, dc:dc + n], src3[:, :, sc:sc + n], writes=[res], key="wst%d" % i, eng="pool")
        return buf, res

    def adaln_gs(self, l):
        for w, base in ((0, 8), (1, 32)):
            self.op("dve", "scalar_tensor_tensor", out=self.GS[w][:, l, :, :], in0=self.MOD[:, l, base:base + 8, :],
                    scalar=1.0, in1=self.G32[w][:, l, :].unsqueeze(2).to_broadcast([128, KC, 2]),
                    op0=ALU.add, op1=ALU.mult,
                    reads=[("MOD", l), R(self.G32[w])], writes=[("GS", w, l)])

    def adaln_unit_load(self, l, j, wbuf, i):
        src3 = self.wada_d[l].rearrange("(kc p) n -> p kc n", p=128)
        self.dma(wbuf[:, :, 384:512], src3[:, :, j * 128:(j + 1) * 128], writes=[("WSTx", i)], key="wax%d" % i, eng="pool")

    def adaln_unit_compute(self, l, j, wbuf, i):
        pst = self.gp()
        for kc in range(KC):
            self.mm(pst[:, 0:2], wbuf[:, kc, 384:512], self.siluT[:, kc, :], start=(kc == 0), stop=(kc == KC - 1),
                    reads=[("WSTx", i), R(self.siluT)], writes=[R(pst)])
        self.op("dve", "tensor_scalar", out=self.MOD[:, l, j, :], in0=pst[:, 0:2], scalar1=self.P["bada"][:, l, j:j + 1],
                scalar2=None, op0=ALU.add, reads=[R(pst), R(self.P["bada"])], writes=[("MOD", l)])

    def adaln_all(self):
        P = self.P
        for l in range(1):
            src3 = self.wada_d[l].rearrange("(kc p) n -> p kc n", p=128)
            for piece in range(12):
                buf, res = self.wst_load(src3, [(piece * 512, 0, 512)])
                pst = self.gp()
                for jc in range(4):
                    for kc in range(KC):
                        self.mm(pst[:, jc * 2:jc * 2 + 2], buf[:, kc, jc * 128:(jc + 1) * 128], self.siluT[:, kc, :],
                                start=(kc == 0), stop=(kc == KC - 1), reads=[res, R(self.siluT)], writes=[R(pst)])
                j0 = piece * 4
                self.op("dve", "tensor_tensor", out=self.MOD[:, l, j0:j0 + 4, :],
                        in0=pst[:, 0:8].rearrange("p (j s) -> p j s", s=2),
                        in1=P["bada"][:, l, j0:j0 + 4].unsqueeze(2).to_broadcast([128, 4, 2]), op=ALU.add,
                        reads=[R(pst), R(P["bada"])], writes=[("MOD", l)])
            for w, base in ((0, 8), (1, 32)):
                self.op("dve", "scalar_tensor_tensor", out=self.GS[w][:, l, :, :], in0=self.MOD[:, l, base:base + 8, :],
                        scalar=1.0, in1=self.G32[w][:, l, :].unsqueeze(2).to_broadcast([128, KC, 2]),
                        op0=ALU.add, op1=ALU.mult,
                        reads=[("MOD", l), R(self.G32[w])], writes=[("GS", w, l)])

    def sumsq_rstd(self, gi, acc):
        t0, n = TGS[gi]
        XT, onesb = self.XT, self.C["ones_b"]
        for kc in range(KC):
            sq = self.sqb[kc % 2]
            self.op("act", "activation", out=sq[:, :n], in_=XT[:, kc, t0:t0 + n], func=AF.Square,
                    reads=[("XT", kc, gi)], writes=[R(sq)])
            self.mm(acc[:, :n], onesb[:], sq[:, :n], start=(kc == 0), stop=(kc == KC - 1),
                    reads=[R(sq), R(onesb)], writes=[R(acc)])
        self.op("act", "activation", out=self.rstd0[:, :n], in_=acc[:, :n], func=AF.Ln, bias=self.epsb[:, 0:1], scale=1.0,
                reads=[R(acc), R(self.epsb)], writes=[R(self.rstd0)])
        self.op("act", "activation", out=self.rstd[:, :n], in_=self.rstd0[:, :n], func=AF.Exp, scale=-0.5,
                reads=[R(self.rstd0)], writes=[R(self.rstd)])

    def norm(self, l, w, skip_ctx):
        shbase = 0 if w == 0 else 24
        for gi, (t0, n) in enumerate(TGS):
            if gi == 4 and skip_ctx:
                continue
            s = 0 if gi < 4 else 1
            acc = self.gp()
            self.sumsq_rstd(gi, acc)
            for kc in range(KC):
                tm = self.tmpf[kc % 2]
                self.op("dve", "scalar_tensor_tensor", out=tm[:, :n], in0=self.XT[:, kc, t0:t0 + n],
                        scalar=self.GS[w][:, l, kc, s:s + 1], in1=self.rstd[:, :n], op0=ALU.mult, op1=ALU.mult,
                        reads=[("XT", kc, gi), ("GS", w, l), R(self.rstd)], writes=[R(tm)])
                if kc % 2 == 0:
                    self.op("act", "activation", out=self.HT[:, kc, t0:t0 + n], in_=tm[:, :n], func=AF.Identity,
                            bias=self.MOD[:, l, shbase + kc, s:s + 1], scale=1.0,
                            reads=[R(tm), ("MOD", l)], writes=[("HT", kc, gi)])
                else:
                    self.op("dve", "tensor_scalar", out=self.HT[:, kc, t0:t0 + n], in0=tm[:, :n],
                            scalar1=self.MOD[:, l, shbase + kc, s:s + 1], scalar2=None, op0=ALU.add,
                            reads=[R(tm), ("MOD", l)], writes=[("HT", kc, gi)])

    def proj_fm(self, wbuf, wres, c0, dst, dname, rope, with_ctx, dst_fn=None):
        if dst_fn is None:
            dst_fn = lambda t0, n: [(dst[:, t0:t0 + n], slice(0, 128))]
        pending = None
        for gi, (t0, n) in enumerate(TGS):
            if gi == 4 and not with_ctx:
                continue
            ps = self.gp()
            for kc in range(KC):
                self.mm(ps[:, :n], wbuf[:, kc, c0:c0 + 128], self.HT[:, kc, t0:t0 + n], start=(kc == 0), stop=(kc == KC - 1),
                        reads=[wres, ("HT", kc, gi)], writes=[R(ps)])
            if rope and gi < 4:
                tail = self.rope_head(ps, n)
                if pending is not None:
                    self.rope_tail(*pending)
                pending = (ps, dst_fn(t0, n), t0, n, (dname, gi)) + tail
            else:
                for (oap, psl) in dst_fn(t0, n):
                    self.op("act", "activation", out=oap, in_=ps[psl, :n], func=AF.Copy,
                            reads=[R(ps)], writes=[(dname, gi)])
        if pending is not None:
            self.rope_tail(*pending)

    def rope_head(self, ps, n):
        i = self.rope_i % 2
        self.rope_i += 1
        qb = self.ropeq[i]
        self.op("act", "activation", out=qb[:, :n], in_=ps[:, :n], func=AF.Copy, reads=[R(ps)], writes=[R(qb)])
        return (i, qb)

    def rope_tail(self, ps, dsts, t0, n, dres, i, qb):
        t1, t2 = self.tmpf[2 * i], self.tmpf[2 * i + 1]
        permb, cosT, sinT = self.C["perm_b"], self.C["cosT"], self.C["sinT"]
        ps2 = self.gp()
        self.mm(ps2[:, :n], permb[:], qb[:, :n], reads=[R(qb), R(permb)], writes=[R(ps2)])
        self.op("dve", "tensor_tensor", out=t1[:, :n], in0=ps[:, :n], in1=cosT[:, t0:t0 + n], op=ALU.mult,
                reads=[R(ps), R(cosT)], writes=[R(t1)])
        self.op("dve", "tensor_tensor", out=t2[:, :n], in0=ps2[:, :n], in1=sinT[:, t0:t0 + n], op=ALU.mult,
                reads=[R(ps2), R(sinT)], writes=[R(t2)])
        for (oap, psl) in dsts:
            self.op("pool", "tensor_tensor", out=oap, in0=t1[psl, :n], in1=t2[psl, :n], op=ALU.add,
                    reads=[R(t1), R(t2)], writes=[dres])

    def proj_tm(self, wbuf, wres, c0, dst_fn, dname, func=None):
        for tb0 in range(0, NTB, 4):
            nb = min(4, NTB - tb0)
            gi = min(tb0 // 4, 4)
            ps = self.gp()
            for j in range(nb):
                tb = tb0 + j
                for kc in range(KC):
                    self.mm(ps[:, j * 128:(j + 1) * 128], self.HT[:, kc, tb * 128:(tb + 1) * 128], wbuf[:, kc, c0:c0 + 128],
                            start=(kc == 0), stop=(kc == KC - 1), reads=[wres, ("HT", kc, gi)], writes=[R(ps)])
            out, in_ = dst_fn(tb0, nb, ps)
            self.op("act", "activation", out=out, in_=in_, func=(func or AF.Copy), reads=[R(ps)], writes=[(dname, gi)])

    def finish_chunk(self, l, ci, OC3, ocname, OCT, octname, need_ctx):
        self.drain()
        i = self.wo_i % 2
        self.wo_i += 1
        WO = self.WO[i]
        wres = ("WO", i)
        self.dma(WO[:], self.wout_d[l][ci * 128:(ci + 1) * 128, :], writes=[wres], key="wo%d" % i, eng="pool")
        identb = self.C["ident_b"]
        ntb = NTB if need_ctx else 16
        for tb0 in range(0, ntb, 4):
            nb = min(4, ntb - tb0)
            gi = min(tb0 // 4, 4)
            pk = self.gp()
            pkb = pk[:].bitcast(BF16)
            for j in range(nb):
                self.tr(pkb[:, j * 128:(j + 1) * 128], OC3[:, tb0 + j, :], identb[:],
                        reads=[(ocname, gi), R(identb)], writes=[R(pk)])
            self.op("dve", "tensor_copy", out=OCT[:, tb0 * 128:(tb0 + nb) * 128], in_=pkb[:, 0:nb * 128],
                    reads=[R(pk)], writes=[(octname, gi)])
        for gi, (t0, n) in enumerate(TGS):
            if gi == 4 and not need_ctx:
                continue
            s = 0 if gi < 4 else 1
            for dc in range(KC):
                def item(use_gp, gi=gi, t0=t0, n=n, s=s, dc=dc):
                    if use_gp:
                        ps = self.gp()
                        pres = R(ps)
                    else:
                        ps, pres = self.ab()
                    self.mm(ps[:, :n], WO[:, dc * 128:(dc + 1) * 128], OCT[:, t0:t0 + n],
                            reads=[wres, (octname, gi)], writes=[pres])
                    self.op("dve", "scalar_tensor_tensor", out=self.XT[:, dc, t0:t0 + n], in0=ps[:, :n],
                            scalar=self.MOD[:, l, 16 + dc, s:s + 1], in1=self.XT[:, dc, t0:t0 + n], op0=ALU.mult, op1=ALU.add,
                            reads=[pres, ("MOD", l), ("XT", dc, gi)], writes=[("XT", dc, gi)])
                self.deferred.append(item)
        self.drain(use_gp=True)

    def drain(self, k=None, use_gp=False):
        while self.deferred and (k is None or k > 0):
            self.deferred.pop(0)(use_gp)
            if k is not None:
                k -= 1

    def chunk_A(self, l, h, need_ctx):
        src3 = self.win_d[l].rearrange("(kc p) n -> p kc n", p=128)
        wbuf, wres = self.wst_load(src3, [(h * 128, 0, 128), (512 + h * 128, 128, 128), (1024 + h * 128, 256, 128)])
        par = h % 2
        QT, KT, V = self.gb(par * 3), self.gb(par * 3 + 1), self.gb(par * 3 + 2)
        qn, kn, vn = ("GB", par * 3), ("GB", par * 3 + 1), ("GB", par * 3 + 2)
        OC, OCT = self.gb(6), self.gb(7)
        V3 = V.rearrange("p (t c) -> p t c", c=130)
        OC3 = OC[:, 0:NT].rearrange("p (t c) -> p t c", c=128)
        allg = list(range(5))
        self.op("pool", "memset", ap=V3[:, :, 128:129], constant=1.0, writes=[(vn, g) for g in allg])
        yield "loaded"
        stage = self.cfg.get("a_stage", 9)
        if stage < 0:
            return
        self.proj_fm(wbuf, wres, 0, QT, qn, stage >= 0.5, need_ctx)
        if stage < 0.7:
            return
        self.proj_fm(wbuf, wres, 128, KT, kn, True, True)
        if stage < 0.8:
            return
        self.proj_tm(wbuf, wres, 256,
                     lambda tb0, nb, ps: (V3[:, tb0:tb0 + nb, 0:128], ps[:, 0:nb * 128].rearrange("p (j c) -> p j c", c=128)),
                     vn)
        if stage < 2:
            return
        yield "proj"
        itc = 0
        wi = wres[1]
        ada_l = l + 1 if (l + 1 < self.depth and self.cfg.get("ada_il", True)) else None
        ada_units = list(range(h * 12, h * 12 + 12)) if ada_l is not None else []
        ada_loaded = None
        for gi, (t0, n) in enumerate(TGS):
            if gi == 4 and not need_ctx:
                continue
            kbs = list(range(NTB)) if gi < 4 else [16, 17]
            nqb = n // 128
            nbk = nqb // 2
            aress = [[("oacc", c, 0), ("oacc", c, 1)] for c in range(2)]

            def emit_pv(ki, kb, pts):
                kgi = min(kb // 4, 4)
                for c in range(2):
                    acc = self.OACC[c]
                    for qb in range(nqb):
                        off = (qb // 2) * 512 + (qb % 2) * 129
                        self.mm(acc[:, off:off + 129], pts[c][:, qb * 128:(qb + 1) * 128], V3[:, kb, 0:129],
                                start=(ki == 0 and qb % 2 == 0), stop=(ki == len(kbs) - 1),
                                reads=[R(pts[c]), (vn, kgi)], writes=[aress[c][qb // 2]], sgc=True)

            pending = None
            for ki, kb in enumerate(kbs):
                kgi = min(kb // 4, 4)
                sts = [self.gp(), self.gp()]
                for c in range(2):
                    hs = slice(c * 64, (c + 1) * 64)
                    self.mm(sts[c][:, :n], KT[hs, kb * 128:(kb + 1) * 128], QT[hs, t0:t0 + n],
                            reads=[(kn, kgi), (qn, gi)], writes=[R(sts[c])])
                pts = [self.ptb(), self.ptb()]
                for c in range(2):
                    self.op("act", "activation", out=pts[c][:, :n], in_=sts[c][:, :n], func=AF.Exp, scale=0.125,
                            reads=[R(sts[c])], writes=[R(pts[c])])
                if pending is not None:
                    emit_pv(*pending)
                pending = (ki, kb, pts)
                if ada_l is not None and gi < 4:
                    if itc % 6 == 5 and ada_loaded is not None:
                        self.adaln_unit_compute(ada_l, ada_loaded, wbuf, wi)
                        ada_loaded = None
                    if itc % 6 == 0 and ada_units:
                        ada_loaded = ada_units.pop(0)
                        self.adaln_unit_load(ada_l, ada_loaded, wbuf, wi)
                    itc += 1
            emit_pv(*pending)
            if stage < 3:
                continue
            for c in range(2):
                acc = self.OACC[c]
                ares = aress[c]
                accv = acc[:].rearrange("p (b x) -> p b x", b=2)[:, 0:nbk, 0:258].rearrange("p b (j c) -> p b j c", c=129)
                zv = accv[:, :, :, 128]
                ov = accv[:, :, :, 0:128]
                sm = self.small
                rz = sm[:, 0:nqb].rearrange("p (b j) -> p b j", j=2)
                ar = ares[0:nbk]
                self.op("dve", "reciprocal", out=rz, in_=zv, reads=ar, writes=[R(sm)])
                o0 = self.O0N[:, 0:nqb, :].rearrange("p (b j) c -> p b j c", j=2)
                oa = self.OA[:, 0:nqb, :].rearrange("p (b j) c -> p b j c", j=2)
                if c == 0:
                    self.op("dve", "tensor_tensor", out=o0, in0=ov, in1=rz.unsqueeze(3).to_broadcast([128, nbk, 2, 128]),
                            op=ALU.mult, reads=ar + [R(sm)], writes=[R(self.O0N)])
                else:
                    rz1 = sm[:, 4:4 + nqb].rearrange("p (b j) -> p b j", j=2)
                    self.op("dve", "tensor_scalar", out=rz1, in0=rz, scalar1=self.NEGLAM[:, l:l + 1], scalar2=None,
                            op0=ALU.mult, reads=[R(sm), R(self.NEGLAM)], writes=[R(sm)])
                    self.op("dve", "tensor_tensor", out=oa, in0=ov, in1=rz1.unsqueeze(3).to_broadcast([128, nbk, 2, 128]),
                            op=ALU.mult, reads=ar + [R(sm)], writes=[R(self.OA)])
                    oaf = self.OA[:, 0:nqb, :]
                    obf = self.OB[:, 0:nqb, :]
                    self.op("pool", "tensor_tensor", out=oaf, in0=oaf, in1=self.O0N[:, 0:nqb, :], op=ALU.add,
                            reads=[R(self.OA), R(self.O0N)], writes=[R(self.OA)])
                    self.op("pool", "tensor_tensor", out=obf, in0=oaf, in1=oaf, op=ALU.mult,
                            reads=[R(self.OA)], writes=[R(self.OB)])
                    ss = sm[:, 8:8 + nqb]
                    self.op("dve", "tensor_reduce", out=ss, in_=obf, axis=AX.X, op=ALU.add,
                            reads=[R(self.OB)], writes=[R(sm)])
                    l1 = sm[:, 12:12 + nqb]
                    rs = sm[:, 16:16 + nqb]
                    self.op("act", "activation", out=l1, in_=ss, func=AF.Ln, bias=self.epsb[:, 1:2], scale=1.0 / 128.0,
                            reads=[R(sm), R(self.epsb)], writes=[R(sm)])
                    self.op("act", "activation", out=rs, in_=l1, func=AF.Exp, scale=-0.5,
                            reads=[R(sm)], writes=[R(sm)])
                    self.op("dve", "tensor_tensor", out=obf, in0=oaf, in1=rs.unsqueeze(2).to_broadcast([128, nqb, 128]),
                            op=ALU.mult, reads=[R(self.OA), R(sm)], writes=[R(self.OB)])
                    tbq = t0 // 128
                    self.op("pool", "tensor_tensor", out=OC3[:, tbq:tbq + nqb, :], in0=obf,
                            in1=self.GA[:, l, :].unsqueeze(1).to_broadcast([128, nqb, 128]), op=ALU.mult,
                            reads=[R(self.OB), R(self.GA)], writes=[(("GB", 6), gi)])
        if ada_l is not None:
            assert ada_loaded is None and not ada_units
            if h == 3:
                self.adaln_gs(ada_l)
        if stage < 4:
            return
        yield "attn"
        self.finish_chunk(l, h, OC3, ("GB", 6), OCT, ("GB", 7), need_ctx)

    def ret_tables(self, l, pp):
        C = self.C
        if True:
            lgf = self.LGC[:, l, 0, pp:pp + 1]
            lgb = self.LGC[:, l, 1, pp:pp + 1]
            self.op("act", "activation", out=self.TQF[pp][:], in_=C["i1"][:], func=AF.Exp, scale=lgf,
                    reads=[R(C["i1"]), R(self.LGC)], writes=[R(self.TQF[pp])])
            self.op("act", "activation", out=self.TQB[pp][:], in_=C["i2"][:], func=AF.Exp, scale=lgb,
                    reads=[R(C["i2"]), R(self.LGC)], writes=[R(self.TQB[pp])])
            self.op("act", "activation", out=self.CFB[pp][:, 0:1], in_=lgf, func=AF.Exp, scale=128.0,
                    reads=[R(self.LGC)], writes=[R(self.CFB[pp])])
            self.op("act", "activation", out=self.CFB[pp][:, 1:2], in_=lgb, func=AF.Exp, scale=128.0,
                    reads=[R(self.LGC)], writes=[R(self.CFB[pp])])
            for h2 in range(2):
                h = 2 * pp + h2
                e1 = self.tmpf[0][:, 0:128]
                e2 = self.tmpf[1][:, 0:128]
                self.op("dve", "tensor_scalar", out=e1, in0=C["r1"][:], scalar1=self.LG[:, l, h:h + 1], scalar2=None,
                        op0=ALU.mult, reads=[R(C["r1"]), R(self.LG)], writes=[R(self.tmpf[0])])
                self.op("dve", "scalar_tensor_tensor", out=e2, in0=C["r2"][:], scalar=self.LG[:, l, 4 + h:5 + h], in1=e1,
                        op0=ALU.mult, op1=ALU.add, reads=[R(C["r2"]), R(self.LG), R(self.tmpf[0])], writes=[R(self.tmpf[1])])
                self.op("act", "activation", out=self.DT[pp][:, h2 * 128:(h2 + 1) * 128], in_=e2, func=AF.Exp,
                        reads=[R(self.tmpf[1])], writes=[R(self.DT[pp])])
            sm = self.small
            self.op("act", "activation", out=sm[:, 20:22], in_=self.LG[:, l, 2 * pp:2 * pp + 2], func=AF.Exp,
                    scale=C["kcol"][:, 0:1], reads=[R(self.LG), R(C["kcol"])], writes=[R(sm)])
            self.op("act", "activation", out=sm[:, 22:24], in_=self.LG[:, l, 4 + 2 * pp:6 + 2 * pp], func=AF.Exp,
                    scale=C["kcol"][:, 1:2], reads=[R(self.LG), R(C["kcol"])], writes=[R(sm)])
            self.op("dve", "tensor_scalar", out=self.TK[pp][:], in0=sm[:, 20:24], scalar1=0.125, scalar2=None, op0=ALU.mult,
                    reads=[R(sm)], writes=[R(self.TK[pp])])

    def chunk_B(self, l, pp, need_ctx):
        src3 = self.win_d[l].rearrange("(kc p) n -> p kc n", p=128)
        wbuf, wres = self.wst_load(src3, [(1536 + pp * 128, 0, 128), (1792 + pp * 128, 128, 128),
                                          (2048 + pp * 128, 256, 128), (2304 + pp * 128, 384, 128)])
        names = [("GB", i) for i in range(8)]
        QT, KT, V, G, SF, SB, OC, OCT = [self.gb(i) for i in range(8)]
        qn, kn, vn, gn, sfn, sbn, ocn, octn = names
        V3 = V.rearrange("p (t c) -> p t c", c=130)
        G3 = G[:, 0:NT].rearrange("p (t c) -> p t c", c=128)
        SF3 = SF[:, 0:NT].rearrange("p (t c) -> p t c", c=128)
        SB3 = SB[:, 0:NT].rearrange("p (t c) -> p t c", c=128)
        OC3 = OC[:, 0:NT].rearrange("p (t c) -> p t c", c=128)
        identb = self.C["ident_b"]
        self.ret_tables(l, pp)
        self.proj_fm(wbuf, wres, 0, QT, qn, True, need_ctx)
        self.proj_fm(wbuf, wres, 128, KT, kn, True, True)
        self.proj_tm(wbuf, wres, 256,
                     lambda tb0, nb, ps: (V3[:, tb0:tb0 + nb, 0:128], ps[:, 0:nb * 128].rearrange("p (j c) -> p j c", c=128)),
                     vn)
        self.proj_tm(wbuf, wres, 384,
                     lambda tb0, nb, ps: (G3[:, tb0:tb0 + nb, :], ps[:, 0:nb * 128].rearrange("p (j c) -> p j c", c=128)),
                     gn, func=AF.Silu)
        orders = [[16, 17] + list(range(16)), [17, 16] + list(range(15, -1, -1))]
        sts_ = [self.stf, self.stb]
        ST3s = [SF3, SB3]
        stns = [sfn, sbn]
        for d in range(2):
            self.op("pool", "memset", ap=sts_[d][:], constant=0.0, writes=[R(sts_[d])])
        NS = len(orders[0])
        pks = {}
        pus = {}
        for i in range(NS + 3):
            for d in range(2):
                if i < NS:
                    tb = orders[d][i]
                    gi = min(tb // 4, 4)
                    pk = self.gp()
                    pkb = pk[:].bitcast(BF16)
                    self.tr(pkb[:, 0:128], KT[:, tb * 128:(tb + 1) * 128], identb[:], reads=[(kn, gi), R(identb)], writes=[R(pk)])
                    pks[(d, i)] = (pk, pkb)
                if 0 <= i - 1 < NS:
                    pk, pkb = pks.pop((d, i - 1))
                    kf = self.kfb[2 * d + (i - 1) % 2]
                    self.op("dve", "tensor_tensor", out=kf[:].rearrange("p (h c) -> p h c", h=2),
                            in0=pkb[:, 0:128].rearrange("p (h c) -> p h c", h=2),
                            in1=self.TK[pp][:, 2 * d:2 * d + 2].unsqueeze(2).to_broadcast([128, 2, 64]), op=ALU.mult,
                            reads=[R(pk), R(self.TK[pp])], writes=[R(kf)])
                if 0 <= i - 2 < NS:
                    tb = orders[d][i - 2]
                    gi = min(tb // 4, 4)
                    kf = self.kfb[2 * d + (i - 2) % 2]
                    pu, pures = self.ab()
                    self.mm(pu[:, 0:128], kf[:], V3[:, tb, 0:128], reads=[R(kf), (vn, gi)], writes=[pures])
                    pus[(d, i - 2)] = (pu, pures)
                if 0 <= i - 3 < NS:
                    tb = orders[d][i - 3]
                    gi = min(tb // 4, 4)
                    st = sts_[d]
                    pu, pures = pus.pop((d, i - 3))
                    self.op("pool", "tensor_copy", out=ST3s[d][:, tb, :], in_=st[:], reads=[R(st)], writes=[(stns[d], gi)])
                    self.op("dve", "scalar_tensor_tensor", out=st[:], in0=st[:], scalar=self.CFB[pp][:, d:d + 1], in1=pu[:, 0:128],
                            op0=ALU.mult, op1=ALU.add, reads=[R(st), R(self.CFB[pp]), pures], writes=[R(st)])
        ntb = NTB if need_ctx else 16
        sm = self.small
        cur = {}

        def front(tb):
            gi = min(tb // 4, 4)
            ts_ = slice(tb * 128, (tb + 1) * 128)
            sts = [self.gp(), self.gp()]
            for h2 in range(2):
                hs = slice(h2 * 64, (h2 + 1) * 64)
                self.mm(sts[h2][:, 0:128], KT[hs, ts_], QT[hs, ts_], reads=[(kn, gi), (qn, gi)], writes=[R(sts[h2])])
            pt = self.ptb()
            for h2 in range(2):
                self.op("dve", "scalar_tensor_tensor", out=pt[:, h2 * 128:(h2 + 1) * 128], in0=sts[h2][:, 0:128], scalar=0.125,
                        in1=self.DT[pp][:, h2 * 128:(h2 + 1) * 128],
                        op0=ALU.mult, op1=ALU.mult, reads=[R(sts[h2]), R(self.DT[pp])], writes=[R(pt)])
            qpool = self.qfb + self.kfb
            qf = qpool[(2 * tb) % 8]
            qb_ = qpool[(2 * tb + 1) % 8]
            self.op("pool", "tensor_tensor", out=qf[:], in0=QT[:, ts_], in1=self.TQF[pp][:], op=ALU.mult,
                    reads=[(qn, gi), R(self.TQF[pp])], writes=[R(qf)])
            self.op("pool", "tensor_tensor", out=qb_[:], in0=QT[:, ts_], in1=self.TQB[pp][:], op=ALU.mult,
                    reads=[(qn, gi), R(self.TQB[pp])], writes=[R(qb_)])
            return (pt, qf, qb_)

        def back(tb, pt, qf, qb_):
            gi = min(tb // 4, 4)
            jj = tb % 4
            if jj == 0:
                cur["po"], cur["pres"] = self.ab()
            po, pres = cur["po"], cur["pres"]
            for h2 in range(2):
                hs = slice(h2 * 64, (h2 + 1) * 64)
                oreg = po[:, jj * 128 + h2 * 64:jj * 128 + (h2 + 1) * 64]
                self.mm(oreg, qf[hs, :], SF3[hs, tb, hs], start=(jj == 0 and h2 == 0), stop=False,
                        reads=[R(qf), (sfn, gi)], writes=[pres], sgc=True)
                self.mm(oreg, qb_[hs, :], SB3[hs, tb, hs], start=False, stop=False,
                        reads=[R(qb_), (sbn, gi)], writes=[pres], sgc=True)
                self.mm(oreg, pt[:, h2 * 128:(h2 + 1) * 128], V3[:, tb, hs], start=False, stop=True,
                        reads=[R(pt), (vn, gi)], writes=[pres], sgc=True)
            last = (jj == 3) or (tb == ntb - 1)
            if not last:
                return
            tb0 = tb - jj
            nb = jj + 1
            ng = 2 * nb
            W = nb * 128
            v3 = lambda a: a[:, 0:W].rearrange("p (g c) -> p g c", c=64)
            osb = self.OA[:].rearrange("p a b -> p (a b)")
            ocn_ = self.OB[:].rearrange("p a b -> p (a b)")
            sqv = self.O0N[:].rearrange("p a b -> p (a b)")
            self.op("dve", "tensor_copy", out=osb[:, 0:W], in_=po[:, 0:W], reads=[pres], writes=[R(self.OA)])
            self.op("dve", "tensor_reduce", out=sm[:, 24:24 + ng], in_=v3(osb), axis=AX.X, op=ALU.add,
                    reads=[R(self.OA)], writes=[R(sm)])
            self.op("dve", "tensor_scalar", out=sm[:, 24:24 + ng], in0=sm[:, 24:24 + ng], scalar1=1.0 / 64.0, scalar2=None, op0=ALU.mult,
                    reads=[R(sm)], writes=[R(sm)])
            self.op("dve", "tensor_tensor", out=v3(ocn_), in0=v3(osb), in1=sm[:, 24:24 + ng].unsqueeze(2).to_broadcast([128, ng, 64]),
                    op=ALU.subtract, reads=[R(self.OA), R(sm)], writes=[R(self.OB)])
            self.op("pool", "tensor_tensor", out=sqv[:, 0:W], in0=ocn_[:, 0:W], in1=ocn_[:, 0:W], op=ALU.mult,
                    reads=[R(self.OB)], writes=[R(self.O0N)])
            self.op("dve", "tensor_reduce", out=sm[:, 44:44 + ng], in_=v3(sqv), axis=AX.X, op=ALU.add,
                    reads=[R(self.O0N)], writes=[R(sm)])
            self.op("act", "activation", out=sm[:, 52:52 + ng], in_=sm[:, 44:44 + ng], func=AF.Ln, bias=self.epsb[:, 1:2], scale=1.0 / 64.0,
                    reads=[R(sm), R(self.epsb)], writes=[R(sm)])
            self.op("act", "activation", out=sm[:, 24:24 + ng], in_=sm[:, 52:52 + ng], func=AF.Exp, scale=-0.5,
                    reads=[R(sm)], writes=[R(sm)])
            self.op("dve", "tensor_tensor", out=v3(osb), in0=v3(ocn_), in1=sm[:, 24:24 + ng].unsqueeze(2).to_broadcast([128, ng, 64]),
                    op=ALU.mult, reads=[R(self.OB), R(sm)], writes=[R(self.OA)])
            self.op("pool", "tensor_tensor", out=OC3[:, tb0:tb0 + nb, :], in0=osb[:, 0:W].rearrange("p (t c) -> p t c", c=128),
                    in1=G3[:, tb0:tb0 + nb, :], op=ALU.mult,
                    reads=[R(self.OA), (gn, gi)], writes=[(ocn, gi)])

        pend = []
        for tb in range(ntb):
            fr = front(tb)
            pend.append((tb,) + fr)
            if len(pend) > 3:
                back(*pend.pop(0))
        while pend:
            back(*pend.pop(0))
        self.finish_chunk(l, 4 + pp, OC3, ocn, OCT, octn, need_ctx)

    def chunk_C(self, l, need_ctx):
        src3 = self.win_d[l].rearrange("(kc p) n -> p kc n", p=128)
        wbuf, wres = self.wst_load(src3, [(2560, 0, 256), (2816, 256, 64), (2816, 320, 64), (2880, 384, 64), (2880, 448, 64)])
        wbufv, wresv = self.wst_load(src3, [(2944, 0, 128)])
        QZ = self.GBT[:, 0:2 * NT].rearrange("p (g t) -> p g t", g=2)
        qzn = "QZ"
        gb01 = [(("GB", i), g_) for i in range(2) for g_ in range(5)]
        KT, V = self.gb(2), self.gb(3)
        kn, vn = ("GB", 2), ("GB", 3)
        OC, OCT = self.gb(6), self.gb(7)
        ocn, octn = ("GB", 6), ("GB", 7)
        V4 = V.rearrange("p (t j c) -> p t j c", j=2, c=65)
        OC3 = OC[:, 0:NT].rearrange("p (t c) -> p t c", c=128)
        allg = list(range(5))
        self.op("pool", "memset", ap=V4[:, :, :, 64:65], constant=1.0, writes=[(vn, g) for g in allg])
        self.op("pool", "memset", ap=QZ[64:128, 0, :], constant=0.0, writes=gb01 + [(qzn, g) for g in allg])
        self.op("pool", "memset", ap=QZ[0:64, 1, :], constant=0.0, writes=gb01 + [(qzn, g) for g in allg])
        self.proj_tm(wbufv, wresv, 0,
                     lambda tb0, nb, ps: (V4[:, tb0:tb0 + nb, :, 0:64],
                                          ps[:, 0:nb * 128].rearrange("p (t j c) -> p t j c", j=2, c=64)),
                     vn)
        ntb = NTB if need_ctx else 16
        sm = self.small
        qz_fn = lambda t0, n: [(QZ[0:64, 0, t0:t0 + n], slice(0, 64)), (QZ[64:128, 1, t0:t0 + n], slice(64, 128))]
        for j in range(2):
            self.proj_fm(wbuf, wres, j * 128, None, qzn, True, need_ctx, dst_fn=qz_fn)
            self.proj_fm(wbuf, wres, 256 + j * 128, KT, kn, True, True)
            items = []
            for n_ in range(ntb):
                if n_ < 16:
                    kbs = ([n_ - 1] if n_ > 0 else []) + [n_] + ([n_ + 1] if n_ < 15 else []) + [16, 17]
                else:
                    kbs = [16, 17]
                for ki, m in enumerate(kbs):
                    items.append((n_, ki, m, len(kbs)))
            cur = {}

            def front(n_, ki, m, nk):
                gi = min(n_ // 4, 4)
                mgi = min(m // 4, 4)
                st = self.gp()
                self.mm(st[:, 0:256], KT[:, m * 128:(m + 1) * 128], QZ[:, :, n_ * 128:(n_ + 1) * 128],
                        reads=[(kn, mgi), (qzn, gi)], writes=[R(st)])
                pt = self.ptb()
                self.op("act", "activation", out=pt[:, 0:256], in_=st[:, 0:256], func=AF.Exp, scale=0.125,
                        reads=[R(st)], writes=[R(pt)])
                if n_ < 16 and m == n_ - 1:
                    self.op("pool", "tensor_tensor", out=pt[:, 0:256], in0=pt[:, 0:256], in1=self.C["maskp"][:], op=ALU.mult,
                            reads=[R(pt), R(self.C["maskp"])], writes=[R(pt)])
                if n_ < 15 and m == n_ + 1:
                    self.op("pool", "tensor_tensor", out=pt[:, 0:256], in0=pt[:, 0:256], in1=self.C["maskn"][:], op=ALU.mult,
                            reads=[R(pt), R(self.C["maskn"])], writes=[R(pt)])
                return pt

            def back(n_, ki, m, nk, pt):
                mgi = min(m // 4, 4)
                r3 = n_ % 3
                if r3 == 0 and ki == 0:
                    cur["po"], cur["pres"] = self.ab()
                    cur["n0"] = n_
                po, pres = cur["po"], cur["pres"]
                for g in range(2):
                    off = r3 * 130 + g * 65
                    self.mm(po[:, off:off + 65], pt[:, g * 128:(g + 1) * 128], V4[:, m, j, 0:65],
                            start=(r3 == 0 and ki == 0 and g == 0), stop=(ki == nk - 1), reads=[R(pt), (vn, mgi)], writes=[pres],
                            sgc=True)
                if ki == nk - 1 and (r3 == 2 or n_ == ntb - 1):
                    n0 = cur["n0"]
                    cnt = n_ - n0 + 1
                    pov = po[:, 0:cnt * 130].rearrange("p (n g c) -> p n g c", g=2, c=65)
                    den = sm[:, 34:34 + 2 * cnt].rearrange("p (n g) -> p n g", g=2)
                    rz = sm[:, 40:40 + 2 * cnt].rearrange("p (n g) -> p n g", g=2)
                    self.op("dve", "tensor_tensor", out=den, in0=pov[:, :, :, 64],
                            in1=self.ESINK[:, l, 2 * j:2 * j + 2].unsqueeze(1).to_broadcast([128, cnt, 2]), op=ALU.add,
                            reads=[pres, R(self.ESINK)], writes=[R(sm)])
                    self.op("dve", "reciprocal", out=rz, in_=den, reads=[R(sm)], writes=[R(sm)])
                    gis = sorted(set(min(q // 4, 4) for q in range(n0, n_ + 1)))
                    self.op("dve", "tensor_tensor", out=OC3[:, n0:n_ + 1, :].rearrange("p n (g c) -> p n g c", g=2),
                            in0=pov[:, :, :, 0:64], in1=rz.unsqueeze(3).to_broadcast([128, cnt, 2, 64]), op=ALU.mult,
                            reads=[pres, R(sm)], writes=[(ocn, g_) for g_ in gis])

            pend = []
            for it in items:
                pt = front(*it)
                pend.append(it + (pt,))
                if len(pend) > 3:
                    back(*pend.pop(0))
            while pend:
                back(*pend.pop(0))
            self.finish_chunk(l, 6 + j, OC3, ocn, OCT, octn, need_ctx)

    def mlp(self, l, need_ctx):
        self.drain()
        self.norm(l, 1, skip_ctx=not need_ctx)
        w1v = self.w1_d[l].rearrange("(kc p) n -> p kc n", p=128)
        w2v = self.w2_d[l].rearrange("(fc p) n -> p fc n", p=128)
        pending = None
        ucount = 0

        def mlp2(W2, w2res, AT, ares, gi, t0, n, s):
            for dc in range(KC):
                ps2, pres = self.ab()
                for fc in range(4):
                    self.mm(ps2[:, :n], W2[:, fc, dc * 128:(dc + 1) * 128], AT[:, fc, :n], start=(fc == 0), stop=(fc == 3),
                            reads=[(w2res[0], 0), (w2res[1], 0), ares], writes=[pres])
                self.op("dve", "scalar_tensor_tensor", out=self.XT[:, dc, t0:t0 + n], in0=ps2[:, :n],
                        scalar=self.MOD[:, l, 40 + dc, s:s + 1], in1=self.XT[:, dc, t0:t0 + n], op0=ALU.mult, op1=ALU.add,
                        reads=[pres, ("MOD", l), ("XT", dc, gi)], writes=[("XT", dc, gi)])

        for fb in range(8):
            W1, w1res = self.wst_load(w1v, [(fb * 512, 0, 512)])
            i2 = fb % 2
            W2 = self.GBT[:, i2 * 2 * GBW:i2 * 2 * GBW + 4096].rearrange("p (f n) -> p f n", f=4)
            w2res = [("GB", 2 * i2), ("GB", 2 * i2 + 1)]
            self.dma(W2, w2v[:, fb * 4:(fb + 1) * 4, :], writes=[(r, g) for r in w2res for g in range(5)],
                     key="w2_%d" % i2, eng="pool")
            for gi, (t0, n) in enumerate(TGS):
                if gi == 4 and not need_ctx:
                    continue
                s = 0 if gi < 4 else 1
                ai = 4 + (ucount % 2)
                ucount += 1
                AT = self.gb(ai)[:, 0:2048].rearrange("p (f n) -> p f n", f=4)
                ares = (("GB", ai), 0)
                aresw = [(("GB", ai), g_) for g_ in range(5)]
                for fc in range(4):
                    ps = self.gp()
                    for kc in range(KC):
                        self.mm(ps[:, :n], W1[:, kc, fc * 128:(fc + 1) * 128], self.HT[:, kc, t0:t0 + n],
                                start=(kc == 0), stop=(kc == KC - 1), reads=[w1res, ("HT", kc, gi)], writes=[R(ps)])
                    rt = self.tmpf[fc % 4]
                    self.op("act", "activation", out=rt[:, :n], in_=ps[:, :n], func=AF.Relu, reads=[R(ps)], writes=[R(rt)])
                    self.op("pool", "tensor_tensor", out=AT[:, fc, :n], in0=rt[:, :n], in1=rt[:, :n], op=ALU.mult,
                            reads=[R(rt)], writes=aresw)
                if pending is not None:
                    mlp2(*pending)
                pending = (W2, w2res, AT, ares, gi, t0, n, s)
        mlp2(*pending)

    def final_norm(self):
        XT, identf = self.XT, self.C["ident_f"]
        gfin = self.P["g_final"]
        YT = self.HT[:].rearrange("p k t -> p (k t)")[:, 0:8192].bitcast(F32).rearrange("p (k t) -> p k t", k=KC)
        ytres = [("HT", kc, g) for kc in range(KC) for g in range(5)]
        for g in range(4):
            t0 = g * 512
            acc = self.gp()
            self.sumsq_rstd(g, acc)
            for kc in range(KC):
                self.op("dve", "scalar_tensor_tensor", out=YT[:, kc, :], in0=XT[:, kc, t0:t0 + 512], scalar=gfin[:, kc:kc + 1],
                        in1=self.rstd[:], op0=ALU.mult, op1=ALU.mult,
                        reads=[("XT", kc, g), R(self.rstd), R(gfin)], writes=(ytres if kc == 0 else []) + [("YT", kc)])
            for j in range(4):
                tb = g * 4 + j
                ob = self.iobuf[tb % 2]
                obres = self.iores[tb % 2]
                for half in range(2):
                    pt = self.gp()
                    for jj in range(4):
                        kc = half * 4 + jj
                        self.tr(pt[:, jj * 128:(jj + 1) * 128], YT[:, kc, j * 128:(j + 1) * 128], identf[:],
                                reads=[("YT", kc), ("HT", 0, 0), R(identf)], writes=[R(pt)])
                    self.op("act", "mul", out=ob[:, half * 512:(half + 1) * 512], in_=pt[:], mul=32.0,
                            reads=[R(pt)], writes=obres)
                self.dma(self.out_d[tb * 128:(tb + 1) * 128, :], ob, reads=obres, key="io%d" % (tb % 2))


_CACHE = {}


def kernel(**inputs):
    cfg = inputs.pop("_cfg", {}) if "_cfg" in inputs else {}
    inputs = {k: np.asarray(v) for k, v in inputs.items()}
    key = tuple(sorted(cfg.items()))
    if key not in _CACHE:
        b = Builder(cfg)
        _CACHE[key] = (b.build(), b)
    nc, b = _CACHE[key]
    in_maps = [prep_inputs(inputs, i, b.depth) for i in range(8)]
    res = run_bass_kernel_spmd(nc, in_maps, core_ids=list(range(8)))
    if cfg.get("_ret_all"):
        return res
    out = np.stack([np.asarray(r["out"]) for r in res.results], axis=0)
    return out.astype(np.float32)
```

```python
import contextlib
import math
import numpy as np
import ml_dtypes
import concourse.bass as bass
import concourse.mybir as mybir
from concourse.bass_utils import run_bass_kernel_spmd

F32 = mybir.dt.float32
BF16 = mybir.dt.bfloat16
ALU = mybir.AluOpType
AF = mybir.ActivationFunctionType
AX = mybir.AxisListType

D = 1024
T = 2048
L = 256
NT = T + L
DEPTH = 4
KC = D // 128
NTB = NT // 128
EPS = 1e-6
TGS = [(0, 512), (512, 512), (1024, 512), (1536, 512), (2048, 256)]


class Op:
    __slots__ = ("eng", "fn", "idx", "deps", "raw", "inc", "count", "dma", "is_dma", "embed")

    def __init__(self, eng, fn, idx):
        self.eng = eng
        self.fn = fn
        self.idx = idx
        self.deps = set()
        self.raw = set()
        self.inc = False
        self.count = 0
        self.dma = None
        self.is_dma = False
        self.embed = False


EMBED_WAITS = True


class Sched:
    ENGS = ["pe", "act", "dve", "pool", "sp"]

    def __init__(self):
        self.ops = {e: [] for e in self.ENGS}
        self.last_w = {}
        self.readers = {}
        self.dma_n = {}

    def add(self, eng, fn, reads=(), writes=(), dma=None):
        op = Op(eng, fn, len(self.ops[eng]))
        xr = [r for r in reads if isinstance(r, tuple) and str(r[0]).startswith(("ps", "oacc"))]
        if xr:
            for r in xr:
                w = self.last_w.get(r)
                if w is not None:
                    op.raw.add(w)
            writes = list(writes) + [r for r in xr if r not in writes]
        for r in reads:
            w = self.last_w.get(r)
            if w is not None:
                op.deps.add(w)
                op.raw.add(w)
        for r in writes:
            w = self.last_w.get(r)
            if w is not None:
                op.deps.add(w)
            rd = self.readers.get(r)
            if rd:
                for o in rd.values():
                    op.deps.add(o)
        for r in reads:
            d = self.readers.setdefault(r, {})
            if dma is not None:
                d[("dma", id(op))] = op
            else:
                d[eng] = op
        for r in writes:
            self.last_w[r] = op
            self.readers[r] = {}
        if dma is not None:
            n = self.dma_n.get(dma, 0) + 1
            self.dma_n[dma] = n
            op.dma = (dma, n)
            op.is_dma = True
        self.ops[eng].append(op)
        return op

    def _needs_wait(self, op, d):
        if d.is_dma:
            return True
        if d.eng != op.eng:
            return True
        if op.is_dma:
            return True
        if op.eng == "pe":
            return False
        return (d in op.raw) and (op.idx - d.idx <= 4)

    def emit(self, nc, stack):
        for e in self.ENGS:
            for op in self.ops[e]:
                for d in op.deps:
                    if not d.is_dma and self._needs_wait(op, d):
                        d.inc = True
        for e in self.ENGS:
            c = 0
            for op in self.ops[e]:
                if op.inc:
                    c += 1
                op.count = c
        sems = {e: stack.enter_context(nc.semaphore("s_" + e)) for e in self.ENGS}
        dsems = {k: stack.enter_context(nc.semaphore("d_%s" % (k,))) for k in self.dma_n}
        block = stack.enter_context(nc.Block())
        stats = {}

        def run(ename, engine):
            known = {}
            nw = 0
            for op in self.ops[ename]:
                need = {}
                for d in op.deps:
                    if not self._needs_wait(op, d):
                        continue
                    if d.is_dma:
                        key = ("d", d.dma[0])
                        val = 16 * d.dma[1]
                    else:
                        key = ("e", d.eng)
                        val = d.count
                    if val > need.get(key, 0):
                        need[key] = val
                todo = []
                for key, val in need.items():
                    if known.get(key, 0) >= val:
                        continue
                    known[key] = val
                    todo.append((dsems[key[1]] if key[0] == "d" else sems[key[1]], val))
                last = todo.pop() if (todo and op.embed and EMBED_WAITS) else None
                for s, val in todo:
                    engine.wait_ge(s, val)
                    nw += 1
                ins = op.fn(engine)
                if last is not None:
                    ins._wait_ge(last[0], last[1])
                if op.is_dma:
                    ins.then_inc(dsems[op.dma[0]], 16)
                elif op.inc:
                    ins.then_inc(sems[ename], 1)
            for op in self.ops[ename]:
                if op.is_dma:
                    key = ("d", op.dma[0])
                    val = 16 * self.dma_n[op.dma[0]]
                    if known.get(key, 0) < val:
                        known[key] = val
                        engine.wait_ge(dsems[op.dma[0]], val)
            stats[ename] = (len(self.ops[ename]), nw)

        block.tensor(lambda e: run("pe", e))
        block.scalar(lambda e: run("act", e))
        block.vector(lambda e: run("dve", e))
        block.gpsimd(lambda e: run("pool", e))
        block.sync(lambda e: run("sp", e))
        self.stats = stats
        return stats


def R(t, *idx):
    return (t.name,) + idx


GBW = 2340
NGB = 8
LAM_INIT = [0.8 - 0.6 * math.exp(-0.3 * l) for l in range(DEPTH)]


def host_consts():
    c = {}
    bf = ml_dtypes.bfloat16
    c["ident_f"] = np.eye(128, dtype=np.float32)
    c["ident_b"] = np.eye(128, dtype=np.float32).astype(bf)
    c["ones_b"] = np.ones((128, 128), dtype=np.float32).astype(bf)
    pm = np.zeros((128, 128), np.float32)
    sign = np.zeros(128, np.float64)
    for f in range(128):
        fl = f % 64
        q = fl // 16
        src = f + 16 if q in (0, 2) else f - 16
        pm[src, f] = 1.0
        sign[f] = -1.0 if q in (0, 2) else 1.0
    c["perm_b"] = pm.astype(bf)
    t = np.arange(T)
    r = (t // 64).astype(np.float64)
    col = (t % 64).astype(np.float64)
    inv = 10000.0 ** (-np.arange(16, dtype=np.float64) / 16)
    ang64 = np.concatenate([r[:, None] * inv, r[:, None] * inv, col[:, None] * inv, col[:, None] * inv], axis=1)
    ang = np.concatenate([ang64, ang64], axis=1).T
    c["cosT"] = np.cos(ang).astype(np.float32).astype(bf)
    c["sinT"] = (np.sin(ang) * sign[:, None]).astype(np.float32).astype(bf)
    il = np.arange(128)
    mp = (il[None, :] <= il[:, None]).astype(np.float32)
    mn = (il[:, None] <= il[None, :]).astype(np.float32)
    c["maskp"] = np.concatenate([mp, mp], axis=1).astype(bf)
    c["maskn"] = np.concatenate([mn, mn], axis=1).astype(bf)
    s_ = il[:, None].astype(np.float32)
    t_ = il[None, :].astype(np.float32)
    c["r1"] = np.maximum(t_ - s_, 0.0).astype(np.float32)
    c["r2"] = np.maximum(s_ - t_, 0.0).astype(np.float32)
    c["i1"] = np.broadcast_to((il + 1.0)[None, :], (128, 128)).astype(np.float32).copy()
    c["i2"] = np.broadcast_to((128.0 - il)[None, :], (128, 128)).astype(np.float32).copy()
    c["kcol"] = np.stack([127.0 - il, il * 1.0], axis=1).astype(np.float32)
    return c


CONST_SPECS = [("ident_f", [128, 128], F32), ("ident_b", [128, 128], BF16), ("ones_b", [128, 128], BF16),
               ("perm_b", [128, 128], BF16), ("cosT", [128, T], BF16), ("sinT", [128, T], BF16),
               ("maskp", [128, 256], BF16), ("maskn", [128, 256], BF16),
               ("r1", [128, 128], F32), ("r2", [128, 128], F32), ("i1", [128, 128], F32), ("i2", [128, 128], F32),
               ("kcol", [128, 2], F32)]

PARAM_SPECS = [("g_final", [128, KC]), ("cc", [128, KC, 2]), ("bada", [128, DEPTH, 48]),
               ("gmix", [128, DEPTH, KC]), ("gmlp", [128, DEPTH, KC]),
               ("lamq", [128, DEPTH, 2, 64]), ("lamk", [128, DEPTH, 2, 64]), ("sublng", [128, DEPTH, 128]),
               ("decbc", [128, DEPTH, 8]), ("deccol", [128, DEPTH, 2, 2]), ("sinkbc", [128, DEPTH, 4])]


def prep_inputs(inputs, b, depth=DEPTH):
    f = np.float32
    m = {}
    m["x"] = np.ascontiguousarray(inputs["x"][b], dtype=f)
    m["ctx"] = np.ascontiguousarray(inputs["ctx"][b], dtype=f)
    m["g_final"] = np.ascontiguousarray(inputs["g_final"].reshape(KC, 128).T, dtype=f)
    cc = np.stack([inputs["c"][b].reshape(KC, 128).T, inputs["c_ctx"].reshape(KC, 128).T], axis=2)
    m["cc"] = np.ascontiguousarray(cc, dtype=f)
    m["bada"] = np.ascontiguousarray(inputs["b_ada"].reshape(DEPTH, 48, 128).transpose(2, 0, 1), dtype=f)
    m["gmix"] = np.ascontiguousarray(inputs["g_mix"].reshape(DEPTH, KC, 128).transpose(2, 0, 1), dtype=f)
    m["gmlp"] = np.ascontiguousarray(inputs["g_mlp"].reshape(DEPTH, KC, 128).transpose(2, 0, 1), dtype=f)
    lamq = np.stack([inputs["lam_q1"], inputs["lam_q2"]], axis=1)
    lamk = np.stack([inputs["lam_k1"], inputs["lam_k2"]], axis=1)
    m["lamq"] = np.ascontiguousarray(np.broadcast_to(lamq[None], (128, DEPTH, 2, 64)), dtype=f)
    m["lamk"] = np.ascontiguousarray(np.broadcast_to(lamk[None], (128, DEPTH, 2, 64)), dtype=f)
    m["sublng"] = np.ascontiguousarray(np.broadcast_to(inputs["subln_g"][None], (128, DEPTH, 128)), dtype=f)
    dec = np.concatenate([inputs["ret_decay_fwd"], inputs["ret_decay_bwd"]], axis=1)
    m["decbc"] = np.ascontiguousarray(np.broadcast_to(dec[None], (128, DEPTH, 8)), dtype=f)
    dcol = np.zeros((128, DEPTH, 2, 2), f)
    for di, nm in enumerate(["ret_decay_fwd", "ret_decay_bwd"]):
        for pp in range(2):
            dcol[0:64, :, di, pp] = inputs[nm][:, 2 * pp][None, :]
            dcol[64:128, :, di, pp] = inputs[nm][:, 2 * pp + 1][None, :]
    m["deccol"] = dcol
    m["sinkbc"] = np.ascontiguousarray(np.broadcast_to(inputs["sink_logit"][None], (128, DEPTH, 4)), dtype=f)
    for k in ["w_ada", "w_in", "w_out", "w_mlp1", "w_mlp2"]:
        m[k] = np.ascontiguousarray(inputs[k][:depth], dtype=f)
    m.update(host_consts())
    return m


class Builder:
    def __init__(self, cfg):
        self.cfg = cfg
        self.depth = cfg.get("depth", DEPTH)
        self.nc = bass.Bass("TRN2", target_bir_lowering=False)
        self.S = Sched()
        self.stack = contextlib.ExitStack()
        self.gp_i = 0
        self.ab_i = 0
        self.pt_i = 0
        self.wst_i = 0
        self.wo_i = 0
        self.oacc_i = 0
        self.rope_i = 0
        self.dbg_names = []
        self.deferred = []

    def dram_in(self, name, shape, dt=F32):
        return self.nc.dram_tensor(name, list(shape), dt, kind="ExternalInput").ap()

    def sb(self, name, shape, dt):
        return self.stack.enter_context(self.nc.sbuf_tensor(name, list(shape), dt))

    def ps(self, name, shape, dt=F32):
        return self.stack.enter_context(self.nc.psum_tensor(name, list(shape), dt))

    def dma(self, out, in_, reads=(), writes=(), key=None, eng="sp", **kw):
        self.S.add(eng, lambda e: e.dma_start(out=out, in_=in_, **kw), reads=reads, writes=writes, dma=key)

    def mm(self, out, lhsT, rhs, start=True, stop=True, reads=(), writes=(), sgc=False):
        if sgc:
            o = self.S.add("pe", lambda e: e.matmul(out, lhsT=lhsT, rhs=rhs, start=start, stop=stop, skip_group_check=True),
                           reads=reads, writes=writes)
        else:
            o = self.S.add("pe", lambda e: e.matmul(out, lhsT=lhsT, rhs=rhs, start=start, stop=stop),
                           reads=reads, writes=writes)
        o.embed = True

    def tr(self, out, in_, ident, reads=(), writes=()):
        o = self.S.add("pe", lambda e: e.transpose(out, in_, ident), reads=reads, writes=writes)
        o.embed = (ident.dtype == BF16)

    def op(self, eng, meth, reads=(), writes=(), **kw):
        o = self.S.add(eng, lambda e: getattr(e, meth)(**kw), reads=reads, writes=writes)
        o.embed = eng in ("act", "dve") or (eng == "pool" and meth in ("tensor_tensor", "tensor_copy"))

    def gp(self):
        t = self.PSG[self.gp_i % len(self.PSG)]
        self.gp_i += 1
        return t

    def ab(self):
        i = self.ab_i % 4
        self.ab_i += 1
        return self.OACC[i // 2][:, (i % 2) * 512:(i % 2 + 1) * 512], ("oacc", i // 2, i % 2)

    def ptb(self):
        t = self.PT[self.pt_i % len(self.PT)]
        self.pt_i += 1
        return t

    def gb(self, i):
        return self.GBT[:, i * GBW:(i + 1) * GBW]

    def dbg(self, name, ap, shape, dt, reads):
        d = self.nc.dram_tensor(name, list(shape), dt, kind="ExternalOutput").ap()
        self.dma(d, ap, reads=reads, key="dbg_" + name)
        self.dbg_names.append(name)

    def build(self):
        nc, S = self.nc, self.S
        cfg = self.cfg
        self.x_d = self.dram_in("x", [T, D])
        self.ctx_d = self.dram_in("ctx", [L, D])
        self.out_d = nc.dram_tensor("out", [T, D], F32, kind="ExternalOutput").ap()
        self.cd = {n: self.dram_in(n, shp, dt) for n, shp, dt in CONST_SPECS}
        self.pd = {n: self.dram_in(n, shp, F32) for n, shp in PARAM_SPECS}
        self.wada_d = self.dram_in("w_ada", [self.depth, D, 6 * D])
        self.win_d = self.dram_in("w_in", [self.depth, D, 3 * D])
        self.wout_d = self.dram_in("w_out", [self.depth, D, D])
        self.w1_d = self.dram_in("w_mlp1", [self.depth, D, 4 * D])
        self.w2_d = self.dram_in("w_mlp2", [self.depth, 4 * D, D])

        self.XT = self.sb("XT", [128, KC, NT], F32)
        self.HT = self.sb("HT", [128, KC, NT], BF16)
        self.GBT = self.sb("GBT", [128, NGB * GBW], BF16)
        self.WST = [self.sb("WST%d" % i, [128, KC, 512], BF16) for i in range(2)]
        self.WO = [self.sb("WO%d" % i, [128, D], BF16) for i in range(2)]
        self.C = {}
        for n, shp, dt in CONST_SPECS:
            t = self.sb("c_" + n, shp, dt)
            self.C[n] = t
            self.dma(t[:], self.cd[n], writes=[R(t)], key="c_" + n)
        self.O0N = self.sb("o0n", [128, 4, 128], F32)
        self.OA = self.sb("oa", [128, 4, 128], F32)
        self.OB = self.sb("ob", [128, 4, 128], F32)
        self.GA = self.sb("GA", [128, DEPTH, 128], F32)
        self.P = {}
        alias = {"lamq": self.O0N, "lamk": self.OB, "sublng": self.GA}
        for n, shp in PARAM_SPECS:
            if n in alias:
                t = alias[n]
                self.dma(t[:].rearrange("p a b -> p (a b)"), self.pd[n].rearrange("p a b c -> p (a b c)") if len(shp) == 4 else self.pd[n].rearrange("p a b -> p (a b)"), writes=[R(t)], key="p_" + n)
            else:
                t = self.sb("p_" + n, shp, F32)
                self.dma(t[:], self.pd[n], writes=[R(t)], key="p_" + n)
            self.P[n] = t
        self.iobuf = [self.gb(i)[:, 0:2048].bitcast(F32) for i in range(2)]
        self.iores = [[(("GB", i), g) for g in range(5)] for i in range(2)]
        self.tmpf = [self.sb("tmpf%d" % i, [128, 512], F32) for i in range(4)]
        self.PT = [self.sb("pt%d" % i, [128, 512], BF16) for i in range(4)]
        self.sqb = [self.PT[0], self.PT[1]]
        self.ropeq = [self.PT[2], self.PT[3]]
        self.rstd = self.tmpf[2]
        self.rstd0 = self.tmpf[3]
        self.small = self.sb("small", [128, 64], F32)
        self.epsb = self.sb("epsb", [128, 4], F32)
        S.add("pool", lambda e: e.memset(self.epsb[:, 0:1], float(D * EPS)), writes=[R(self.epsb)])
        S.add("pool", lambda e: e.memset(self.epsb[:, 1:2], float(EPS)), writes=[R(self.epsb)])
        S.add("pool", lambda e: e.memset(self.epsb[:, 2:3], 1.0), writes=[R(self.epsb)])
        self.MOD = self.sb("MOD", [128, DEPTH, 48, 2], F32)
        self.GS = [self.sb("GS%d" % i, [128, DEPTH, KC, 2], F32) for i in range(2)]
        self.G32 = [self.sb("G32_%d" % i, [128, DEPTH, KC], F32) for i in range(2)]
        self.siluT = self.sb("siluT", [128, KC, 2], BF16)
        self.LG = self.sb("LG", [128, DEPTH, 8], F32)
        self.LGC = self.sb("LGC", [128, DEPTH, 2, 2], F32)
        self.ESINK = self.sb("ESINK", [128, DEPTH, 4], F32)
        self.NEGLAM = self.sb("NEGLAM", [128, DEPTH], F32)
        self.lame = self.sb("lame", [128, DEPTH, 2], F32)
        _tqf = self.sb("TQF", [128, 128], F32)
        _tqb = self.sb("TQB", [128, 128], F32)
        _dt = self.sb("DT", [128, 256], F32)
        _tk = self.sb("TK", [128, 4], F32)
        _cfb = self.sb("CFB", [128, 2], F32)
        self.TQF, self.TQB, self.DT, self.TK, self.CFB = [_tqf] * 2, [_tqb] * 2, [_dt] * 2, [_tk] * 2, [_cfb] * 2
        self.stf = self.sb("stf", [128, 128], F32)
        self.stb = self.sb("stb", [128, 128], F32)
        self.kfb = [self.sb("kfb%d" % i, [128, 128], BF16) for i in range(4)]
        self.qfb = [self.sb("qfb%d" % i, [128, 128], BF16) for i in range(4)]
        self.PSG = [self.ps("ps%d" % i, [128, 512], F32) for i in range(4)]
        self.OACC = [self.ps("oacc%d" % i, [128, 1024], F32) for i in range(2)]

        self.load_input()
        self.prologue_params()
        self.adaln_all()
        for l in range(self.depth):
            need_ctx = l < DEPTH - 1
            self.norm(l, 0, skip_ctx=False)
            if cfg.get("dbg_h") == l:
                self.dbg("dbg_h", self.HT[:], [128, KC, NT], BF16, [("HT", kc, g) for kc in range(KC) for g in range(5)])
            order = []
            mx = cfg.get("mixers", "ABC")
            if "A" in mx:
                order += [("A", h) for h in range(4)]
            if "B" in mx:
                order += [("B", 0), ("B", 1)]
            if "C" in mx:
                order += [("C", 0)]
            if "A" in mx:
                g = [self.chunk_A(l, h, need_ctx) for h in range(4)]
                if cfg.get("a_overlap", True) and cfg.get("a_stage", 9) >= 4:
                    next(g[0])
                    next(g[0])
                    for h in range(4):
                        if h < 3:
                            next(g[h + 1])
                        next(g[h])
                        if h < 3:
                            next(g[h + 1])
                        for _ in g[h]:
                            pass
                else:
                    for h in range(4):
                        for _ in g[h]:
                            pass
            for kind, i in order:
                if kind == "A":
                    continue
                elif kind == "B":
                    self.chunk_B(l, i, need_ctx)
                else:
                    self.chunk_C(l, need_ctx)
            if cfg.get("mlp", True):
                self.mlp(l, need_ctx)
        if cfg.get("dbg_x"):
            self.dbg("dbg_x", self.XT[:], [128, KC, NT], F32, [("XT", kc, g) for kc in range(KC) for g in range(5)])
        self.drain()
        self.final_norm()
        S.emit(nc, self.stack)
        return nc

    def xt_res(self, kc, gi):
        return ("XT", kc, gi)

    def load_input(self):
        S = self.S
        XT, PS, identf = self.XT, self.PSG, self.C["ident_f"]
        for tb in range(NTB):
            buf = self.iobuf[tb % 2]
            bres = self.iores[tb % 2]
            gi = min(tb // 4, 4)
            src = self.x_d[tb * 128:(tb + 1) * 128, :] if tb < 16 else self.ctx_d[(tb - 16) * 128:(tb - 15) * 128, :]
            self.dma(buf, src, writes=bres, key="io%d" % (tb % 2))
            for half in range(2):
                pt = self.gp()
                for j in range(4):
                    kc = half * 4 + j
                    self.tr(pt[:, j * 128:(j + 1) * 128], buf[:, kc * 128:(kc + 1) * 128], identf[:],
                            reads=bres + [R(identf)], writes=[R(pt)])
                self.op("dve", "tensor_copy", out=XT[:, half * 4:half * 4 + 4, tb * 128:(tb + 1) * 128],
                        in_=pt[:].rearrange("p (j t) -> p j t", j=4),
                        reads=[R(pt)], writes=[("XT", k, gi) for k in range(half * 4, half * 4 + 4)])

    def prologue_params(self):
        P = self.P
        sm = self.small
        self.op("act", "activation", out=self.siluT[:], in_=P["cc"][:], func=AF.Silu,
                reads=[R(P["cc"])], writes=[R(self.siluT)])
        self.op("dve", "tensor_scalar", out=self.G32[0][:], in0=P["gmix"][:], scalar1=32.0, scalar2=None, op0=ALU.mult,
                reads=[R(P["gmix"])], writes=[R(self.G32[0])])
        self.op("dve", "tensor_scalar", out=self.G32[1][:], in0=P["gmlp"][:], scalar1=32.0, scalar2=None, op0=ALU.mult,
                reads=[R(P["gmlp"])], writes=[R(self.G32[1])])
        for src, dst, n in ((P["decbc"], self.LG, DEPTH * 8), (P["deccol"], self.LGC, DEPTH * 4)):
            sv = src[:].rearrange("p a b -> p (a b)") if len(src.shape) == 3 else src[:].rearrange("p a b c -> p (a b c)")
            dv = dst[:].rearrange("p a b -> p (a b)") if len(dst.shape) == 3 else dst[:].rearrange("p a b c -> p (a b c)")
            self.op("act", "activation", out=sm[:, 0:n], in_=sv, func=AF.Exp, scale=-1.0,
                    reads=[R(src)], writes=[R(sm)])
            self.op("act", "activation", out=sm[:, 32:32 + n], in_=sm[:, 0:n], func=AF.Ln, bias=self.epsb[:, 2:3], scale=1.0,
                    reads=[R(sm), R(self.epsb)], writes=[R(sm)])
            self.op("dve", "tensor_scalar", out=dv, in0=sm[:, 32:32 + n], scalar1=-1.0, scalar2=None, op0=ALU.mult,
                    reads=[R(sm)], writes=[R(dst)])
        self.op("act", "activation", out=self.ESINK[:], in_=P["sinkbc"][:], func=AF.Exp,
                reads=[R(P["sinkbc"])], writes=[R(self.ESINK)])
        fl = lambda t: t[:].rearrange("p a b -> p (a b)")
        self.op("dve", "tensor_tensor", out=fl(self.OA), in0=fl(P["lamq"]), in1=fl(P["lamk"]), op=ALU.mult,
                reads=[R(P["lamq"]), R(P["lamk"])], writes=[R(self.OA)])
        self.op("dve", "tensor_reduce", out=self.lame[:].rearrange("p a b -> p (a b)"),
                in_=fl(self.OA).rearrange("p (a c) -> p a c", c=64), axis=AX.X, op=ALU.add,
                reads=[R(self.OA)], writes=[R(self.lame)])
        self.op("act", "activation", out=self.lame[:], in_=self.lame[:], func=AF.Exp,
                reads=[R(self.lame)], writes=[R(self.lame)])
        for l in range(DEPTH):
            self.op("dve", "tensor_scalar", out=self.NEGLAM[:, l:l + 1], in0=self.lame[:, l, 1:2],
                    scalar1=self.lame[:, l, 0:1], scalar2=-LAM_INIT[l], op0=ALU.subtract, op1=ALU.add,
                    reads=[R(self.lame)], writes=[R(self.NEGLAM)])
            self.op("dve", "tensor_scalar", out=self.GA[:, l, :], in0=self.GA[:, l, :],
                    scalar1=1.0 - LAM_INIT[l], scalar2=None, op0=ALU.mult,
                    reads=[R(self.GA)], writes=[R(self.GA)])

    def wst_load(self, src3, pieces, keyname="wst"):
        i = self.wst_i % 2
        self.wst_i += 1
        buf = self.WST[i]
        res = ("WST", i)
        for (sc, dc, n) in pieces:
            self.dma(buf[:, :, dc:dc + n], src3[:, :, sc:sc + n], writes=[res], key="wst%d" % i, eng="pool")
        return buf, res

    def adaln_gs(self, l):
        for w, base in ((0, 8), (1, 32)):
            self.op("dve", "scalar_tensor_tensor", out=self.GS[w][:, l, :, :], in0=self.MOD[:, l, base:base + 8, :],
                    scalar=1.0, in1=self.G32[w][:, l, :].unsqueeze(2).to_broadcast([128, KC, 2]),
                    op0=ALU.add, op1=ALU.mult,
                    reads=[("MOD", l), R(self.G32[w])], writes=[("GS", w, l)])

    def adaln_unit_load(self, l, j, wbuf, i):
        src3 = self.wada_d[l].rearrange("(kc p) n -> p kc n", p=128)
        self.dma(wbuf[:, :, 384:512], src3[:, :, j * 128:(j + 1) * 128], writes=[("WSTx", i)], key="wax%d" % i, eng="pool")

    def adaln_unit_compute(self, l, j, wbuf, i):
        pst = self.gp()
        for kc in range(KC):
            self.mm(pst[:, 0:2], wbuf[:, kc, 384:512], self.siluT[:, kc, :], start=(kc == 0), stop=(kc == KC - 1),
                    reads=[("WSTx", i), R(self.siluT)], writes=[R(pst)])
        self.op("dve", "tensor_scalar", out=self.MOD[:, l, j, :], in0=pst[:, 0:2], scalar1=self.P["bada"][:, l, j:j + 1],
                scalar2=None, op0=ALU.add, reads=[R(pst), R(self.P["bada"])], writes=[("MOD", l)])

    def adaln_all(self):
        P = self.P
        for l in range(1):
            src3 = self.wada_d[l].rearrange("(kc p) n -> p kc n", p=128)
            for piece in range(12):
                buf, res = self.wst_load(src3, [(piece * 512, 0, 512)])
                pst = self.gp()
                for jc in range(4):
                    for kc in range(KC):
                        self.mm(pst[:, jc * 2:jc * 2 + 2], buf[:, kc, jc * 128:(jc + 1) * 128], self.siluT[:, kc, :],
                                start=(kc == 0), stop=(kc == KC - 1), reads=[res, R(self.siluT)], writes=[R(pst)])
                j0 = piece * 4
                self.op("dve", "tensor_tensor", out=self.MOD[:, l, j0:j0 + 4, :],
                        in0=pst[:, 0:8].rearrange("p (j s) -> p j s", s=2),
                        in1=P["bada"][:, l, j0:j0 + 4].unsqueeze(2).to_broadcast([128, 4, 2]), op=ALU.add,
                        reads=[R(pst), R(P["bada"])], writes=[("MOD", l)])
            for w, base in ((0, 8), (1, 32)):
                self.op("dve", "scalar_tensor_tensor", out=self.GS[w][:, l, :, :], in0=self.MOD[:, l, base:base + 8, :],
                        scalar=1.0, in1=self.G32[w][:, l, :].unsqueeze(2).to_broadcast([128, KC, 2]),
                        op0=ALU.add, op1=ALU.mult,
                        reads=[("MOD", l), R(self.G32[w])], writes=[("GS", w, l)])

    def sumsq_rstd(self, gi, acc):
        t0, n = TGS[gi]
        XT, onesb = self.XT, self.C["ones_b"]
        for kc in range(KC):
            sq = self.sqb[kc % 2]
            self.op("act", "activation", out=sq[:, :n], in_=XT[:, kc, t0:t0 + n], func=AF.Square,
                    reads=[("XT", kc, gi)], writes=[R(sq)])
            self.mm(acc[:, :n], onesb[:], sq[:, :n], start=(kc == 0), stop=(kc == KC - 1),
                    reads=[R(sq), R(onesb)], writes=[R(acc)])
        self.op("act", "activation", out=self.rstd0[:, :n], in_=acc[:, :n], func=AF.Ln, bias=self.epsb[:, 0:1], scale=1.0,
                reads=[R(acc), R(self.epsb)], writes=[R(self.rstd0)])
        self.op("act", "activation", out=self.rstd[:, :n], in_=self.rstd0[:, :n], func=AF.Exp, scale=-0.5,
                reads=[R(self.rstd0)], writes=[R(self.rstd)])

    def norm(self, l, w, skip_ctx):
        shbase = 0 if w == 0 else 24
        for gi, (t0, n) in enumerate(TGS):
            if gi == 4 and skip_ctx:
                continue
            s = 0 if gi < 4 else 1
            acc = self.gp()
            self.sumsq_rstd(gi, acc)
            for kc in range(KC):
                tm = self.tmpf[kc % 2]
                self.op("dve", "scalar_tensor_tensor", out=tm[:, :n], in0=self.XT[:, kc, t0:t0 + n],
                        scalar=self.GS[w][:, l, kc, s:s + 1], in1=self.rstd[:, :n], op0=ALU.mult, op1=ALU.mult,
                        reads=[("XT", kc, gi), ("GS", w, l), R(self.rstd)], writes=[R(tm)])
                if kc % 2 == 0:
                    self.op("act", "activation", out=self.HT[:, kc, t0:t0 + n], in_=tm[:, :n], func=AF.Identity,
                            bias=self.MOD[:, l, shbase + kc, s:s + 1], scale=1.0,
                            reads=[R(tm), ("MOD", l)], writes=[("HT", kc, gi)])
                else:
                    self.op("dve", "tensor_scalar", out=self.HT[:, kc, t0:t0 + n], in0=tm[:, :n],
                            scalar1=self.MOD[:, l, shbase + kc, s:s + 1], scalar2=None, op0=ALU.add,
                            reads=[R(tm), ("MOD", l)], writes=[("HT", kc, gi)])

    def proj_fm(self, wbuf, wres, c0, dst, dname, rope, with_ctx, dst_fn=None):
        if dst_fn is None:
            dst_fn = lambda t0, n: [(dst[:, t0:t0 + n], slice(0, 128))]
        pending = None
        for gi, (t0, n) in enumerate(TGS):
            if gi == 4 and not with_ctx:
                continue
            ps = self.gp()
            for kc in range(KC):
                self.mm(ps[:, :n], wbuf[:, kc, c0:c0 + 128], self.HT[:, kc, t0:t0 + n], start=(kc == 0), stop=(kc == KC - 1),
                        reads=[wres, ("HT", kc, gi)], writes=[R(ps)])
            if rope and gi < 4:
                tail = self.rope_head(ps, n)
                if pending is not None:
                    self.rope_tail(*pending)
                pending = (ps, dst_fn(t0, n), t0, n, (dname, gi)) + tail
            else:
                for (oap, psl) in dst_fn(t0, n):
                    self.op("act", "activation", out=oap, in_=ps[psl, :n], func=AF.Copy,
                            reads=[R(ps)], writes=[(dname, gi)])
        if pending is not None:
            self.rope_tail(*pending)

    def rope_head(self, ps, n):
        i = self.rope_i % 2
        self.rope_i += 1
        qb = self.ropeq[i]
        self.op("act", "activation", out=qb[:, :n], in_=ps[:, :n], func=AF.Copy, reads=[R(ps)], writes=[R(qb)])
        return (i, qb)

    def rope_tail(self, ps, dsts, t0, n, dres, i, qb):
        t1, t2 = self.tmpf[2 * i], self.tmpf[2 * i + 1]
        permb, cosT, sinT = self.C["perm_b"], self.C["cosT"], self.C["sinT"]
        ps2 = self.gp()
        self.mm(ps2[:, :n], permb[:], qb[:, :n], reads=[R(qb), R(permb)], writes=[R(ps2)])
        self.op("dve", "tensor_tensor", out=t1[:, :n], in0=ps[:, :n], in1=cosT[:, t0:t0 + n], op=ALU.mult,
                reads=[R(ps), R(cosT)], writes=[R(t1)])
        self.op("dve", "tensor_tensor", out=t2[:, :n], in0=ps2[:, :n], in1=sinT[:, t0:t0 + n], op=ALU.mult,
                reads=[R(ps2), R(sinT)], writes=[R(t2)])
        for (oap, psl) in dsts:
            self.op("pool", "tensor_tensor", out=oap, in0=t1[psl, :n], in1=t2[psl, :n], op=ALU.add,
                    reads=[R(t1), R(t2)], writes=[dres])

    def proj_tm(self, wbuf, wres, c0, dst_fn, dname, func=None):
        for tb0 in range(0, NTB, 4):
            nb = min(4, NTB - tb0)
            gi = min(tb0 // 4, 4)
            ps = self.gp()
            for j in range(nb):
                tb = tb0 + j
                for kc in range(KC):
                    self.mm(ps[:, j * 128:(j + 1) * 128], self.HT[:, kc, tb * 128:(tb + 1) * 128], wbuf[:, kc, c0:c0 + 128],
                            start=(kc == 0), stop=(kc == KC - 1), reads=[wres, ("HT", kc, gi)], writes=[R(ps)])
            out, in_ = dst_fn(tb0, nb, ps)
            self.op("act", "activation", out=out, in_=in_, func=(func or AF.Copy), reads=[R(ps)], writes=[(dname, gi)])

    def finish_chunk(self, l, ci, OC3, ocname, OCT, octname, need_ctx):
        self.drain()
        i = self.wo_i % 2
        self.wo_i += 1
        WO = self.WO[i]
        wres = ("WO", i)
        self.dma(WO[:], self.wout_d[l][ci * 128:(ci + 1) * 128, :], writes=[wres], key="wo%d" % i, eng="pool")
        identb = self.C["ident_b"]
        ntb = NTB if need_ctx else 16
        for tb0 in range(0, ntb, 4):
            nb = min(4, ntb - tb0)
            gi = min(tb0 // 4, 4)
            pk = self.gp()
            pkb = pk[:].bitcast(BF16)
            for j in range(nb):
                self.tr(pkb[:, j * 128:(j + 1) * 128], OC3[:, tb0 + j, :], identb[:],
                        reads=[(ocname, gi), R(identb)], writes=[R(pk)])
            self.op("dve", "tensor_copy", out=OCT[:, tb0 * 128:(tb0 + nb) * 128], in_=pkb[:, 0:nb * 128],
                    reads=[R(pk)], writes=[(octname, gi)])
        for gi, (t0, n) in enumerate(TGS):
            if gi == 4 and not need_ctx:
                continue
            s = 0 if gi < 4 else 1
            for dc in range(KC):
                def item(use_gp, gi=gi, t0=t0, n=n, s=s, dc=dc):
                    if use_gp:
                        ps = self.gp()
                        pres = R(ps)
                    else:
                        ps, pres = self.ab()
                    self.mm(ps[:, :n], WO[:, dc * 128:(dc + 1) * 128], OCT[:, t0:t0 + n],
                            reads=[wres, (octname, gi)], writes=[pres])
                    self.op("dve", "scalar_tensor_tensor", out=self.XT[:, dc, t0:t0 + n], in0=ps[:, :n],
                            scalar=self.MOD[:, l, 16 + dc, s:s + 1], in1=self.XT[:, dc, t0:t0 + n], op0=ALU.mult, op1=ALU.add,
                            reads=[pres, ("MOD", l), ("XT", dc, gi)], writes=[("XT", dc, gi)])
                self.deferred.append(item)
        self.drain(use_gp=True)

    def drain(self, k=None, use_gp=False):
        while self.deferred and (k is None or k > 0):
            self.deferred.pop(0)(use_gp)
            if k is not None:
                k -= 1

    def chunk_A(self, l, h, need_ctx):
        src3 = self.win_d[l].rearrange("(kc p) n -> p kc n", p=128)
        wbuf, wres = self.wst_load(src3, [(h * 128, 0, 128), (512 + h * 128, 128, 128), (1024 + h * 128, 256, 128)])
        par = h % 2
        QT, KT, V = self.gb(par * 3), self.gb(par * 3 + 1), self.gb(par * 3 + 2)
        qn, kn, vn = ("GB", par * 3), ("GB", par * 3 + 1), ("GB", par * 3 + 2)
        OC, OCT = self.gb(6), self.gb(7)
        V3 = V.rearrange("p (t c) -> p t c", c=130)
        OC3 = OC[:, 0:NT].rearrange("p (t c) -> p t c", c=128)
        allg = list(range(5))
        self.op("pool", "memset", ap=V3[:, :, 128:129], constant=1.0, writes=[(vn, g) for g in allg])
        yield "loaded"
        stage = self.cfg.get("a_stage", 9)
        if stage < 0:
            return
        self.proj_fm(wbuf, wres, 0, QT, qn, stage >= 0.5, need_ctx)
        if stage < 0.7:
            return
        self.proj_fm(wbuf, wres, 128, KT, kn, True, True)
        if stage < 0.8:
            return
        self.proj_tm(wbuf, wres, 256,
                     lambda tb0, nb, ps: (V3[:, tb0:tb0 + nb, 0:128], ps[:, 0:nb * 128].rearrange("p (j c) -> p j c", c=128)),
                     vn)
        if stage < 2:
            return
        yield "proj"
        itc = 0
        wi = wres[1]
        ada_l = l + 1 if (l + 1 < self.depth and self.cfg.get("ada_il", True)) else None
        ada_units = list(range(h * 12, h * 12 + 12)) if ada_l is not None else []
        ada_loaded = None
        for gi, (t0, n) in enumerate(TGS):
            if gi == 4 and not need_ctx:
                continue
            kbs = list(range(NTB)) if gi < 4 else [16, 17]
            nqb = n // 128
            nbk = nqb // 2
            aress = [[("oacc", c, 0), ("oacc", c, 1)] for c in range(2)]

            def emit_pv(ki, kb, pts):
                kgi = min(kb // 4, 4)
                for c in range(2):
                    acc = self.OACC[c]
                    for qb in range(nqb):
                        off = (qb // 2) * 512 + (qb % 2) * 129
                        self.mm(acc[:, off:off + 129], pts[c][:, qb * 128:(qb + 1) * 128], V3[:, kb, 0:129],
                                start=(ki == 0 and qb % 2 == 0), stop=(ki == len(kbs) - 1),
                                reads=[R(pts[c]), (vn, kgi)], writes=[aress[c][qb // 2]], sgc=True)

            pending = None
            for ki, kb in enumerate(kbs):
                kgi = min(kb // 4, 4)
                sts = [self.gp(), self.gp()]
                for c in range(2):
                    hs = slice(c * 64, (c + 1) * 64)
                    self.mm(sts[c][:, :n], KT[hs, kb * 128:(kb + 1) * 128], QT[hs, t0:t0 + n],
                            reads=[(kn, kgi), (qn, gi)], writes=[R(sts[c])])
                pts = [self.ptb(), self.ptb()]
                for c in range(2):
                    self.op("act", "activation", out=pts[c][:, :n], in_=sts[c][:, :n], func=AF.Exp, scale=0.125,
                            reads=[R(sts[c])], writes=[R(pts[c])])
                if pending is not None:
                    emit_pv(*pending)
                pending = (ki, kb, pts)
                if ada_l is not None and gi < 4:
                    if itc % 6 == 5 and ada_loaded is not None:
                        self.adaln_unit_compute(ada_l, ada_loaded, wbuf, wi)
                        ada_loaded = None
                    if itc % 6 == 0 and ada_units:
                        ada_loaded = ada_units.pop(0)
                        self.adaln_unit_load(ada_l, ada_loaded, wbuf, wi)
                    itc += 1
            emit_pv(*pending)
            if stage < 3:
                continue
            for c in range(2):
                acc = self.OACC[c]
                ares = aress[c]
                accv = acc[:].rearrange("p (b x) -> p b x", b=2)[:, 0:nbk, 0:258].rearrange("p b (j c) -> p b j c", c=129)
                zv = accv[:, :, :, 128]
                ov = accv[:, :, :, 0:128]
                sm = self.small
                rz = sm[:, 0:nqb].rearrange("p (b j) -> p b j", j=2)
                ar = ares[0:nbk]
                self.op("dve", "reciprocal", out=rz, in_=zv, reads=ar, writes=[R(sm)])
                o0 = self.O0N[:, 0:nqb, :].rearrange("p (b j) c -> p b j c", j=2)
                oa = self.OA[:, 0:nqb, :].rearrange("p (b j) c -> p b j c", j=2)
                if c == 0:
                    self.op("dve", "tensor_tensor", out=o0, in0=ov, in1=rz.unsqueeze(3).to_broadcast([128, nbk, 2, 128]),
                            op=ALU.mult, reads=ar + [R(sm)], writes=[R(self.O0N)])
                else:
                    rz1 = sm[:, 4:4 + nqb].rearrange("p (b j) -> p b j", j=2)
                    self.op("dve", "tensor_scalar", out=rz1, in0=rz, scalar1=self.NEGLAM[:, l:l + 1], scalar2=None,
                            op0=ALU.mult, reads=[R(sm), R(self.NEGLAM)], writes=[R(sm)])
                    self.op("dve", "tensor_tensor", out=oa, in0=ov, in1=rz1.unsqueeze(3).to_broadcast([128, nbk, 2, 128]),
                            op=ALU.mult, reads=ar + [R(sm)], writes=[R(self.OA)])
                    oaf = self.OA[:, 0:nqb, :]
                    obf = self.OB[:, 0:nqb, :]
                    self.op("pool", "tensor_tensor", out=oaf, in0=oaf, in1=self.O0N[:, 0:nqb, :], op=ALU.add,
                            reads=[R(self.OA), R(self.O0N)], writes=[R(self.OA)])
                    self.op("pool", "tensor_tensor", out=obf, in0=oaf, in1=oaf, op=ALU.mult,
                            reads=[R(self.OA)], writes=[R(self.OB)])
                    ss = sm[:, 8:8 + nqb]
                    self.op("dve", "tensor_reduce", out=ss, in_=obf, axis=AX.X, op=ALU.add,
                            reads=[R(self.OB)], writes=[R(sm)])
                    l1 = sm[:, 12:12 + nqb]
                    rs = sm[:, 16:16 + nqb]
                    self.op("act", "activation", out=l1, in_=ss, func=AF.Ln, bias=self.epsb[:, 1:2], scale=1.0 / 128.0,
                            reads=[R(sm), R(self.epsb)], writes=[R(sm)])
                    self.op("act", "activation", out=rs, in_=l1, func=AF.Exp, scale=-0.5,
                            reads=[R(sm)], writes=[R(sm)])
                    self.op("dve", "tensor_tensor", out=obf, in0=oaf, in1=rs.unsqueeze(2).to_broadcast([128, nqb, 128]),
                            op=ALU.mult, reads=[R(self.OA), R(sm)], writes=[R(self.OB)])
                    tbq = t0 // 128
                    self.op("pool", "tensor_tensor", out=OC3[:, tbq:tbq + nqb, :], in0=obf,
                            in1=self.GA[:, l, :].unsqueeze(1).to_broadcast([128, nqb, 128]), op=ALU.mult,
                            reads=[R(self.OB), R(self.GA)], writes=[(("GB", 6), gi)])
        if ada_l is not None:
            assert ada_loaded is None and not ada_units
            if h == 3:
                self.adaln_gs(ada_l)
        if stage < 4:
            return
        yield "attn"
        self.finish_chunk(l, h, OC3, ("GB", 6), OCT, ("GB", 7), need_ctx)

    def ret_tables(self, l, pp):
        C = self.C
        if True:
            lgf = self.LGC[:, l, 0, pp:pp + 1]
            lgb = self.LGC[:, l, 1, pp:pp + 1]
            self.op("act", "activation", out=self.TQF[pp][:], in_=C["i1"][:], func=AF.Exp, scale=lgf,
                    reads=[R(C["i1"]), R(self.LGC)], writes=[R(self.TQF[pp])])
            self.op("act", "activation", out=self.TQB[pp][:], in_=C["i2"][:], func=AF.Exp, scale=lgb,
                    reads=[R(C["i2"]), R(self.LGC)], writes=[R(self.TQB[pp])])
            self.op("act", "activation", out=self.CFB[pp][:, 0:1], in_=lgf, func=AF.Exp, scale=128.0,
                    reads=[R(self.LGC)], writes=[R(self.CFB[pp])])
            self.op("act", "activation", out=self.CFB[pp][:, 1:2], in_=lgb, func=AF.Exp, scale=128.0,
                    reads=[R(self.LGC)], writes=[R(self.CFB[pp])])
            for h2 in range(2):
                h = 2 * pp + h2
                e1 = self.tmpf[0][:, 0:128]
                e2 = self.tmpf[1][:, 0:128]
                self.op("dve", "tensor_scalar", out=e1, in0=C["r1"][:], scalar1=self.LG[:, l, h:h + 1], scalar2=None,
                        op0=ALU.mult, reads=[R(C["r1"]), R(self.LG)], writes=[R(self.tmpf[0])])
                self.op("dve", "scalar_tensor_tensor", out=e2, in0=C["r2"][:], scalar=self.LG[:, l, 4 + h:5 + h], in1=e1,
                        op0=ALU.mult, op1=ALU.add, reads=[R(C["r2"]), R(self.LG), R(self.tmpf[0])], writes=[R(self.tmpf[1])])
                self.op("act", "activation", out=self.DT[pp][:, h2 * 128:(h2 + 1) * 128], in_=e2, func=AF.Exp,
                        reads=[R(self.tmpf[1])], writes=[R(self.DT[pp])])
            sm = self.small
            self.op("act", "activation", out=sm[:, 20:22], in_=self.LG[:, l, 2 * pp:2 * pp + 2], func=AF.Exp,
                    scale=C["kcol"][:, 0:1], reads=[R(self.LG), R(C["kcol"])], writes=[R(sm)])
            self.op("act", "activation", out=sm[:, 22:24], in_=self.LG[:, l, 4 + 2 * pp:6 + 2 * pp], func=AF.Exp,
                    scale=C["kcol"][:, 1:2], reads=[R(self.LG), R(C["kcol"])], writes=[R(sm)])
            self.op("dve", "tensor_scalar", out=self.TK[pp][:], in0=sm[:, 20:24], scalar1=0.125, scalar2=None, op0=ALU.mult,
                    reads=[R(sm)], writes=[R(self.TK[pp])])

    def chunk_B(self, l, pp, need_ctx):
        src3 = self.win_d[l].rearrange("(kc p) n -> p kc n", p=128)
        wbuf, wres = self.wst_load(src3, [(1536 + pp * 128, 0, 128), (1792 + pp * 128, 128, 128),
                                          (2048 + pp * 128, 256, 128), (2304 + pp * 128, 384, 128)])
        names = [("GB", i) for i in range(8)]
        QT, KT, V, G, SF, SB, OC, OCT = [self.gb(i) for i in range(8)]
        qn, kn, vn, gn, sfn, sbn, ocn, octn = names
        V3 = V.rearrange("p (t c) -> p t c", c=130)
        G3 = G[:, 0:NT].rearrange("p (t c) -> p t c", c=128)
        SF3 = SF[:, 0:NT].rearrange("p (t c) -> p t c", c=128)
        SB3 = SB[:, 0:NT].rearrange("p (t c) -> p t c", c=128)
        OC3 = OC[:, 0:NT].rearrange("p (t c) -> p t c", c=128)
        identb = self.C["ident_b"]
        self.ret_tables(l, pp)
        self.proj_fm(wbuf, wres, 0, QT, qn, True, need_ctx)
        self.proj_fm(wbuf, wres, 128, KT, kn, True, True)
        self.proj_tm(wbuf, wres, 256,
                     lambda tb0, nb, ps: (V3[:, tb0:tb0 + nb, 0:128], ps[:, 0:nb * 128].rearrange("p (j c) -> p j c", c=128)),
                     vn)
        self.proj_tm(wbuf, wres, 384,
                     lambda tb0, nb, ps: (G3[:, tb0:tb0 + nb, :], ps[:, 0:nb * 128].rearrange("p (j c) -> p j c", c=128)),
                     gn, func=AF.Silu)
        orders = [[16, 17] + list(range(16)), [17, 16] + list(range(15, -1, -1))]
        sts_ = [self.stf, self.stb]
        ST3s = [SF3, SB3]
        stns = [sfn, sbn]
        for d in range(2):
            self.op("pool", "memset", ap=sts_[d][:], constant=0.0, writes=[R(sts_[d])])
        NS = len(orders[0])
        pks = {}
        pus = {}
        for i in range(NS + 3):
            for d in range(2):
                if i < NS:
                    tb = orders[d][i]
                    gi = min(tb // 4, 4)
                    pk = self.gp()
                    pkb = pk[:].bitcast(BF16)
                    self.tr(pkb[:, 0:128], KT[:, tb * 128:(tb + 1) * 128], identb[:], reads=[(kn, gi), R(identb)], writes=[R(pk)])
                    pks[(d, i)] = (pk, pkb)
                if 0 <= i - 1 < NS:
                    pk, pkb = pks.pop((d, i - 1))
                    kf = self.kfb[2 * d + (i - 1) % 2]
                    self.op("dve", "tensor_tensor", out=kf[:].rearrange("p (h c) -> p h c", h=2),
                            in0=pkb[:, 0:128].rearrange("p (h c) -> p h c", h=2),
                            in1=self.TK[pp][:, 2 * d:2 * d + 2].unsqueeze(2).to_broadcast([128, 2, 64]), op=ALU.mult,
                            reads=[R(pk), R(self.TK[pp])], writes=[R(kf)])
                if 0 <= i - 2 < NS:
                    tb = orders[d][i - 2]
                    gi = min(tb // 4, 4)
                    kf = self.kfb[2 * d + (i - 2) % 2]
                    pu, pures = self.ab()
                    self.mm(pu[:, 0:128], kf[:], V3[:, tb, 0:128], reads=[R(kf), (vn, gi)], writes=[pures])
                    pus[(d, i - 2)] = (pu, pures)
                if 0 <= i - 3 < NS:
                    tb = orders[d][i - 3]
                    gi = min(tb // 4, 4)
                    st = sts_[d]
                    pu, pures = pus.pop((d, i - 3))
                    self.op("pool", "tensor_copy", out=ST3s[d][:, tb, :], in_=st[:], reads=[R(st)], writes=[(stns[d], gi)])
                    self.op("dve", "scalar_tensor_tensor", out=st[:], in0=st[:], scalar=self.CFB[pp][:, d:d + 1], in1=pu[:, 0:128],
                            op0=ALU.mult, op1=ALU.add, reads=[R(st), R(self.CFB[pp]), pures], writes=[R(st)])
        ntb = NTB if need_ctx else 16
        sm = self.small
        cur = {}

        def front(tb):
            gi = min(tb // 4, 4)
            ts_ = slice(tb * 128, (tb + 1) * 128)
            sts = [self.gp(), self.gp()]
            for h2 in range(2):
                hs = slice(h2 * 64, (h2 + 1) * 64)
                self.mm(sts[h2][:, 0:128], KT[hs, ts_], QT[hs, ts_], reads=[(kn, gi), (qn, gi)], writes=[R(sts[h2])])
            pt = self.ptb()
            for h2 in range(2):
                self.op("dve", "scalar_tensor_tensor", out=pt[:, h2 * 128:(h2 + 1) * 128], in0=sts[h2][:, 0:128], scalar=0.125,
                        in1=self.DT[pp][:, h2 * 128:(h2 + 1) * 128],
                        op0=ALU.mult, op1=ALU.mult, reads=[R(sts[h2]), R(self.DT[pp])], writes=[R(pt)])
            qpool = self.qfb + self.kfb
            qf = qpool[(2 * tb) % 8]
            qb_ = qpool[(2 * tb + 1) % 8]
            self.op("pool", "tensor_tensor", out=qf[:], in0=QT[:, ts_], in1=self.TQF[pp][:], op=ALU.mult,
                    reads=[(qn, gi), R(self.TQF[pp])], writes=[R(qf)])
            self.op("pool", "tensor_tensor", out=qb_[:], in0=QT[:, ts_], in1=self.TQB[pp][:], op=ALU.mult,
                    reads=[(qn, gi), R(self.TQB[pp])], writes=[R(qb_)])
            return (pt, qf, qb_)

        def back(tb, pt, qf, qb_):
            gi = min(tb // 4, 4)
            jj = tb % 4
            if jj == 0:
                cur["po"], cur["pres"] = self.ab()
            po, pres = cur["po"], cur["pres"]
            for h2 in range(2):
                hs = slice(h2 * 64, (h2 + 1) * 64)
                oreg = po[:, jj * 128 + h2 * 64:jj * 128 + (h2 + 1) * 64]
                self.mm(oreg, qf[hs, :], SF3[hs, tb, hs], start=(jj == 0 and h2 == 0), stop=False,
                        reads=[R(qf), (sfn, gi)], writes=[pres], sgc=True)
                self.mm(oreg, qb_[hs, :], SB3[hs, tb, hs], start=False, stop=False,
                        reads=[R(qb_), (sbn, gi)], writes=[pres], sgc=True)
                self.mm(oreg, pt[:, h2 * 128:(h2 + 1) * 128], V3[:, tb, hs], start=False, stop=True,
                        reads=[R(pt), (vn, gi)], writes=[pres], sgc=True)
            last = (jj == 3) or (tb == ntb - 1)
            if not last:
                return
            tb0 = tb - jj
            nb = jj + 1
            ng = 2 * nb
            W = nb * 128
            v3 = lambda a: a[:, 0:W].rearrange("p (g c) -> p g c", c=64)
            osb = self.OA[:].rearrange("p a b -> p (a b)")
            ocn_ = self.OB[:].rearrange("p a b -> p (a b)")
            sqv = self.O0N[:].rearrange("p a b -> p (a b)")
            self.op("dve", "tensor_copy", out=osb[:, 0:W], in_=po[:, 0:W], reads=[pres], writes=[R(self.OA)])
            self.op("dve", "tensor_reduce", out=sm[:, 24:24 + ng], in_=v3(osb), axis=AX.X, op=ALU.add,
                    reads=[R(self.OA)], writes=[R(sm)])
            self.op("dve", "tensor_scalar", out=sm[:, 24:24 + ng], in0=sm[:, 24:24 + ng], scalar1=1.0 / 64.0, scalar2=None, op0=ALU.mult,
                    reads=[R(sm)], writes=[R(sm)])
            self.op("dve", "tensor_tensor", out=v3(ocn_), in0=v3(osb), in1=sm[:, 24:24 + ng].unsqueeze(2).to_broadcast([128, ng, 64]),
                    op=ALU.subtract, reads=[R(self.OA), R(sm)], writes=[R(self.OB)])
            self.op("pool", "tensor_tensor", out=sqv[:, 0:W], in0=ocn_[:, 0:W], in1=ocn_[:, 0:W], op=ALU.mult,
                    reads=[R(self.OB)], writes=[R(self.O0N)])
            self.op("dve", "tensor_reduce", out=sm[:, 44:44 + ng], in_=v3(sqv), axis=AX.X, op=ALU.add,
                    reads=[R(self.O0N)], writes=[R(sm)])
            self.op("act", "activation", out=sm[:, 52:52 + ng], in_=sm[:, 44:44 + ng], func=AF.Ln, bias=self.epsb[:, 1:2], scale=1.0 / 64.0,
                    reads=[R(sm), R(self.epsb)], writes=[R(sm)])
            self.op("act", "activation", out=sm[:, 24:24 + ng], in_=sm[:, 52:52 + ng], func=AF.Exp, scale=-0.5,
                    reads=[R(sm)], writes=[R(sm)])
            self.op("dve", "tensor_tensor", out=v3(osb), in0=v3(ocn_), in1=sm[:, 24:24 + ng].unsqueeze(2).to_broadcast([128, ng, 64]),
                    op=ALU.mult, reads=[R(self.OB), R(sm)], writes=[R(self.OA)])
            self.op("pool", "tensor_tensor", out=OC3[:, tb0:tb0 + nb, :], in0=osb[:, 0:W].rearrange("p (t c) -> p t c", c=128),
                    in1=G3[:, tb0:tb0 + nb, :], op=ALU.mult,
                    reads=[R(self.OA), (gn, gi)], writes=[(ocn, gi)])

        pend = []
        for tb in range(ntb):
            fr = front(tb)
            pend.append((tb,) + fr)
            if len(pend) > 3:
                back(*pend.pop(0))
        while pend:
            back(*pend.pop(0))
        self.finish_chunk(l, 4 + pp, OC3, ocn, OCT, octn, need_ctx)

    def chunk_C(self, l, need_ctx):
        src3 = self.win_d[l].rearrange("(kc p) n -> p kc n", p=128)
        wbuf, wres = self.wst_load(src3, [(2560, 0, 256), (2816, 256, 64), (2816, 320, 64), (2880, 384, 64), (2880, 448, 64)])
        wbufv, wresv = self.wst_load(src3, [(2944, 0, 128)])
        QZ = self.GBT[:, 0:2 * NT].rearrange("p (g t) -> p g t", g=2)
        qzn = "QZ"
        gb01 = [(("GB", i), g_) for i in range(2) for g_ in range(5)]
        KT, V = self.gb(2), self.gb(3)
        kn, vn = ("GB", 2), ("GB", 3)
        OC, OCT = self.gb(6), self.gb(7)
        ocn, octn = ("GB", 6), ("GB", 7)
        V4 = V.rearrange("p (t j c) -> p t j c", j=2, c=65)
        OC3 = OC[:, 0:NT].rearrange("p (t c) -> p t c", c=128)
        allg = list(range(5))
        self.op("pool", "memset", ap=V4[:, :, :, 64:65], constant=1.0, writes=[(vn, g) for g in allg])
        self.op("pool", "memset", ap=QZ[64:128, 0, :], constant=0.0, writes=gb01 + [(qzn, g) for g in allg])
        self.op("pool", "memset", ap=QZ[0:64, 1, :], constant=0.0, writes=gb01 + [(qzn, g) for g in allg])
        self.proj_tm(wbufv, wresv, 0,
                     lambda tb0, nb, ps: (V4[:, tb0:tb0 + nb, :, 0:64],
                                          ps[:, 0:nb * 128].rearrange("p (t j c) -> p t j c", j=2, c=64)),
                     vn)
        ntb = NTB if need_ctx else 16
        sm = self.small
        qz_fn = lambda t0, n: [(QZ[0:64, 0, t0:t0 + n], slice(0, 64)), (QZ[64:128, 1, t0:t0 + n], slice(64, 128))]
        for j in range(2):
            self.proj_fm(wbuf, wres, j * 128, None, qzn, True, need_ctx, dst_fn=qz_fn)
            self.proj_fm(wbuf, wres, 256 + j * 128, KT, kn, True, True)
            items = []
            for n_ in range(ntb):
                if n_ < 16:
                    kbs = ([n_ - 1] if n_ > 0 else []) + [n_] + ([n_ + 1] if n_ < 15 else []) + [16, 17]
                else:
                    kbs = [16, 17]
                for ki, m in enumerate(kbs):
                    items.append((n_, ki, m, len(kbs)))
            cur = {}

            def front(n_, ki, m, nk):
                gi = min(n_ // 4, 4)
                mgi = min(m // 4, 4)
                st = self.gp()
                self.mm(st[:, 0:256], KT[:, m * 128:(m + 1) * 128], QZ[:, :, n_ * 128:(n_ + 1) * 128],
                        reads=[(kn, mgi), (qzn, gi)], writes=[R(st)])
                pt = self.ptb()
                self.op("act", "activation", out=pt[:, 0:256], in_=st[:, 0:256], func=AF.Exp, scale=0.125,
                        reads=[R(st)], writes=[R(pt)])
                if n_ < 16 and m == n_ - 1:
                    self.op("pool", "tensor_tensor", out=pt[:, 0:256], in0=pt[:, 0:256], in1=self.C["maskp"][:], op=ALU.mult,
                            reads=[R(pt), R(self.C["maskp"])], writes=[R(pt)])
                if n_ < 15 and m == n_ + 1:
                    self.op("pool", "tensor_tensor", out=pt[:, 0:256], in0=pt[:, 0:256], in1=self.C["maskn"][:], op=ALU.mult,
                            reads=[R(pt), R(self.C["maskn"])], writes=[R(pt)])
                return pt

            def back(n_, ki, m, nk, pt):
                mgi = min(m // 4, 4)
                r3 = n_ % 3
                if r3 == 0 and ki == 0:
                    cur["po"], cur["pres"] = self.ab()
                    cur["n0"] = n_
                po, pres = cur["po"], cur["pres"]
                for g in range(2):
                    off = r3 * 130 + g * 65
                    self.mm(po[:, off:off + 65], pt[:, g * 128:(g + 1) * 128], V4[:, m, j, 0:65],
                            start=(r3 == 0 and ki == 0 and g == 0), stop=(ki == nk - 1), reads=[R(pt), (vn, mgi)], writes=[pres],
                            sgc=True)
                if ki == nk - 1 and (r3 == 2 or n_ == ntb - 1):
                    n0 = cur["n0"]
                    cnt = n_ - n0 + 1
                    pov = po[:, 0:cnt * 130].rearrange("p (n g c) -> p n g c", g=2, c=65)
                    den = sm[:, 34:34 + 2 * cnt].rearrange("p (n g) -> p n g", g=2)
                    rz = sm[:, 40:40 + 2 * cnt].rearrange("p (n g) -> p n g", g=2)
                    self.op("dve", "tensor_tensor", out=den, in0=pov[:, :, :, 64],
                            in1=self.ESINK[:, l, 2 * j:2 * j + 2].unsqueeze(1).to_broadcast([128, cnt, 2]), op=ALU.add,
                            reads=[pres, R(self.ESINK)], writes=[R(sm)])
                    self.op("dve", "reciprocal", out=rz, in_=den, reads=[R(sm)], writes=[R(sm)])
                    gis = sorted(set(min(q // 4, 4) for q in range(n0, n_ + 1)))
                    self.op("dve", "tensor_tensor", out=OC3[:, n0:n_ + 1, :].rearrange("p n (g c) -> p n g c", g=2),
                            in0=pov[:, :, :, 0:64], in1=rz.unsqueeze(3).to_broadcast([128, cnt, 2, 64]), op=ALU.mult,
                            reads=[pres, R(sm)], writes=[(ocn, g_) for g_ in gis])

            pend = []
            for it in items:
                pt = front(*it)
                pend.append(it + (pt,))
                if len(pend) > 3:
                    back(*pend.pop(0))
            while pend:
                back(*pend.pop(0))
            self.finish_chunk(l, 6 + j, OC3, ocn, OCT, octn, need_ctx)

    def mlp(self, l, need_ctx):
        self.drain()
        w1v = self.w1_d[l].rearrange("(kc p) n -> p kc n", p=128)
        w2v = self.w2_d[l].rearrange("(fc p) n -> p fc n", p=128)

        def load_fb(fb):
            W1, w1res = self.wst_load(w1v, [(fb * 512, 0, 512)])
            i2 = fb % 2
            W2 = self.GBT[:, i2 * 2 * GBW:i2 * 2 * GBW + 4096].rearrange("p (f n) -> p f n", f=4)
            w2res = [("GB", 2 * i2), ("GB", 2 * i2 + 1)]
            self.dma(W2, w2v[:, fb * 4:(fb + 1) * 4, :], writes=[(r, g) for r in w2res for g in range(5)],
                     key="w2_%d" % i2, eng="pool")
            return W1, w1res, W2, w2res

        pre = load_fb(0)
        self.norm(l, 1, skip_ctx=not need_ctx)
        pending = None
        ucount = 0

        def mlp2(W2, w2res, AT, ares, gi, t0, n, s):
            for dc in range(KC):
                ps2, pres = self.ab()
                for fc in range(4):
                    self.mm(ps2[:, :n], W2[:, fc, dc * 128:(dc + 1) * 128], AT[:, fc, :n], start=(fc == 0), stop=(fc == 3),
                            reads=[(w2res[0], 0), (w2res[1], 0), ares], writes=[pres])
                self.op("dve", "scalar_tensor_tensor", out=self.XT[:, dc, t0:t0 + n], in0=ps2[:, :n],
                        scalar=self.MOD[:, l, 40 + dc, s:s + 1], in1=self.XT[:, dc, t0:t0 + n], op0=ALU.mult, op1=ALU.add,
                        reads=[pres, ("MOD", l), ("XT", dc, gi)], writes=[("XT", dc, gi)])

        for fb in range(8):
            W1, w1res, W2, w2res = pre if fb == 0 else load_fb(fb)
            for gi, (t0, n) in enumerate(TGS):
                if gi == 4 and not need_ctx:
                    continue
                s = 0 if gi < 4 else 1
                ai = 4 + (ucount % 2)
                ucount += 1
                AT = self.gb(ai)[:, 0:2048].rearrange("p (f n) -> p f n", f=4)
                ares = (("GB", ai), 0)
                aresw = [(("GB", ai), g_) for g_ in range(5)]
                for fc in range(4):
                    ps = self.gp()
                    for kc in range(KC):
                        self.mm(ps[:, :n], W1[:, kc, fc * 128:(fc + 1) * 128], self.HT[:, kc, t0:t0 + n],
                                start=(kc == 0), stop=(kc == KC - 1), reads=[w1res, ("HT", kc, gi)], writes=[R(ps)])
                    rt = self.tmpf[fc % 4]
                    self.op("act", "activation", out=rt[:, :n], in_=ps[:, :n], func=AF.Relu, reads=[R(ps)], writes=[R(rt)])
                    self.op("pool", "tensor_tensor", out=AT[:, fc, :n], in0=rt[:, :n], in1=rt[:, :n], op=ALU.mult,
                            reads=[R(rt)], writes=aresw)
                if pending is not None:
                    mlp2(*pending)
                pending = (W2, w2res, AT, ares, gi, t0, n, s)
        mlp2(*pending)

    def final_norm(self):
        XT, identf = self.XT, self.C["ident_f"]
        gfin = self.P["g_final"]
        YT = self.HT[:].rearrange("p k t -> p (k t)")[:, 0:8192].bitcast(F32).rearrange("p (k t) -> p k t", k=KC)
        ytres = [("HT", kc, g) for kc in range(KC) for g in range(5)]
        for g in range(4):
            t0 = g * 512
            acc = self.gp()
            self.sumsq_rstd(g, acc)
            for kc in range(KC):
                self.op("dve", "scalar_tensor_tensor", out=YT[:, kc, :], in0=XT[:, kc, t0:t0 + 512], scalar=gfin[:, kc:kc + 1],
                        in1=self.rstd[:], op0=ALU.mult, op1=ALU.mult,
                        reads=[("XT", kc, g), R(self.rstd), R(gfin)], writes=(ytres if kc == 0 else []) + [("YT", kc)])
            for j in range(4):
                tb = g * 4 + j
                ob = self.iobuf[tb % 2]
                obres = self.iores[tb % 2]
                for half in range(2):
                    pt = self.gp()
                    for jj in range(4):
                        kc = half * 4 + jj
                        self.tr(pt[:, jj * 128:(jj + 1) * 128], YT[:, kc, j * 128:(j + 1) * 128], identf[:],
                                reads=[("YT", kc), ("HT", 0, 0), R(identf)], writes=[R(pt)])
                    self.op("act", "mul", out=ob[:, half * 512:(half + 1) * 512], in_=pt[:], mul=32.0,
                            reads=[R(pt)], writes=obres)
                self.dma(self.out_d[tb * 128:(tb + 1) * 128, :], ob, reads=obres, key="io%d" % (tb % 2))


_CACHE = {}


def kernel(**inputs):
    cfg = inputs.pop("_cfg", {}) if "_cfg" in inputs else {}
    inputs = {k: np.asarray(v) for k, v in inputs.items()}
    key = tuple(sorted(cfg.items()))
    if key not in _CACHE:
        b = Builder(cfg)
        _CACHE[key] = (b.build(), b)
    nc, b = _CACHE[key]
    in_maps = [prep_inputs(inputs, i, b.depth) for i in range(8)]
    res = run_bass_kernel_spmd(nc, in_maps, core_ids=list(range(8)))
    if cfg.get("_ret_all"):
        return res
    out = np.stack([np.asarray(r["out"]) for r in res.results], axis=0)
    return out.astype(np.float32)
```
